# Optimizing a Trainium2 kernel written in Bass

```python
import jax, jax.numpy as jnp
from jax import lax
import numpy as np

D_MODEL = 1024
BATCH = 8
SEQ = 4096
DEPTH = 4

N_HEADS = 16
HEAD_DIM = D_MODEL // N_HEADS
D_FF = 2816
CONV_WIDTH = 3
Q_BLOCK = 128
NORM_EPS = 1e-6
ATTN_SCALE = HEAD_DIM ** -0.5
N_MIXERS = 2
N_FOX = (DEPTH + 1) // 2
N_SB = DEPTH // 2

kernel_name = "hybrid_fox_stickbreaking_convffn"


def rms_norm(x, g):
    xf = x.astype(jnp.float32)
    y = xf * lax.rsqrt(jnp.mean(xf * xf, axis=-1, keepdims=True) + NORM_EPS)
    return (y * g.astype(jnp.float32)).astype(x.dtype)


def split_heads(t):
    b, s, _ = t.shape
    return t.reshape(b, s, N_HEADS, HEAD_DIM).transpose(0, 2, 1, 3)


def merge_heads(t):
    b, h, s, dh = t.shape
    return t.transpose(0, 2, 1, 3).reshape(b, s, h * dh)


def sweep_query_blocks(block_fn, n_blocks, b, s):
    out = lax.map(block_fn, jnp.arange(n_blocks))
    return out.transpose(1, 2, 0, 3, 4).reshape(b, N_HEADS, s, HEAD_DIM)


def fox_mixer(xn, w_qkvf, b_f, w_o):
    b, s, _ = xn.shape
    proj = xn @ w_qkvf
    q = split_heads(proj[..., :D_MODEL]).astype(jnp.float32)
    k = split_heads(proj[..., D_MODEL:2 * D_MODEL]).astype(jnp.float32)
    v = split_heads(proj[..., 2 * D_MODEL:3 * D_MODEL]).astype(jnp.float32)
    f_logit = (proj[..., 3 * D_MODEL:] + b_f).astype(jnp.float32)
    c = jnp.cumsum(jax.nn.log_sigmoid(f_logit), axis=1).transpose(0, 2, 1)
    key_pos = jnp.arange(s)

    def block(i):
        start = i * Q_BLOCK
        qb = lax.dynamic_slice_in_dim(q, start, Q_BLOCK, axis=2)
        cq = lax.dynamic_slice_in_dim(c, start, Q_BLOCK, axis=2)
        q_pos = start + jnp.arange(Q_BLOCK)
        logits = (jnp.einsum('bhqd,bhkd->bhqk', qb, k) * ATTN_SCALE
                  + cq[..., :, None] - c[..., None, :])
        mask = key_pos[None, :] <= q_pos[:, None]
        logits = jnp.where(mask, logits, -jnp.inf)
        p = jax.nn.softmax(logits, axis=-1)
        return jnp.einsum('bhqk,bhkd->bhqd', p, v)

    o = sweep_query_blocks(block, s // Q_BLOCK, b, s)
    return merge_heads(o).astype(xn.dtype) @ w_o


def stick_breaking_mixer(xn, w_qkv, w_o):
    b, s, _ = xn.shape
    proj = xn @ w_qkv
    q = split_heads(proj[..., :D_MODEL]).astype(jnp.float32)
    k = split_heads(proj[..., D_MODEL:2 * D_MODEL]).astype(jnp.float32)
    v = split_heads(proj[..., 2 * D_MODEL:]).astype(jnp.float32)
    key_pos = jnp.arange(s)

    def block(i):
        start = i * Q_BLOCK
        qb = lax.dynamic_slice_in_dim(q, start, Q_BLOCK, axis=2)
        q_pos = start + jnp.arange(Q_BLOCK)
        z = jnp.einsum('bhqd,bhkd->bhqk', qb, k) * ATTN_SCALE
        mask = key_pos[None, :] < q_pos[:, None]
        log_not_beta = jnp.where(mask, jax.nn.log_sigmoid(-z), 0.0)
        tail = lax.cumsum(log_not_beta, axis=3, reverse=True) - log_not_beta
        log_a = jax.nn.log_sigmoid(z) + tail
        a = jnp.where(mask, jnp.exp(log_a), 0.0)
        return jnp.einsum('bhqk,bhkd->bhqd', a, v)

    o = sweep_query_blocks(block, s // Q_BLOCK, b, s)
    return merge_heads(o).astype(xn.dtype) @ w_o


def conv_ffn(xn, w_up, w_conv, b_conv, w_down):
    s = xn.shape[1]
    h = xn @ w_up
    h_pad = jnp.pad(h, ((0, 0), (CONV_WIDTH - 1, 0), (0, 0)))
    hc = b_conv + sum(w_conv[j] * h_pad[:, j:j + s, :] for j in range(CONV_WIDTH))
    u, g = hc[..., :D_FF], hc[..., D_FF:]
    return (jax.nn.silu(g) * u) @ w_down


def setup_inputs(seed: int = 0) -> dict:
    key = jax.random.key(seed)
    ks = jax.random.split(key, 16)
    f32 = jnp.float32
    d, h, f = D_MODEL, N_HEADS, D_FF
    x = jax.random.normal(ks[0], (BATCH, SEQ, d), f32)
    attn_norm = 1.0 + 0.02 * jax.random.normal(ks[1], (DEPTH, d), f32)
    ffn_norm = 1.0 + 0.02 * jax.random.normal(ks[2], (DEPTH, d), f32)
    final_norm = 1.0 + 0.02 * jax.random.normal(ks[3], (d,), f32)
    fox_w_qkvf = jax.random.normal(ks[4], (N_FOX, d, 3 * d + h), f32) * d ** -0.5
    fox_b_f = 3.0 + 0.5 * jax.random.normal(ks[5], (N_FOX, h), f32)
    fox_w_o = jax.random.normal(ks[6], (N_FOX, d, d), f32) * d ** -0.5
    sb_w_qkv = jax.random.normal(ks[7], (N_SB, d, 3 * d), f32) * d ** -0.5
    sb_w_o = jax.random.normal(ks[8], (N_SB, d, d), f32) * d ** -0.5
    ffn_w_up = jax.random.normal(ks[9], (DEPTH, d, 2 * f), f32) * d ** -0.5
    ffn_w_conv = jax.random.normal(ks[10], (DEPTH, CONV_WIDTH, 2 * f), f32) * CONV_WIDTH ** -0.5
    ffn_b_conv = 0.01 * jax.random.normal(ks[11], (DEPTH, 2 * f), f32)
    ffn_w_down = jax.random.normal(ks[12], (DEPTH, f, d), f32) * f ** -0.5
    return {"x": x, "attn_norm": attn_norm, "ffn_norm": ffn_norm, "final_norm": final_norm,
            "fox_w_qkvf": fox_w_qkvf, "fox_b_f": fox_b_f, "fox_w_o": fox_w_o,
            "sb_w_qkv": sb_w_qkv, "sb_w_o": sb_w_o,
            "ffn_w_up": ffn_w_up, "ffn_w_conv": ffn_w_conv, "ffn_b_conv": ffn_b_conv,
            "ffn_w_down": ffn_w_down}


def reference(x, attn_norm, ffn_norm, final_norm, fox_w_qkvf, fox_b_f, fox_w_o,
              sb_w_qkv, sb_w_o, ffn_w_up, ffn_w_conv, ffn_b_conv, ffn_w_down):
    hs = x
    for i in range(DEPTH):
        xn = rms_norm(hs, attn_norm[i])
        if i % N_MIXERS == 0:
            j = i // N_MIXERS
            hs = hs + fox_mixer(xn, fox_w_qkvf[j], fox_b_f[j], fox_w_o[j])
        else:
            j = i // N_MIXERS
            hs = hs + stick_breaking_mixer(xn, sb_w_qkv[j], sb_w_o[j])
        xn = rms_norm(hs, ffn_norm[i])
        hs = hs + conv_ffn(xn, ffn_w_up[i], ffn_w_conv[i], ffn_b_conv[i], ffn_w_down[i])
    return rms_norm(hs, final_norm)
```

```python
import contextlib
import numpy as np
import concourse.bass as bass
import concourse.mybir as mybir
from concourse.bass_utils import run_bass_kernel_spmd

F32 = mybir.dt.float32
BF16 = mybir.dt.bfloat16
AF = mybir.ActivationFunctionType
ALU = mybir.AluOpType

D = 1024
KC = 8
TG = 512
H = 16
DH = 64
FF = 2816
FC = 22
DEPTH = 4
EPS = 1e-6
NEG = -30000.0
VS = 193

COMPUTE = ("pe", "act", "dve", "pool")


class _Op:
    __slots__ = ("eng", "fn", "deps", "dma_key", "dma_val", "signal", "count", "idx", "dma_waits")


class Tracker:
    def __init__(self):
        self.ops = []
        self.last_w = {}
        self.readers = {}
        self.dma_cum = {}
        self.fence_idx = None
        self.last_eng = {}
        self.last_dma = {}

    def fence(self, fn):
        deps = set(self.last_eng.values()) | set(self.last_dma.values())
        idx = self.add("pool", fn, extra_deps=deps)
        self.fence_idx = idx
        return idx

    def add(self, eng, fn, reads=(), writes=(), dma=None, extra_deps=()):
        op = _Op()
        op.eng = eng
        op.fn = fn
        op.dma_key = dma
        op.signal = False
        op.count = 0
        op.idx = len(self.ops)
        deps = set()
        for r in reads:
            w = self.last_w.get(r)
            if w is not None:
                deps.add(w)
        for r in writes:
            w = self.last_w.get(r)
            if w is not None:
                deps.add(w)
            for rd in self.readers.get(r, ()):
                deps.add(rd)
        deps |= set(extra_deps)
        if self.fence_idx is not None:
            deps.add(self.fence_idx)
        deps.discard(op.idx)
        op.deps = deps
        op.dma_waits = {}
        for d in deps:
            k = self.ops[d].dma_key
            if k is not None:
                op.dma_waits[k] = self.dma_cum[k]
        if dma is not None:
            self.last_dma[dma] = op.idx
        else:
            self.last_eng[eng] = op.idx
        if dma is not None:
            self.dma_cum[dma] = self.dma_cum.get(dma, 0) + 16
            op.dma_val = self.dma_cum[dma]
        else:
            op.dma_val = 0
        self.ops.append(op)
        for r in writes:
            self.last_w[r] = op.idx
            self.readers[r] = []
        for r in reads:
            if r not in writes:
                self.readers.setdefault(r, []).append(op.idx)
        return op.idx

    def dma_keys(self):
        return list(self.dma_cum.keys())

    def emit(self, nc, block, sems, dma_sems):
        ops = self.ops
        for op in ops:
            for d in op.deps:
                dop = ops[d]
                if dop.dma_key is not None:
                    continue
                if dop.eng == op.eng and op.eng in ("pe", "sp"):
                    continue
                dop.signal = True
        cnt = {e: 0 for e in COMPUTE + ("sp",)}
        for op in ops:
            if op.dma_key is None and op.signal:
                cnt[op.eng] += 1
                op.count = cnt[op.eng]
        per_eng = {e: [] for e in COMPUTE + ("sp",)}
        for op in ops:
            per_eng[op.eng].append(op)

        def run(eng_name, eng):
            waited = {}
            for op in per_eng[eng_name]:
                wl = {}
                for d in op.deps:
                    dop = ops[d]
                    if dop.dma_key is not None:
                        k = ("dma", dop.dma_key)
                        v = op.dma_waits[dop.dma_key]
                    else:
                        if dop.eng == eng_name and eng_name in ("pe", "sp"):
                            continue
                        k = ("eng", dop.eng)
                        v = dop.count
                    if v > wl.get(k, 0):
                        wl[k] = v
                for k, v in wl.items():
                    if waited.get(k, 0) >= v:
                        continue
                    waited[k] = v
                    s = dma_sems[k[1]] if k[0] == "dma" else sems[k[1]]
                    eng.wait_ge(s, v)
                ins = op.fn(eng)
                if op.dma_key is not None:
                    ins.then_inc(dma_sems[op.dma_key], 16)
                elif op.signal:
                    ins.then_inc(sems[op.eng], 1)

        @block.sync
        def _(e):
            run("sp", e)

        @block.scalar
        def _(e):
            run("act", e)

        @block.vector
        def _(e):
            run("dve", e)

        @block.gpsimd
        def _(e):
            run("pool", e)

        @block.tensor
        def _(e):
            run("pe", e)


class Arena:
    def __init__(self, t, nbytes):
        self.t = t
        self.nbytes = nbytes
        self.off = 0

    def reset(self):
        self.off = 0

    def f32(self, n):
        assert self.off % 4 == 0
        a = self.off // 4
        self.off += 4 * n
        assert self.off <= self.nbytes, ("arena overflow", self.off, self.nbytes)
        return self.t[:, a:a + n]

    def bf16(self, n):
        n2 = (n + 1) // 2
        v = self.f32(n2)
        return v.bitcast(BF16)[:, 0:n]


def build_program(S, layers, final=True):
    NTG = S // TG
    NT = S // 128
    NV = DEPTH * VS + 8
    nc = bass.Bass("TRN2", target_bir_lowering=False)

    def din(name, shape, dt=F32):
        return nc.dram_tensor(name, list(shape), dt, kind="ExternalInput").ap()

    x_d = din("x", [S, D])
    vec_d = din("vecs", [128, NV])
    cb_d = din("cb", [128, 6 * 128])
    id_d = din("ident", [128, 128])
    wqkv_d = [din("wqkv%d" % l, [H, 128, KC * 192]) for l in range(DEPTH)]
    wf_d = [din("wf%d" % j, [128, KC * 96]) for j in range(2)]
    wo_d = [din("wo%d" % l, [128, 8 * D]) for l in range(DEPTH)]
    wup_d = [din("wup%d" % l, [FC, 128, KC * 256]) for l in range(DEPTH)]
    wdn_d = [din("wdn%d" % l, [KC, 128, FC * 128]) for l in range(DEPTH)]
    out_d = nc.dram_tensor("out", [S, D], F32, kind="ExternalOutput").ap()
    wup_s = [nc.dram_tensor("wup_s%d" % l, [FC, 128, KC * 256], BF16, kind="Internal").ap() for l in range(DEPTH)]
    wdn_s = [nc.dram_tensor("wdn_s%d" % l, [KC, 128, FC * 128], BF16, kind="Internal").ap() for l in range(DEPTH)]
    ot_s = nc.dram_tensor("ot_s", [D, S], BF16, kind="Internal").ap()
    caug_s = nc.dram_tensor("caug_s", [3, H, S], BF16, kind="Internal").ap()

    T = Tracker()

    with contextlib.ExitStack() as st:
        hsT = st.enter_context(nc.sbuf_tensor("hsT", [128, KC, S], F32))
        vecs = st.enter_context(nc.sbuf_tensor("vecs_sb", [128, NV], F32))
        cb = st.enter_context(nc.sbuf_tensor("cb_sb", [128, 6 * 128], BF16))
        id32 = st.enter_context(nc.sbuf_tensor("id32", [128, 128], F32))
        ones32 = st.enter_context(nc.sbuf_tensor("ones32", [128, 128], F32))
        fsc = st.enter_context(nc.sbuf_tensor("fsc", [128, 8], F32))
        one1 = st.enter_context(nc.sbuf_tensor("one1", [128, 64], F32))
        AR_BYTES = 75600
        art = st.enter_context(nc.sbuf_tensor("arena", [128, AR_BYTES // 4], F32))
        ar = Arena(art, AR_BYTES)
        ps = [st.enter_context(nc.psum_tensor("ps%d" % i, [128, 512], F32)) for i in range(8)]

        identb = cb[:, 0:128]
        maskF = cb[:, 128:256]
        maskS = cb[:, 256:384]
        NTm = cb[:, 384:512]
        NOm = cb[:, 512:640]
        ones64 = cb[:, 640:704]

        def vcol(l, k):
            o = l * VS + k
            return vecs[:, o:o + 1]
        FIN = DEPTH * VS

        def fence():
            T.fence(lambda e: e.memset(fsc[:, 0:1], 0.0))

        T.add("sp", lambda e: e.dma_start(out=vecs[:], in_=vec_d), writes=["vecs"], dma="vecs")
        T.add("sp", lambda e: e.dma_start(out=id32[:], in_=id_d), writes=["id32"], dma="id32")
        T.add("pool", lambda e: e.dma_start(out=cb[:], in_=cb_d), writes=["cb"], dma="cb")
        T.add("pool", lambda e: e.memset(ones32[:], 1.0 / D), writes=["ones32"])
        T.add("pool", lambda e: e.memset(one1[:], 1.0), writes=["one1"])

        def cast_ffn_weights(l):
            for f in range(FC):
                T.add("pool", lambda e, f=f: e.dma_start(out=wup_s[l][f], in_=wup_d[l][f]),
                      writes=["wup_s%d.%d" % (l, f)], dma="wups")
            for co in range(KC):
                T.add("pool", lambda e, co=co: e.dma_start(out=wdn_s[l][co], in_=wdn_d[l][co]),
                      writes=["wdn_s%d.%d" % (l, co)], dma="wdns")

        ar.reset()
        xin = [ar.f32(D) for _ in range(2)]
        for t in range(NT):
            xb = xin[t % 2]
            T.add("sp", lambda e, t=t, xb=xb: e.dma_start(out=xb, in_=x_d[t * 128:(t + 1) * 128, :]),
                  writes=["xin%d" % (t % 2)], dma="xin%d" % (t % 2))
            for half in range(2):
                pb = ps[(2 * t + half) % 4]
                pbn = "ps%d" % ((2 * t + half) % 4)

                def tr(e, xb=xb, pb=pb, half=half):
                    r = None
                    for q in range(4):
                        c = half * 4 + q
                        r = e.transpose(pb[:, q * 128:(q + 1) * 128], xb[:, c * 128:(c + 1) * 128], id32[:])
                    return r
                T.add("pe", tr, reads=["xin%d" % (t % 2), "id32"], writes=[pbn])
                tgi = t // 4
                wr = ["hsw.%d.%d.%d" % (half * 4 + q, tgi, t % 4) for q in range(4)]
                if half == 0:
                    T.add("dve", lambda e, pb=pb, half=half, t=t: e.tensor_copy(
                        out=hsT[:, half * 4:half * 4 + 4, t * 128:(t + 1) * 128],
                        in_=pb[:].rearrange("p (q n) -> p q n", q=4)), reads=[pbn], writes=wr)
                else:
                    T.add("act", lambda e, pb=pb, half=half, t=t: e.activation(
                        out=hsT[:, half * 4:half * 4 + 4, t * 128:(t + 1) * 128],
                        in_=pb[:].rearrange("p (q n) -> p q n", q=4), func=AF.Copy), reads=[pbn], writes=wr)

        def rstd_for_tg(g, sqb, out_ap, out_res, psn, psn_name):
            for c in range(KC):
                sb_ = sqb[c % 2]
                T.add("pool", lambda e, c=c, sb_=sb_: e.tensor_tensor(
                    out=sb_, in0=hsT[:, c, g * TG:(g + 1) * TG], in1=hsT[:, c, g * TG:(g + 1) * TG], op=ALU.mult),
                    reads=["hs.%d.%d" % (c, g)], writes=["sq%d" % (c % 2)])
                T.add("pe", lambda e, c=c, sb_=sb_: e.matmul(psn[:], lhsT=NOm, rhs=sb_, start=(c == 0), stop=(c == KC - 1)),
                      reads=["sq%d" % (c % 2), "cb"] + ([psn_name] if c else []), writes=[psn_name])
            T.add("act", lambda e: e.activation(out=out_ap, in_=psn[:], func=AF.Ln, bias=EPS, scale=-1.0 / D), reads=[psn_name], writes=[out_res])
            T.add("act", lambda e: e.activation(out=out_ap, in_=out_ap, func=AF.Exp, scale=-0.5), reads=[out_res], writes=[out_res])

        def make_xn(g, gcol0, l, rstd_ap, rstd_res, xn, xp="xnT", npool=0):
            for c in range(KC):
                T.add("pool" if c >= KC - npool else "dve", lambda e, c=c: e.scalar_tensor_tensor(
                    out=xn[:, c, :], in0=hsT[:, c, g * TG:(g + 1) * TG], scalar=vcol(l, gcol0 + c), in1=rstd_ap,
                    op0=ALU.mult, op1=ALU.mult), reads=["hs.%d.%d" % (c, g), rstd_res, "vecs"], writes=["%s.%d" % (xp, c)])
        XR = ["xnT.%d" % c for c in range(KC)]

        def attention_layer(l):
            fox = (l % 2 == 0)
            j_ = l // 2
            fence()
            ar.reset()
            rstd_bc = ar.f32(S)
            xnTb = [ar.bf16(KC * TG).rearrange("p (c n) -> p c n", c=KC) for _ in range(2)]
            xnT = xnTb[0]
            mark = ar.off
            sqb = [ar.bf16(TG) for _ in range(2)]
            psS = [ps[0], ps[1], ps[4], ps[5]]
            psSn = ["psb0", "psb1", "psb4", "psb5"]
            psO2, psD, psQ, psK, psV, psN = [ps[2], ps[3]], ps[6], ps[4], ps[5], ps[6], ps[7]

            for g in range(NTG):
                rstd_for_tg(g, sqb, rstd_bc[:, g * TG:(g + 1) * TG], "rstd.%d" % g, psN, "psN")

            CAUG = ["caug_s.%d.%d" % (p_, g_) for p_ in range(3) for g_ in range(NTG)]
            if fox:
                wf = ar.bf16(KC * 96).rearrange("p (c n) -> p c n", c=KC)
                ft = [ar.f32(TG) for _ in range(4)]
                fb = [ar.bf16(TG) for _ in range(3)]
                onesr = ar.f32(TG)
                T.add("pool", lambda e: e.dma_start(out=wf, in_=wf_d[j_].rearrange("p (c n) -> p c n", c=KC)), writes=["wf"], dma="wf")
                T.add("pool", lambda e: e.memset(onesr[:, :], 1.0), writes=["onesr"])

                def fphase(g):
                    make_xn(g, 0, l, rstd_bc[:, g * TG:(g + 1) * TG], "rstd.%d" % g, xnT)

                    def fmm(e):
                        r = None
                        for c in range(KC):
                            r = e.matmul(psN[0:96, :], lhsT=wf[:, c, :], rhs=xnT[:, c, :], start=(c == 0), stop=(c == KC - 1))
                        return r
                    T.add("pe", fmm, reads=["wf"] + XR, writes=["psN"])
                    cprev = ft[2 + (g + 1) % 2]
                    ccur = ft[2 + g % 2]
                    cn = "fc%d" % (g % 2)
                    cpn = "fc%d" % ((g + 1) % 2)
                    T.add("act", lambda e: e.activation(out=ft[0][0:80, :], in_=psN[0:80, :], func=AF.Identity, bias=vcol(l, 192)[0:80, :]),
                          reads=["psN", "vecs"], writes=["ft0"])
                    T.add("act", lambda e: e.activation(out=ft[0][0:80, :], in_=ft[0][0:80, :], func=AF.Exp, scale=-1.0), reads=["ft0"], writes=["ft0"])
                    T.add("act", lambda e: e.activation(out=ft[1][0:80, :], in_=ft[0][0:80, :], func=AF.Ln, bias=1.0), reads=["ft0"], writes=["ft1"])
                    if g == 0:
                        T.add("dve", lambda e: e.tensor_tensor_scan(out=ccur[0:80, :], data0=onesr[0:80, :], data1=ft[1][0:80, :],
                              initial=0.0, op0=ALU.mult, op1=ALU.subtract), reads=["onesr", "ft1"], writes=[cn])
                    else:
                        T.add("dve", lambda e: e.tensor_tensor_scan(out=ccur[0:80, :], data0=onesr[0:80, :], data1=ft[1][0:80, :],
                              initial=cprev[0:80, TG - 1:TG], op0=ALU.mult, op1=ALU.subtract),
                              reads=["onesr", "ft1", cpn], writes=[cn])
                    T.add("dve", lambda e: e.tensor_copy(out=fb[0][0:80, :], in_=ccur[0:80, :]), reads=[cn], writes=["fb0"])
                    T.add("dve", lambda e: e.tensor_tensor(out=ft[0][0:80, :], in0=ccur[0:80, :], in1=fb[0][0:80, :], op=ALU.subtract),
                          reads=[cn, "fb0"], writes=["ft0"])
                    T.add("dve", lambda e: e.tensor_copy(out=fb[1][0:80, :], in_=ft[0][0:80, :]), reads=["ft0"], writes=["fb1"])
                    T.add("dve", lambda e: e.tensor_tensor(out=ft[1][0:80, :], in0=ft[0][0:80, :], in1=fb[1][0:80, :], op=ALU.subtract),
                          reads=["ft0", "fb1"], writes=["ft1"])
                    T.add("dve", lambda e: e.tensor_copy(out=fb[2][0:80, :], in_=ft[1][0:80, :]), reads=["ft1"], writes=["fb2"])
                    for part in range(3):
                        T.add("sp", lambda e, part=part: e.dma_start(out=caug_s[part, :, g * TG:(g + 1) * TG],
                              in_=fb[part][32 * part:32 * part + 16, :]), reads=["fb%d" % part], writes=["caug_s.%d.%d" % (part, g)], dma="caug_w")
                for g in range(NTG):
                    fphase(g)
            fence()
            ar.off = mark
            wh = [ar.bf16(KC * 192).rearrange("p (c n) -> p c n", c=KC) for _ in range(2)]
            Qp = ar.bf16(S)
            Kp = ar.bf16(S)
            VW = DH + 2
            Vt = ar.bf16(NT * VW).rearrange("p (t d) -> p t d", t=NT)
            OTs = [ar.bf16(TG) for _ in range(1)]
            if fox:
                PT = [ar.bf16(TG) for _ in range(4)]
                rden = ar.f32(TG)
                rbc = ar.f32(TG)
                T.add("pool", lambda e: e.memset(Vt[:, :, DH:DH + 1], 1.0), writes=["V.ones"])
            else:
                Eb = [ar.f32(TG) for _ in range(1)]
                SPb = [ar.bf16(TG) for _ in range(3)]
                aTb = [ar.bf16(TG) for _ in range(3)]
                Rbb = [ar.bf16(TG) for _ in range(4)]
                R32 = ar.f32(TG)
            if fox:
                T.add("pool", lambda e: e.memset(Qp[64:70, :], -1.0), writes=["Qp.aug"])
                T.add("pool", lambda e: e.memset(Kp[64:70, :], 1.0), writes=["Kp.aug"])
            else:
                T.add("pool", lambda e: e.memset(Qp[64:128, :], 0.0), writes=["Qp.aug"])
                T.add("pool", lambda e: e.memset(Kp[64:128, :], 0.0), writes=["Kp.aug"])
            KR = 70 if fox else 128

            def project(h, g, whb, whn):
                xnT = xnTb[g % 2]
                xp = "xnT" if g % 2 == 0 else "xnU"
                XR = ["%s.%d" % (xp, c) for c in range(KC)]
                make_xn(g, 0, l, rstd_bc[:, g * TG:(g + 1) * TG], "rstd.%d" % g, xnT, xp, npool=0)

                def qmm(e):
                    r = None
                    for c in range(KC):
                        r = e.matmul(psQ[0:64, :], lhsT=whb[:, c, 0:64], rhs=xnT[:, c, :], start=(c == 0), stop=(c == KC - 1))
                    return r

                def kmm(e):
                    r = None
                    for c in range(KC):
                        r = e.matmul(psK[0:64, :], lhsT=whb[:, c, 64:128], rhs=xnT[:, c, :], start=(c == 0), stop=(c == KC - 1))
                    return r

                def vmm(e):
                    r = None
                    for tt in range(4):
                        for c in range(KC):
                            r = e.matmul(psV[:, tt * 64:(tt + 1) * 64], lhsT=xnT[:, c, tt * 128:(tt + 1) * 128], rhs=whb[:, c, 128:192],
                                         start=(c == 0), stop=(c == KC - 1))
                    return r
                T.add("pe", qmm, reads=XR + [whn], writes=["psb4"])
                T.add("pe", kmm, reads=XR + [whn], writes=["psb5"])
                T.add("pe", vmm, reads=XR + [whn], writes=["psb6"])
                T.add("act", lambda e: e.activation(out=Qp[0:64, g * TG:(g + 1) * TG], in_=psQ[0:64, :], func=AF.Copy, scale=0.125),
                      reads=["psb4"], writes=["Qp.%d" % g])
                T.add("act", lambda e: e.activation(out=Kp[0:64, g * TG:(g + 1) * TG], in_=psK[0:64, :], func=AF.Copy), reads=["psb5"], writes=["Kp.%d" % g])
                T.add("act", lambda e: e.activation(out=Vt[:, g * 4:(g + 1) * 4, 0:DH], in_=psV[:, 0:256].rearrange("p (t d) -> p t d", t=4), func=AF.Copy),
                      reads=["psb6"], writes=["V.%d" % g])

            def attend(h, g):
                if fox:
                    blocks = [(4 * g + m, m) for m in range(4)] + [(j, None) for j in range(4 * g - 1, -1, -1)]
                else:
                    blocks = [(4 * g + m, m) for m in range(3, -1, -1)] + [(j, None) for j in range(4 * g - 1, -1, -1)]
                nb = len(blocks)
                qres = ["Qp.%d" % g, "Qp.aug"]
                q0 = g * TG
                mk = maskF if fox else maskS

                def score_op(i):
                    j, m = blocks[i]
                    pS = psS[i % 4]
                    c0 = 0 if m is None else m * 128
                    kres = ["Kp.%d" % (j // 4), "Kp.aug"]

                    def f(e):
                        kk = Kp[0:KR, j * 128:(j + 1) * 128]
                        if m is None:
                            return e.matmul(pS[:, :], lhsT=kk, rhs=Qp[0:KR, q0:q0 + TG], start=True, stop=True)
                        e.matmul(pS[:, c0:c0 + 128], lhsT=identb, rhs=mk, start=True, stop=False)
                        r = e.matmul(pS[:, c0:c0 + 128], lhsT=kk, rhs=Qp[0:KR, q0 + c0:q0 + c0 + 128], start=False, stop=True)
                        if c0 + 128 < TG:
                            r = e.matmul(pS[:, c0 + 128:TG], lhsT=kk, rhs=Qp[0:KR, q0 + c0 + 128:q0 + TG], start=False, stop=True)
                        return r
                    T.add("pe", f, reads=qres + kres + ["cb"], writes=[psSn[i % 4]])

                ob = OTs[0]
                obn = "OTs0"
                psO = psO2[g % 2]
                pOn = "psO%d" % (g % 2)
                if fox:
                    def tail_op(i):
                        j, m = blocks[i]
                        pS = psS[i % 4]
                        c0 = 0 if m is None else m * 128
                        pt = PT[i % 4]
                        ptn = "PT%d" % (i % 4)
                        T.add("act", lambda e: e.activation(out=pt[:, c0:TG], in_=pS[:, c0:TG], func=AF.Exp),
                              reads=[psSn[i % 4]], writes=[ptn])
                        T.add("pe", lambda e: e.matmul(psO[0:DH + 1, c0:TG], lhsT=Vt[:, j, 0:DH + 1], rhs=pt[:, c0:TG], start=(i == 0), stop=(i == nb - 1)),
                              reads=[ptn, "V.%d" % (j // 4), "V.ones"] + ([pOn] if i else []), writes=[pOn])
                    score_op(0)
                    if nb > 1:
                        score_op(1)
                    for i in range(nb):
                        if i + 2 < nb:
                            score_op(i + 2)
                        tail_op(i)
                    T.add("dve", lambda e: e.reciprocal(out=rden[64:65, :], in_=psO[64:65, :]), reads=[pOn], writes=["rden"])
                    T.add("pe", lambda e: e.matmul(psD[0:64, :], lhsT=one1[64:65, 0:64], rhs=rden[64:65, :], start=True, stop=True),
                          reads=["rden", "one1"], writes=["psb6"])
                    T.add("dve", lambda e: e.tensor_copy(out=rbc[0:64, :], in_=psD[0:64, :]), reads=["psb6"], writes=["rbc"])
                    T.add("dve", lambda e: e.tensor_tensor(out=ob[0:64, :], in0=psO[0:64, :], in1=rbc[0:64, :], op=ALU.mult),
                          reads=[pOn, "rbc"], writes=[obn])
                else:
                    T.add("pool", lambda e: e.memset(R32[:, :], 0.0), writes=["R32"])

                    def s1(i):
                        j, m = blocks[i]
                        c0 = 0 if m is None else m * 128
                        score_op(i)
                        pS = psS[i % 4]
                        Ei = Eb[0]
                        SPi = SPb[i % 3]
                        T.add("act", lambda e: e.activation(out=Ei[:, c0:TG], in_=pS[:, c0:TG], func=AF.Exp), reads=[psSn[i % 4]], writes=["E0"])
                        T.add("act", lambda e: e.activation(out=SPi[:, c0:TG], in_=Ei[:, c0:TG], func=AF.Ln, bias=1.0), reads=["E0"], writes=["SP%d" % (i % 3)])
                        if i + 1 < nb:
                            j2, m2 = blocks[i + 1]
                            c02 = 0 if m2 is None else m2 * 128
                            Rn = Rbb[(i + 1) % 4]
                            T.add("dve", lambda e: e.tensor_tensor(out=R32[:, c0:TG], in0=R32[:, c0:TG], in1=SPi[:, c0:TG], op=ALU.add),
                                  reads=["R32", "SP%d" % (i % 3)], writes=["R32"])
                            T.add("dve", lambda e: e.tensor_copy(out=Rn[:, c02:TG], in_=R32[:, c02:TG]), reads=["R32"], writes=["Rb%d" % ((i + 1) % 4)])

                    def s2(i):
                        j, m = blocks[i]
                        c0 = 0 if m is None else m * 128
                        pS = psS[i % 4]
                        SPi = SPb[i % 3]
                        ai = aTb[i % 3]
                        Rb = Rbb[i % 4]
                        c1 = c0 + 128 if m is not None else c0
                        last = (i == nb - 1)

                        def st2(e):
                            r = e.matmul(pS[:, c0:TG], lhsT=NTm, rhs=SPi[:, c0:TG], start=False, stop=(c1 >= TG))
                            if c1 < TG:
                                r = e.matmul(pS[:, c1:TG], lhsT=NOm, rhs=Rb[:, c1:TG], start=False, stop=True)
                            return r
                        T.add("pe", st2, reads=["SP%d" % (i % 3), "cb", psSn[i % 4]] + (["Rb%d" % (i % 4)] if c1 < TG else []), writes=[psSn[i % 4]])
                        T.add("act", lambda e: e.activation(out=ai[:, c0:TG], in_=pS[:, c0:TG], func=AF.Exp), reads=[psSn[i % 4]], writes=["aT%d" % (i % 3)])

                    def s3(i):
                        j, m = blocks[i]
                        c0 = 0 if m is None else m * 128
                        ai = aTb[i % 3]
                        last = (i == nb - 1)
                        T.add("pe", lambda e: e.matmul(psO[0:64, c0:TG], lhsT=Vt[:, j, 0:DH], rhs=ai[:, c0:TG], start=(i == 0), stop=last),
                              reads=["aT%d" % (i % 3), "V.%d" % (j // 4)] + ([pOn] if i else []), writes=[pOn])
                    s1(0)
                    if nb > 1:
                        s1(1)
                    for i in range(nb):
                        if i + 2 < nb:
                            s1(i + 2)
                        s2(i)
                        if i >= 1:
                            s3(i - 1)
                    s3(nb - 1)
                    T.add("dve", lambda e: e.tensor_copy(out=ob[0:64, :], in_=psO[0:64, :]), reads=[pOn], writes=[obn])
                T.add("sp", lambda e: e.dma_start(out=ot_s[h * 64:(h + 1) * 64, g * TG:(g + 1) * TG], in_=ob[0:64, :]),
                      reads=[obn], writes=["ot_s.%d.%d" % (h, g)], dma="otw")

            for h in range(H):
                whb = wh[h % 2]
                whn = "wh%d" % (h % 2)
                T.add("pool", lambda e, h=h, whb=whb: e.dma_start(out=whb, in_=wqkv_d[l][h].rearrange("p (c n) -> p c n", c=KC)),
                      writes=[whn], dma=whn)
                if h == 1:
                    cast_ffn_weights(l)
                if fox:
                    T.add("sp", lambda e, h=h: e.dma_start(out=Qp[64:67, :], in_=caug_s[:, h, :]), reads=CAUG, writes=["Qp.aug"], dma="aug")
                    T.add("sp", lambda e, h=h: e.dma_start(out=Kp[67:70, :], in_=caug_s[:, h, :]), reads=CAUG, writes=["Kp.aug"], dma="aug")
                for g in range(NTG):
                    project(h, g, whb, whn)
                for g in range(NTG):
                    attend(h, g)

            fence()
            ar.reset()
            wo = ar.bf16(8 * D).rearrange("p (c n) -> p c n", c=8)
            OTin = [ar.bf16(8 * TG).rearrange("p (c n) -> p c n", c=8) for _ in range(2)]
            T.add("pool", lambda e: e.dma_start(out=wo, in_=wo_d[l].rearrange("p (c n) -> p c n", c=8)), writes=["wo"], dma="wo")

            def wo_tg(g):
                ob = OTin[g % 2]
                obn = "OTin%d" % (g % 2)
                T.add("sp", lambda e: e.dma_start(out=ob, in_=ot_s[:, g * TG:(g + 1) * TG].rearrange("(c p) n -> p c n", p=128)),
                      reads=["ot_s.%d.%d" % (h_, g) for h_ in range(H)], writes=[obn], dma=obn)
                for co in range(KC):
                    py = ps[co % 2]
                    pyn = "ps%d" % (co % 2)

                    def omm(e, co=co, py=py):
                        r = None
                        for c in range(8):
                            r = e.matmul(py[:, :], lhsT=wo[:, c, co * 128:(co + 1) * 128], rhs=ob[:, c, :], start=(c == 0), stop=(c == 7))
                        return r
                    T.add("pe", omm, reads=["wo", obn], writes=[pyn])
                    T.add("dve", lambda e, co=co, py=py: e.tensor_tensor(out=hsT[:, co, g * TG:(g + 1) * TG], in0=py[:, :],
                          in1=hsT[:, co, g * TG:(g + 1) * TG], op=ALU.add), reads=[pyn, "hs.%d.%d" % (co, g)], writes=["hs.%d.%d" % (co, g)])
            for g in range(NTG):
                wo_tg(g)

        def ffn_layer(l):
            fence()
            ar.reset()
            xnT = ar.bf16(KC * TG).rearrange("p (c n) -> p c n", c=KC)
            actT = ar.bf16(FC * TG).rearrange("p (c n) -> p c n", c=FC)
            wu = [ar.bf16(KC * 256).rearrange("p (c n) -> p c n", c=KC) for _ in range(3)]
            wd = [ar.bf16(FC * 128).rearrange("p (c n) -> p c n", c=FC) for _ in range(2)]
            hext = [[ar.f32(516) for _ in range(2)] for _ in range(2)]
            tcv = [[ar.f32(TG) for _ in range(2)] for _ in range(2)]
            sqb = [ar.bf16(TG) for _ in range(2)]
            rstd = ar.f32(TG)
            halo = ar.f32(2 * FC * 2).rearrange("p (c n) -> p c n", n=2)
            psU = [ps[0], ps[2]]
            psG = [ps[1], ps[3]]
            psY = [ps[4], ps[5]]
            psN = ps[7]
            T.add("pool", lambda e: e.memset(halo[:, :, :], 0.0), writes=["halo"])

            def prep(g):
                rstd_for_tg(g, sqb, rstd[:, :], "rstdf", psN, "psN")
                make_xn(g, 8, l, rstd[:, :], "rstdf", xnT)

            def gate(f):
                par = f % 2
                T.add("act", lambda e: e.activation(out=tcv[par][1], in_=tcv[par][1], func=AF.Silu), reads=["tcv%d1" % par], writes=["tcv%d1" % par])
                T.add("pool", lambda e: e.tensor_tensor(out=actT[:, f, :], in0=tcv[par][1], in1=tcv[par][0], op=ALU.mult),
                      reads=["tcv%d1" % par, "tcv%d0" % par], writes=["actT.%d" % f])

            def up(g):
                for f in range(FC):
                    par = f % 2
                    wub = wu[f % 3]
                    wun = "wu%d" % (f % 3)
                    T.add("sp", lambda e, f=f, wub=wub: e.dma_start(out=wub, in_=wup_s[l][f].rearrange("p (c n) -> p c n", c=KC)),
                          reads=["wup_s%d.%d" % (l, f)], writes=[wun], dma=wun)
                    for ug in range(2):
                        if ug == 1 and f >= 1:
                            gate(f - 1)
                        pp = (psU if ug == 0 else psG)[par]
                        ppn = "psUG%d%d" % (ug, par)

                        def upmm(e, ug=ug, pp=pp, wub=wub):
                            r = None
                            for c in range(KC):
                                r = e.matmul(pp[:, :], lhsT=wub[:, c, ug * 128:(ug + 1) * 128], rhs=xnT[:, c, :], start=(c == 0), stop=(c == KC - 1))
                            return r
                        T.add("pe", upmm, reads=XR + [wun], writes=[ppn])
                        hx = hext[par][ug]
                        hxn = "hext%d%d" % (par, ug)
                        ch = ug * FC + f
                        tc_ = tcv[par][ug]
                        tcn = "tcv%d%d" % (par, ug)
                        wc = 16 + ch * 3
                        T.add("pool", lambda e, hx=hx, ch=ch: e.tensor_copy(out=hx[:, 0:2], in_=halo[:, ch, :]), reads=["halo.%d" % ch, "halo"], writes=[hxn + "h"])
                        T.add("act", lambda e, hx=hx, pp=pp: e.activation(out=hx[:, 2:514], in_=pp[:, :], func=AF.Copy), reads=[ppn], writes=[hxn])
                        T.add("act", lambda e, tc_=tc_, pp=pp, wc=wc, ch=ch: e.activation(out=tc_, in_=pp[:, :], func=AF.Identity,
                              scale=vcol(l, wc + 2), bias=vcol(l, 148 + ch)), reads=[ppn, "vecs"], writes=[tcn])
                        T.add("pool", lambda e, hx=hx, ch=ch: e.tensor_copy(out=halo[:, ch, :], in_=hx[:, 512:514]), reads=[hxn], writes=["halo.%d" % ch])
                        T.add("dve", lambda e, hx=hx, tc_=tc_, wc=wc: e.scalar_tensor_tensor(out=tc_, in0=hx[:, 1:513], scalar=vcol(l, wc + 1), in1=tc_,
                              op0=ALU.mult, op1=ALU.add), reads=[hxn, hxn + "h", tcn, "vecs"], writes=[tcn])
                        T.add("dve", lambda e, hx=hx, tc_=tc_, wc=wc: e.scalar_tensor_tensor(out=tc_, in0=hx[:, 0:512], scalar=vcol(l, wc), in1=tc_,
                              op0=ALU.mult, op1=ALU.add), reads=[hxn, hxn + "h", tcn, "vecs"], writes=[tcn])
                gate(FC - 1)

            def down(g):
                ar_ = ["actT.%d" % f for f in range(FC)]
                for co in range(KC):
                    wdb = wd[co % 2]
                    wdn = "wd%d" % (co % 2)
                    T.add("sp", lambda e, co=co, wdb=wdb: e.dma_start(out=wdb, in_=wdn_s[l][co].rearrange("p (c n) -> p c n", c=FC)),
                          reads=["wdn_s%d.%d" % (l, co)], writes=[wdn], dma=wdn)
                    py = psY[co % 2]
                    pyn = "psY%d" % (co % 2)

                    def dmm(e, wdb=wdb, py=py):
                        r = None
                        for f in range(FC):
                            r = e.matmul(py[:, :], lhsT=wdb[:, f, :], rhs=actT[:, f, :], start=(f == 0), stop=(f == FC - 1))
                        return r
                    T.add("pe", dmm, reads=ar_ + [wdn], writes=[pyn])
                    T.add("dve", lambda e, co=co, py=py: e.tensor_tensor(out=hsT[:, co, g * TG:(g + 1) * TG], in0=py[:, :],
                          in1=hsT[:, co, g * TG:(g + 1) * TG], op=ALU.add), reads=[pyn, "hs.%d.%d" % (co, g)], writes=["hs.%d.%d" % (co, g)])
            prep(0)
            for g in range(NTG):
                up(g)
                if g + 1 < NTG:
                    prep(g + 1)
                down(g)

        def final_phase(normed):
            fence()
            ar.reset()
            yT = ar.f32(KC * TG).rearrange("p (c n) -> p c n", c=KC)
            ot = [ar.f32(D) for _ in range(2)]
            sqb = [ar.bf16(TG) for _ in range(2)]
            rstd = ar.f32(TG)
            psN = ps[7]
            outs = []

            def fin_tg(g):
                if normed:
                    rstd_for_tg(g, sqb, rstd[:, :], "rstdf", psN, "psN")
                    for c in range(KC):
                        T.add("dve", lambda e, c=c: e.scalar_tensor_tensor(out=yT[:, c, :], in0=hsT[:, c, g * TG:(g + 1) * TG], scalar=vecs[:, FIN + c:FIN + c + 1],
                              in1=rstd[:, :], op0=ALU.mult, op1=ALU.mult), reads=["hs.%d.%d" % (c, g), "rstdf", "vecs"], writes=["yT.%d" % c])
                else:
                    for c in range(KC):
                        T.add("dve", lambda e, c=c: e.tensor_copy(out=yT[:, c, :], in_=hsT[:, c, g * TG:(g + 1) * TG]),
                              reads=["hs.%d.%d" % (c, g)], writes=["yT.%d" % c])
                for tt in range(4):
                    t = g * 4 + tt
                    ob = ot[t % 2]
                    obn = "ot%d" % (t % 2)
                    for half in range(2):
                        pb = ps[(2 * t + half) % 4]
                        pbn = "psf%d" % ((2 * t + half) % 4)

                        def tr(e, pb=pb, half=half, tt=tt):
                            r = None
                            for q in range(4):
                                c = half * 4 + q
                                r = e.transpose(pb[:, q * 128:(q + 1) * 128], yT[:, c, tt * 128:(tt + 1) * 128], id32[:])
                            return r
                        T.add("pe", tr, reads=["yT.%d" % (half * 4 + q) for q in range(4)] + ["id32"], writes=[pbn])
                        if half == 0:
                            T.add("dve", lambda e, pb=pb, ob=ob: e.tensor_copy(out=ob[:, 0:512], in_=pb[:, :]), reads=[pbn], writes=[obn + "a"])
                        else:
                            T.add("act", lambda e, pb=pb, ob=ob: e.activation(out=ob[:, 512:1024], in_=pb[:, :], func=AF.Copy), reads=[pbn], writes=[obn + "b"])
                    T.add("sp", lambda e, ob=ob, t=t: e.dma_start(out=out_d[t * 128:(t + 1) * 128, :], in_=ob), reads=[obn + "a", obn + "b"],
                          writes=["out.%d" % t], dma="out%d" % (t % 2))
                    outs.append("out.%d" % t)
            for g in range(NTG):
                fin_tg(g)
            T.add("sp", lambda e: e.wait_ge(sems["sp"], 0), reads=outs, writes=[])

        sems = {e: st.enter_context(nc.semaphore("s_" + e)) for e in COMPUTE + ("sp",)}
        for (l, what) in layers:
            if what == "attn":
                attention_layer(l)
            else:
                ffn_layer(l)
        final_phase(final)

        dsems = {k: st.enter_context(nc.semaphore("d_" + k)) for k in T.dma_keys()}
        block = st.enter_context(nc.Block())
        T.emit(nc, block, sems, dsems)
    return nc


_CONST_CACHE = {}


def _consts():
    if "cb" not in _CONST_CACHE:
        p = np.arange(128)[:, None]
        q = np.arange(128)[None, :]
        cbm = np.zeros((128, 768), np.float32)
        cbm[:, 0:128] = np.eye(128, dtype=np.float32)
        cbm[:, 128:256] = np.where(p > q, NEG, 0.0)
        cbm[:, 256:384] = np.where(p >= q, NEG, 0.0)
        cbm[:, 384:512] = np.where(p >= q, -1.0, 0.0)
        cbm[:, 512:640] = -1.0
        cbm[:, 640:704] = 1.0
        _CONST_CACHE["cb"] = cbm
        _CONST_CACHE["ident"] = np.eye(128, dtype=np.float32)
    return _CONST_CACHE["cb"], _CONST_CACHE["ident"]


def layout_inputs(attn_norm, ffn_norm, final_norm, fox_w_qkvf, fox_b_f, fox_w_o,
                  sb_w_qkv, sb_w_o, ffn_w_up, ffn_w_conv, ffn_b_conv, ffn_w_down):
    f32 = np.float32
    m = {}
    cbm, ident = _consts()
    m["cb"] = cbm
    m["ident"] = ident
    NV = DEPTH * VS + 8
    vec = np.zeros((128, NV), f32)
    for l in range(DEPTH):
        o = l * VS
        vec[:, o:o + 8] = np.asarray(attn_norm[l], f32).reshape(8, 128).T
        vec[:, o + 8:o + 16] = np.asarray(ffn_norm[l], f32).reshape(8, 128).T
        wc = np.asarray(ffn_w_conv[l], f32).reshape(3, 44, 128).transpose(2, 1, 0).reshape(128, 132)
        vec[:, o + 16:o + 148] = wc
        vec[:, o + 148:o + 192] = np.asarray(ffn_b_conv[l], f32).reshape(44, 128).T
        if l % 2 == 0:
            bf = np.asarray(fox_b_f[l // 2], f32)
            for r0 in (0, 32, 64):
                vec[r0:r0 + 16, o + 192] = bf
    vec[:, DEPTH * VS:DEPTH * VS + 8] = np.asarray(final_norm, f32).reshape(8, 128).T
    m["vecs"] = vec
    for l in range(DEPTH):
        j = l // 2
        if l % 2 == 0:
            w = np.asarray(fox_w_qkvf[j], f32)
            wo = np.asarray(fox_w_o[j], f32)
            wfm = np.zeros((128, KC, 96), f32)
            wfr = w[:, 3 * D:3 * D + H].reshape(KC, 128, H).transpose(1, 0, 2)
            for r0 in (0, 32, 64):
                wfm[:, :, r0:r0 + 16] = wfr
            m["wf%d" % j] = wfm.reshape(128, KC * 96)
        else:
            w = np.asarray(sb_w_qkv[j], f32)
            wo = np.asarray(sb_w_o[j], f32)
        parts = [w[:, i * D:(i + 1) * D].reshape(KC, 128, H, DH).transpose(2, 1, 0, 3) for i in range(3)]
        m["wqkv%d" % l] = np.ascontiguousarray(np.concatenate(parts, axis=3)).reshape(H, 128, KC * 192)
        m["wo%d" % l] = np.ascontiguousarray(wo.reshape(8, 128, D).transpose(1, 0, 2)).reshape(128, 8 * D)
        wu = np.asarray(ffn_w_up[l], f32)
        pu = wu[:, 0:FF].reshape(KC, 128, FC, 128).transpose(2, 1, 0, 3)
        pg = wu[:, FF:2 * FF].reshape(KC, 128, FC, 128).transpose(2, 1, 0, 3)
        m["wup%d" % l] = np.ascontiguousarray(np.concatenate([pu, pg], axis=3)).reshape(FC, 128, KC * 256)
        wdn = np.asarray(ffn_w_down[l], f32)
        m["wdn%d" % l] = np.ascontiguousarray(wdn.reshape(FC, 128, KC, 128).transpose(2, 1, 0, 3)).reshape(KC, 128, FC * 128)
    return m


ALL_LAYERS = [(l, w) for l in range(DEPTH) for w in ("attn", "ffn")]


def kernel(x, attn_norm, ffn_norm, final_norm, fox_w_qkvf, fox_b_f, fox_w_o,
           sb_w_qkv, sb_w_o, ffn_w_up, ffn_w_conv, ffn_b_conv, ffn_w_down):
    x = np.asarray(x, np.float32)
    B, S, _ = x.shape
    shared = layout_inputs(attn_norm, ffn_norm, final_norm, fox_w_qkvf, fox_b_f, fox_w_o,
                           sb_w_qkv, sb_w_o, ffn_w_up, ffn_w_conv, ffn_b_conv, ffn_w_down)
    nc = build_program(S, ALL_LAYERS, True)
    in_maps = []
    for b in range(B):
        mm = dict(shared)
        mm["x"] = np.ascontiguousarray(x[b])
        in_maps.append(mm)
    res = run_bass_kernel_spmd(nc, in_maps, core_ids=list(range(B)))
    return np.stack([np.asarray(r["out"], np.float32) for r in res.results], axis=0)
```

```python
import contextlib
import numpy as np
import concourse.bass as bass
import concourse.mybir as mybir
from concourse.bass_utils import run_bass_kernel_spmd

F32 = mybir.dt.float32
BF16 = mybir.dt.bfloat16
AF = mybir.ActivationFunctionType
ALU = mybir.AluOpType

D = 1024
KC = 8
TG = 512
H = 16
DH = 64
FF = 2816
FC = 22
DEPTH = 4
EPS = 1e-6
NEG = -30000.0
VS = 193

COMPUTE = ("pe", "act", "dve", "pool")


class _Op:
    __slots__ = ("eng", "fn", "deps", "dma_key", "dma_val", "signal", "count", "idx", "dma_waits")


class Tracker:
    def __init__(self):
        self.ops = []
        self.last_w = {}
        self.readers = {}
        self.dma_cum = {}
        self.fence_idx = None
        self.last_eng = {}
        self.last_dma = {}

    def fence(self, fn):
        deps = set(self.last_eng.values()) | set(self.last_dma.values())
        idx = self.add("pool", fn, extra_deps=deps)
        self.fence_idx = idx
        return idx

    def add(self, eng, fn, reads=(), writes=(), dma=None, extra_deps=()):
        op = _Op()
        op.eng = eng
        op.fn = fn
        op.dma_key = dma
        op.signal = False
        op.count = 0
        op.idx = len(self.ops)
        deps = set()
        for r in reads:
            w = self.last_w.get(r)
            if w is not None:
                deps.add(w)
        for r in writes:
            w = self.last_w.get(r)
            if w is not None:
                deps.add(w)
            for rd in self.readers.get(r, ()):
                deps.add(rd)
        deps |= set(extra_deps)
        if self.fence_idx is not None:
            deps.add(self.fence_idx)
        deps.discard(op.idx)
        op.deps = deps
        op.dma_waits = {}
        for d in deps:
            k = self.ops[d].dma_key
            if k is not None:
                op.dma_waits[k] = self.dma_cum[k]
        if dma is not None:
            self.last_dma[dma] = op.idx
        else:
            self.last_eng[eng] = op.idx
        if dma is not None:
            self.dma_cum[dma] = self.dma_cum.get(dma, 0) + 16
            op.dma_val = self.dma_cum[dma]
        else:
            op.dma_val = 0
        self.ops.append(op)
        for r in writes:
            self.last_w[r] = op.idx
            self.readers[r] = []
        for r in reads:
            if r not in writes:
                self.readers.setdefault(r, []).append(op.idx)
        return op.idx

    def dma_keys(self):
        return list(self.dma_cum.keys())

    def emit(self, nc, block, sems, dma_sems):
        ops = self.ops
        for op in ops:
            for d in op.deps:
                dop = ops[d]
                if dop.dma_key is not None:
                    continue
                if dop.eng == op.eng and op.eng in ("pe", "sp"):
                    continue
                dop.signal = True
        cnt = {e: 0 for e in COMPUTE + ("sp",)}
        for op in ops:
            if op.dma_key is None and op.signal:
                cnt[op.eng] += 1
                op.count = cnt[op.eng]
        per_eng = {e: [] for e in COMPUTE + ("sp",)}
        for op in ops:
            per_eng[op.eng].append(op)

        def run(eng_name, eng):
            waited = {}
            for op in per_eng[eng_name]:
                wl = {}
                for d in op.deps:
                    dop = ops[d]
                    if dop.dma_key is not None:
                        k = ("dma", dop.dma_key)
                        v = op.dma_waits[dop.dma_key]
                    else:
                        if dop.eng == eng_name and eng_name in ("pe", "sp"):
                            continue
                        k = ("eng", dop.eng)
                        v = dop.count
                    if v > wl.get(k, 0):
                        wl[k] = v
                for k, v in wl.items():
                    if waited.get(k, 0) >= v:
                        continue
                    waited[k] = v
                    s = dma_sems[k[1]] if k[0] == "dma" else sems[k[1]]
                    eng.wait_ge(s, v)
                ins = op.fn(eng)
                if op.dma_key is not None:
                    ins.then_inc(dma_sems[op.dma_key], 16)
                elif op.signal:
                    ins.then_inc(sems[op.eng], 1)

        @block.sync
        def _(e):
            run("sp", e)

        @block.scalar
        def _(e):
            run("act", e)

        @block.vector
        def _(e):
            run("dve", e)

        @block.gpsimd
        def _(e):
            run("pool", e)

        @block.tensor
        def _(e):
            run("pe", e)


class Arena:
    def __init__(self, t, nbytes):
        self.t = t
        self.nbytes = nbytes
        self.off = 0

    def reset(self):
        self.off = 0

    def f32(self, n):
        assert self.off % 4 == 0
        a = self.off // 4
        self.off += 4 * n
        assert self.off <= self.nbytes, ("arena overflow", self.off, self.nbytes)
        return self.t[:, a:a + n]

    def bf16(self, n):
        n2 = (n + 1) // 2
        v = self.f32(n2)
        return v.bitcast(BF16)[:, 0:n]


def build_program(S, layers, final=True):
    NTG = S // TG
    NT = S // 128
    NV = DEPTH * VS + 8
    nc = bass.Bass("TRN2", target_bir_lowering=False)

    def din(name, shape, dt=F32):
        return nc.dram_tensor(name, list(shape), dt, kind="ExternalInput").ap()

    x_d = din("x", [S, D])
    vec_d = din("vecs", [128, NV])
    cb_d = din("cb", [128, 6 * 128])
    id_d = din("ident", [128, 128])
    wqkv_d = [din("wqkv%d" % l, [H, 128, KC * 192]) for l in range(DEPTH)]
    wf_d = [din("wf%d" % j, [128, KC * 96]) for j in range(2)]
    wo_d = [din("wo%d" % l, [128, 8 * D]) for l in range(DEPTH)]
    wup_d = [din("wup%d" % l, [FC, 128, KC * 256]) for l in range(DEPTH)]
    wdn_d = [din("wdn%d" % l, [KC, 128, FC * 128]) for l in range(DEPTH)]
    out_d = nc.dram_tensor("out", [S, D], F32, kind="ExternalOutput").ap()
    wup_s = [nc.dram_tensor("wup_s%d" % l, [FC, 128, KC * 256], BF16, kind="Internal").ap() for l in range(DEPTH)]
    wdn_s = [nc.dram_tensor("wdn_s%d" % l, [KC, 128, FC * 128], BF16, kind="Internal").ap() for l in range(DEPTH)]
    ot_s = nc.dram_tensor("ot_s", [D, S], BF16, kind="Internal").ap()
    caug_s = nc.dram_tensor("caug_s", [3, H, S], BF16, kind="Internal").ap()

    T = Tracker()

    with contextlib.ExitStack() as st:
        hsT = st.enter_context(nc.sbuf_tensor("hsT", [128, KC, S], F32))
        vecs = st.enter_context(nc.sbuf_tensor("vecs_sb", [128, NV], F32))
        cb = st.enter_context(nc.sbuf_tensor("cb_sb", [128, 6 * 128], BF16))
        id32 = st.enter_context(nc.sbuf_tensor("id32", [128, 128], F32))
        ones32 = st.enter_context(nc.sbuf_tensor("ones32", [128, 128], F32))
        fsc = st.enter_context(nc.sbuf_tensor("fsc", [128, 8], F32))
        one1 = st.enter_context(nc.sbuf_tensor("one1", [128, 64], F32))
        AR_BYTES = 75600
        art = st.enter_context(nc.sbuf_tensor("arena", [128, AR_BYTES // 4], F32))
        ar = Arena(art, AR_BYTES)
        ps = [st.enter_context(nc.psum_tensor("ps%d" % i, [128, 512], F32)) for i in range(8)]

        identb = cb[:, 0:128]
        maskF = cb[:, 128:256]
        maskS = cb[:, 256:384]
        NTm = cb[:, 384:512]
        NOm = cb[:, 512:640]
        ones64 = cb[:, 640:704]

        def vcol(l, k):
            o = l * VS + k
            return vecs[:, o:o + 1]
        FIN = DEPTH * VS

        def fence():
            T.fence(lambda e: e.memset(fsc[:, 0:1], 0.0))

        T.add("sp", lambda e: e.dma_start(out=vecs[:], in_=vec_d), writes=["vecs"], dma="vecs")
        T.add("sp", lambda e: e.dma_start(out=id32[:], in_=id_d), writes=["id32"], dma="id32")
        T.add("pool", lambda e: e.dma_start(out=cb[:], in_=cb_d), writes=["cb"], dma="cb")
        T.add("pool", lambda e: e.memset(ones32[:], 1.0 / D), writes=["ones32"])
        T.add("pool", lambda e: e.memset(one1[:], 1.0), writes=["one1"])

        def cast_ffn_weights(l):
            for f in range(FC):
                T.add("pool", lambda e, f=f: e.dma_start(out=wup_s[l][f], in_=wup_d[l][f]),
                      writes=["wup_s%d.%d" % (l, f)], dma="wups")
            for co in range(KC):
                T.add("pool", lambda e, co=co: e.dma_start(out=wdn_s[l][co], in_=wdn_d[l][co]),
                      writes=["wdn_s%d.%d" % (l, co)], dma="wdns")

        ar.reset()
        xin = [ar.f32(D) for _ in range(2)]
        for t in range(NT):
            xb = xin[t % 2]
            T.add("sp", lambda e, t=t, xb=xb: e.dma_start(out=xb, in_=x_d[t * 128:(t + 1) * 128, :]),
                  writes=["xin%d" % (t % 2)], dma="xin%d" % (t % 2))
            for half in range(2):
                pb = ps[(2 * t + half) % 4]
                pbn = "ps%d" % ((2 * t + half) % 4)

                def tr(e, xb=xb, pb=pb, half=half):
                    r = None
                    for q in range(4):
                        c = half * 4 + q
                        r = e.transpose(pb[:, q * 128:(q + 1) * 128], xb[:, c * 128:(c + 1) * 128], id32[:])
                    return r
                T.add("pe", tr, reads=["xin%d" % (t % 2), "id32"], writes=[pbn])
                tgi = t // 4
                wr = ["hsw.%d.%d.%d" % (half * 4 + q, tgi, t % 4) for q in range(4)]
                if half == 0:
                    T.add("dve", lambda e, pb=pb, half=half, t=t: e.tensor_copy(
                        out=hsT[:, half * 4:half * 4 + 4, t * 128:(t + 1) * 128],
                        in_=pb[:].rearrange("p (q n) -> p q n", q=4)), reads=[pbn], writes=wr)
                else:
                    T.add("act", lambda e, pb=pb, half=half, t=t: e.activation(
                        out=hsT[:, half * 4:half * 4 + 4, t * 128:(t + 1) * 128],
                        in_=pb[:].rearrange("p (q n) -> p q n", q=4), func=AF.Copy), reads=[pbn], writes=wr)

        def rstd_for_tg(g, sqb, out_ap, out_res, psn, psn_name):
            for c in range(KC):
                sb_ = sqb[c % 2]
                T.add("pool", lambda e, c=c, sb_=sb_: e.tensor_tensor(
                    out=sb_, in0=hsT[:, c, g * TG:(g + 1) * TG], in1=hsT[:, c, g * TG:(g + 1) * TG], op=ALU.mult),
                    reads=["hs.%d.%d" % (c, g)], writes=["sq%d" % (c % 2)])
                T.add("pe", lambda e, c=c, sb_=sb_: e.matmul(psn[:], lhsT=NOm, rhs=sb_, start=(c == 0), stop=(c == KC - 1)),
                      reads=["sq%d" % (c % 2), "cb"] + ([psn_name] if c else []), writes=[psn_name])
            T.add("act", lambda e: e.activation(out=out_ap, in_=psn[:], func=AF.Ln, bias=EPS, scale=-1.0 / D), reads=[psn_name], writes=[out_res])
            T.add("act", lambda e: e.activation(out=out_ap, in_=out_ap, func=AF.Exp, scale=-0.5), reads=[out_res], writes=[out_res])

        def make_xn(g, gcol0, l, rstd_ap, rstd_res, xn, xp="xnT", npool=0):
            for c in range(KC):
                T.add("pool" if c >= KC - npool else "dve", lambda e, c=c: e.scalar_tensor_tensor(
                    out=xn[:, c, :], in0=hsT[:, c, g * TG:(g + 1) * TG], scalar=vcol(l, gcol0 + c), in1=rstd_ap,
                    op0=ALU.mult, op1=ALU.mult), reads=["hs.%d.%d" % (c, g), rstd_res, "vecs"], writes=["%s.%d" % (xp, c)])
        XR = ["xnT.%d" % c for c in range(KC)]

        def attention_layer(l):
            fox = (l % 2 == 0)
            j_ = l // 2
            fence()
            ar.reset()
            rstd_bc = ar.f32(S)
            xnTb = [ar.bf16(KC * TG).rearrange("p (c n) -> p c n", c=KC) for _ in range(2)]
            xnT = xnTb[0]
            mark = ar.off
            sqb = [ar.bf16(TG) for _ in range(2)]
            psS = [ps[0], ps[1], ps[4], ps[5]]
            psSn = ["psb0", "psb1", "psb4", "psb5"]
            psO2, psD, psQ, psK, psV, psN = [ps[2], ps[3]], ps[6], ps[4], ps[5], ps[6], ps[7]

            for g in range(NTG):
                rstd_for_tg(g, sqb, rstd_bc[:, g * TG:(g + 1) * TG], "rstd.%d" % g, psN, "psN")

            CAUG = ["caug_s.%d.%d" % (p_, g_) for p_ in range(3) for g_ in range(NTG)]
            if fox:
                wf = ar.bf16(KC * 96).rearrange("p (c n) -> p c n", c=KC)
                ft = [ar.f32(TG) for _ in range(4)]
                fb = [ar.bf16(TG) for _ in range(3)]
                onesr = ar.f32(TG)
                T.add("pool", lambda e: e.dma_start(out=wf, in_=wf_d[j_].rearrange("p (c n) -> p c n", c=KC)), writes=["wf"], dma="wf")
                T.add("pool", lambda e: e.memset(onesr[:, :], 1.0), writes=["onesr"])

                def fphase(g):
                    make_xn(g, 0, l, rstd_bc[:, g * TG:(g + 1) * TG], "rstd.%d" % g, xnT)

                    def fmm(e):
                        r = None
                        for c in range(KC):
                            r = e.matmul(psN[0:96, :], lhsT=wf[:, c, :], rhs=xnT[:, c, :], start=(c == 0), stop=(c == KC - 1))
                        return r
                    T.add("pe", fmm, reads=["wf"] + XR, writes=["psN"])
                    cprev = ft[2 + (g + 1) % 2]
                    ccur = ft[2 + g % 2]
                    cn = "fc%d" % (g % 2)
                    cpn = "fc%d" % ((g + 1) % 2)
                    T.add("act", lambda e: e.activation(out=ft[0][0:80, :], in_=psN[0:80, :], func=AF.Identity, bias=vcol(l, 192)[0:80, :]),
                          reads=["psN", "vecs"], writes=["ft0"])
                    T.add("act", lambda e: e.activation(out=ft[0][0:80, :], in_=ft[0][0:80, :], func=AF.Exp, scale=-1.0), reads=["ft0"], writes=["ft0"])
                    T.add("act", lambda e: e.activation(out=ft[1][0:80, :], in_=ft[0][0:80, :], func=AF.Ln, bias=1.0), reads=["ft0"], writes=["ft1"])
                    if g == 0:
                        T.add("dve", lambda e: e.tensor_tensor_scan(out=ccur[0:80, :], data0=onesr[0:80, :], data1=ft[1][0:80, :],
                              initial=0.0, op0=ALU.mult, op1=ALU.subtract), reads=["onesr", "ft1"], writes=[cn])
                    else:
                        T.add("dve", lambda e: e.tensor_tensor_scan(out=ccur[0:80, :], data0=onesr[0:80, :], data1=ft[1][0:80, :],
                              initial=cprev[0:80, TG - 1:TG], op0=ALU.mult, op1=ALU.subtract),
                              reads=["onesr", "ft1", cpn], writes=[cn])
                    T.add("dve", lambda e: e.tensor_copy(out=fb[0][0:80, :], in_=ccur[0:80, :]), reads=[cn], writes=["fb0"])
                    T.add("dve", lambda e: e.tensor_tensor(out=ft[0][0:80, :], in0=ccur[0:80, :], in1=fb[0][0:80, :], op=ALU.subtract),
                          reads=[cn, "fb0"], writes=["ft0"])
                    T.add("dve", lambda e: e.tensor_copy(out=fb[1][0:80, :], in_=ft[0][0:80, :]), reads=["ft0"], writes=["fb1"])
                    T.add("dve", lambda e: e.tensor_tensor(out=ft[1][0:80, :], in0=ft[0][0:80, :], in1=fb[1][0:80, :], op=ALU.subtract),
                          reads=["ft0", "fb1"], writes=["ft1"])
                    T.add("dve", lambda e: e.tensor_copy(out=fb[2][0:80, :], in_=ft[1][0:80, :]), reads=["ft1"], writes=["fb2"])
                    for part in range(3):
                        T.add("sp", lambda e, part=part: e.dma_start(out=caug_s[part, :, g * TG:(g + 1) * TG],
                              in_=fb[part][32 * part:32 * part + 16, :]), reads=["fb%d" % part], writes=["caug_s.%d.%d" % (part, g)], dma="caug_w")
                for g in range(NTG):
                    fphase(g)
            fence()
            ar.off = mark
            wh = [ar.bf16(KC * 192).rearrange("p (c n) -> p c n", c=KC) for _ in range(2)]
            Qp = ar.bf16(S)
            Kp = ar.bf16(S)
            VW = DH + 2
            Vt = ar.bf16(NT * VW).rearrange("p (t d) -> p t d", t=NT)
            OTs = [ar.bf16(TG) for _ in range(1)]
            if fox:
                PT = [ar.bf16(TG) for _ in range(4)]
                rden = ar.f32(TG)
                rbc = ar.f32(TG)
                T.add("pool", lambda e: e.memset(Vt[:, :, DH:DH + 1], 1.0), writes=["V.ones"])
            else:
                Eb = [ar.f32(TG) for _ in range(1)]
                SPb = [ar.bf16(TG) for _ in range(3)]
                aTb = [ar.bf16(TG) for _ in range(3)]
                Rbb = [ar.bf16(TG) for _ in range(4)]
                R32 = ar.f32(TG)
            if fox:
                T.add("pool", lambda e: e.memset(Qp[64:70, :], -1.0), writes=["Qp.aug"])
                T.add("pool", lambda e: e.memset(Kp[64:70, :], 1.0), writes=["Kp.aug"])
            else:
                T.add("pool", lambda e: e.memset(Qp[64:128, :], 0.0), writes=["Qp.aug"])
                T.add("pool", lambda e: e.memset(Kp[64:128, :], 0.0), writes=["Kp.aug"])
            KR = 70 if fox else 128

            pending = []

            def flush():
                for fn_ in pending:
                    fn_()
                del pending[:]

            def project(h, g, whb, whn, skip_make=False):
                xnT = xnTb[g % 2]
                xp = "xnT" if g % 2 == 0 else "xnU"
                XR = ["%s.%d" % (xp, c) for c in range(KC)]
                if not skip_make:
                    make_xn(g, 0, l, rstd_bc[:, g * TG:(g + 1) * TG], "rstd.%d" % g, xnT, xp, npool=0)

                def qmm(e):
                    r = None
                    for c in range(KC):
                        r = e.matmul(psQ[0:64, :], lhsT=whb[:, c, 0:64], rhs=xnT[:, c, :], start=(c == 0), stop=(c == KC - 1))
                    return r

                def kmm(e):
                    r = None
                    for c in range(KC):
                        r = e.matmul(psK[0:64, :], lhsT=whb[:, c, 64:128], rhs=xnT[:, c, :], start=(c == 0), stop=(c == KC - 1))
                    return r

                def vmm(e):
                    r = None
                    for tt in range(4):
                        for c in range(KC):
                            r = e.matmul(psV[:, tt * 64:(tt + 1) * 64], lhsT=xnT[:, c, tt * 128:(tt + 1) * 128], rhs=whb[:, c, 128:192],
                                         start=(c == 0), stop=(c == KC - 1))
                    return r
                T.add("pe", qmm, reads=XR + [whn], writes=["psb4"])
                T.add("pe", kmm, reads=XR + [whn], writes=["psb5"])
                T.add("pe", vmm, reads=XR + [whn], writes=["psb6"])
                T.add("act", lambda e: e.activation(out=Qp[0:64, g * TG:(g + 1) * TG], in_=psQ[0:64, :], func=AF.Copy, scale=0.125),
                      reads=["psb4"], writes=["Qp.%d" % g])
                T.add("act", lambda e: e.activation(out=Kp[0:64, g * TG:(g + 1) * TG], in_=psK[0:64, :], func=AF.Copy), reads=["psb5"], writes=["Kp.%d" % g])
                T.add("act", lambda e: e.activation(out=Vt[:, g * 4:(g + 1) * 4, 0:DH], in_=psV[:, 0:256].rearrange("p (t d) -> p t d", t=4), func=AF.Copy),
                      reads=["psb6"], writes=["V.%d" % g])

            def attend(h, g):
                if fox:
                    blocks = [(4 * g + m, m) for m in range(4)] + [(j, None) for j in range(4 * g - 1, -1, -1)]
                else:
                    blocks = [(4 * g + m, m) for m in range(3, -1, -1)] + [(j, None) for j in range(4 * g - 1, -1, -1)]
                nb = len(blocks)
                qres = ["Qp.%d" % g, "Qp.aug"]
                q0 = g * TG
                mk = maskF if fox else maskS

                def score_op(i):
                    j, m = blocks[i]
                    pS = psS[i % 4]
                    c0 = 0 if m is None else m * 128
                    kres = ["Kp.%d" % (j // 4), "Kp.aug"]

                    def f(e):
                        kk = Kp[0:KR, j * 128:(j + 1) * 128]
                        if m is None:
                            return e.matmul(pS[:, :], lhsT=kk, rhs=Qp[0:KR, q0:q0 + TG], start=True, stop=True)
                        e.matmul(pS[:, c0:c0 + 128], lhsT=identb, rhs=mk, start=True, stop=False)
                        r = e.matmul(pS[:, c0:c0 + 128], lhsT=kk, rhs=Qp[0:KR, q0 + c0:q0 + c0 + 128], start=False, stop=True)
                        if c0 + 128 < TG:
                            r = e.matmul(pS[:, c0 + 128:TG], lhsT=kk, rhs=Qp[0:KR, q0 + c0 + 128:q0 + TG], start=False, stop=True)
                        return r
                    T.add("pe", f, reads=qres + kres + ["cb"], writes=[psSn[i % 4]])

                ob = OTs[0]
                obn = "OTs0"
                psO = psO2[g % 2]
                pOn = "psO%d" % (g % 2)
                if fox:
                    def tail_op(i):
                        j, m = blocks[i]
                        pS = psS[i % 4]
                        c0 = 0 if m is None else m * 128
                        pt = PT[i % 4]
                        ptn = "PT%d" % (i % 4)
                        T.add("act", lambda e: e.activation(out=pt[:, c0:TG], in_=pS[:, c0:TG], func=AF.Exp),
                              reads=[psSn[i % 4]], writes=[ptn])
                        T.add("pe", lambda e: e.matmul(psO[0:DH + 1, c0:TG], lhsT=Vt[:, j, 0:DH + 1], rhs=pt[:, c0:TG], start=(i == 0), stop=(i == nb - 1)),
                              reads=[ptn, "V.%d" % (j // 4), "V.ones"] + ([pOn] if i else []), writes=[pOn])
                    score_op(0)
                    if nb > 1:
                        score_op(1)
                    for i in range(nb):
                        if i + 2 < nb:
                            score_op(i + 2)
                        if i == 3:
                            flush()
                        tail_op(i)
                    T.add("dve", lambda e: e.reciprocal(out=rden[64:65, :], in_=psO[64:65, :]), reads=[pOn], writes=["rden"])

                    def part_b():
                        T.add("pe", lambda e: e.matmul(psD[0:64, :], lhsT=one1[64:65, 0:64], rhs=rden[64:65, :], start=True, stop=True),
                              reads=["rden", "one1"], writes=["psb6"])
                        T.add("dve", lambda e: e.tensor_copy(out=rbc[0:64, :], in_=psD[0:64, :]), reads=["psb6"], writes=["rbc"])
                        T.add("dve", lambda e: e.tensor_tensor(out=ob[0:64, :], in0=psO[0:64, :], in1=rbc[0:64, :], op=ALU.mult),
                              reads=[pOn, "rbc"], writes=[obn])
                        T.add("sp", lambda e: e.dma_start(out=ot_s[h * 64:(h + 1) * 64, g * TG:(g + 1) * TG], in_=ob[0:64, :]),
                              reads=[obn], writes=["ot_s.%d.%d" % (h, g)], dma="otw")
                    pending.append(part_b)
                    return
                else:
                    T.add("pool", lambda e: e.memset(R32[:, :], 0.0), writes=["R32"])

                    def s1(i):
                        j, m = blocks[i]
                        c0 = 0 if m is None else m * 128
                        score_op(i)
                        pS = psS[i % 4]
                        Ei = Eb[0]
                        SPi = SPb[i % 3]
                        T.add("act", lambda e: e.activation(out=Ei[:, c0:TG], in_=pS[:, c0:TG], func=AF.Exp), reads=[psSn[i % 4]], writes=["E0"])
                        T.add("act", lambda e: e.activation(out=SPi[:, c0:TG], in_=Ei[:, c0:TG], func=AF.Ln, bias=1.0), reads=["E0"], writes=["SP%d" % (i % 3)])
                        if i + 1 < nb:
                            j2, m2 = blocks[i + 1]
                            c02 = 0 if m2 is None else m2 * 128
                            Rn = Rbb[(i + 1) % 4]
                            T.add("dve", lambda e: e.tensor_tensor(out=R32[:, c0:TG], in0=R32[:, c0:TG], in1=SPi[:, c0:TG], op=ALU.add),
                                  reads=["R32", "SP%d" % (i % 3)], writes=["R32"])
                            T.add("dve", lambda e: e.tensor_copy(out=Rn[:, c02:TG], in_=R32[:, c02:TG]), reads=["R32"], writes=["Rb%d" % ((i + 1) % 4)])

                    def s2(i):
                        j, m = blocks[i]
                        c0 = 0 if m is None else m * 128
                        pS = psS[i % 4]
                        SPi = SPb[i % 3]
                        ai = aTb[i % 3]
                        Rb = Rbb[i % 4]
                        c1 = c0 + 128 if m is not None else c0
                        last = (i == nb - 1)

                        def st2(e):
                            r = e.matmul(pS[:, c0:TG], lhsT=NTm, rhs=SPi[:, c0:TG], start=False, stop=(c1 >= TG))
                            if c1 < TG:
                                r = e.matmul(pS[:, c1:TG], lhsT=NOm, rhs=Rb[:, c1:TG], start=False, stop=True)
                            return r
                        T.add("pe", st2, reads=["SP%d" % (i % 3), "cb", psSn[i % 4]] + (["Rb%d" % (i % 4)] if c1 < TG else []), writes=[psSn[i % 4]])
                        T.add("act", lambda e: e.activation(out=ai[:, c0:TG], in_=pS[:, c0:TG], func=AF.Exp), reads=[psSn[i % 4]], writes=["aT%d" % (i % 3)])

                    def s3(i):
                        j, m = blocks[i]
                        c0 = 0 if m is None else m * 128
                        ai = aTb[i % 3]
                        last = (i == nb - 1)
                        T.add("pe", lambda e: e.matmul(psO[0:64, c0:TG], lhsT=Vt[:, j, 0:DH], rhs=ai[:, c0:TG], start=(i == 0), stop=last),
                              reads=["aT%d" % (i % 3), "V.%d" % (j // 4)] + ([pOn] if i else []), writes=[pOn])
                    s1(0)
                    if nb > 1:
                        s1(1)
                    for i in range(nb):
                        if i + 2 < nb:
                            s1(i + 2)
                        s2(i)
                        if i >= 1:
                            s3(i - 1)
                    s3(nb - 1)
                    T.add("dve", lambda e: e.tensor_copy(out=ob[0:64, :], in_=psO[0:64, :]), reads=[pOn], writes=[obn])
                T.add("sp", lambda e: e.dma_start(out=ot_s[h * 64:(h + 1) * 64, g * TG:(g + 1) * TG], in_=ob[0:64, :]),
                      reads=[obn], writes=["ot_s.%d.%d" % (h, g)], dma="otw")

            for h in range(H):
                whb = wh[h % 2]
                whn = "wh%d" % (h % 2)
                T.add("pool", lambda e, h=h, whb=whb: e.dma_start(out=whb, in_=wqkv_d[l][h].rearrange("p (c n) -> p c n", c=KC)),
                      writes=[whn], dma=whn)
                if h == 1:
                    cast_ffn_weights(l)
                if fox:
                    T.add("sp", lambda e, h=h: e.dma_start(out=Qp[64:67, :], in_=caug_s[:, h, :]), reads=CAUG, writes=["Qp.aug"], dma="aug")
                    T.add("sp", lambda e, h=h: e.dma_start(out=Kp[67:70, :], in_=caug_s[:, h, :]), reads=CAUG, writes=["Kp.aug"], dma="aug")
                for g in range(NTG):
                    project(h, g, whb, whn, skip_make=(g == 0 and h > 0))
                for g in range(NTG):
                    if g == NTG - 1 and h + 1 < H:
                        make_xn(0, 0, l, rstd_bc[:, 0:TG], "rstd.0", xnTb[0], "xnT", npool=0)
                    attend(h, g)
            flush()

            fence()
            ar.reset()
            wo = ar.bf16(8 * D).rearrange("p (c n) -> p c n", c=8)
            OTin = [ar.bf16(8 * TG).rearrange("p (c n) -> p c n", c=8) for _ in range(2)]
            T.add("pool", lambda e: e.dma_start(out=wo, in_=wo_d[l].rearrange("p (c n) -> p c n", c=8)), writes=["wo"], dma="wo")

            def wo_tg(g):
                ob = OTin[g % 2]
                obn = "OTin%d" % (g % 2)
                T.add("sp", lambda e: e.dma_start(out=ob, in_=ot_s[:, g * TG:(g + 1) * TG].rearrange("(c p) n -> p c n", p=128)),
                      reads=["ot_s.%d.%d" % (h_, g) for h_ in range(H)], writes=[obn], dma=obn)
                for co in range(KC):
                    py = ps[co % 2]
                    pyn = "ps%d" % (co % 2)

                    def omm(e, co=co, py=py):
                        r = None
                        for c in range(8):
                            r = e.matmul(py[:, :], lhsT=wo[:, c, co * 128:(co + 1) * 128], rhs=ob[:, c, :], start=(c == 0), stop=(c == 7))
                        return r
                    T.add("pe", omm, reads=["wo", obn], writes=[pyn])
                    T.add("dve", lambda e, co=co, py=py: e.tensor_tensor(out=hsT[:, co, g * TG:(g + 1) * TG], in0=py[:, :],
                          in1=hsT[:, co, g * TG:(g + 1) * TG], op=ALU.add), reads=[pyn, "hs.%d.%d" % (co, g)], writes=["hs.%d.%d" % (co, g)])
            for g in range(NTG):
                wo_tg(g)

        def ffn_layer(l):
            fence()
            ar.reset()
            xnT = ar.bf16(KC * TG).rearrange("p (c n) -> p c n", c=KC)
            actT = ar.bf16(FC * TG).rearrange("p (c n) -> p c n", c=FC)
            wu = [ar.bf16(KC * 256).rearrange("p (c n) -> p c n", c=KC) for _ in range(3)]
            wd = [ar.bf16(FC * 128).rearrange("p (c n) -> p c n", c=FC) for _ in range(2)]
            hext = [[ar.f32(516) for _ in range(2)] for _ in range(2)]
            tcv = [[ar.f32(TG) for _ in range(2)] for _ in range(2)]
            sqb = [ar.bf16(TG) for _ in range(2)]
            rstd = ar.f32(TG)
            halo = ar.f32(2 * FC * 2).rearrange("p (c n) -> p c n", n=2)
            psU = [ps[0], ps[2]]
            psG = [ps[1], ps[3]]
            psY = [ps[4], ps[5]]
            psN = ps[7]
            T.add("pool", lambda e: e.memset(halo[:, :, :], 0.0), writes=["halo"])

            def prep(g):
                rstd_for_tg(g, sqb, rstd[:, :], "rstdf", psN, "psN")
                make_xn(g, 8, l, rstd[:, :], "rstdf", xnT)

            def gate(f):
                par = f % 2
                T.add("act", lambda e: e.activation(out=tcv[par][1], in_=tcv[par][1], func=AF.Silu), reads=["tcv%d1" % par], writes=["tcv%d1" % par])
                T.add("pool", lambda e: e.tensor_tensor(out=actT[:, f, :], in0=tcv[par][1], in1=tcv[par][0], op=ALU.mult),
                      reads=["tcv%d1" % par, "tcv%d0" % par], writes=["actT.%d" % f])

            def up(g):
                for f in range(FC):
                    par = f % 2
                    wub = wu[f % 3]
                    wun = "wu%d" % (f % 3)
                    T.add("sp", lambda e, f=f, wub=wub: e.dma_start(out=wub, in_=wup_s[l][f].rearrange("p (c n) -> p c n", c=KC)),
                          reads=["wup_s%d.%d" % (l, f)], writes=[wun], dma=wun)
                    for ug in range(2):
                        if ug == 1 and f >= 1:
                            gate(f - 1)
                        pp = (psU if ug == 0 else psG)[par]
                        ppn = "psUG%d%d" % (ug, par)

                        def upmm(e, ug=ug, pp=pp, wub=wub):
                            r = None
                            for c in range(KC):
                                r = e.matmul(pp[:, :], lhsT=wub[:, c, ug * 128:(ug + 1) * 128], rhs=xnT[:, c, :], start=(c == 0), stop=(c == KC - 1))
                            return r
                        T.add("pe", upmm, reads=XR + [wun], writes=[ppn])
                        hx = hext[par][ug]
                        hxn = "hext%d%d" % (par, ug)
                        ch = ug * FC + f
                        tc_ = tcv[par][ug]
                        tcn = "tcv%d%d" % (par, ug)
                        wc = 16 + ch * 3
                        T.add("pool", lambda e, hx=hx, ch=ch: e.tensor_copy(out=hx[:, 0:2], in_=halo[:, ch, :]), reads=["halo.%d" % ch, "halo"], writes=[hxn + "h"])
                        T.add("act", lambda e, hx=hx, pp=pp: e.activation(out=hx[:, 2:514], in_=pp[:, :], func=AF.Copy), reads=[ppn], writes=[hxn])
                        T.add("act", lambda e, tc_=tc_, pp=pp, wc=wc, ch=ch: e.activation(out=tc_, in_=pp[:, :], func=AF.Identity,
                              scale=vcol(l, wc + 2), bias=vcol(l, 148 + ch)), reads=[ppn, "vecs"], writes=[tcn])
                        T.add("pool", lambda e, hx=hx, ch=ch: e.tensor_copy(out=halo[:, ch, :], in_=hx[:, 512:514]), reads=[hxn], writes=["halo.%d" % ch])
                        T.add("dve", lambda e, hx=hx, tc_=tc_, wc=wc: e.scalar_tensor_tensor(out=tc_, in0=hx[:, 1:513], scalar=vcol(l, wc + 1), in1=tc_,
                              op0=ALU.mult, op1=ALU.add), reads=[hxn, hxn + "h", tcn, "vecs"], writes=[tcn])
                        T.add("dve", lambda e, hx=hx, tc_=tc_, wc=wc: e.scalar_tensor_tensor(out=tc_, in0=hx[:, 0:512], scalar=vcol(l, wc), in1=tc_,
                              op0=ALU.mult, op1=ALU.add), reads=[hxn, hxn + "h", tcn, "vecs"], writes=[tcn])
                gate(FC - 1)

            def down(g):
                ar_ = ["actT.%d" % f for f in range(FC)]
                for co in range(KC):
                    wdb = wd[co % 2]
                    wdn = "wd%d" % (co % 2)
                    T.add("sp", lambda e, co=co, wdb=wdb: e.dma_start(out=wdb, in_=wdn_s[l][co].rearrange("p (c n) -> p c n", c=FC)),
                          reads=["wdn_s%d.%d" % (l, co)], writes=[wdn], dma=wdn)
                    py = psY[co % 2]
                    pyn = "psY%d" % (co % 2)

                    def dmm(e, wdb=wdb, py=py):
                        r = None
                        for f in range(FC):
                            r = e.matmul(py[:, :], lhsT=wdb[:, f, :], rhs=actT[:, f, :], start=(f == 0), stop=(f == FC - 1))
                        return r
                    T.add("pe", dmm, reads=ar_ + [wdn], writes=[pyn])
                    T.add("dve", lambda e, co=co, py=py: e.tensor_tensor(out=hsT[:, co, g * TG:(g + 1) * TG], in0=py[:, :],
                          in1=hsT[:, co, g * TG:(g + 1) * TG], op=ALU.add), reads=[pyn, "hs.%d.%d" % (co, g)], writes=["hs.%d.%d" % (co, g)])
            prep(0)
            for g in range(NTG):
                up(g)
                if g + 1 < NTG:
                    prep(g + 1)
                down(g)

        def final_phase(normed):
            fence()
            ar.reset()
            yT = ar.f32(KC * TG).rearrange("p (c n) -> p c n", c=KC)
            ot = [ar.f32(D) for _ in range(2)]
            sqb = [ar.bf16(TG) for _ in range(2)]
            rstd = ar.f32(TG)
            psN = ps[7]
            outs = []

            def fin_tg(g):
                if normed:
                    rstd_for_tg(g, sqb, rstd[:, :], "rstdf", psN, "psN")
                    for c in range(KC):
                        T.add("dve", lambda e, c=c: e.scalar_tensor_tensor(out=yT[:, c, :], in0=hsT[:, c, g * TG:(g + 1) * TG], scalar=vecs[:, FIN + c:FIN + c + 1],
                              in1=rstd[:, :], op0=ALU.mult, op1=ALU.mult), reads=["hs.%d.%d" % (c, g), "rstdf", "vecs"], writes=["yT.%d" % c])
                else:
                    for c in range(KC):
                        T.add("dve", lambda e, c=c: e.tensor_copy(out=yT[:, c, :], in_=hsT[:, c, g * TG:(g + 1) * TG]),
                              reads=["hs.%d.%d" % (c, g)], writes=["yT.%d" % c])
                for tt in range(4):
                    t = g * 4 + tt
                    ob = ot[t % 2]
                    obn = "ot%d" % (t % 2)
                    for half in range(2):
                        pb = ps[(2 * t + half) % 4]
                        pbn = "psf%d" % ((2 * t + half) % 4)

                        def tr(e, pb=pb, half=half, tt=tt):
                            r = None
                            for q in range(4):
                                c = half * 4 + q
                                r = e.transpose(pb[:, q * 128:(q + 1) * 128], yT[:, c, tt * 128:(tt + 1) * 128], id32[:])
                            return r
                        T.add("pe", tr, reads=["yT.%d" % (half * 4 + q) for q in range(4)] + ["id32"], writes=[pbn])
                        if half == 0:
                            T.add("dve", lambda e, pb=pb, ob=ob: e.tensor_copy(out=ob[:, 0:512], in_=pb[:, :]), reads=[pbn], writes=[obn + "a"])
                        else:
                            T.add("act", lambda e, pb=pb, ob=ob: e.activation(out=ob[:, 512:1024], in_=pb[:, :], func=AF.Copy), reads=[pbn], writes=[obn + "b"])
                    T.add("sp", lambda e, ob=ob, t=t: e.dma_start(out=out_d[t * 128:(t + 1) * 128, :], in_=ob), reads=[obn + "a", obn + "b"],
                          writes=["out.%d" % t], dma="out%d" % (t % 2))
                    outs.append("out.%d" % t)
            for g in range(NTG):
                fin_tg(g)
            T.add("sp", lambda e: e.wait_ge(sems["sp"], 0), reads=outs, writes=[])

        sems = {e: st.enter_context(nc.semaphore("s_" + e)) for e in COMPUTE + ("sp",)}
        for (l, what) in layers:
            if what == "attn":
                attention_layer(l)
            else:
                ffn_layer(l)
        final_phase(final)

        dsems = {k: st.enter_context(nc.semaphore("d_" + k)) for k in T.dma_keys()}
        block = st.enter_context(nc.Block())
        T.emit(nc, block, sems, dsems)
    return nc


_CONST_CACHE = {}


def _consts():
    if "cb" not in _CONST_CACHE:
        p = np.arange(128)[:, None]
        q = np.arange(128)[None, :]
        cbm = np.zeros((128, 768), np.float32)
        cbm[:, 0:128] = np.eye(128, dtype=np.float32)
        cbm[:, 128:256] = np.where(p > q, NEG, 0.0)
        cbm[:, 256:384] = np.where(p >= q, NEG, 0.0)
        cbm[:, 384:512] = np.where(p >= q, -1.0, 0.0)
        cbm[:, 512:640] = -1.0
        cbm[:, 640:704] = 1.0
        _CONST_CACHE["cb"] = cbm
        _CONST_CACHE["ident"] = np.eye(128, dtype=np.float32)
    return _CONST_CACHE["cb"], _CONST_CACHE["ident"]


def layout_inputs(attn_norm, ffn_norm, final_norm, fox_w_qkvf, fox_b_f, fox_w_o,
                  sb_w_qkv, sb_w_o, ffn_w_up, ffn_w_conv, ffn_b_conv, ffn_w_down):
    f32 = np.float32
    m = {}
    cbm, ident = _consts()
    m["cb"] = cbm
    m["ident"] = ident
    NV = DEPTH * VS + 8
    vec = np.zeros((128, NV), f32)
    for l in range(DEPTH):
        o = l * VS
        vec[:, o:o + 8] = np.asarray(attn_norm[l], f32).reshape(8, 128).T
        vec[:, o + 8:o + 16] = np.asarray(ffn_norm[l], f32).reshape(8, 128).T
        wc = np.asarray(ffn_w_conv[l], f32).reshape(3, 44, 128).transpose(2, 1, 0).reshape(128, 132)
        vec[:, o + 16:o + 148] = wc
        vec[:, o + 148:o + 192] = np.asarray(ffn_b_conv[l], f32).reshape(44, 128).T
        if l % 2 == 0:
            bf = np.asarray(fox_b_f[l // 2], f32)
            for r0 in (0, 32, 64):
                vec[r0:r0 + 16, o + 192] = bf
    vec[:, DEPTH * VS:DEPTH * VS + 8] = np.asarray(final_norm, f32).reshape(8, 128).T
    m["vecs"] = vec
    for l in range(DEPTH):
        j = l // 2
        if l % 2 == 0:
            w = np.asarray(fox_w_qkvf[j], f32)
            wo = np.asarray(fox_w_o[j], f32)
            wfm = np.zeros((128, KC, 96), f32)
            wfr = w[:, 3 * D:3 * D + H].reshape(KC, 128, H).transpose(1, 0, 2)
            for r0 in (0, 32, 64):
                wfm[:, :, r0:r0 + 16] = wfr
            m["wf%d" % j] = wfm.reshape(128, KC * 96)
        else:
            w = np.asarray(sb_w_qkv[j], f32)
            wo = np.asarray(sb_w_o[j], f32)
        parts = [w[:, i * D:(i + 1) * D].reshape(KC, 128, H, DH).transpose(2, 1, 0, 3) for i in range(3)]
        m["wqkv%d" % l] = np.ascontiguousarray(np.concatenate(parts, axis=3)).reshape(H, 128, KC * 192)
        m["wo%d" % l] = np.ascontiguousarray(wo.reshape(8, 128, D).transpose(1, 0, 2)).reshape(128, 8 * D)
        wu = np.asarray(ffn_w_up[l], f32)
        pu = wu[:, 0:FF].reshape(KC, 128, FC, 128).transpose(2, 1, 0, 3)
        pg = wu[:, FF:2 * FF].reshape(KC, 128, FC, 128).transpose(2, 1, 0, 3)
        m["wup%d" % l] = np.ascontiguousarray(np.concatenate([pu, pg], axis=3)).reshape(FC, 128, KC * 256)
        wdn = np.asarray(ffn_w_down[l], f32)
        m["wdn%d" % l] = np.ascontiguousarray(wdn.reshape(FC, 128, KC, 128).transpose(2, 1, 0, 3)).reshape(KC, 128, FC * 128)
    return m


ALL_LAYERS = [(l, w) for l in range(DEPTH) for w in ("attn", "ffn")]


def kernel(x, attn_norm, ffn_norm, final_norm, fox_w_qkvf, fox_b_f, fox_w_o,
           sb_w_qkv, sb_w_o, ffn_w_up, ffn_w_conv, ffn_b_conv, ffn_w_down):
    x = np.asarray(x, np.float32)
    B, S, _ = x.shape
    shared = layout_inputs(attn_norm, ffn_norm, final_norm, fox_w_qkvf, fox_b_f, fox_w_o,
                           sb_w_qkv, sb_w_o, ffn_w_up, ffn_w_conv, ffn_b_conv, ffn_w_down)
    nc = build_program(S, ALL_LAYERS, True)
    in_maps = []
    for b in range(B):
        mm = dict(shared)
        mm["x"] = np.ascontiguousarray(x[b])
        in_maps.append(mm)
    res = run_bass_kernel_spmd(nc, in_maps, core_ids=list(range(B)))
    return np.stack([np.asarray(r["out"], np.float32) for r in res.results], axis=0)
```

```python
import contextlib
import numpy as np
import concourse.bass as bass
import concourse.mybir as mybir
from concourse.bass_utils import run_bass_kernel_spmd

F32 = mybir.dt.float32
BF16 = mybir.dt.bfloat16
AF = mybir.ActivationFunctionType
ALU = mybir.AluOpType

D = 1024
KC = 8
TG = 512
H = 16
DH = 64
FF = 2816
FC = 22
DEPTH = 4
EPS = 1e-6
NEG = -30000.0
VS = 193

COMPUTE = ("pe", "act", "dve", "pool")


class _Op:
    __slots__ = ("eng", "fn", "deps", "dma_key", "dma_val", "signal", "count", "idx", "dma_waits")


class Tracker:
    def __init__(self):
        self.ops = []
        self.last_w = {}
        self.readers = {}
        self.dma_cum = {}
        self.fence_idx = None
        self.last_eng = {}
        self.last_dma = {}

    def fence(self, fn):
        deps = set(self.last_eng.values()) | set(self.last_dma.values())
        idx = self.add("pool", fn, extra_deps=deps)
        self.fence_idx = idx
        return idx

    def add(self, eng, fn, reads=(), writes=(), dma=None, extra_deps=()):
        op = _Op()
        op.eng = eng
        op.fn = fn
        op.dma_key = dma
        op.signal = False
        op.count = 0
        op.idx = len(self.ops)
        deps = set()
        for r in reads:
            w = self.last_w.get(r)
            if w is not None:
                deps.add(w)
        for r in writes:
            w = self.last_w.get(r)
            if w is not None:
                deps.add(w)
            for rd in self.readers.get(r, ()):
                deps.add(rd)
        deps |= set(extra_deps)
        if self.fence_idx is not None:
            deps.add(self.fence_idx)
        deps.discard(op.idx)
        op.deps = deps
        op.dma_waits = {}
        for d in deps:
            k = self.ops[d].dma_key
            if k is not None:
                op.dma_waits[k] = self.dma_cum[k]
        if dma is not None:
            self.last_dma[dma] = op.idx
        else:
            self.last_eng[eng] = op.idx
        if dma is not None:
            self.dma_cum[dma] = self.dma_cum.get(dma, 0) + 16
            op.dma_val = self.dma_cum[dma]
        else:
            op.dma_val = 0
        self.ops.append(op)
        for r in writes:
            self.last_w[r] = op.idx
            self.readers[r] = []
        for r in reads:
            if r not in writes:
                self.readers.setdefault(r, []).append(op.idx)
        return op.idx

    def dma_keys(self):
        return list(self.dma_cum.keys())

    def emit(self, nc, block, sems, dma_sems):
        ops = self.ops
        for op in ops:
            for d in op.deps:
                dop = ops[d]
                if dop.dma_key is not None:
                    continue
                if dop.eng == op.eng and op.eng in ("pe", "sp"):
                    continue
                dop.signal = True
        cnt = {e: 0 for e in COMPUTE + ("sp",)}
        for op in ops:
            if op.dma_key is None and op.signal:
                cnt[op.eng] += 1
                op.count = cnt[op.eng]
        per_eng = {e: [] for e in COMPUTE + ("sp",)}
        for op in ops:
            per_eng[op.eng].append(op)

        def run(eng_name, eng):
            waited = {}
            for op in per_eng[eng_name]:
                wl = {}
                for d in op.deps:
                    dop = ops[d]
                    if dop.dma_key is not None:
                        k = ("dma", dop.dma_key)
                        v = op.dma_waits[dop.dma_key]
                    else:
                        if dop.eng == eng_name and eng_name in ("pe", "sp"):
                            continue
                        k = ("eng", dop.eng)
                        v = dop.count
                    if v > wl.get(k, 0):
                        wl[k] = v
                for k, v in wl.items():
                    if waited.get(k, 0) >= v:
                        continue
                    waited[k] = v
                    s = dma_sems[k[1]] if k[0] == "dma" else sems[k[1]]
                    eng.wait_ge(s, v)
                ins = op.fn(eng)
                if op.dma_key is not None:
                    ins.then_inc(dma_sems[op.dma_key], 16)
                elif op.signal:
                    ins.then_inc(sems[op.eng], 1)

        @block.sync
        def _(e):
            run("sp", e)

        @block.scalar
        def _(e):
            run("act", e)

        @block.vector
        def _(e):
            run("dve", e)

        @block.gpsimd
        def _(e):
            run("pool", e)

        @block.tensor
        def _(e):
            run("pe", e)


class Arena:
    def __init__(self, t, nbytes):
        self.t = t
        self.nbytes = nbytes
        self.off = 0

    def reset(self):
        self.off = 0

    def f32(self, n):
        assert self.off % 4 == 0
        a = self.off // 4
        self.off += 4 * n
        assert self.off <= self.nbytes, ("arena overflow", self.off, self.nbytes)
        return self.t[:, a:a + n]

    def bf16(self, n):
        n2 = (n + 1) // 2
        v = self.f32(n2)
        return v.bitcast(BF16)[:, 0:n]


def build_program(S, layers, final=True):
    NTG = S // TG
    NT = S // 128
    NV = DEPTH * VS + 8
    nc = bass.Bass("TRN2", target_bir_lowering=False)

    def din(name, shape, dt=F32):
        return nc.dram_tensor(name, list(shape), dt, kind="ExternalInput").ap()

    x_d = din("x", [S, D])
    vec_d = din("vecs", [128, NV])
    cb_d = din("cb", [128, 6 * 128])
    id_d = din("ident", [128, 128])
    wqkv_d = [din("wqkv%d" % l, [H, 128, KC * 192]) for l in range(DEPTH)]
    wf_d = [din("wf%d" % j, [128, KC * 96]) for j in range(2)]
    wo_d = [din("wo%d" % l, [128, 8 * D]) for l in range(DEPTH)]
    wup_d = [din("wup%d" % l, [FC, 128, KC * 256]) for l in range(DEPTH)]
    wdn_d = [din("wdn%d" % l, [KC, 128, FC * 128]) for l in range(DEPTH)]
    out_d = nc.dram_tensor("out", [S, D], F32, kind="ExternalOutput").ap()
    wup_s = [nc.dram_tensor("wup_s%d" % l, [FC, 128, KC * 256], BF16, kind="Internal").ap() for l in range(DEPTH)]
    wdn_s = [nc.dram_tensor("wdn_s%d" % l, [KC, 128, FC * 128], BF16, kind="Internal").ap() for l in range(DEPTH)]
    ot_s = nc.dram_tensor("ot_s", [D, S], BF16, kind="Internal").ap()
    caug_s = nc.dram_tensor("caug_s", [3, H, S], BF16, kind="Internal").ap()

    T = Tracker()

    with contextlib.ExitStack() as st:
        hsT = st.enter_context(nc.sbuf_tensor("hsT", [128, KC, S], F32))
        vecs = st.enter_context(nc.sbuf_tensor("vecs_sb", [128, NV], F32))
        cb = st.enter_context(nc.sbuf_tensor("cb_sb", [128, 6 * 128], BF16))
        id32 = st.enter_context(nc.sbuf_tensor("id32", [128, 128], F32))
        ones32 = st.enter_context(nc.sbuf_tensor("ones32", [128, 128], F32))
        fsc = st.enter_context(nc.sbuf_tensor("fsc", [128, 8], F32))
        one1 = st.enter_context(nc.sbuf_tensor("one1", [128, 64], F32))
        AR_BYTES = 75600
        art = st.enter_context(nc.sbuf_tensor("arena", [128, AR_BYTES // 4], F32))
        ar = Arena(art, AR_BYTES)
        ps = [st.enter_context(nc.psum_tensor("ps%d" % i, [128, 512], F32)) for i in range(8)]

        identb = cb[:, 0:128]
        maskF = cb[:, 128:256]
        maskS = cb[:, 256:384]
        NTm = cb[:, 384:512]
        NOm = cb[:, 512:640]
        ones64 = cb[:, 640:704]

        def vcol(l, k):
            o = l * VS + k
            return vecs[:, o:o + 1]
        FIN = DEPTH * VS

        def fence():
            T.fence(lambda e: e.memset(fsc[:, 0:1], 0.0))

        T.add("sp", lambda e: e.dma_start(out=vecs[:], in_=vec_d), writes=["vecs"], dma="vecs")
        T.add("sp", lambda e: e.dma_start(out=id32[:], in_=id_d), writes=["id32"], dma="id32")
        T.add("pool", lambda e: e.dma_start(out=cb[:], in_=cb_d), writes=["cb"], dma="cb")
        T.add("pool", lambda e: e.memset(ones32[:], 1.0 / D), writes=["ones32"])
        T.add("pool", lambda e: e.memset(one1[:], 1.0), writes=["one1"])

        def cast_ffn_weights(l):
            for f in range(FC):
                T.add("pool", lambda e, f=f: e.dma_start(out=wup_s[l][f], in_=wup_d[l][f]),
                      writes=["wup_s%d.%d" % (l, f)], dma="wups")
            for co in range(KC):
                T.add("pool", lambda e, co=co: e.dma_start(out=wdn_s[l][co], in_=wdn_d[l][co]),
                      writes=["wdn_s%d.%d" % (l, co)], dma="wdns")

        ar.reset()
        xin = [ar.f32(D) for _ in range(2)]
        for t in range(NT):
            xb = xin[t % 2]
            T.add("sp", lambda e, t=t, xb=xb: e.dma_start(out=xb, in_=x_d[t * 128:(t + 1) * 128, :]),
                  writes=["xin%d" % (t % 2)], dma="xin%d" % (t % 2))
            for half in range(2):
                pb = ps[(2 * t + half) % 4]
                pbn = "ps%d" % ((2 * t + half) % 4)

                def tr(e, xb=xb, pb=pb, half=half):
                    r = None
                    for q in range(4):
                        c = half * 4 + q
                        r = e.transpose(pb[:, q * 128:(q + 1) * 128], xb[:, c * 128:(c + 1) * 128], id32[:])
                    return r
                T.add("pe", tr, reads=["xin%d" % (t % 2), "id32"], writes=[pbn])
                tgi = t // 4
                wr = ["hsw.%d.%d.%d" % (half * 4 + q, tgi, t % 4) for q in range(4)]
                if half == 0:
                    T.add("dve", lambda e, pb=pb, half=half, t=t: e.tensor_copy(
                        out=hsT[:, half * 4:half * 4 + 4, t * 128:(t + 1) * 128],
                        in_=pb[:].rearrange("p (q n) -> p q n", q=4)), reads=[pbn], writes=wr)
                else:
                    T.add("act", lambda e, pb=pb, half=half, t=t: e.activation(
                        out=hsT[:, half * 4:half * 4 + 4, t * 128:(t + 1) * 128],
                        in_=pb[:].rearrange("p (q n) -> p q n", q=4), func=AF.Copy), reads=[pbn], writes=wr)

        def rstd_for_tg(g, sqb, out_ap, out_res, psn, psn_name):
            for c in range(KC):
                sb_ = sqb[c % 2]
                T.add("pool", lambda e, c=c, sb_=sb_: e.tensor_tensor(
                    out=sb_, in0=hsT[:, c, g * TG:(g + 1) * TG], in1=hsT[:, c, g * TG:(g + 1) * TG], op=ALU.mult),
                    reads=["hs.%d.%d" % (c, g)], writes=["sq%d" % (c % 2)])
                T.add("pe", lambda e, c=c, sb_=sb_: e.matmul(psn[:], lhsT=NOm, rhs=sb_, start=(c == 0), stop=(c == KC - 1)),
                      reads=["sq%d" % (c % 2), "cb"] + ([psn_name] if c else []), writes=[psn_name])
            T.add("act", lambda e: e.activation(out=out_ap, in_=psn[:], func=AF.Ln, bias=EPS, scale=-1.0 / D), reads=[psn_name], writes=[out_res])
            T.add("act", lambda e: e.activation(out=out_ap, in_=out_ap, func=AF.Exp, scale=-0.5), reads=[out_res], writes=[out_res])

        def make_xn(g, gcol0, l, rstd_ap, rstd_res, xn, xp="xnT", npool=0):
            for c in range(KC):
                T.add("pool" if c >= KC - npool else "dve", lambda e, c=c: e.scalar_tensor_tensor(
                    out=xn[:, c, :], in0=hsT[:, c, g * TG:(g + 1) * TG], scalar=vcol(l, gcol0 + c), in1=rstd_ap,
                    op0=ALU.mult, op1=ALU.mult), reads=["hs.%d.%d" % (c, g), rstd_res, "vecs"], writes=["%s.%d" % (xp, c)])
        XR = ["xnT.%d" % c for c in range(KC)]

        def attention_layer(l):
            fox = (l % 2 == 0)
            j_ = l // 2
            fence()
            ar.reset()
            rstd_bc = ar.f32(S)
            xnTb = [ar.bf16(KC * TG).rearrange("p (c n) -> p c n", c=KC) for _ in range(2)]
            xnT = xnTb[0]
            mark = ar.off
            sqb = [ar.bf16(TG) for _ in range(2)]
            psS = [ps[0], ps[1], ps[4], ps[5]]
            psSn = ["psb0", "psb1", "psb4", "psb5"]
            psO2, psD, psQ, psK, psV, psN = [ps[2], ps[3]], ps[6], ps[4], ps[5], ps[6], ps[7]

            for g in range(NTG):
                rstd_for_tg(g, sqb, rstd_bc[:, g * TG:(g + 1) * TG], "rstd.%d" % g, psN, "psN")

            CAUG = ["caug_s.%d.%d" % (p_, g_) for p_ in range(3) for g_ in range(NTG)]
            if fox:
                wf = ar.bf16(KC * 96).rearrange("p (c n) -> p c n", c=KC)
                ft = [ar.f32(TG) for _ in range(4)]
                fb = [ar.bf16(TG) for _ in range(3)]
                onesr = ar.f32(TG)
                T.add("pool", lambda e: e.dma_start(out=wf, in_=wf_d[j_].rearrange("p (c n) -> p c n", c=KC)), writes=["wf"], dma="wf")
                T.add("pool", lambda e: e.memset(onesr[:, :], 1.0), writes=["onesr"])

                def fphase(g):
                    make_xn(g, 0, l, rstd_bc[:, g * TG:(g + 1) * TG], "rstd.%d" % g, xnT)

                    def fmm(e):
                        r = None
                        for c in range(KC):
                            r = e.matmul(psN[0:96, :], lhsT=wf[:, c, :], rhs=xnT[:, c, :], start=(c == 0), stop=(c == KC - 1))
                        return r
                    T.add("pe", fmm, reads=["wf"] + XR, writes=["psN"])
                    cprev = ft[2 + (g + 1) % 2]
                    ccur = ft[2 + g % 2]
                    cn = "fc%d" % (g % 2)
                    cpn = "fc%d" % ((g + 1) % 2)
                    T.add("act", lambda e: e.activation(out=ft[0][0:80, :], in_=psN[0:80, :], func=AF.Identity, bias=vcol(l, 192)[0:80, :]),
                          reads=["psN", "vecs"], writes=["ft0"])
                    T.add("act", lambda e: e.activation(out=ft[0][0:80, :], in_=ft[0][0:80, :], func=AF.Exp, scale=-1.0), reads=["ft0"], writes=["ft0"])
                    T.add("act", lambda e: e.activation(out=ft[1][0:80, :], in_=ft[0][0:80, :], func=AF.Ln, bias=1.0), reads=["ft0"], writes=["ft1"])
                    if g == 0:
                        T.add("dve", lambda e: e.tensor_tensor_scan(out=ccur[0:80, :], data0=onesr[0:80, :], data1=ft[1][0:80, :],
                              initial=0.0, op0=ALU.mult, op1=ALU.subtract), reads=["onesr", "ft1"], writes=[cn])
                    else:
                        T.add("dve", lambda e: e.tensor_tensor_scan(out=ccur[0:80, :], data0=onesr[0:80, :], data1=ft[1][0:80, :],
                              initial=cprev[0:80, TG - 1:TG], op0=ALU.mult, op1=ALU.subtract),
                              reads=["onesr", "ft1", cpn], writes=[cn])
                    T.add("dve", lambda e: e.tensor_copy(out=fb[0][0:80, :], in_=ccur[0:80, :]), reads=[cn], writes=["fb0"])
                    T.add("dve", lambda e: e.tensor_tensor(out=ft[0][0:80, :], in0=ccur[0:80, :], in1=fb[0][0:80, :], op=ALU.subtract),
                          reads=[cn, "fb0"], writes=["ft0"])
                    T.add("dve", lambda e: e.tensor_copy(out=fb[1][0:80, :], in_=ft[0][0:80, :]), reads=["ft0"], writes=["fb1"])
                    T.add("dve", lambda e: e.tensor_tensor(out=ft[1][0:80, :], in0=ft[0][0:80, :], in1=fb[1][0:80, :], op=ALU.subtract),
                          reads=["ft0", "fb1"], writes=["ft1"])
                    T.add("dve", lambda e: e.tensor_copy(out=fb[2][0:80, :], in_=ft[1][0:80, :]), reads=["ft1"], writes=["fb2"])
                    for part in range(3):
                        T.add("sp", lambda e, part=part: e.dma_start(out=caug_s[part, :, g * TG:(g + 1) * TG],
                              in_=fb[part][32 * part:32 * part + 16, :]), reads=["fb%d" % part], writes=["caug_s.%d.%d" % (part, g)], dma="caug_w")
                for g in range(NTG):
                    fphase(g)
            fence()
            ar.off = mark
            wh = [ar.bf16(KC * 192).rearrange("p (c n) -> p c n", c=KC) for _ in range(2)]
            Qp = ar.bf16(S)
            Kp = ar.bf16(S)
            VW = 128 if fox else DH + 2
            Vt = ar.bf16(NT * VW).rearrange("p (t d) -> p t d", t=NT)
            OTs = [ar.bf16(TG) for _ in range(1)]
            if fox:
                PT = [ar.bf16(TG) for _ in range(4)]
                rden = ar.f32(TG)
                rbc = ar.f32(TG)
                T.add("pool", lambda e: e.memset(Vt[:, :, DH:VW], 0.0), writes=["V.ones"])
                T.add("pool", lambda e: e.memset(Vt[:, :, DH:DH + 1], 1.0), writes=["V.ones"])
            else:
                Eb = [ar.f32(TG) for _ in range(1)]
                SPb = [ar.bf16(TG) for _ in range(3)]
                aTb = [ar.bf16(TG) for _ in range(3)]
                Rbb = [ar.bf16(TG) for _ in range(4)]
                R32 = ar.f32(TG)
            if fox:
                T.add("pool", lambda e: e.memset(Qp[64:128, :], 0.0), writes=["Qp.aug"])
                T.add("pool", lambda e: e.memset(Kp[64:128, :], 0.0), writes=["Kp.aug"])
                T.add("pool", lambda e: e.memset(Qp[64:70, :], -1.0), writes=["Qp.aug"])
                T.add("pool", lambda e: e.memset(Kp[64:70, :], 1.0), writes=["Kp.aug"])
            else:
                T.add("pool", lambda e: e.memset(Qp[64:128, :], 0.0), writes=["Qp.aug"])
                T.add("pool", lambda e: e.memset(Kp[64:128, :], 0.0), writes=["Kp.aug"])
            KR = 128

            pending = []

            def flush():
                for fn_ in pending:
                    fn_()
                del pending[:]

            def project(h, g, whb, whn, skip_make=False):
                xnT = xnTb[g % 2]
                xp = "xnT" if g % 2 == 0 else "xnU"
                XR = ["%s.%d" % (xp, c) for c in range(KC)]
                if not skip_make:
                    make_xn(g, 0, l, rstd_bc[:, g * TG:(g + 1) * TG], "rstd.%d" % g, xnT, xp, npool=0)

                def qmm(e):
                    r = None
                    for c in range(KC):
                        r = e.matmul(psQ[0:64, :], lhsT=whb[:, c, 0:64], rhs=xnT[:, c, :], start=(c == 0), stop=(c == KC - 1))
                    return r

                def kmm(e):
                    r = None
                    for c in range(KC):
                        r = e.matmul(psK[0:64, :], lhsT=whb[:, c, 64:128], rhs=xnT[:, c, :], start=(c == 0), stop=(c == KC - 1))
                    return r

                def vmm(e):
                    r = None
                    for tt in range(4):
                        for c in range(KC):
                            r = e.matmul(psV[:, tt * 64:(tt + 1) * 64], lhsT=xnT[:, c, tt * 128:(tt + 1) * 128], rhs=whb[:, c, 128:192],
                                         start=(c == 0), stop=(c == KC - 1))
                    return r
                T.add("pe", qmm, reads=XR + [whn], writes=["psb4"])
                T.add("pe", kmm, reads=XR + [whn], writes=["psb5"])
                T.add("pe", vmm, reads=XR + [whn], writes=["psb6"])
                T.add("act", lambda e: e.activation(out=Qp[0:64, g * TG:(g + 1) * TG], in_=psQ[0:64, :], func=AF.Copy, scale=0.125),
                      reads=["psb4"], writes=["Qp.%d" % g])
                T.add("act", lambda e: e.activation(out=Kp[0:64, g * TG:(g + 1) * TG], in_=psK[0:64, :], func=AF.Copy), reads=["psb5"], writes=["Kp.%d" % g])
                T.add("act", lambda e: e.activation(out=Vt[:, g * 4:(g + 1) * 4, 0:DH], in_=psV[:, 0:256].rearrange("p (t d) -> p t d", t=4), func=AF.Copy),
                      reads=["psb6"], writes=["V.%d" % g])

            def attend(h, g):
                if fox:
                    blocks = [(4 * g + m, m) for m in range(4)] + [(j, None) for j in range(4 * g - 1, -1, -1)]
                else:
                    blocks = [(4 * g + m, m) for m in range(3, -1, -1)] + [(j, None) for j in range(4 * g - 1, -1, -1)]
                nb = len(blocks)
                qres = ["Qp.%d" % g, "Qp.aug"]
                q0 = g * TG
                mk = maskF if fox else maskS

                def score_op(i):
                    j, m = blocks[i]
                    pS = psS[i % 4]
                    c0 = 0 if m is None else m * 128
                    kres = ["Kp.%d" % (j // 4), "Kp.aug"]

                    def f(e):
                        kk = Kp[0:KR, j * 128:(j + 1) * 128]
                        if m is None:
                            return e.matmul(pS[:, :], lhsT=kk, rhs=Qp[0:KR, q0:q0 + TG], start=True, stop=True)
                        e.matmul(pS[:, c0:c0 + 128], lhsT=identb, rhs=mk, start=True, stop=False)
                        r = e.matmul(pS[:, c0:c0 + 128], lhsT=kk, rhs=Qp[0:KR, q0 + c0:q0 + c0 + 128], start=False, stop=True)
                        if c0 + 128 < TG:
                            r = e.matmul(pS[:, c0 + 128:TG], lhsT=kk, rhs=Qp[0:KR, q0 + c0 + 128:q0 + TG], start=False, stop=True)
                        return r
                    T.add("pe", f, reads=qres + kres + ["cb"], writes=[psSn[i % 4]])

                ob = OTs[0]
                obn = "OTs0"
                psO = psO2[g % 2]
                pOn = "psO%d" % (g % 2)
                if fox:
                    def tail_op(i):
                        j, m = blocks[i]
                        pS = psS[i % 4]
                        c0 = 0 if m is None else m * 128
                        pt = PT[i % 4]
                        ptn = "PT%d" % (i % 4)
                        T.add("act", lambda e: e.activation(out=pt[:, c0:TG], in_=pS[:, c0:TG], func=AF.Exp),
                              reads=[psSn[i % 4]], writes=[ptn])
                        T.add("pe", lambda e: e.matmul(psO[:, c0:TG], lhsT=Vt[:, j, :], rhs=pt[:, c0:TG], start=(i == 0), stop=(i == nb - 1)),
                              reads=[ptn, "V.%d" % (j // 4), "V.ones"] + ([pOn] if i else []), writes=[pOn])
                    score_op(0)
                    if nb > 1:
                        score_op(1)
                    for i in range(nb):
                        if i + 2 < nb:
                            score_op(i + 2)
                        if i == 3:
                            flush()
                        tail_op(i)
                    T.add("dve", lambda e: e.reciprocal(out=rden[64:65, :], in_=psO[64:65, :]), reads=[pOn], writes=["rden"])

                    def part_b():
                        T.add("pe", lambda e: e.matmul(psD[0:64, :], lhsT=one1[64:65, 0:64], rhs=rden[64:65, :], start=True, stop=True),
                              reads=["rden", "one1"], writes=["psb6"])
                        T.add("dve", lambda e: e.tensor_copy(out=rbc[0:64, :], in_=psD[0:64, :]), reads=["psb6"], writes=["rbc"])
                        T.add("dve", lambda e: e.tensor_tensor(out=ob[0:64, :], in0=psO[0:64, :], in1=rbc[0:64, :], op=ALU.mult),
                              reads=[pOn, "rbc"], writes=[obn])
                        T.add("sp", lambda e: e.dma_start(out=ot_s[h * 64:(h + 1) * 64, g * TG:(g + 1) * TG], in_=ob[0:64, :]),
                              reads=[obn], writes=["ot_s.%d.%d" % (h, g)], dma="otw")
                    pending.append(part_b)
                    return
                else:
                    T.add("pool", lambda e: e.memset(R32[:, :], 0.0), writes=["R32"])

                    def s1(i):
                        j, m = blocks[i]
                        c0 = 0 if m is None else m * 128
                        score_op(i)
                        pS = psS[i % 4]
                        Ei = Eb[0]
                        SPi = SPb[i % 3]
                        T.add("act", lambda e: e.activation(out=Ei[:, c0:TG], in_=pS[:, c0:TG], func=AF.Exp), reads=[psSn[i % 4]], writes=["E0"])
                        T.add("act", lambda e: e.activation(out=SPi[:, c0:TG], in_=Ei[:, c0:TG], func=AF.Ln, bias=1.0), reads=["E0"], writes=["SP%d" % (i % 3)])
                        if i + 1 < nb:
                            j2, m2 = blocks[i + 1]
                            c02 = 0 if m2 is None else m2 * 128
                            Rn = Rbb[(i + 1) % 4]
                            T.add("dve", lambda e: e.tensor_tensor(out=R32[:, c0:TG], in0=R32[:, c0:TG], in1=SPi[:, c0:TG], op=ALU.add),
                                  reads=["R32", "SP%d" % (i % 3)], writes=["R32"])
                            T.add("dve", lambda e: e.tensor_copy(out=Rn[:, c02:TG], in_=R32[:, c02:TG]), reads=["R32"], writes=["Rb%d" % ((i + 1) % 4)])

                    def s2(i):
                        j, m = blocks[i]
                        c0 = 0 if m is None else m * 128
                        pS = psS[i % 4]
                        SPi = SPb[i % 3]
                        ai = aTb[i % 3]
                        Rb = Rbb[i % 4]
                        c1 = c0 + 128 if m is not None else c0
                        last = (i == nb - 1)

                        def st2(e):
                            r = e.matmul(pS[:, c0:TG], lhsT=NTm, rhs=SPi[:, c0:TG], start=False, stop=(c1 >= TG))
                            if c1 < TG:
                                r = e.matmul(pS[:, c1:TG], lhsT=NOm, rhs=Rb[:, c1:TG], start=False, stop=True)
                            return r
                        T.add("pe", st2, reads=["SP%d" % (i % 3), "cb", psSn[i % 4]] + (["Rb%d" % (i % 4)] if c1 < TG else []), writes=[psSn[i % 4]])
                        T.add("act", lambda e: e.activation(out=ai[:, c0:TG], in_=pS[:, c0:TG], func=AF.Exp), reads=[psSn[i % 4]], writes=["aT%d" % (i % 3)])

                    def s3(i):
                        j, m = blocks[i]
                        c0 = 0 if m is None else m * 128
                        ai = aTb[i % 3]
                        last = (i == nb - 1)
                        T.add("pe", lambda e: e.matmul(psO[0:64, c0:TG], lhsT=Vt[:, j, 0:DH], rhs=ai[:, c0:TG], start=(i == 0), stop=last),
                              reads=["aT%d" % (i % 3), "V.%d" % (j // 4)] + ([pOn] if i else []), writes=[pOn])
                    s1(0)
                    if nb > 1:
                        s1(1)
                    for i in range(nb):
                        if i + 2 < nb:
                            s1(i + 2)
                        s2(i)
                        if i >= 1:
                            s3(i - 1)
                    s3(nb - 1)
                    T.add("dve", lambda e: e.tensor_copy(out=ob[0:64, :], in_=psO[0:64, :]), reads=[pOn], writes=[obn])
                T.add("sp", lambda e: e.dma_start(out=ot_s[h * 64:(h + 1) * 64, g * TG:(g + 1) * TG], in_=ob[0:64, :]),
                      reads=[obn], writes=["ot_s.%d.%d" % (h, g)], dma="otw")

            for h in range(H):
                whb = wh[h % 2]
                whn = "wh%d" % (h % 2)
                T.add("pool", lambda e, h=h, whb=whb: e.dma_start(out=whb, in_=wqkv_d[l][h].rearrange("p (c n) -> p c n", c=KC)),
                      writes=[whn], dma=whn)
                if h == 1:
                    cast_ffn_weights(l)
                if fox:
                    T.add("sp", lambda e, h=h: e.dma_start(out=Qp[64:67, :], in_=caug_s[:, h, :]), reads=CAUG, writes=["Qp.aug"], dma="aug")
                    T.add("sp", lambda e, h=h: e.dma_start(out=Kp[67:70, :], in_=caug_s[:, h, :]), reads=CAUG, writes=["Kp.aug"], dma="aug")
                for g in range(NTG):
                    project(h, g, whb, whn, skip_make=(g == 0 and h > 0))
                for g in range(NTG):
                    if g == NTG - 1 and h + 1 < H:
                        make_xn(0, 0, l, rstd_bc[:, 0:TG], "rstd.0", xnTb[0], "xnT", npool=0)
                    attend(h, g)
            flush()

            fence()
            ar.reset()
            wo = ar.bf16(8 * D).rearrange("p (c n) -> p c n", c=8)
            OTin = [ar.bf16(8 * TG).rearrange("p (c n) -> p c n", c=8) for _ in range(2)]
            T.add("pool", lambda e: e.dma_start(out=wo, in_=wo_d[l].rearrange("p (c n) -> p c n", c=8)), writes=["wo"], dma="wo")

            def wo_tg(g):
                ob = OTin[g % 2]
                obn = "OTin%d" % (g % 2)
                T.add("sp", lambda e: e.dma_start(out=ob, in_=ot_s[:, g * TG:(g + 1) * TG].rearrange("(c p) n -> p c n", p=128)),
                      reads=["ot_s.%d.%d" % (h_, g) for h_ in range(H)], writes=[obn], dma=obn)
                for co in range(KC):
                    py = ps[co % 2]
                    pyn = "ps%d" % (co % 2)

                    def omm(e, co=co, py=py):
                        r = None
                        for c in range(8):
                            r = e.matmul(py[:, :], lhsT=wo[:, c, co * 128:(co + 1) * 128], rhs=ob[:, c, :], start=(c == 0), stop=(c == 7))
                        return r
                    T.add("pe", omm, reads=["wo", obn], writes=[pyn])
                    T.add("dve", lambda e, co=co, py=py: e.tensor_tensor(out=hsT[:, co, g * TG:(g + 1) * TG], in0=py[:, :],
                          in1=hsT[:, co, g * TG:(g + 1) * TG], op=ALU.add), reads=[pyn, "hs.%d.%d" % (co, g)], writes=["hs.%d.%d" % (co, g)])
            for g in range(NTG):
                wo_tg(g)

        def ffn_layer(l):
            fence()
            ar.reset()
            xnT = ar.bf16(KC * TG).rearrange("p (c n) -> p c n", c=KC)
            actT = ar.bf16(FC * TG).rearrange("p (c n) -> p c n", c=FC)
            wu = [ar.bf16(KC * 256).rearrange("p (c n) -> p c n", c=KC) for _ in range(3)]
            wd = [ar.bf16(FC * 128).rearrange("p (c n) -> p c n", c=FC) for _ in range(2)]
            hext = [[ar.f32(516) for _ in range(2)] for _ in range(2)]
            tcv = [[ar.f32(TG) for _ in range(2)] for _ in range(2)]
            sqb = [ar.bf16(TG) for _ in range(2)]
            rstd = ar.f32(TG)
            halo = ar.f32(2 * FC * 2).rearrange("p (c n) -> p c n", n=2)
            psU = [ps[0], ps[2]]
            psG = [ps[1], ps[3]]
            psY = [ps[4], ps[5]]
            psN = ps[7]
            T.add("pool", lambda e: e.memset(halo[:, :, :], 0.0), writes=["halo"])

            def prep(g):
                rstd_for_tg(g, sqb, rstd[:, :], "rstdf", psN, "psN")
                make_xn(g, 8, l, rstd[:, :], "rstdf", xnT)

            def gate(f):
                par = f % 2
                T.add("act", lambda e: e.activation(out=tcv[par][1], in_=tcv[par][1], func=AF.Silu), reads=["tcv%d1" % par], writes=["tcv%d1" % par])
                T.add("pool", lambda e: e.tensor_tensor(out=actT[:, f, :], in0=tcv[par][1], in1=tcv[par][0], op=ALU.mult),
                      reads=["tcv%d1" % par, "tcv%d0" % par], writes=["actT.%d" % f])

            def up(g):
                for f in range(FC):
                    par = f % 2
                    wub = wu[f % 3]
                    wun = "wu%d" % (f % 3)
                    T.add("sp", lambda e, f=f, wub=wub: e.dma_start(out=wub, in_=wup_s[l][f].rearrange("p (c n) -> p c n", c=KC)),
                          reads=["wup_s%d.%d" % (l, f)], writes=[wun], dma=wun)
                    for ug in range(2):
                        pp = (psU if ug == 0 else psG)[par]
                        ppn = "psUG%d%d" % (ug, par)

                        def upmm(e, ug=ug, pp=pp, wub=wub):
                            r = None
                            for c in range(KC):
                                r = e.matmul(pp[:, :], lhsT=wub[:, c, ug * 128:(ug + 1) * 128], rhs=xnT[:, c, :], start=(c == 0), stop=(c == KC - 1))
                            return r
                        T.add("pe", upmm, reads=XR + [wun], writes=[ppn])
                        hx = hext[par][ug]
                        hxn = "hext%d%d" % (par, ug)
                        ch = ug * FC + f
                        tc_ = tcv[par][ug]
                        tcn = "tcv%d%d" % (par, ug)
                        wc = 16 + ch * 3
                        T.add("pool", lambda e, hx=hx, ch=ch: e.tensor_copy(out=hx[:, 0:2], in_=halo[:, ch, :]), reads=["halo.%d" % ch, "halo"], writes=[hxn + "h"])
                        T.add("act", lambda e, hx=hx, pp=pp: e.activation(out=hx[:, 2:514], in_=pp[:, :], func=AF.Copy), reads=[ppn], writes=[hxn])
                        T.add("act", lambda e, tc_=tc_, pp=pp, wc=wc, ch=ch: e.activation(out=tc_, in_=pp[:, :], func=AF.Identity,
                              scale=vcol(l, wc + 2), bias=vcol(l, 148 + ch)), reads=[ppn, "vecs"], writes=[tcn])
                        T.add("pool", lambda e, hx=hx, ch=ch: e.tensor_copy(out=halo[:, ch, :], in_=hx[:, 512:514]), reads=[hxn], writes=["halo.%d" % ch])
                        T.add("dve", lambda e, hx=hx, tc_=tc_, wc=wc: e.scalar_tensor_tensor(out=tc_, in0=hx[:, 1:513], scalar=vcol(l, wc + 1), in1=tc_,
                              op0=ALU.mult, op1=ALU.add), reads=[hxn, hxn + "h", tcn, "vecs"], writes=[tcn])
                        T.add("dve", lambda e, hx=hx, tc_=tc_, wc=wc: e.scalar_tensor_tensor(out=tc_, in0=hx[:, 0:512], scalar=vcol(l, wc), in1=tc_,
                              op0=ALU.mult, op1=ALU.add), reads=[hxn, hxn + "h", tcn, "vecs"], writes=[tcn])
                    if f >= 1:
                        gate(f - 1)
                gate(FC - 1)

            def down(g):
                ar_ = ["actT.%d" % f for f in range(FC)]
                for co in range(KC):
                    wdb = wd[co % 2]
                    wdn = "wd%d" % (co % 2)
                    T.add("sp", lambda e, co=co, wdb=wdb: e.dma_start(out=wdb, in_=wdn_s[l][co].rearrange("p (c n) -> p c n", c=FC)),
                          reads=["wdn_s%d.%d" % (l, co)], writes=[wdn], dma=wdn)
                    py = psY[co % 2]
                    pyn = "psY%d" % (co % 2)

                    def dmm(e, wdb=wdb, py=py):
                        r = None
                        for f in range(FC):
                            r = e.matmul(py[:, :], lhsT=wdb[:, f, :], rhs=actT[:, f, :], start=(f == 0), stop=(f == FC - 1))
                        return r
                    T.add("pe", dmm, reads=ar_ + [wdn], writes=[pyn])
                    T.add("dve", lambda e, co=co, py=py: e.tensor_tensor(out=hsT[:, co, g * TG:(g + 1) * TG], in0=py[:, :],
                          in1=hsT[:, co, g * TG:(g + 1) * TG], op=ALU.add), reads=[pyn, "hs.%d.%d" % (co, g)], writes=["hs.%d.%d" % (co, g)])
            prep(0)
            for g in range(NTG):
                up(g)
                if g + 1 < NTG:
                    prep(g + 1)
                down(g)

        def final_phase(normed):
            fence()
            ar.reset()
            yT = ar.f32(KC * TG).rearrange("p (c n) -> p c n", c=KC)
            ot = [ar.f32(D) for _ in range(2)]
            sqb = [ar.bf16(TG) for _ in range(2)]
            rstd = ar.f32(TG)
            psN = ps[7]
            outs = []

            def fin_tg(g):
                if normed:
                    rstd_for_tg(g, sqb, rstd[:, :], "rstdf", psN, "psN")
                    for c in range(KC):
                        T.add("dve", lambda e, c=c: e.scalar_tensor_tensor(out=yT[:, c, :], in0=hsT[:, c, g * TG:(g + 1) * TG], scalar=vecs[:, FIN + c:FIN + c + 1],
                              in1=rstd[:, :], op0=ALU.mult, op1=ALU.mult), reads=["hs.%d.%d" % (c, g), "rstdf", "vecs"], writes=["yT.%d" % c])
                else:
                    for c in range(KC):
                        T.add("dve", lambda e, c=c: e.tensor_copy(out=yT[:, c, :], in_=hsT[:, c, g * TG:(g + 1) * TG]),
                              reads=["hs.%d.%d" % (c, g)], writes=["yT.%d" % c])
                for tt in range(4):
                    t = g * 4 + tt
                    ob = ot[t % 2]
                    obn = "ot%d" % (t % 2)
                    for half in range(2):
                        pb = ps[(2 * t + half) % 4]
                        pbn = "psf%d" % ((2 * t + half) % 4)

                        def tr(e, pb=pb, half=half, tt=tt):
                            r = None
                            for q in range(4):
                                c = half * 4 + q
                                r = e.transpose(pb[:, q * 128:(q + 1) * 128], yT[:, c, tt * 128:(tt + 1) * 128], id32[:])
                            return r
                        T.add("pe", tr, reads=["yT.%d" % (half * 4 + q) for q in range(4)] + ["id32"], writes=[pbn])
                        if half == 0:
                            T.add("dve", lambda e, pb=pb, ob=ob: e.tensor_copy(out=ob[:, 0:512], in_=pb[:, :]), reads=[pbn], writes=[obn + "a"])
                        else:
                            T.add("act", lambda e, pb=pb, ob=ob: e.activation(out=ob[:, 512:1024], in_=pb[:, :], func=AF.Copy), reads=[pbn], writes=[obn + "b"])
                    T.add("sp", lambda e, ob=ob, t=t: e.dma_start(out=out_d[t * 128:(t + 1) * 128, :], in_=ob), reads=[obn + "a", obn + "b"],
                          writes=["out.%d" % t], dma="out%d" % (t % 2))
                    outs.append("out.%d" % t)
            for g in range(NTG):
                fin_tg(g)
            T.add("sp", lambda e: e.wait_ge(sems["sp"], 0), reads=outs, writes=[])

        sems = {e: st.enter_context(nc.semaphore("s_" + e)) for e in COMPUTE + ("sp",)}
        for (l, what) in layers:
            if what == "attn":
                attention_layer(l)
            else:
                ffn_layer(l)
        final_phase(final)

        dsems = {k: st.enter_context(nc.semaphore("d_" + k)) for k in T.dma_keys()}
        block = st.enter_context(nc.Block())
        T.emit(nc, block, sems, dsems)
    return nc


_CONST_CACHE = {}


def _consts():
    if "cb" not in _CONST_CACHE:
        p = np.arange(128)[:, None]
        q = np.arange(128)[None, :]
        cbm = np.zeros((128, 768), np.float32)
        cbm[:, 0:128] = np.eye(128, dtype=np.float32)
        cbm[:, 128:256] = np.where(p > q, NEG, 0.0)
        cbm[:, 256:384] = np.where(p >= q, NEG, 0.0)
        cbm[:, 384:512] = np.where(p >= q, -1.0, 0.0)
        cbm[:, 512:640] = -1.0
        cbm[:, 640:704] = 1.0
        _CONST_CACHE["cb"] = cbm
        _CONST_CACHE["ident"] = np.eye(128, dtype=np.float32)
    return _CONST_CACHE["cb"], _CONST_CACHE["ident"]


def layout_inputs(attn_norm, ffn_norm, final_norm, fox_w_qkvf, fox_b_f, fox_w_o,
                  sb_w_qkv, sb_w_o, ffn_w_up, ffn_w_conv, ffn_b_conv, ffn_w_down):
    f32 = np.float32
    m = {}
    cbm, ident = _consts()
    m["cb"] = cbm
    m["ident"] = ident
    NV = DEPTH * VS + 8
    vec = np.zeros((128, NV), f32)
    for l in range(DEPTH):
        o = l * VS
        vec[:, o:o + 8] = np.asarray(attn_norm[l], f32).reshape(8, 128).T
        vec[:, o + 8:o + 16] = np.asarray(ffn_norm[l], f32).reshape(8, 128).T
        wc = np.asarray(ffn_w_conv[l], f32).reshape(3, 44, 128).transpose(2, 1, 0).reshape(128, 132)
        vec[:, o + 16:o + 148] = wc
        vec[:, o + 148:o + 192] = np.asarray(ffn_b_conv[l], f32).reshape(44, 128).T
        if l % 2 == 0:
            bf = np.asarray(fox_b_f[l // 2], f32)
            for r0 in (0, 32, 64):
                vec[r0:r0 + 16, o + 192] = bf
    vec[:, DEPTH * VS:DEPTH * VS + 8] = np.asarray(final_norm, f32).reshape(8, 128).T
    m["vecs"] = vec
    for l in range(DEPTH):
        j = l // 2
        if l % 2 == 0:
            w = np.asarray(fox_w_qkvf[j], f32)
            wo = np.asarray(fox_w_o[j], f32)
            wfm = np.zeros((128, KC, 96), f32)
            wfr = w[:, 3 * D:3 * D + H].reshape(KC, 128, H).transpose(1, 0, 2)
            for r0 in (0, 32, 64):
                wfm[:, :, r0:r0 + 16] = wfr
            m["wf%d" % j] = wfm.reshape(128, KC * 96)
        else:
            w = np.asarray(sb_w_qkv[j], f32)
            wo = np.asarray(sb_w_o[j], f32)
        parts = [w[:, i * D:(i + 1) * D].reshape(KC, 128, H, DH).transpose(2, 1, 0, 3) for i in range(3)]
        m["wqkv%d" % l] = np.ascontiguousarray(np.concatenate(parts, axis=3)).reshape(H, 128, KC * 192)
        m["wo%d" % l] = np.ascontiguousarray(wo.reshape(8, 128, D).transpose(1, 0, 2)).reshape(128, 8 * D)
        wu = np.asarray(ffn_w_up[l], f32)
        pu = wu[:, 0:FF].reshape(KC, 128, FC, 128).transpose(2, 1, 0, 3)
        pg = wu[:, FF:2 * FF].reshape(KC, 128, FC, 128).transpose(2, 1, 0, 3)
        m["wup%d" % l] = np.ascontiguousarray(np.concatenate([pu, pg], axis=3)).reshape(FC, 128, KC * 256)
        wdn = np.asarray(ffn_w_down[l], f32)
        m["wdn%d" % l] = np.ascontiguousarray(wdn.reshape(FC, 128, KC, 128).transpose(2, 1, 0, 3)).reshape(KC, 128, FC * 128)
    return m


ALL_LAYERS = [(l, w) for l in range(DEPTH) for w in ("attn", "ffn")]


def kernel(x, attn_norm, ffn_norm, final_norm, fox_w_qkvf, fox_b_f, fox_w_o,
           sb_w_qkv, sb_w_o, ffn_w_up, ffn_w_conv, ffn_b_conv, ffn_w_down):
    x = np.asarray(x, np.float32)
    B, S, _ = x.shape
    shared = layout_inputs(attn_norm, ffn_norm, final_norm, fox_w_qkvf, fox_b_f, fox_w_o,
                           sb_w_qkv, sb_w_o, ffn_w_up, ffn_w_conv, ffn_b_conv, ffn_w_down)
    nc = build_program(S, ALL_LAYERS, True)
    in_maps = []
    for b in range(B):
        mm = dict(shared)
        mm["x"] = np.ascontiguousarray(x[b])
        in_maps.append(mm)
    res = run_bass_kernel_spmd(nc, in_maps, core_ids=list(range(B)))
    return np.stack([np.asarray(r["out"], np.float32) for r in res.results], axis=0)
```

```python
import contextlib
import numpy as np
import concourse.bass as bass
import concourse.mybir as mybir
from concourse.bass_utils import run_bass_kernel_spmd

F32 = mybir.dt.float32
BF16 = mybir.dt.bfloat16
AF = mybir.ActivationFunctionType
ALU = mybir.AluOpType

D = 1024
KC = 8
TG = 512
H = 16
DH = 64
FF = 2816
FC = 22
DEPTH = 4
EPS = 1e-6
NEG = -30000.0
VS = 193

COMPUTE = ("pe", "act", "dve", "pool")


class _Op:
    __slots__ = ("eng", "fn", "deps", "dma_key", "dma_val", "signal", "count", "idx", "dma_waits")


class Tracker:
    def __init__(self):
        self.ops = []
        self.last_w = {}
        self.readers = {}
        self.dma_cum = {}
        self.fence_idx = None
        self.last_eng = {}
        self.last_dma = {}

    def fence(self, fn):
        deps = set(self.last_eng.values()) | set(self.last_dma.values())
        idx = self.add("pool", fn, extra_deps=deps)
        self.fence_idx = idx
        return idx

    def add(self, eng, fn, reads=(), writes=(), dma=None, extra_deps=()):
        op = _Op()
        op.eng = eng
        op.fn = fn
        op.dma_key = dma
        op.signal = False
        op.count = 0
        op.idx = len(self.ops)
        deps = set()
        for r in reads:
            w = self.last_w.get(r)
            if w is not None:
                deps.add(w)
        for r in writes:
            w = self.last_w.get(r)
            if w is not None:
                deps.add(w)
            for rd in self.readers.get(r, ()):
                deps.add(rd)
        deps |= set(extra_deps)
        if self.fence_idx is not None:
            deps.add(self.fence_idx)
        deps.discard(op.idx)
        op.deps = deps
        op.dma_waits = {}
        for d in deps:
            k = self.ops[d].dma_key
            if k is not None:
                op.dma_waits[k] = self.dma_cum[k]
        if dma is not None:
            self.last_dma[dma] = op.idx
        else:
            self.last_eng[eng] = op.idx
        if dma is not None:
            self.dma_cum[dma] = self.dma_cum.get(dma, 0) + 16
            op.dma_val = self.dma_cum[dma]
        else:
            op.dma_val = 0
        self.ops.append(op)
        for r in writes:
            self.last_w[r] = op.idx
            self.readers[r] = []
        for r in reads:
            if r not in writes:
                self.readers.setdefault(r, []).append(op.idx)
        return op.idx

    def dma_keys(self):
        return list(self.dma_cum.keys())

    def emit(self, nc, block, sems, dma_sems):
        ops = self.ops
        for op in ops:
            for d in op.deps:
                dop = ops[d]
                if dop.dma_key is not None:
                    continue
                if dop.eng == op.eng and op.eng in ("pe", "sp"):
                    continue
                dop.signal = True
        cnt = {e: 0 for e in COMPUTE + ("sp",)}
        for op in ops:
            if op.dma_key is None and op.signal:
                cnt[op.eng] += 1
                op.count = cnt[op.eng]
        per_eng = {e: [] for e in COMPUTE + ("sp",)}
        for op in ops:
            per_eng[op.eng].append(op)

        def run(eng_name, eng):
            waited = {}
            for op in per_eng[eng_name]:
                wl = {}
                for d in op.deps:
                    dop = ops[d]
                    if dop.dma_key is not None:
                        k = ("dma", dop.dma_key)
                        v = op.dma_waits[dop.dma_key]
                    else:
                        if dop.eng == eng_name and eng_name in ("pe", "sp"):
                            continue
                        k = ("eng", dop.eng)
                        v = dop.count
                    if v > wl.get(k, 0):
                        wl[k] = v
                for k, v in wl.items():
                    if waited.get(k, 0) >= v:
                        continue
                    waited[k] = v
                    s = dma_sems[k[1]] if k[0] == "dma" else sems[k[1]]
                    eng.wait_ge(s, v)
                ins = op.fn(eng)
                if op.dma_key is not None:
                    ins.then_inc(dma_sems[op.dma_key], 16)
                elif op.signal:
                    ins.then_inc(sems[op.eng], 1)

        @block.sync
        def _(e):
            run("sp", e)

        @block.scalar
        def _(e):
            run("act", e)

        @block.vector
        def _(e):
            run("dve", e)

        @block.gpsimd
        def _(e):
            run("pool", e)

        @block.tensor
        def _(e):
            run("pe", e)


class Arena:
    def __init__(self, t, nbytes):
        self.t = t
        self.nbytes = nbytes
        self.off = 0

    def reset(self):
        self.off = 0

    def f32(self, n):
        assert self.off % 4 == 0
        a = self.off // 4
        self.off += 4 * n
        assert self.off <= self.nbytes, ("arena overflow", self.off, self.nbytes)
        return self.t[:, a:a + n]

    def bf16(self, n):
        n2 = (n + 1) // 2
        v = self.f32(n2)
        return v.bitcast(BF16)[:, 0:n]


def build_program(S, layers, final=True):
    NTG = S // TG
    NT = S // 128
    NV = DEPTH * VS + 8
    nc = bass.Bass("TRN2", target_bir_lowering=False)

    def din(name, shape, dt=F32):
        return nc.dram_tensor(name, list(shape), dt, kind="ExternalInput").ap()

    x_d = din("x", [S, D])
    vec_d = din("vecs", [128, NV])
    cb_d = din("cb", [128, 6 * 128])
    id_d = din("ident", [128, 128])
    wqkv_d = [din("wqkv%d" % l, [H, 128, KC * 192]) for l in range(DEPTH)]
    wf_d = [din("wf%d" % j, [128, KC * 96]) for j in range(2)]
    wo_d = [din("wo%d" % l, [128, 8 * D]) for l in range(DEPTH)]
    wup_d = [din("wup%d" % l, [FC, 128, KC * 256]) for l in range(DEPTH)]
    wdn_d = [din("wdn%d" % l, [KC, 128, FC * 128]) for l in range(DEPTH)]
    out_d = nc.dram_tensor("out", [S, D], F32, kind="ExternalOutput").ap()
    wup_s = [nc.dram_tensor("wup_s%d" % l, [FC, 128, KC * 256], BF16, kind="Internal").ap() for l in range(DEPTH)]
    wdn_s = [nc.dram_tensor("wdn_s%d" % l, [KC, 128, FC * 128], BF16, kind="Internal").ap() for l in range(DEPTH)]
    ot_s = nc.dram_tensor("ot_s", [D, S], BF16, kind="Internal").ap()
    caug_s = nc.dram_tensor("caug_s", [3, H, S], BF16, kind="Internal").ap()

    T = Tracker()

    with contextlib.ExitStack() as st:
        hsT = st.enter_context(nc.sbuf_tensor("hsT", [128, KC, S], F32))
        vecs = st.enter_context(nc.sbuf_tensor("vecs_sb", [128, NV], F32))
        cb = st.enter_context(nc.sbuf_tensor("cb_sb", [128, 6 * 128], BF16))
        id32 = st.enter_context(nc.sbuf_tensor("id32", [128, 128], F32))
        ones32 = st.enter_context(nc.sbuf_tensor("ones32", [128, 128], F32))
        fsc = st.enter_context(nc.sbuf_tensor("fsc", [128, 8], F32))
        one1 = st.enter_context(nc.sbuf_tensor("one1", [128, 64], F32))
        AR_BYTES = 75600
        art = st.enter_context(nc.sbuf_tensor("arena", [128, AR_BYTES // 4], F32))
        ar = Arena(art, AR_BYTES)
        ps = [st.enter_context(nc.psum_tensor("ps%d" % i, [128, 512], F32)) for i in range(8)]

        identb = cb[:, 0:128]
        maskF = cb[:, 128:256]
        maskS = cb[:, 256:384]
        NTm = cb[:, 384:512]
        NOm = cb[:, 512:640]
        ones64 = cb[:, 640:704]

        def vcol(l, k):
            o = l * VS + k
            return vecs[:, o:o + 1]
        FIN = DEPTH * VS

        def fence():
            T.fence(lambda e: e.memset(fsc[:, 0:1], 0.0))

        T.add("sp", lambda e: e.dma_start(out=vecs[:], in_=vec_d), writes=["vecs"], dma="vecs")
        T.add("sp", lambda e: e.dma_start(out=id32[:], in_=id_d), writes=["id32"], dma="id32")
        T.add("pool", lambda e: e.dma_start(out=cb[:], in_=cb_d), writes=["cb"], dma="cb")
        T.add("pool", lambda e: e.memset(ones32[:], 1.0 / D), writes=["ones32"])
        T.add("pool", lambda e: e.memset(one1[:], 1.0), writes=["one1"])

        def cast_ffn_weights(l):
            for f in range(FC):
                T.add("pool", lambda e, f=f: e.dma_start(out=wup_s[l][f], in_=wup_d[l][f]),
                      writes=["wup_s%d.%d" % (l, f)], dma="wups")
            for co in range(KC):
                T.add("pool", lambda e, co=co: e.dma_start(out=wdn_s[l][co], in_=wdn_d[l][co]),
                      writes=["wdn_s%d.%d" % (l, co)], dma="wdns")

        ar.reset()
        xin = [ar.f32(D) for _ in range(2)]
        for t in range(NT):
            xb = xin[t % 2]
            T.add("sp", lambda e, t=t, xb=xb: e.dma_start(out=xb, in_=x_d[t * 128:(t + 1) * 128, :]),
                  writes=["xin%d" % (t % 2)], dma="xin%d" % (t % 2))
            for half in range(2):
                pb = ps[(2 * t + half) % 4]
                pbn = "ps%d" % ((2 * t + half) % 4)

                def tr(e, xb=xb, pb=pb, half=half):
                    r = None
                    for q in range(4):
                        c = half * 4 + q
                        r = e.transpose(pb[:, q * 128:(q + 1) * 128], xb[:, c * 128:(c + 1) * 128], id32[:])
                    return r
                T.add("pe", tr, reads=["xin%d" % (t % 2), "id32"], writes=[pbn])
                tgi = t // 4
                wr = ["hsw.%d.%d.%d" % (half * 4 + q, tgi, t % 4) for q in range(4)]
                if half == 0:
                    T.add("dve", lambda e, pb=pb, half=half, t=t: e.tensor_copy(
                        out=hsT[:, half * 4:half * 4 + 4, t * 128:(t + 1) * 128],
                        in_=pb[:].rearrange("p (q n) -> p q n", q=4)), reads=[pbn], writes=wr)
                else:
                    T.add("act", lambda e, pb=pb, half=half, t=t: e.activation(
                        out=hsT[:, half * 4:half * 4 + 4, t * 128:(t + 1) * 128],
                        in_=pb[:].rearrange("p (q n) -> p q n", q=4), func=AF.Copy), reads=[pbn], writes=wr)

        def rstd_for_tg(g, sqb, out_ap, out_res, psn, psn_name):
            for c in range(KC):
                sb_ = sqb[c % 2]
                T.add("pool", lambda e, c=c, sb_=sb_: e.tensor_tensor(
                    out=sb_, in0=hsT[:, c, g * TG:(g + 1) * TG], in1=hsT[:, c, g * TG:(g + 1) * TG], op=ALU.mult),
                    reads=["hs.%d.%d" % (c, g)], writes=["sq%d" % (c % 2)])
                T.add("pe", lambda e, c=c, sb_=sb_: e.matmul(psn[:], lhsT=NOm, rhs=sb_, start=(c == 0), stop=(c == KC - 1)),
                      reads=["sq%d" % (c % 2), "cb"] + ([psn_name] if c else []), writes=[psn_name])
            T.add("act", lambda e: e.activation(out=out_ap, in_=psn[:], func=AF.Ln, bias=EPS, scale=-1.0 / D), reads=[psn_name], writes=[out_res])
            T.add("act", lambda e: e.activation(out=out_ap, in_=out_ap, func=AF.Exp, scale=-0.5), reads=[out_res], writes=[out_res])

        def make_xn(g, gcol0, l, rstd_ap, rstd_res, xn, xp="xnT", npool=0):
            for c in range(KC):
                T.add("pool" if c >= KC - npool else "dve", lambda e, c=c: e.scalar_tensor_tensor(
                    out=xn[:, c, :], in0=hsT[:, c, g * TG:(g + 1) * TG], scalar=vcol(l, gcol0 + c), in1=rstd_ap,
                    op0=ALU.mult, op1=ALU.mult), reads=["hs.%d.%d" % (c, g), rstd_res, "vecs"], writes=["%s.%d" % (xp, c)])
        XR = ["xnT.%d" % c for c in range(KC)]

        def attention_layer(l):
            fox = (l % 2 == 0)
            j_ = l // 2
            fence()
            ar.reset()
            rstd_bc = ar.f32(S)
            xnTb = [ar.bf16(KC * TG).rearrange("p (c n) -> p c n", c=KC) for _ in range(2)]
            xnT = xnTb[0]
            mark = ar.off
            sqb = [ar.bf16(TG) for _ in range(2)]
            psS = [ps[0], ps[1], ps[4], ps[5]]
            psSn = ["psb0", "psb1", "psb4", "psb5"]
            psO2, psD, psQ, psK, psV, psN = [ps[2], ps[3]], ps[6], ps[4], ps[5], ps[6], ps[7]

            for g in range(NTG):
                rstd_for_tg(g, sqb, rstd_bc[:, g * TG:(g + 1) * TG], "rstd.%d" % g, psN, "psN")

            CAUG = ["caug_s.%d.%d" % (p_, g_) for p_ in range(3) for g_ in range(NTG)]
            if fox:
                wf = ar.bf16(KC * 96).rearrange("p (c n) -> p c n", c=KC)
                ft = [ar.f32(TG) for _ in range(4)]
                fb = [ar.bf16(TG) for _ in range(3)]
                onesr = ar.f32(TG)
                T.add("pool", lambda e: e.dma_start(out=wf, in_=wf_d[j_].rearrange("p (c n) -> p c n", c=KC)), writes=["wf"], dma="wf")
                T.add("pool", lambda e: e.memset(onesr[:, :], 1.0), writes=["onesr"])

                def fphase(g):
                    make_xn(g, 0, l, rstd_bc[:, g * TG:(g + 1) * TG], "rstd.%d" % g, xnT)

                    def fmm(e):
                        r = None
                        for c in range(KC):
                            r = e.matmul(psN[0:96, :], lhsT=wf[:, c, :], rhs=xnT[:, c, :], start=(c == 0), stop=(c == KC - 1))
                        return r
                    T.add("pe", fmm, reads=["wf"] + XR, writes=["psN"])
                    cprev = ft[2 + (g + 1) % 2]
                    ccur = ft[2 + g % 2]
                    cn = "fc%d" % (g % 2)
                    cpn = "fc%d" % ((g + 1) % 2)
                    T.add("act", lambda e: e.activation(out=ft[0][0:80, :], in_=psN[0:80, :], func=AF.Identity, bias=vcol(l, 192)[0:80, :]),
                          reads=["psN", "vecs"], writes=["ft0"])
                    T.add("act", lambda e: e.activation(out=ft[0][0:80, :], in_=ft[0][0:80, :], func=AF.Exp, scale=-1.0), reads=["ft0"], writes=["ft0"])
                    T.add("act", lambda e: e.activation(out=ft[1][0:80, :], in_=ft[0][0:80, :], func=AF.Ln, bias=1.0), reads=["ft0"], writes=["ft1"])
                    if g == 0:
                        T.add("dve", lambda e: e.tensor_tensor_scan(out=ccur[0:80, :], data0=onesr[0:80, :], data1=ft[1][0:80, :],
                              initial=0.0, op0=ALU.mult, op1=ALU.subtract), reads=["onesr", "ft1"], writes=[cn])
                    else:
                        T.add("dve", lambda e: e.tensor_tensor_scan(out=ccur[0:80, :], data0=onesr[0:80, :], data1=ft[1][0:80, :],
                              initial=cprev[0:80, TG - 1:TG], op0=ALU.mult, op1=ALU.subtract),
                              reads=["onesr", "ft1", cpn], writes=[cn])
                    T.add("dve", lambda e: e.tensor_copy(out=fb[0][0:80, :], in_=ccur[0:80, :]), reads=[cn], writes=["fb0"])
                    T.add("dve", lambda e: e.tensor_tensor(out=ft[0][0:80, :], in0=ccur[0:80, :], in1=fb[0][0:80, :], op=ALU.subtract),
                          reads=[cn, "fb0"], writes=["ft0"])
                    T.add("dve", lambda e: e.tensor_copy(out=fb[1][0:80, :], in_=ft[0][0:80, :]), reads=["ft0"], writes=["fb1"])
                    T.add("dve", lambda e: e.tensor_tensor(out=ft[1][0:80, :], in0=ft[0][0:80, :], in1=fb[1][0:80, :], op=ALU.subtract),
                          reads=["ft0", "fb1"], writes=["ft1"])
                    T.add("dve", lambda e: e.tensor_copy(out=fb[2][0:80, :], in_=ft[1][0:80, :]), reads=["ft1"], writes=["fb2"])
                    for part in range(3):
                        T.add("sp", lambda e, part=part: e.dma_start(out=caug_s[part, :, g * TG:(g + 1) * TG],
                              in_=fb[part][32 * part:32 * part + 16, :]), reads=["fb%d" % part], writes=["caug_s.%d.%d" % (part, g)], dma="caug_w")
                for g in range(NTG):
                    fphase(g)
            fence()
            ar.off = mark
            wh = [ar.bf16(KC * 192).rearrange("p (c n) -> p c n", c=KC) for _ in range(2)]
            Qp = ar.bf16(S)
            Kp = ar.bf16(S)
            VW = 128 if fox else DH + 2
            Vt = ar.bf16(NT * VW).rearrange("p (t d) -> p t d", t=NT)
            OTs = [ar.bf16(TG) for _ in range(1)]
            if fox:
                PT = [ar.bf16(TG) for _ in range(4)]
                rden = ar.f32(TG)
                rbc = ar.f32(TG)
                T.add("pool", lambda e: e.memset(Vt[:, :, DH:VW], 0.0), writes=["V.ones"])
                T.add("pool", lambda e: e.memset(Vt[:, :, DH:DH + 1], 1.0), writes=["V.ones"])
            else:
                Eb = [ar.f32(TG) for _ in range(1)]
                SPb = [ar.bf16(TG) for _ in range(3)]
                aTb = [ar.bf16(TG) for _ in range(3)]
                Rbb = [ar.bf16(TG) for _ in range(4)]
                R32 = ar.f32(TG)
            if fox:
                T.add("pool", lambda e: e.memset(Qp[64:128, :], 0.0), writes=["Qp.aug"])
                T.add("pool", lambda e: e.memset(Kp[64:128, :], 0.0), writes=["Kp.aug"])
                T.add("pool", lambda e: e.memset(Qp[64:70, :], -1.0), writes=["Qp.aug"])
                T.add("pool", lambda e: e.memset(Kp[64:70, :], 1.0), writes=["Kp.aug"])
            else:
                T.add("pool", lambda e: e.memset(Qp[64:128, :], 0.0), writes=["Qp.aug"])
                T.add("pool", lambda e: e.memset(Kp[64:128, :], 0.0), writes=["Kp.aug"])
            KR = 128

            pending = []

            def flush():
                for fn_ in pending:
                    fn_()
                del pending[:]

            def project(h, g, whb, whn, skip_make=False):
                xnT = xnTb[g % 2]
                xp = "xnT" if g % 2 == 0 else "xnU"
                XR = ["%s.%d" % (xp, c) for c in range(KC)]
                if not skip_make:
                    make_xn(g, 0, l, rstd_bc[:, g * TG:(g + 1) * TG], "rstd.%d" % g, xnT, xp, npool=0)

                def qmm(e):
                    r = None
                    for c in range(KC):
                        r = e.matmul(psQ[0:64, :], lhsT=whb[:, c, 0:64], rhs=xnT[:, c, :], start=(c == 0), stop=(c == KC - 1))
                    return r

                def kmm(e):
                    r = None
                    for c in range(KC):
                        r = e.matmul(psK[0:64, :], lhsT=whb[:, c, 64:128], rhs=xnT[:, c, :], start=(c == 0), stop=(c == KC - 1))
                    return r

                def vmm(e):
                    r = None
                    for tt in range(4):
                        for c in range(KC):
                            r = e.matmul(psV[:, tt * 64:(tt + 1) * 64], lhsT=xnT[:, c, tt * 128:(tt + 1) * 128], rhs=whb[:, c, 128:192],
                                         start=(c == 0), stop=(c == KC - 1))
                    return r
                T.add("pe", qmm, reads=XR + [whn], writes=["psb4"])
                T.add("pe", kmm, reads=XR + [whn], writes=["psb5"])
                T.add("pe", vmm, reads=XR + [whn], writes=["psb6"])
                T.add("act", lambda e: e.activation(out=Qp[0:64, g * TG:(g + 1) * TG], in_=psQ[0:64, :], func=AF.Copy, scale=0.125),
                      reads=["psb4"], writes=["Qp.%d" % g])
                T.add("act", lambda e: e.activation(out=Kp[0:64, g * TG:(g + 1) * TG], in_=psK[0:64, :], func=AF.Copy), reads=["psb5"], writes=["Kp.%d" % g])
                T.add("act", lambda e: e.activation(out=Vt[:, g * 4:(g + 1) * 4, 0:DH], in_=psV[:, 0:256].rearrange("p (t d) -> p t d", t=4), func=AF.Copy),
                      reads=["psb6"], writes=["V.%d" % g])

            def attend_head(h, hoist):
                seq = []
                for g in range(NTG):
                    if fox:
                        blocks = [(4 * g + m, m) for m in range(4)] + [(j, None) for j in range(4 * g - 1, -1, -1)]
                    else:
                        blocks = [(4 * g + m, m) for m in range(3, -1, -1)] + [(j, None) for j in range(4 * g - 1, -1, -1)]
                    for i, (j, m) in enumerate(blocks):
                        seq.append((g, i, len(blocks), j, m))
                N = len(seq)
                mk = maskF if fox else maskS
                ob = OTs[0]
                obn = "OTs0"

                def score_op(n):
                    g, i, nb, j, m = seq[n]
                    q0 = g * TG
                    pS = psS[n % 4]
                    c0 = 0 if m is None else m * 128

                    def f(e):
                        kk = Kp[0:KR, j * 128:(j + 1) * 128]
                        if m is None:
                            return e.matmul(pS[:, :], lhsT=kk, rhs=Qp[0:KR, q0:q0 + TG], start=True, stop=True)
                        e.matmul(pS[:, c0:c0 + 128], lhsT=identb, rhs=mk, start=True, stop=False)
                        r = e.matmul(pS[:, c0:c0 + 128], lhsT=kk, rhs=Qp[0:KR, q0 + c0:q0 + c0 + 128], start=False, stop=True)
                        if c0 + 128 < TG:
                            r = e.matmul(pS[:, c0 + 128:TG], lhsT=kk, rhs=Qp[0:KR, q0 + c0 + 128:q0 + TG], start=False, stop=True)
                        return r
                    T.add("pe", f, reads=["Qp.%d" % g, "Qp.aug", "Kp.%d" % (j // 4), "Kp.aug", "cb"], writes=[psSn[n % 4]])

                if fox:
                    def s1(n):
                        score_op(n)

                    def s2(n):
                        g, i, nb, j, m = seq[n]
                        psO = psO2[g % 2]
                        pOn = "psO%d" % (g % 2)
                        pS = psS[n % 4]
                        c0 = 0 if m is None else m * 128
                        pt = PT[n % 4]
                        ptn = "PT%d" % (n % 4)
                        if i == 3:
                            flush()
                        T.add("act", lambda e: e.activation(out=pt[:, c0:TG], in_=pS[:, c0:TG], func=AF.Exp),
                              reads=[psSn[n % 4]], writes=[ptn])
                        T.add("pe", lambda e: e.matmul(psO[:, c0:TG], lhsT=Vt[:, j, :], rhs=pt[:, c0:TG], start=(i == 0), stop=(i == nb - 1)),
                              reads=[ptn, "V.%d" % (j // 4), "V.ones"] + ([pOn] if i else []), writes=[pOn])
                        if i == nb - 1:
                            T.add("dve", lambda e: e.reciprocal(out=rden[64:65, :], in_=psO[64:65, :]), reads=[pOn], writes=["rden"])

                            def part_b():
                                T.add("pe", lambda e: e.matmul(psD[0:64, :], lhsT=one1[64:65, 0:64], rhs=rden[64:65, :], start=True, stop=True),
                                      reads=["rden", "one1"], writes=["psb6"])
                                T.add("dve", lambda e: e.tensor_copy(out=rbc[0:64, :], in_=psD[0:64, :]), reads=["psb6"], writes=["rbc"])
                                T.add("dve", lambda e: e.tensor_tensor(out=ob[0:64, :], in0=psO[0:64, :], in1=rbc[0:64, :], op=ALU.mult),
                                      reads=[pOn, "rbc"], writes=[obn])
                                T.add("sp", lambda e: e.dma_start(out=ot_s[h * 64:(h + 1) * 64, g * TG:(g + 1) * TG], in_=ob[0:64, :]),
                                      reads=[obn], writes=["ot_s.%d.%d" % (h, g)], dma="otw")
                            pending.append(part_b)

                    def s3(n):
                        pass
                else:
                    def s1(n):
                        g, i, nb, j, m = seq[n]
                        c0 = 0 if m is None else m * 128
                        if i == 0:
                            T.add("pool", lambda e: e.memset(R32[:, :], 0.0), writes=["R32"])
                        score_op(n)
                        pS = psS[n % 4]
                        Ei = Eb[0]
                        SPi = SPb[n % 3]
                        T.add("act", lambda e: e.activation(out=Ei[:, c0:TG], in_=pS[:, c0:TG], func=AF.Exp), reads=[psSn[n % 4]], writes=["E0"])
                        T.add("act", lambda e: e.activation(out=SPi[:, c0:TG], in_=Ei[:, c0:TG], func=AF.Ln, bias=1.0), reads=["E0"], writes=["SP%d" % (n % 3)])
                        if i + 1 < nb:
                            g2, i2, nb2, j2, m2 = seq[n + 1]
                            c02 = 0 if m2 is None else m2 * 128
                            Rn = Rbb[(n + 1) % 4]
                            T.add("dve", lambda e: e.tensor_tensor(out=R32[:, c0:TG], in0=R32[:, c0:TG], in1=SPi[:, c0:TG], op=ALU.add),
                                  reads=["R32", "SP%d" % (n % 3)], writes=["R32"])
                            T.add("dve", lambda e: e.tensor_copy(out=Rn[:, c02:TG], in_=R32[:, c02:TG]), reads=["R32"], writes=["Rb%d" % ((n + 1) % 4)])

                    def s2(n):
                        g, i, nb, j, m = seq[n]
                        c0 = 0 if m is None else m * 128
                        pS = psS[n % 4]
                        SPi = SPb[n % 3]
                        ai = aTb[n % 3]
                        Rb = Rbb[n % 4]
                        c1 = c0 + 128 if m is not None else c0

                        def st2(e):
                            r = e.matmul(pS[:, c0:TG], lhsT=NTm, rhs=SPi[:, c0:TG], start=False, stop=(c1 >= TG))
                            if c1 < TG:
                                r = e.matmul(pS[:, c1:TG], lhsT=NOm, rhs=Rb[:, c1:TG], start=False, stop=True)
                            return r
                        T.add("pe", st2, reads=["SP%d" % (n % 3), "cb", psSn[n % 4]] + (["Rb%d" % (n % 4)] if c1 < TG else []), writes=[psSn[n % 4]])
                        T.add("act", lambda e: e.activation(out=ai[:, c0:TG], in_=pS[:, c0:TG], func=AF.Exp), reads=[psSn[n % 4]], writes=["aT%d" % (n % 3)])

                    def s3(n):
                        g, i, nb, j, m = seq[n]
                        psO = psO2[g % 2]
                        pOn = "psO%d" % (g % 2)
                        c0 = 0 if m is None else m * 128
                        ai = aTb[n % 3]
                        last = (i == nb - 1)
                        T.add("pe", lambda e: e.matmul(psO[0:64, c0:TG], lhsT=Vt[:, j, 0:DH], rhs=ai[:, c0:TG], start=(i == 0), stop=last),
                              reads=["aT%d" % (n % 3), "V.%d" % (j // 4)] + ([pOn] if i else []), writes=[pOn])
                        if last:
                            T.add("dve", lambda e: e.tensor_copy(out=ob[0:64, :], in_=psO[0:64, :]), reads=[pOn], writes=[obn])
                            T.add("sp", lambda e: e.dma_start(out=ot_s[h * 64:(h + 1) * 64, g * TG:(g + 1) * TG], in_=ob[0:64, :]),
                                  reads=[obn], writes=["ot_s.%d.%d" % (h, g)], dma="otw")

                s1(0)
                s1(1)
                for n in range(N):
                    if seq[n][0] == NTG - 1 and seq[n][1] == 0 and hoist is not None:
                        hoist()
                    if n + 2 < N:
                        s1(n + 2)
                    s2(n)
                    if n >= 1:
                        s3(n - 1)
                s3(N - 1)

            for h in range(H):
                whb = wh[h % 2]
                whn = "wh%d" % (h % 2)
                T.add("pool", lambda e, h=h, whb=whb: e.dma_start(out=whb, in_=wqkv_d[l][h].rearrange("p (c n) -> p c n", c=KC)),
                      writes=[whn], dma=whn)
                if h == 1:
                    cast_ffn_weights(l)
                if fox:
                    T.add("sp", lambda e, h=h: e.dma_start(out=Qp[64:67, :], in_=caug_s[:, h, :]), reads=CAUG, writes=["Qp.aug"], dma="aug")
                    T.add("sp", lambda e, h=h: e.dma_start(out=Kp[67:70, :], in_=caug_s[:, h, :]), reads=CAUG, writes=["Kp.aug"], dma="aug")
                for g in range(NTG):
                    project(h, g, whb, whn, skip_make=(g == 0 and h > 0))
                hoist = None
                if h + 1 < H:
                    hoist = lambda: make_xn(0, 0, l, rstd_bc[:, 0:TG], "rstd.0", xnTb[0], "xnT", npool=0)
                attend_head(h, hoist)
            flush()

            fence()
            ar.reset()
            wo = ar.bf16(8 * D).rearrange("p (c n) -> p c n", c=8)
            OTin = [ar.bf16(8 * TG).rearrange("p (c n) -> p c n", c=8) for _ in range(2)]
            T.add("pool", lambda e: e.dma_start(out=wo, in_=wo_d[l].rearrange("p (c n) -> p c n", c=8)), writes=["wo"], dma="wo")

            def wo_tg(g):
                ob = OTin[g % 2]
                obn = "OTin%d" % (g % 2)
                T.add("sp", lambda e: e.dma_start(out=ob, in_=ot_s[:, g * TG:(g + 1) * TG].rearrange("(c p) n -> p c n", p=128)),
                      reads=["ot_s.%d.%d" % (h_, g) for h_ in range(H)], writes=[obn], dma=obn)
                for co in range(KC):
                    py = ps[co % 2]
                    pyn = "ps%d" % (co % 2)

                    def omm(e, co=co, py=py):
                        r = None
                        for c in range(8):
                            r = e.matmul(py[:, :], lhsT=wo[:, c, co * 128:(co + 1) * 128], rhs=ob[:, c, :], start=(c == 0), stop=(c == 7))
                        return r
                    T.add("pe", omm, reads=["wo", obn], writes=[pyn])
                    T.add("dve", lambda e, co=co, py=py: e.tensor_tensor(out=hsT[:, co, g * TG:(g + 1) * TG], in0=py[:, :],
                          in1=hsT[:, co, g * TG:(g + 1) * TG], op=ALU.add), reads=[pyn, "hs.%d.%d" % (co, g)], writes=["hs.%d.%d" % (co, g)])
            for g in range(NTG):
                wo_tg(g)

        def ffn_layer(l):
            fence()
            ar.reset()
            xnT = ar.bf16(KC * TG).rearrange("p (c n) -> p c n", c=KC)
            actT = ar.bf16(FC * TG).rearrange("p (c n) -> p c n", c=FC)
            wu = [ar.bf16(KC * 256).rearrange("p (c n) -> p c n", c=KC) for _ in range(3)]
            wd = [ar.bf16(FC * 128).rearrange("p (c n) -> p c n", c=FC) for _ in range(2)]
            hext = [[ar.f32(516) for _ in range(2)] for _ in range(2)]
            tcv = [[ar.f32(TG) for _ in range(2)] for _ in range(2)]
            sqb = [ar.bf16(TG) for _ in range(2)]
            rstd = ar.f32(TG)
            halo = ar.f32(2 * FC * 2).rearrange("p (c n) -> p c n", n=2)
            psU = [ps[0], ps[2]]
            psG = [ps[1], ps[3]]
            psY = [ps[4], ps[5]]
            psN = ps[7]
            T.add("pool", lambda e: e.memset(halo[:, :, :], 0.0), writes=["halo"])

            def prep(g):
                rstd_for_tg(g, sqb, rstd[:, :], "rstdf", psN, "psN")
                make_xn(g, 8, l, rstd[:, :], "rstdf", xnT)

            def gate(f):
                par = f % 2
                T.add("act", lambda e: e.activation(out=tcv[par][1], in_=tcv[par][1], func=AF.Silu), reads=["tcv%d1" % par], writes=["tcv%d1" % par])
                T.add("pool", lambda e: e.tensor_tensor(out=actT[:, f, :], in0=tcv[par][1], in1=tcv[par][0], op=ALU.mult),
                      reads=["tcv%d1" % par, "tcv%d0" % par], writes=["actT.%d" % f])

            def up(g):
                for f in range(FC):
                    par = f % 2
                    wub = wu[f % 3]
                    wun = "wu%d" % (f % 3)
                    T.add("sp", lambda e, f=f, wub=wub: e.dma_start(out=wub, in_=wup_s[l][f].rearrange("p (c n) -> p c n", c=KC)),
                          reads=["wup_s%d.%d" % (l, f)], writes=[wun], dma=wun)
                    for ug in range(2):
                        pp = (psU if ug == 0 else psG)[par]
                        ppn = "psUG%d%d" % (ug, par)

                        def upmm(e, ug=ug, pp=pp, wub=wub):
                            r = None
                            for c in range(KC):
                                r = e.matmul(pp[:, :], lhsT=wub[:, c, ug * 128:(ug + 1) * 128], rhs=xnT[:, c, :], start=(c == 0), stop=(c == KC - 1))
                            return r
                        T.add("pe", upmm, reads=XR + [wun], writes=[ppn])
                        hx = hext[par][ug]
                        hxn = "hext%d%d" % (par, ug)
                        ch = ug * FC + f
                        tc_ = tcv[par][ug]
                        tcn = "tcv%d%d" % (par, ug)
                        wc = 16 + ch * 3
                        T.add("pool", lambda e, hx=hx, ch=ch: e.tensor_copy(out=hx[:, 0:2], in_=halo[:, ch, :]), reads=["halo.%d" % ch, "halo"], writes=[hxn + "h"])
                        T.add("act", lambda e, hx=hx, pp=pp: e.activation(out=hx[:, 2:514], in_=pp[:, :], func=AF.Copy), reads=[ppn], writes=[hxn])
                        T.add("act", lambda e, tc_=tc_, pp=pp, wc=wc, ch=ch: e.activation(out=tc_, in_=pp[:, :], func=AF.Identity,
                              scale=vcol(l, wc + 2), bias=vcol(l, 148 + ch)), reads=[ppn, "vecs"], writes=[tcn])
                        T.add("pool", lambda e, hx=hx, ch=ch: e.tensor_copy(out=halo[:, ch, :], in_=hx[:, 512:514]), reads=[hxn], writes=["halo.%d" % ch])
                        T.add("dve", lambda e, hx=hx, tc_=tc_, wc=wc: e.scalar_tensor_tensor(out=tc_, in0=hx[:, 1:513], scalar=vcol(l, wc + 1), in1=tc_,
                              op0=ALU.mult, op1=ALU.add), reads=[hxn, hxn + "h", tcn, "vecs"], writes=[tcn])
                        T.add("dve", lambda e, hx=hx, tc_=tc_, wc=wc: e.scalar_tensor_tensor(out=tc_, in0=hx[:, 0:512], scalar=vcol(l, wc), in1=tc_,
                              op0=ALU.mult, op1=ALU.add), reads=[hxn, hxn + "h", tcn, "vecs"], writes=[tcn])
                    if f >= 1:
                        gate(f - 1)
                gate(FC - 1)

            def down(g):
                ar_ = ["actT.%d" % f for f in range(FC)]
                for co in range(KC):
                    wdb = wd[co % 2]
                    wdn = "wd%d" % (co % 2)
                    T.add("sp", lambda e, co=co, wdb=wdb: e.dma_start(out=wdb, in_=wdn_s[l][co].rearrange("p (c n) -> p c n", c=FC)),
                          reads=["wdn_s%d.%d" % (l, co)], writes=[wdn], dma=wdn)
                    py = psY[co % 2]
                    pyn = "psY%d" % (co % 2)

                    def dmm(e, wdb=wdb, py=py):
                        r = None
                        for f in range(FC):
                            r = e.matmul(py[:, :], lhsT=wdb[:, f, :], rhs=actT[:, f, :], start=(f == 0), stop=(f == FC - 1))
                        return r
                    T.add("pe", dmm, reads=ar_ + [wdn], writes=[pyn])
                    T.add("dve", lambda e, co=co, py=py: e.tensor_tensor(out=hsT[:, co, g * TG:(g + 1) * TG], in0=py[:, :],
                          in1=hsT[:, co, g * TG:(g + 1) * TG], op=ALU.add), reads=[pyn, "hs.%d.%d" % (co, g)], writes=["hs.%d.%d" % (co, g)])
            prep(0)
            for g in range(NTG):
                up(g)
                if g + 1 < NTG:
                    prep(g + 1)
                down(g)

        def final_phase(normed):
            fence()
            ar.reset()
            yT = ar.f32(KC * TG).rearrange("p (c n) -> p c n", c=KC)
            ot = [ar.f32(D) for _ in range(2)]
            sqb = [ar.bf16(TG) for _ in range(2)]
            rstd = ar.f32(TG)
            psN = ps[7]
            outs = []

            def fin_tg(g):
                if normed:
                    rstd_for_tg(g, sqb, rstd[:, :], "rstdf", psN, "psN")
                    for c in range(KC):
                        T.add("dve", lambda e, c=c: e.scalar_tensor_tensor(out=yT[:, c, :], in0=hsT[:, c, g * TG:(g + 1) * TG], scalar=vecs[:, FIN + c:FIN + c + 1],
                              in1=rstd[:, :], op0=ALU.mult, op1=ALU.mult), reads=["hs.%d.%d" % (c, g), "rstdf", "vecs"], writes=["yT.%d" % c])
                else:
                    for c in range(KC):
                        T.add("dve", lambda e, c=c: e.tensor_copy(out=yT[:, c, :], in_=hsT[:, c, g * TG:(g + 1) * TG]),
                              reads=["hs.%d.%d" % (c, g)], writes=["yT.%d" % c])
                for tt in range(4):
                    t = g * 4 + tt
                    ob = ot[t % 2]
                    obn = "ot%d" % (t % 2)
                    for half in range(2):
                        pb = ps[(2 * t + half) % 4]
                        pbn = "psf%d" % ((2 * t + half) % 4)

                        def tr(e, pb=pb, half=half, tt=tt):
                            r = None
                            for q in range(4):
                                c = half * 4 + q
                                r = e.transpose(pb[:, q * 128:(q + 1) * 128], yT[:, c, tt * 128:(tt + 1) * 128], id32[:])
                            return r
                        T.add("pe", tr, reads=["yT.%d" % (half * 4 + q) for q in range(4)] + ["id32"], writes=[pbn])
                        if half == 0:
                            T.add("dve", lambda e, pb=pb, ob=ob: e.tensor_copy(out=ob[:, 0:512], in_=pb[:, :]), reads=[pbn], writes=[obn + "a"])
                        else:
                            T.add("act", lambda e, pb=pb, ob=ob: e.activation(out=ob[:, 512:1024], in_=pb[:, :], func=AF.Copy), reads=[pbn], writes=[obn + "b"])
                    T.add("sp", lambda e, ob=ob, t=t: e.dma_start(out=out_d[t * 128:(t + 1) * 128, :], in_=ob), reads=[obn + "a", obn + "b"],
                          writes=["out.%d" % t], dma="out%d" % (t % 2))
                    outs.append("out.%d" % t)
            for g in range(NTG):
                fin_tg(g)
            T.add("sp", lambda e: e.wait_ge(sems["sp"], 0), reads=outs, writes=[])

        sems = {e: st.enter_context(nc.semaphore("s_" + e)) for e in COMPUTE + ("sp",)}
        for (l, what) in layers:
            if what == "attn":
                attention_layer(l)
            else:
                ffn_layer(l)
        final_phase(final)

        dsems = {k: st.enter_context(nc.semaphore("d_" + k)) for k in T.dma_keys()}
        block = st.enter_context(nc.Block())
        T.emit(nc, block, sems, dsems)
    return nc


_CONST_CACHE = {}


def _consts():
    if "cb" not in _CONST_CACHE:
        p = np.arange(128)[:, None]
        q = np.arange(128)[None, :]
        cbm = np.zeros((128, 768), np.float32)
        cbm[:, 0:128] = np.eye(128, dtype=np.float32)
        cbm[:, 128:256] = np.where(p > q, NEG, 0.0)
        cbm[:, 256:384] = np.where(p >= q, NEG, 0.0)
        cbm[:, 384:512] = np.where(p >= q, -1.0, 0.0)
        cbm[:, 512:640] = -1.0
        cbm[:, 640:704] = 1.0
        _CONST_CACHE["cb"] = cbm
        _CONST_CACHE["ident"] = np.eye(128, dtype=np.float32)
    return _CONST_CACHE["cb"], _CONST_CACHE["ident"]


def layout_inputs(attn_norm, ffn_norm, final_norm, fox_w_qkvf, fox_b_f, fox_w_o,
                  sb_w_qkv, sb_w_o, ffn_w_up, ffn_w_conv, ffn_b_conv, ffn_w_down):
    f32 = np.float32
    m = {}
    cbm, ident = _consts()
    m["cb"] = cbm
    m["ident"] = ident
    NV = DEPTH * VS + 8
    vec = np.zeros((128, NV), f32)
    for l in range(DEPTH):
        o = l * VS
        vec[:, o:o + 8] = np.asarray(attn_norm[l], f32).reshape(8, 128).T
        vec[:, o + 8:o + 16] = np.asarray(ffn_norm[l], f32).reshape(8, 128).T
        wc = np.asarray(ffn_w_conv[l], f32).reshape(3, 44, 128).transpose(2, 1, 0).reshape(128, 132)
        vec[:, o + 16:o + 148] = wc
        vec[:, o + 148:o + 192] = np.asarray(ffn_b_conv[l], f32).reshape(44, 128).T
        if l % 2 == 0:
            bf = np.asarray(fox_b_f[l // 2], f32)
            for r0 in (0, 32, 64):
                vec[r0:r0 + 16, o + 192] = bf
    vec[:, DEPTH * VS:DEPTH * VS + 8] = np.asarray(final_norm, f32).reshape(8, 128).T
    m["vecs"] = vec
    for l in range(DEPTH):
        j = l // 2
        if l % 2 == 0:
            w = np.asarray(fox_w_qkvf[j], f32)
            wo = np.asarray(fox_w_o[j], f32)
            wfm = np.zeros((128, KC, 96), f32)
            wfr = w[:, 3 * D:3 * D + H].reshape(KC, 128, H).transpose(1, 0, 2)
            for r0 in (0, 32, 64):
                wfm[:, :, r0:r0 + 16] = wfr
            m["wf%d" % j] = wfm.reshape(128, KC * 96)
        else:
            w = np.asarray(sb_w_qkv[j], f32)
            wo = np.asarray(sb_w_o[j], f32)
        parts = [w[:, i * D:(i + 1) * D].reshape(KC, 128, H, DH).transpose(2, 1, 0, 3) for i in range(3)]
        m["wqkv%d" % l] = np.ascontiguousarray(np.concatenate(parts, axis=3)).reshape(H, 128, KC * 192)
        m["wo%d" % l] = np.ascontiguousarray(wo.reshape(8, 128, D).transpose(1, 0, 2)).reshape(128, 8 * D)
        wu = np.asarray(ffn_w_up[l], f32)
        pu = wu[:, 0:FF].reshape(KC, 128, FC, 128).transpose(2, 1, 0, 3)
        pg = wu[:, FF:2 * FF].reshape(KC, 128, FC, 128).transpose(2, 1, 0, 3)
        m["wup%d" % l] = np.ascontiguousarray(np.concatenate([pu, pg], axis=3)).reshape(FC, 128, KC * 256)
        wdn = np.asarray(ffn_w_down[l], f32)
        m["wdn%d" % l] = np.ascontiguousarray(wdn.reshape(FC, 128, KC, 128).transpose(2, 1, 0, 3)).reshape(KC, 128, FC * 128)
    return m


ALL_LAYERS = [(l, w) for l in range(DEPTH) for w in ("attn", "ffn")]


def kernel(x, attn_norm, ffn_norm, final_norm, fox_w_qkvf, fox_b_f, fox_w_o,
           sb_w_qkv, sb_w_o, ffn_w_up, ffn_w_conv, ffn_b_conv, ffn_w_down):
    x = np.asarray(x, np.float32)
    B, S, _ = x.shape
    shared = layout_inputs(attn_norm, ffn_norm, final_norm, fox_w_qkvf, fox_b_f, fox_w_o,
                           sb_w_qkv, sb_w_o, ffn_w_up, ffn_w_conv, ffn_b_conv, ffn_w_down)
    nc = build_program(S, ALL_LAYERS, True)
    in_maps = []
    for b in range(B):
        mm = dict(shared)
        mm["x"] = np.ascontiguousarray(x[b])
        in_maps.append(mm)
    res = run_bass_kernel_spmd(nc, in_maps, core_ids=list(range(B)))
    return np.stack([np.asarray(r["out"], np.float32) for r in res.results], axis=0)
```

```python
import contextlib
import numpy as np
import concourse.bass as bass
import concourse.mybir as mybir
from concourse.bass_utils import run_bass_kernel_spmd

F32 = mybir.dt.float32
BF16 = mybir.dt.bfloat16
AF = mybir.ActivationFunctionType
ALU = mybir.AluOpType

D = 1024
KC = 8
TG = 512
H = 16
DH = 64
FF = 2816
FC = 22
DEPTH = 4
EPS = 1e-6
NEG = -30000.0
VS = 193

COMPUTE = ("pe", "act", "dve", "pool")


class _Op:
    __slots__ = ("eng", "fn", "deps", "dma_key", "dma_val", "signal", "count", "idx", "dma_waits")


class Tracker:
    def __init__(self):
        self.ops = []
        self.last_w = {}
        self.readers = {}
        self.dma_cum = {}
        self.fence_idx = None
        self.last_eng = {}
        self.last_dma = {}

    def fence(self, fn):
        deps = set(self.last_eng.values()) | set(self.last_dma.values())
        idx = self.add("pool", fn, extra_deps=deps)
        self.fence_idx = idx
        return idx

    def add(self, eng, fn, reads=(), writes=(), dma=None, extra_deps=()):
        op = _Op()
        op.eng = eng
        op.fn = fn
        op.dma_key = dma
        op.signal = False
        op.count = 0
        op.idx = len(self.ops)
        deps = set()
        for r in reads:
            w = self.last_w.get(r)
            if w is not None:
                deps.add(w)
        for r in writes:
            w = self.last_w.get(r)
            if w is not None:
                deps.add(w)
            for rd in self.readers.get(r, ()):
                deps.add(rd)
        deps |= set(extra_deps)
        if self.fence_idx is not None:
            deps.add(self.fence_idx)
        deps.discard(op.idx)
        op.deps = deps
        op.dma_waits = {}
        for d in deps:
            k = self.ops[d].dma_key
            if k is not None:
                op.dma_waits[k] = self.dma_cum[k]
        if dma is not None:
            self.last_dma[dma] = op.idx
        else:
            self.last_eng[eng] = op.idx
        if dma is not None:
            self.dma_cum[dma] = self.dma_cum.get(dma, 0) + 16
            op.dma_val = self.dma_cum[dma]
        else:
            op.dma_val = 0
        self.ops.append(op)
        for r in writes:
            self.last_w[r] = op.idx
            self.readers[r] = []
        for r in reads:
            if r not in writes:
                self.readers.setdefault(r, []).append(op.idx)
        return op.idx

    def dma_keys(self):
        return list(self.dma_cum.keys())

    def emit(self, nc, block, sems, dma_sems):
        ops = self.ops
        for op in ops:
            for d in op.deps:
                dop = ops[d]
                if dop.dma_key is not None:
                    continue
                if dop.eng == op.eng and op.eng in ("pe", "sp"):
                    continue
                dop.signal = True
        cnt = {e: 0 for e in COMPUTE + ("sp",)}
        for op in ops:
            if op.dma_key is None and op.signal:
                cnt[op.eng] += 1
                op.count = cnt[op.eng]
        per_eng = {e: [] for e in COMPUTE + ("sp",)}
        for op in ops:
            per_eng[op.eng].append(op)

        def run(eng_name, eng):
            waited = {}
            for op in per_eng[eng_name]:
                wl = {}
                for d in op.deps:
                    dop = ops[d]
                    if dop.dma_key is not None:
                        k = ("dma", dop.dma_key)
                        v = op.dma_waits[dop.dma_key]
                    else:
                        if dop.eng == eng_name and eng_name in ("pe", "sp"):
                            continue
                        k = ("eng", dop.eng)
                        v = dop.count
                    if v > wl.get(k, 0):
                        wl[k] = v
                for k, v in wl.items():
                    if waited.get(k, 0) >= v:
                        continue
                    waited[k] = v
                    s = dma_sems[k[1]] if k[0] == "dma" else sems[k[1]]
                    eng.wait_ge(s, v)
                ins = op.fn(eng)
                if op.dma_key is not None:
                    ins.then_inc(dma_sems[op.dma_key], 16)
                elif op.signal:
                    ins.then_inc(sems[op.eng], 1)

        @block.sync
        def _(e):
            run("sp", e)

        @block.scalar
        def _(e):
            run("act", e)

        @block.vector
        def _(e):
            run("dve", e)

        @block.gpsimd
        def _(e):
            run("pool", e)

        @block.tensor
        def _(e):
            run("pe", e)


class Arena:
    def __init__(self, t, nbytes):
        self.t = t
        self.nbytes = nbytes
        self.off = 0

    def reset(self):
        self.off = 0

    def f32(self, n):
        assert self.off % 4 == 0
        a = self.off // 4
        self.off += 4 * n
        assert self.off <= self.nbytes, ("arena overflow", self.off, self.nbytes)
        return self.t[:, a:a + n]

    def bf16(self, n):
        n2 = (n + 1) // 2
        v = self.f32(n2)
        return v.bitcast(BF16)[:, 0:n]


def build_program(S, layers, final=True):
    NTG = S // TG
    NT = S // 128
    NV = DEPTH * VS + 8
    nc = bass.Bass("TRN2", target_bir_lowering=False)

    def din(name, shape, dt=F32):
        return nc.dram_tensor(name, list(shape), dt, kind="ExternalInput").ap()

    x_d = din("x", [S, D])
    vec_d = din("vecs", [128, NV])
    cb_d = din("cb", [128, 6 * 128])
    id_d = din("ident", [128, 128])
    wqkv_d = [din("wqkv%d" % l, [H, 128, KC * 192]) for l in range(DEPTH)]
    wf_d = [din("wf%d" % j, [128, KC * 96]) for j in range(2)]
    wo_d = [din("wo%d" % l, [128, 8 * D]) for l in range(DEPTH)]
    wup_d = [din("wup%d" % l, [FC, 128, KC * 256]) for l in range(DEPTH)]
    wdn_d = [din("wdn%d" % l, [KC, 128, FC * 128]) for l in range(DEPTH)]
    out_d = nc.dram_tensor("out", [S, D], F32, kind="ExternalOutput").ap()
    wup_s = [nc.dram_tensor("wup_s%d" % l, [FC, 128, KC * 256], BF16, kind="Internal").ap() for l in range(DEPTH)]
    wdn_s = [nc.dram_tensor("wdn_s%d" % l, [KC, 128, FC * 128], BF16, kind="Internal").ap() for l in range(DEPTH)]
    ot_s = nc.dram_tensor("ot_s", [D, S], BF16, kind="Internal").ap()
    caug_s = nc.dram_tensor("caug_s", [3, H, S], BF16, kind="Internal").ap()

    T = Tracker()

    with contextlib.ExitStack() as st:
        hsT = st.enter_context(nc.sbuf_tensor("hsT", [128, KC, S], F32))
        vecs = st.enter_context(nc.sbuf_tensor("vecs_sb", [128, NV], F32))
        cb = st.enter_context(nc.sbuf_tensor("cb_sb", [128, 6 * 128], BF16))
        id32 = st.enter_context(nc.sbuf_tensor("id32", [128, 128], F32))
        ones32 = st.enter_context(nc.sbuf_tensor("ones32", [128, 128], F32))
        fsc = st.enter_context(nc.sbuf_tensor("fsc", [128, 8], F32))
        one1 = st.enter_context(nc.sbuf_tensor("one1", [128, 64], F32))
        AR_BYTES = 75600
        art = st.enter_context(nc.sbuf_tensor("arena", [128, AR_BYTES // 4], F32))
        ar = Arena(art, AR_BYTES)
        ps = [st.enter_context(nc.psum_tensor("ps%d" % i, [128, 512], F32)) for i in range(8)]

        identb = cb[:, 0:128]
        maskF = cb[:, 128:256]
        maskS = cb[:, 256:384]
        NTm = cb[:, 384:512]
        NOm = cb[:, 512:640]
        ones64 = cb[:, 640:704]

        def vcol(l, k):
            o = l * VS + k
            return vecs[:, o:o + 1]
        FIN = DEPTH * VS

        def fence():
            T.fence(lambda e: e.memset(fsc[:, 0:1], 0.0))

        T.add("sp", lambda e: e.dma_start(out=vecs[:], in_=vec_d), writes=["vecs"], dma="vecs")
        T.add("sp", lambda e: e.dma_start(out=id32[:], in_=id_d), writes=["id32"], dma="id32")
        T.add("pool", lambda e: e.dma_start(out=cb[:], in_=cb_d), writes=["cb"], dma="cb")
        T.add("pool", lambda e: e.memset(ones32[:], 1.0 / D), writes=["ones32"])
        T.add("pool", lambda e: e.memset(one1[:], 1.0), writes=["one1"])

        def cast_ffn_weights(l):
            for f in range(FC):
                T.add("pool", lambda e, f=f: e.dma_start(out=wup_s[l][f], in_=wup_d[l][f]),
                      writes=["wup_s%d.%d" % (l, f)], dma="wups")
            for co in range(KC):
                T.add("pool", lambda e, co=co: e.dma_start(out=wdn_s[l][co], in_=wdn_d[l][co]),
                      writes=["wdn_s%d.%d" % (l, co)], dma="wdns")

        ar.reset()
        xin = [ar.f32(D) for _ in range(2)]
        for t in range(NT):
            xb = xin[t % 2]
            T.add("sp", lambda e, t=t, xb=xb: e.dma_start(out=xb, in_=x_d[t * 128:(t + 1) * 128, :]),
                  writes=["xin%d" % (t % 2)], dma="xin%d" % (t % 2))
            for half in range(2):
                pb = ps[(2 * t + half) % 4]
                pbn = "ps%d" % ((2 * t + half) % 4)

                def tr(e, xb=xb, pb=pb, half=half):
                    r = None
                    for q in range(4):
                        c = half * 4 + q
                        r = e.transpose(pb[:, q * 128:(q + 1) * 128], xb[:, c * 128:(c + 1) * 128], id32[:])
                    return r
                T.add("pe", tr, reads=["xin%d" % (t % 2), "id32"], writes=[pbn])
                tgi = t // 4
                wr = ["hsw.%d.%d.%d" % (half * 4 + q, tgi, t % 4) for q in range(4)]
                if half == 0:
                    T.add("dve", lambda e, pb=pb, half=half, t=t: e.tensor_copy(
                        out=hsT[:, half * 4:half * 4 + 4, t * 128:(t + 1) * 128],
                        in_=pb[:].rearrange("p (q n) -> p q n", q=4)), reads=[pbn], writes=wr)
                else:
                    T.add("act", lambda e, pb=pb, half=half, t=t: e.activation(
                        out=hsT[:, half * 4:half * 4 + 4, t * 128:(t + 1) * 128],
                        in_=pb[:].rearrange("p (q n) -> p q n", q=4), func=AF.Copy), reads=[pbn], writes=wr)

        def rstd_for_tg(g, sqb, out_ap, out_res, psn, psn_name):
            for c in range(KC):
                sb_ = sqb[c % 2]
                T.add("pool", lambda e, c=c, sb_=sb_: e.tensor_tensor(
                    out=sb_, in0=hsT[:, c, g * TG:(g + 1) * TG], in1=hsT[:, c, g * TG:(g + 1) * TG], op=ALU.mult),
                    reads=["hs.%d.%d" % (c, g)], writes=["sq%d" % (c % 2)])
                T.add("pe", lambda e, c=c, sb_=sb_: e.matmul(psn[:], lhsT=NOm, rhs=sb_, start=(c == 0), stop=(c == KC - 1)),
                      reads=["sq%d" % (c % 2), "cb"] + ([psn_name] if c else []), writes=[psn_name])
            T.add("act", lambda e: e.activation(out=out_ap, in_=psn[:], func=AF.Ln, bias=EPS, scale=-1.0 / D), reads=[psn_name], writes=[out_res])
            T.add("act", lambda e: e.activation(out=out_ap, in_=out_ap, func=AF.Exp, scale=-0.5), reads=[out_res], writes=[out_res])

        def make_xn(g, gcol0, l, rstd_ap, rstd_res, xn, xp="xnT", npool=0):
            for c in range(KC):
                T.add("pool" if c >= KC - npool else "dve", lambda e, c=c: e.scalar_tensor_tensor(
                    out=xn[:, c, :], in0=hsT[:, c, g * TG:(g + 1) * TG], scalar=vcol(l, gcol0 + c), in1=rstd_ap,
                    op0=ALU.mult, op1=ALU.mult), reads=["hs.%d.%d" % (c, g), rstd_res, "vecs"], writes=["%s.%d" % (xp, c)])
        XR = ["xnT.%d" % c for c in range(KC)]

        def attention_layer(l):
            fox = (l % 2 == 0)
            j_ = l // 2
            fence()
            ar.reset()
            rstd_bc = ar.f32(S)
            xnTb = [ar.bf16(KC * TG).rearrange("p (c n) -> p c n", c=KC) for _ in range(2)]
            xnT = xnTb[0]
            mark = ar.off
            sqb = [ar.bf16(TG) for _ in range(2)]
            psS = [ps[0], ps[1], ps[4], ps[5]]
            psSn = ["psb0", "psb1", "psb4", "psb5"]
            psO2, psD, psQ, psK, psV, psN = [ps[2], ps[3]], ps[6], ps[4], ps[5], ps[6], ps[7]

            for g in range(NTG):
                rstd_for_tg(g, sqb, rstd_bc[:, g * TG:(g + 1) * TG], "rstd.%d" % g, psN, "psN")

            CAUG = ["caug_s.%d.%d" % (p_, g_) for p_ in range(3) for g_ in range(NTG)]
            if fox:
                wf = ar.bf16(KC * 96).rearrange("p (c n) -> p c n", c=KC)
                ft = [ar.f32(TG) for _ in range(4)]
                fb = [ar.bf16(TG) for _ in range(3)]
                onesr = ar.f32(TG)
                T.add("pool", lambda e: e.dma_start(out=wf, in_=wf_d[j_].rearrange("p (c n) -> p c n", c=KC)), writes=["wf"], dma="wf")
                T.add("pool", lambda e: e.memset(onesr[:, :], 1.0), writes=["onesr"])

                def fphase(g):
                    make_xn(g, 0, l, rstd_bc[:, g * TG:(g + 1) * TG], "rstd.%d" % g, xnT)

                    def fmm(e):
                        r = None
                        for c in range(KC):
                            r = e.matmul(psN[0:96, :], lhsT=wf[:, c, :], rhs=xnT[:, c, :], start=(c == 0), stop=(c == KC - 1))
                        return r
                    T.add("pe", fmm, reads=["wf"] + XR, writes=["psN"])
                    cprev = ft[2 + (g + 1) % 2]
                    ccur = ft[2 + g % 2]
                    cn = "fc%d" % (g % 2)
                    cpn = "fc%d" % ((g + 1) % 2)
                    T.add("act", lambda e: e.activation(out=ft[0][0:80, :], in_=psN[0:80, :], func=AF.Identity, bias=vcol(l, 192)[0:80, :]),
                          reads=["psN", "vecs"], writes=["ft0"])
                    T.add("act", lambda e: e.activation(out=ft[0][0:80, :], in_=ft[0][0:80, :], func=AF.Exp, scale=-1.0), reads=["ft0"], writes=["ft0"])
                    T.add("act", lambda e: e.activation(out=ft[1][0:80, :], in_=ft[0][0:80, :], func=AF.Ln, bias=1.0), reads=["ft0"], writes=["ft1"])
                    if g == 0:
                        T.add("dve", lambda e: e.tensor_tensor_scan(out=ccur[0:80, :], data0=onesr[0:80, :], data1=ft[1][0:80, :],
                              initial=0.0, op0=ALU.mult, op1=ALU.subtract), reads=["onesr", "ft1"], writes=[cn])
                    else:
                        T.add("dve", lambda e: e.tensor_tensor_scan(out=ccur[0:80, :], data0=onesr[0:80, :], data1=ft[1][0:80, :],
                              initial=cprev[0:80, TG - 1:TG], op0=ALU.mult, op1=ALU.subtract),
                              reads=["onesr", "ft1", cpn], writes=[cn])
                    T.add("dve", lambda e: e.tensor_copy(out=fb[0][0:80, :], in_=ccur[0:80, :]), reads=[cn], writes=["fb0"])
                    T.add("dve", lambda e: e.tensor_tensor(out=ft[0][0:80, :], in0=ccur[0:80, :], in1=fb[0][0:80, :], op=ALU.subtract),
                          reads=[cn, "fb0"], writes=["ft0"])
                    T.add("dve", lambda e: e.tensor_copy(out=fb[1][0:80, :], in_=ft[0][0:80, :]), reads=["ft0"], writes=["fb1"])
                    T.add("dve", lambda e: e.tensor_tensor(out=ft[1][0:80, :], in0=ft[0][0:80, :], in1=fb[1][0:80, :], op=ALU.subtract),
                          reads=["ft0", "fb1"], writes=["ft1"])
                    T.add("dve", lambda e: e.tensor_copy(out=fb[2][0:80, :], in_=ft[1][0:80, :]), reads=["ft1"], writes=["fb2"])
                    for part in range(3):
                        T.add("sp", lambda e, part=part: e.dma_start(out=caug_s[part, :, g * TG:(g + 1) * TG],
                              in_=fb[part][32 * part:32 * part + 16, :]), reads=["fb%d" % part], writes=["caug_s.%d.%d" % (part, g)], dma="caug_w")
                for g in range(NTG):
                    fphase(g)
            fence()
            ar.off = mark
            wh = [ar.bf16(KC * 192).rearrange("p (c n) -> p c n", c=KC) for _ in range(2)]
            Qp = ar.bf16(S)
            Kp = ar.bf16(S)
            VW = 128 if fox else DH + 2
            Vt = ar.bf16(NT * VW).rearrange("p (t d) -> p t d", t=NT)
            OTs = [ar.bf16(TG) for _ in range(1)]
            if fox:
                PT = [ar.bf16(TG) for _ in range(4)]
                rden = ar.f32(TG)
                rbc = ar.f32(TG)
                T.add("pool", lambda e: e.memset(Vt[:, :, DH:VW], 0.0), writes=["V.ones"])
                T.add("pool", lambda e: e.memset(Vt[:, :, DH:DH + 1], 1.0), writes=["V.ones"])
            else:
                Eb = [ar.f32(TG) for _ in range(1)]
                SPb = [ar.bf16(TG) for _ in range(3)]
                aTb = [ar.bf16(TG) for _ in range(3)]
                Rbb = [ar.bf16(TG) for _ in range(4)]
                R32 = ar.f32(TG)
            if fox:
                T.add("pool", lambda e: e.memset(Qp[64:128, :], 0.0), writes=["Qp.aug"])
                T.add("pool", lambda e: e.memset(Kp[64:128, :], 0.0), writes=["Kp.aug"])
                T.add("pool", lambda e: e.memset(Qp[64:70, :], -1.0), writes=["Qp.aug"])
                T.add("pool", lambda e: e.memset(Kp[64:70, :], 1.0), writes=["Kp.aug"])
            else:
                T.add("pool", lambda e: e.memset(Qp[64:128, :], 0.0), writes=["Qp.aug"])
                T.add("pool", lambda e: e.memset(Kp[64:128, :], 0.0), writes=["Kp.aug"])
            KR = 128

            pending = []

            def flush():
                for fn_ in pending:
                    fn_()
                del pending[:]

            def project(h, g, whb, whn, skip_make=False):
                xnT = xnTb[g % 2]
                xp = "xnT" if g % 2 == 0 else "xnU"
                XR = ["%s.%d" % (xp, c) for c in range(KC)]
                if not skip_make:
                    make_xn(g, 0, l, rstd_bc[:, g * TG:(g + 1) * TG], "rstd.%d" % g, xnT, xp, npool=0)

                def qmm(e):
                    r = None
                    for c in range(KC):
                        r = e.matmul(psQ[0:64, :], lhsT=whb[:, c, 0:64], rhs=xnT[:, c, :], start=(c == 0), stop=(c == KC - 1))
                    return r

                def kmm(e):
                    r = None
                    for c in range(KC):
                        r = e.matmul(psK[0:64, :], lhsT=whb[:, c, 64:128], rhs=xnT[:, c, :], start=(c == 0), stop=(c == KC - 1))
                    return r

                def vmm(e):
                    r = None
                    for tt in range(4):
                        for c in range(KC):
                            r = e.matmul(psV[:, tt * 64:(tt + 1) * 64], lhsT=xnT[:, c, tt * 128:(tt + 1) * 128], rhs=whb[:, c, 128:192],
                                         start=(c == 0), stop=(c == KC - 1))
                    return r
                T.add("pe", qmm, reads=XR + [whn], writes=["psb4"])
                T.add("pe", kmm, reads=XR + [whn], writes=["psb5"])
                T.add("pe", vmm, reads=XR + [whn], writes=["psb6"])
                T.add("act", lambda e: e.activation(out=Qp[0:64, g * TG:(g + 1) * TG], in_=psQ[0:64, :], func=AF.Copy, scale=0.125),
                      reads=["psb4"], writes=["Qp.%d" % g])
                T.add("act", lambda e: e.activation(out=Kp[0:64, g * TG:(g + 1) * TG], in_=psK[0:64, :], func=AF.Copy), reads=["psb5"], writes=["Kp.%d" % g])
                T.add("act", lambda e: e.activation(out=Vt[:, g * 4:(g + 1) * 4, 0:DH], in_=psV[:, 0:256].rearrange("p (t d) -> p t d", t=4), func=AF.Copy),
                      reads=["psb6"], writes=["V.%d" % g])

            def attend_head(h, hoist):
                seq = []
                for g in range(NTG):
                    if fox:
                        blocks = [(4 * g + m, m) for m in range(4)] + [(j, None) for j in range(4 * g - 1, -1, -1)]
                    else:
                        blocks = [(4 * g + m, m) for m in range(3, -1, -1)] + [(j, None) for j in range(4 * g - 1, -1, -1)]
                    for i, (j, m) in enumerate(blocks):
                        seq.append((g, i, len(blocks), j, m))
                N = len(seq)
                mk = maskF if fox else maskS
                ob = OTs[0]
                obn = "OTs0"

                def score_op(n):
                    g, i, nb, j, m = seq[n]
                    q0 = g * TG
                    pS = psS[n % 4]
                    c0 = 0 if m is None else m * 128

                    def f(e):
                        kk = Kp[0:KR, j * 128:(j + 1) * 128]
                        if m is None:
                            return e.matmul(pS[:, :], lhsT=kk, rhs=Qp[0:KR, q0:q0 + TG], start=True, stop=True)
                        e.matmul(pS[:, c0:c0 + 128], lhsT=identb, rhs=mk, start=True, stop=False)
                        r = e.matmul(pS[:, c0:c0 + 128], lhsT=kk, rhs=Qp[0:KR, q0 + c0:q0 + c0 + 128], start=False, stop=True)
                        if c0 + 128 < TG:
                            r = e.matmul(pS[:, c0 + 128:TG], lhsT=kk, rhs=Qp[0:KR, q0 + c0 + 128:q0 + TG], start=False, stop=True)
                        return r
                    T.add("pe", f, reads=["Qp.%d" % g, "Qp.aug", "Kp.%d" % (j // 4), "Kp.aug", "cb"], writes=[psSn[n % 4]])

                if fox:
                    def s1(n):
                        score_op(n)

                    def s2(n):
                        g, i, nb, j, m = seq[n]
                        psO = psO2[g % 2]
                        pOn = "psO%d" % (g % 2)
                        pS = psS[n % 4]
                        c0 = 0 if m is None else m * 128
                        pt = PT[n % 4]
                        ptn = "PT%d" % (n % 4)
                        if i == 3:
                            flush()
                        T.add("act", lambda e: e.activation(out=pt[:, c0:TG], in_=pS[:, c0:TG], func=AF.Exp),
                              reads=[psSn[n % 4]], writes=[ptn])
                        T.add("pe", lambda e: e.matmul(psO[:, c0:TG], lhsT=Vt[:, j, :], rhs=pt[:, c0:TG], start=(i == 0), stop=(i == nb - 1)),
                              reads=[ptn, "V.%d" % (j // 4), "V.ones"] + ([pOn] if i else []), writes=[pOn])
                        if i == nb - 1:
                            T.add("dve", lambda e: e.reciprocal(out=rden[64:65, :], in_=psO[64:65, :]), reads=[pOn], writes=["rden"])

                            def part_b():
                                T.add("pe", lambda e: e.matmul(psD[0:64, :], lhsT=one1[64:65, 0:64], rhs=rden[64:65, :], start=True, stop=True),
                                      reads=["rden", "one1"], writes=["psb6"])
                                T.add("dve", lambda e: e.tensor_copy(out=rbc[0:64, :], in_=psD[0:64, :]), reads=["psb6"], writes=["rbc"])
                                T.add("dve", lambda e: e.tensor_tensor(out=ob[0:64, :], in0=psO[0:64, :], in1=rbc[0:64, :], op=ALU.mult),
                                      reads=[pOn, "rbc"], writes=[obn])
                                T.add("sp", lambda e: e.dma_start(out=ot_s[h * 64:(h + 1) * 64, g * TG:(g + 1) * TG], in_=ob[0:64, :]),
                                      reads=[obn], writes=["ot_s.%d.%d" % (h, g)], dma="otw")
                            pending.append(part_b)

                    def s3(n):
                        pass
                else:
                    def s1(n):
                        g, i, nb, j, m = seq[n]
                        c0 = 0 if m is None else m * 128
                        if i == 0:
                            T.add("pool", lambda e: e.memset(R32[:, :], 0.0), writes=["R32"])
                        score_op(n)
                        pS = psS[n % 4]
                        Ei = Eb[0]
                        SPi = SPb[n % 3]
                        T.add("act", lambda e: e.activation(out=Ei[:, c0:TG], in_=pS[:, c0:TG], func=AF.Exp), reads=[psSn[n % 4]], writes=["E0"])
                        T.add("act", lambda e: e.activation(out=SPi[:, c0:TG], in_=Ei[:, c0:TG], func=AF.Ln, bias=1.0), reads=["E0"], writes=["SP%d" % (n % 3)])
                        if i + 1 < nb:
                            g2, i2, nb2, j2, m2 = seq[n + 1]
                            c02 = 0 if m2 is None else m2 * 128
                            Rn = Rbb[(n + 1) % 4]
                            T.add("dve", lambda e: e.tensor_tensor(out=R32[:, c0:TG], in0=R32[:, c0:TG], in1=SPi[:, c0:TG], op=ALU.add),
                                  reads=["R32", "SP%d" % (n % 3)], writes=["R32"])
                            T.add("dve", lambda e: e.tensor_copy(out=Rn[:, c02:TG], in_=R32[:, c02:TG]), reads=["R32"], writes=["Rb%d" % ((n + 1) % 4)])

                    def s2(n):
                        g, i, nb, j, m = seq[n]
                        c0 = 0 if m is None else m * 128
                        pS = psS[n % 4]
                        SPi = SPb[n % 3]
                        ai = aTb[n % 3]
                        Rb = Rbb[n % 4]
                        c1 = c0 + 128 if m is not None else c0

                        def st2(e):
                            r = e.matmul(pS[:, c0:TG], lhsT=NTm, rhs=SPi[:, c0:TG], start=False, stop=(c1 >= TG))
                            if c1 < TG:
                                r = e.matmul(pS[:, c1:TG], lhsT=NOm, rhs=Rb[:, c1:TG], start=False, stop=True)
                            return r
                        T.add("pe", st2, reads=["SP%d" % (n % 3), "cb", psSn[n % 4]] + (["Rb%d" % (n % 4)] if c1 < TG else []), writes=[psSn[n % 4]])
                        T.add("act", lambda e: e.activation(out=ai[:, c0:TG], in_=pS[:, c0:TG], func=AF.Exp), reads=[psSn[n % 4]], writes=["aT%d" % (n % 3)])

                    def s3(n):
                        g, i, nb, j, m = seq[n]
                        psO = psO2[g % 2]
                        pOn = "psO%d" % (g % 2)
                        c0 = 0 if m is None else m * 128
                        ai = aTb[n % 3]
                        last = (i == nb - 1)
                        T.add("pe", lambda e: e.matmul(psO[0:64, c0:TG], lhsT=Vt[:, j, 0:DH], rhs=ai[:, c0:TG], start=(i == 0), stop=last),
                              reads=["aT%d" % (n % 3), "V.%d" % (j // 4)] + ([pOn] if i else []), writes=[pOn])
                        if last:
                            T.add("dve", lambda e: e.tensor_copy(out=ob[0:64, :], in_=psO[0:64, :]), reads=[pOn], writes=[obn])
                            T.add("sp", lambda e: e.dma_start(out=ot_s[h * 64:(h + 1) * 64, g * TG:(g + 1) * TG], in_=ob[0:64, :]),
                                  reads=[obn], writes=["ot_s.%d.%d" % (h, g)], dma="otw")

                s1(0)
                s1(1)
                for n in range(N):
                    if seq[n][0] == NTG - 1 and seq[n][1] == 0 and hoist is not None:
                        hoist()
                    if n + 2 < N:
                        s1(n + 2)
                    s2(n)
                    if n >= 1:
                        s3(n - 1)
                s3(N - 1)

            for h in range(H):
                whb = wh[h % 2]
                whn = "wh%d" % (h % 2)
                T.add("pool", lambda e, h=h, whb=whb: e.dma_start(out=whb, in_=wqkv_d[l][h].rearrange("p (c n) -> p c n", c=KC)),
                      writes=[whn], dma=whn)
                if h == 1:
                    cast_ffn_weights(l)
                if fox:
                    T.add("sp", lambda e, h=h: e.dma_start(out=Qp[64:67, :], in_=caug_s[:, h, :]), reads=CAUG, writes=["Qp.aug"], dma="aug")
                    T.add("sp", lambda e, h=h: e.dma_start(out=Kp[67:70, :], in_=caug_s[:, h, :]), reads=CAUG, writes=["Kp.aug"], dma="aug")
                for g in range(NTG):
                    project(h, g, whb, whn, skip_make=(g == 0 and h > 0))
                hoist = None
                if h + 1 < H:
                    hoist = lambda: make_xn(0, 0, l, rstd_bc[:, 0:TG], "rstd.0", xnTb[0], "xnT", npool=0)
                attend_head(h, hoist)
            flush()

            fence()
            ar.reset()
            wo = ar.bf16(8 * D).rearrange("p (c n) -> p c n", c=8)
            OTin = [ar.bf16(8 * TG).rearrange("p (c n) -> p c n", c=8) for _ in range(2)]
            T.add("pool", lambda e: e.dma_start(out=wo, in_=wo_d[l].rearrange("p (c n) -> p c n", c=8)), writes=["wo"], dma="wo")

            def wo_tg(g):
                ob = OTin[g % 2]
                obn = "OTin%d" % (g % 2)
                T.add("sp", lambda e: e.dma_start(out=ob, in_=ot_s[:, g * TG:(g + 1) * TG].rearrange("(c p) n -> p c n", p=128)),
                      reads=["ot_s.%d.%d" % (h_, g) for h_ in range(H)], writes=[obn], dma=obn)
                for co in range(KC):
                    py = ps[co % 2]
                    pyn = "ps%d" % (co % 2)

                    def omm(e, co=co, py=py):
                        r = None
                        for c in range(8):
                            r = e.matmul(py[:, :], lhsT=wo[:, c, co * 128:(co + 1) * 128], rhs=ob[:, c, :], start=(c == 0), stop=(c == 7))
                        return r
                    T.add("pe", omm, reads=["wo", obn], writes=[pyn])
                    T.add("dve", lambda e, co=co, py=py: e.tensor_tensor(out=hsT[:, co, g * TG:(g + 1) * TG], in0=py[:, :],
                          in1=hsT[:, co, g * TG:(g + 1) * TG], op=ALU.add), reads=[pyn, "hs.%d.%d" % (co, g)], writes=["hs.%d.%d" % (co, g)])
            for g in range(NTG):
                wo_tg(g)

        def ffn_layer(l):
            fence()
            ar.reset()
            xnT = ar.bf16(KC * TG).rearrange("p (c n) -> p c n", c=KC)
            actT = ar.bf16(FC * TG).rearrange("p (c n) -> p c n", c=FC)
            wu = [ar.bf16(KC * 256).rearrange("p (c n) -> p c n", c=KC) for _ in range(2)]
            wd = [ar.bf16(FC * 128).rearrange("p (c n) -> p c n", c=FC) for _ in range(2)]
            hext = [[ar.f32(516) for _ in range(2)] for _ in range(2)]
            tcv = [[ar.f32(TG) for _ in range(2)] for _ in range(3)]
            sqb = [ar.bf16(TG) for _ in range(2)]
            rstd = ar.f32(TG)
            halo = ar.f32(2 * FC * 2).rearrange("p (c n) -> p c n", n=2)
            psU = [ps[0], ps[2]]
            psG = [ps[1], ps[3]]
            psY = [ps[4], ps[5]]
            psN = ps[7]
            T.add("pool", lambda e: e.memset(halo[:, :, :], 0.0), writes=["halo"])

            def prep(g):
                rstd_for_tg(g, sqb, rstd[:, :], "rstdf", psN, "psN")
                make_xn(g, 8, l, rstd[:, :], "rstdf", xnT)

            def gate(f):
                par = f % 3
                T.add("act", lambda e: e.activation(out=tcv[par][1], in_=tcv[par][1], func=AF.Silu), reads=["tcv%d1" % par], writes=["tcv%d1" % par])
                T.add("pool", lambda e: e.tensor_tensor(out=actT[:, f, :], in0=tcv[par][1], in1=tcv[par][0], op=ALU.mult),
                      reads=["tcv%d1" % par, "tcv%d0" % par], writes=["actT.%d" % f])

            def up(g):
                for f in range(FC):
                    par = f % 2
                    wub = wu[f % 2]
                    wun = "wu%d" % (f % 2)
                    T.add("sp", lambda e, f=f, wub=wub: e.dma_start(out=wub, in_=wup_s[l][f].rearrange("p (c n) -> p c n", c=KC)),
                          reads=["wup_s%d.%d" % (l, f)], writes=[wun], dma=wun)
                    for ug in range(2):
                        pp = (psU if ug == 0 else psG)[par]
                        ppn = "psUG%d%d" % (ug, par)

                        def upmm(e, ug=ug, pp=pp, wub=wub):
                            r = None
                            for c in range(KC):
                                r = e.matmul(pp[:, :], lhsT=wub[:, c, ug * 128:(ug + 1) * 128], rhs=xnT[:, c, :], start=(c == 0), stop=(c == KC - 1))
                            return r
                        T.add("pe", upmm, reads=XR + [wun], writes=[ppn])
                        hx = hext[par][ug]
                        hxn = "hext%d%d" % (par, ug)
                        ch = ug * FC + f
                        tc_ = tcv[f % 3][ug]
                        tcn = "tcv%d%d" % (f % 3, ug)
                        wc = 16 + ch * 3
                        T.add("pool", lambda e, hx=hx, ch=ch: e.tensor_copy(out=hx[:, 0:2], in_=halo[:, ch, :]), reads=["halo.%d" % ch, "halo"], writes=[hxn + "h"])
                        T.add("act", lambda e, hx=hx, pp=pp: e.activation(out=hx[:, 2:514], in_=pp[:, :], func=AF.Copy), reads=[ppn], writes=[hxn])
                        T.add("act", lambda e, tc_=tc_, pp=pp, wc=wc, ch=ch: e.activation(out=tc_, in_=pp[:, :], func=AF.Identity,
                              scale=vcol(l, wc + 2), bias=vcol(l, 148 + ch)), reads=[ppn, "vecs"], writes=[tcn])
                        T.add("pool", lambda e, hx=hx, ch=ch: e.tensor_copy(out=halo[:, ch, :], in_=hx[:, 512:514]), reads=[hxn], writes=["halo.%d" % ch])
                        T.add("dve", lambda e, hx=hx, tc_=tc_, wc=wc: e.scalar_tensor_tensor(out=tc_, in0=hx[:, 1:513], scalar=vcol(l, wc + 1), in1=tc_,
                              op0=ALU.mult, op1=ALU.add), reads=[hxn, hxn + "h", tcn, "vecs"], writes=[tcn])
                        T.add("dve", lambda e, hx=hx, tc_=tc_, wc=wc: e.scalar_tensor_tensor(out=tc_, in0=hx[:, 0:512], scalar=vcol(l, wc), in1=tc_,
                              op0=ALU.mult, op1=ALU.add), reads=[hxn, hxn + "h", tcn, "vecs"], writes=[tcn])
                    if f >= 1:
                        gate(f - 1)
                gate(FC - 1)

            def down(g):
                ar_ = ["actT.%d" % f for f in range(FC)]
                for co in range(KC):
                    wdb = wd[co % 2]
                    wdn = "wd%d" % (co % 2)
                    T.add("sp", lambda e, co=co, wdb=wdb: e.dma_start(out=wdb, in_=wdn_s[l][co].rearrange("p (c n) -> p c n", c=FC)),
                          reads=["wdn_s%d.%d" % (l, co)], writes=[wdn], dma=wdn)
                    py = psY[co % 2]
                    pyn = "psY%d" % (co % 2)

                    def dmm(e, wdb=wdb, py=py):
                        r = None
                        for f in range(FC):
                            r = e.matmul(py[:, :], lhsT=wdb[:, f, :], rhs=actT[:, f, :], start=(f == 0), stop=(f == FC - 1))
                        return r
                    T.add("pe", dmm, reads=ar_ + [wdn], writes=[pyn])
                    T.add("dve", lambda e, co=co, py=py: e.tensor_tensor(out=hsT[:, co, g * TG:(g + 1) * TG], in0=py[:, :],
                          in1=hsT[:, co, g * TG:(g + 1) * TG], op=ALU.add), reads=[pyn, "hs.%d.%d" % (co, g)], writes=["hs.%d.%d" % (co, g)])
            prep(0)
            for g in range(NTG):
                up(g)
                if g + 1 < NTG:
                    prep(g + 1)
                down(g)

        def final_phase(normed):
            fence()
            ar.reset()
            yT = ar.f32(KC * TG).rearrange("p (c n) -> p c n", c=KC)
            ot = [ar.f32(D) for _ in range(2)]
            sqb = [ar.bf16(TG) for _ in range(2)]
            rstd = ar.f32(TG)
            psN = ps[7]
            outs = []

            def fin_tg(g):
                if normed:
                    rstd_for_tg(g, sqb, rstd[:, :], "rstdf", psN, "psN")
                    for c in range(KC):
                        T.add("dve", lambda e, c=c: e.scalar_tensor_tensor(out=yT[:, c, :], in0=hsT[:, c, g * TG:(g + 1) * TG], scalar=vecs[:, FIN + c:FIN + c + 1],
                              in1=rstd[:, :], op0=ALU.mult, op1=ALU.mult), reads=["hs.%d.%d" % (c, g), "rstdf", "vecs"], writes=["yT.%d" % c])
                else:
                    for c in range(KC):
                        T.add("dve", lambda e, c=c: e.tensor_copy(out=yT[:, c, :], in_=hsT[:, c, g * TG:(g + 1) * TG]),
                              reads=["hs.%d.%d" % (c, g)], writes=["yT.%d" % c])
                for tt in range(4):
                    t = g * 4 + tt
                    ob = ot[t % 2]
                    obn = "ot%d" % (t % 2)
                    for half in range(2):
                        pb = ps[(2 * t + half) % 4]
                        pbn = "psf%d" % ((2 * t + half) % 4)

                        def tr(e, pb=pb, half=half, tt=tt):
                            r = None
                            for q in range(4):
                                c = half * 4 + q
                                r = e.transpose(pb[:, q * 128:(q + 1) * 128], yT[:, c, tt * 128:(tt + 1) * 128], id32[:])
                            return r
                        T.add("pe", tr, reads=["yT.%d" % (half * 4 + q) for q in range(4)] + ["id32"], writes=[pbn])
                        if half == 0:
                            T.add("dve", lambda e, pb=pb, ob=ob: e.tensor_copy(out=ob[:, 0:512], in_=pb[:, :]), reads=[pbn], writes=[obn + "a"])
                        else:
                            T.add("act", lambda e, pb=pb, ob=ob: e.activation(out=ob[:, 512:1024], in_=pb[:, :], func=AF.Copy), reads=[pbn], writes=[obn + "b"])
                    T.add("sp", lambda e, ob=ob, t=t: e.dma_start(out=out_d[t * 128:(t + 1) * 128, :], in_=ob), reads=[obn + "a", obn + "b"],
                          writes=["out.%d" % t], dma="out%d" % (t % 2))
                    outs.append("out.%d" % t)
            for g in range(NTG):
                fin_tg(g)
            T.add("sp", lambda e: e.wait_ge(sems["sp"], 0), reads=outs, writes=[])

        sems = {e: st.enter_context(nc.semaphore("s_" + e)) for e in COMPUTE + ("sp",)}
        for (l, what) in layers:
            if what == "attn":
                attention_layer(l)
            else:
                ffn_layer(l)
        final_phase(final)

        dsems = {k: st.enter_context(nc.semaphore("d_" + k)) for k in T.dma_keys()}
        block = st.enter_context(nc.Block())
        T.emit(nc, block, sems, dsems)
    return nc


_CONST_CACHE = {}


def _consts():
    if "cb" not in _CONST_CACHE:
        p = np.arange(128)[:, None]
        q = np.arange(128)[None, :]
        cbm = np.zeros((128, 768), np.float32)
        cbm[:, 0:128] = np.eye(128, dtype=np.float32)
        cbm[:, 128:256] = np.where(p > q, NEG, 0.0)
        cbm[:, 256:384] = np.where(p >= q, NEG, 0.0)
        cbm[:, 384:512] = np.where(p >= q, -1.0, 0.0)
        cbm[:, 512:640] = -1.0
        cbm[:, 640:704] = 1.0
        _CONST_CACHE["cb"] = cbm
        _CONST_CACHE["ident"] = np.eye(128, dtype=np.float32)
    return _CONST_CACHE["cb"], _CONST_CACHE["ident"]


def layout_inputs(attn_norm, ffn_norm, final_norm, fox_w_qkvf, fox_b_f, fox_w_o,
                  sb_w_qkv, sb_w_o, ffn_w_up, ffn_w_conv, ffn_b_conv, ffn_w_down):
    f32 = np.float32
    m = {}
    cbm, ident = _consts()
    m["cb"] = cbm
    m["ident"] = ident
    NV = DEPTH * VS + 8
    vec = np.zeros((128, NV), f32)
    for l in range(DEPTH):
        o = l * VS
        vec[:, o:o + 8] = np.asarray(attn_norm[l], f32).reshape(8, 128).T
        vec[:, o + 8:o + 16] = np.asarray(ffn_norm[l], f32).reshape(8, 128).T
        wc = np.asarray(ffn_w_conv[l], f32).reshape(3, 44, 128).transpose(2, 1, 0).reshape(128, 132)
        vec[:, o + 16:o + 148] = wc
        vec[:, o + 148:o + 192] = np.asarray(ffn_b_conv[l], f32).reshape(44, 128).T
        if l % 2 == 0:
            bf = np.asarray(fox_b_f[l // 2], f32)
            for r0 in (0, 32, 64):
                vec[r0:r0 + 16, o + 192] = bf
    vec[:, DEPTH * VS:DEPTH * VS + 8] = np.asarray(final_norm, f32).reshape(8, 128).T
    m["vecs"] = vec
    for l in range(DEPTH):
        j = l // 2
        if l % 2 == 0:
            w = np.asarray(fox_w_qkvf[j], f32)
            wo = np.asarray(fox_w_o[j], f32)
            wfm = np.zeros((128, KC, 96), f32)
            wfr = w[:, 3 * D:3 * D + H].reshape(KC, 128, H).transpose(1, 0, 2)
            for r0 in (0, 32, 64):
                wfm[:, :, r0:r0 + 16] = wfr
            m["wf%d" % j] = wfm.reshape(128, KC * 96)
        else:
            w = np.asarray(sb_w_qkv[j], f32)
            wo = np.asarray(sb_w_o[j], f32)
        parts = [w[:, i * D:(i + 1) * D].reshape(KC, 128, H, DH).transpose(2, 1, 0, 3) for i in range(3)]
        m["wqkv%d" % l] = np.ascontiguousarray(np.concatenate(parts, axis=3)).reshape(H, 128, KC * 192)
        m["wo%d" % l] = np.ascontiguousarray(wo.reshape(8, 128, D).transpose(1, 0, 2)).reshape(128, 8 * D)
        wu = np.asarray(ffn_w_up[l], f32)
        pu = wu[:, 0:FF].reshape(KC, 128, FC, 128).transpose(2, 1, 0, 3)
        pg = wu[:, FF:2 * FF].reshape(KC, 128, FC, 128).transpose(2, 1, 0, 3)
        m["wup%d" % l] = np.ascontiguousarray(np.concatenate([pu, pg], axis=3)).reshape(FC, 128, KC * 256)
        wdn = np.asarray(ffn_w_down[l], f32)
        m["wdn%d" % l] = np.ascontiguousarray(wdn.reshape(FC, 128, KC, 128).transpose(2, 1, 0, 3)).reshape(KC, 128, FC * 128)
    return m


ALL_LAYERS = [(l, w) for l in range(DEPTH) for w in ("attn", "ffn")]


def kernel(x, attn_norm, ffn_norm, final_norm, fox_w_qkvf, fox_b_f, fox_w_o,
           sb_w_qkv, sb_w_o, ffn_w_up, ffn_w_conv, ffn_b_conv, ffn_w_down):
    x = np.asarray(x, np.float32)
    B, S, _ = x.shape
    shared = layout_inputs(attn_norm, ffn_norm, final_norm, fox_w_qkvf, fox_b_f, fox_w_o,
                           sb_w_qkv, sb_w_o, ffn_w_up, ffn_w_conv, ffn_b_conv, ffn_w_down)
    nc = build_program(S, ALL_LAYERS, True)
    in_maps = []
    for b in range(B):
        mm = dict(shared)
        mm["x"] = np.ascontiguousarray(x[b])
        in_maps.append(mm)
    res = run_bass_kernel_spmd(nc, in_maps, core_ids=list(range(B)))
    return np.stack([np.asarray(r["out"], np.float32) for r in res.results], axis=0)
```

```python
import contextlib
import numpy as np
import concourse.bass as bass
import concourse.mybir as mybir
from concourse.bass_utils import run_bass_kernel_spmd

F32 = mybir.dt.float32
BF16 = mybir.dt.bfloat16
AF = mybir.ActivationFunctionType
ALU = mybir.AluOpType

D = 1024
KC = 8
TG = 512
H = 16
DH = 64
FF = 2816
FC = 22
DEPTH = 4
EPS = 1e-6
NEG = -30000.0
VS = 193

COMPUTE = ("pe", "act", "dve", "pool")


class _Op:
    __slots__ = ("eng", "fn", "deps", "dma_key", "dma_val", "signal", "count", "idx", "dma_waits")


class Tracker:
    def __init__(self):
        self.ops = []
        self.last_w = {}
        self.readers = {}
        self.dma_cum = {}
        self.fence_idx = None
        self.last_eng = {}
        self.last_dma = {}

    def fence(self, fn):
        deps = set(self.last_eng.values()) | set(self.last_dma.values())
        idx = self.add("pool", fn, extra_deps=deps)
        self.fence_idx = idx
        return idx

    def add(self, eng, fn, reads=(), writes=(), dma=None, extra_deps=()):
        op = _Op()
        op.eng = eng
        op.fn = fn
        op.dma_key = dma
        op.signal = False
        op.count = 0
        op.idx = len(self.ops)
        deps = set()
        for r in reads:
            w = self.last_w.get(r)
            if w is not None:
                deps.add(w)
        for r in writes:
            w = self.last_w.get(r)
            if w is not None:
                deps.add(w)
            for rd in self.readers.get(r, ()):
                deps.add(rd)
        deps |= set(extra_deps)
        if self.fence_idx is not None:
            deps.add(self.fence_idx)
        deps.discard(op.idx)
        op.deps = deps
        op.dma_waits = {}
        for d in deps:
            k = self.ops[d].dma_key
            if k is not None:
                op.dma_waits[k] = self.dma_cum[k]
        if dma is not None:
            self.last_dma[dma] = op.idx
        else:
            self.last_eng[eng] = op.idx
        if dma is not None:
            self.dma_cum[dma] = self.dma_cum.get(dma, 0) + 16
            op.dma_val = self.dma_cum[dma]
        else:
            op.dma_val = 0
        self.ops.append(op)
        for r in writes:
            self.last_w[r] = op.idx
            self.readers[r] = []
        for r in reads:
            if r not in writes:
                self.readers.setdefault(r, []).append(op.idx)
        return op.idx

    def dma_keys(self):
        return list(self.dma_cum.keys())

    def emit(self, nc, block, sems, dma_sems):
        ops = self.ops
        for op in ops:
            for d in op.deps:
                dop = ops[d]
                if dop.dma_key is not None:
                    continue
                if dop.eng == op.eng and op.eng in ("pe", "sp"):
                    continue
                dop.signal = True
        cnt = {e: 0 for e in COMPUTE + ("sp",)}
        for op in ops:
            if op.dma_key is None and op.signal:
                cnt[op.eng] += 1
                op.count = cnt[op.eng]
        per_eng = {e: [] for e in COMPUTE + ("sp",)}
        for op in ops:
            per_eng[op.eng].append(op)

        def run(eng_name, eng):
            waited = {}
            for op in per_eng[eng_name]:
                wl = {}
                for d in op.deps:
                    dop = ops[d]
                    if dop.dma_key is not None:
                        k = ("dma", dop.dma_key)
                        v = op.dma_waits[dop.dma_key]
                    else:
                        if dop.eng == eng_name and eng_name in ("pe", "sp"):
                            continue
                        k = ("eng", dop.eng)
                        v = dop.count
                    if v > wl.get(k, 0):
                        wl[k] = v
                for k, v in wl.items():
                    if waited.get(k, 0) >= v:
                        continue
                    waited[k] = v
                    s = dma_sems[k[1]] if k[0] == "dma" else sems[k[1]]
                    eng.wait_ge(s, v)
                ins = op.fn(eng)
                if op.dma_key is not None:
                    ins.then_inc(dma_sems[op.dma_key], 16)
                elif op.signal:
                    ins.then_inc(sems[op.eng], 1)

        @block.sync
        def _(e):
            run("sp", e)

        @block.scalar
        def _(e):
            run("act", e)

        @block.vector
        def _(e):
            run("dve", e)

        @block.gpsimd
        def _(e):
            run("pool", e)

        @block.tensor
        def _(e):
            run("pe", e)


class Arena:
    def __init__(self, t, nbytes):
        self.t = t
        self.nbytes = nbytes
        self.off = 0

    def reset(self):
        self.off = 0

    def f32(self, n):
        assert self.off % 4 == 0
        a = self.off // 4
        self.off += 4 * n
        assert self.off <= self.nbytes, ("arena overflow", self.off, self.nbytes)
        return self.t[:, a:a + n]

    def bf16(self, n):
        n2 = (n + 1) // 2
        v = self.f32(n2)
        return v.bitcast(BF16)[:, 0:n]


def build_program(S, layers, final=True):
    NTG = S // TG
    NT = S // 128
    NV = DEPTH * VS + 8
    nc = bass.Bass("TRN2", target_bir_lowering=False)

    def din(name, shape, dt=F32):
        return nc.dram_tensor(name, list(shape), dt, kind="ExternalInput").ap()

    x_d = din("x", [S, D])
    vec_d = din("vecs", [128, NV])
    cb_d = din("cb", [128, 6 * 128])
    id_d = din("ident", [128, 128])
    wqkv_d = [din("wqkv%d" % l, [H, 128, KC * 192]) for l in range(DEPTH)]
    wf_d = [din("wf%d" % j, [128, KC * 96]) for j in range(2)]
    wo_d = [din("wo%d" % l, [128, 8 * D]) for l in range(DEPTH)]
    wup_d = [din("wup%d" % l, [FC, 128, KC * 256]) for l in range(DEPTH)]
    wdn_d = [din("wdn%d" % l, [KC, 128, FC * 128]) for l in range(DEPTH)]
    out_d = nc.dram_tensor("out", [S, D], F32, kind="ExternalOutput").ap()
    wup_s = [nc.dram_tensor("wup_s%d" % l, [FC, 128, KC * 256], BF16, kind="Internal").ap() for l in range(DEPTH)]
    wdn_s = [nc.dram_tensor("wdn_s%d" % l, [KC, 128, FC * 128], BF16, kind="Internal").ap() for l in range(DEPTH)]
    ot_s = nc.dram_tensor("ot_s", [D, S], BF16, kind="Internal").ap()
    caug_s = nc.dram_tensor("caug_s", [3, H, S], BF16, kind="Internal").ap()

    T = Tracker()

    with contextlib.ExitStack() as st:
        hsT = st.enter_context(nc.sbuf_tensor("hsT", [128, KC, S], F32))
        vecs = st.enter_context(nc.sbuf_tensor("vecs_sb", [128, NV], F32))
        cb = st.enter_context(nc.sbuf_tensor("cb_sb", [128, 6 * 128], BF16))
        id32 = st.enter_context(nc.sbuf_tensor("id32", [128, 128], F32))
        ones32 = st.enter_context(nc.sbuf_tensor("ones32", [128, 128], F32))
        fsc = st.enter_context(nc.sbuf_tensor("fsc", [128, 8], F32))
        one1 = st.enter_context(nc.sbuf_tensor("one1", [128, 64], F32))
        AR_BYTES = 75600
        art = st.enter_context(nc.sbuf_tensor("arena", [128, AR_BYTES // 4], F32))
        ar = Arena(art, AR_BYTES)
        ps = [st.enter_context(nc.psum_tensor("ps%d" % i, [128, 512], F32)) for i in range(8)]

        identb = cb[:, 0:128]
        maskF = cb[:, 128:256]
        maskS = cb[:, 256:384]
        NTm = cb[:, 384:512]
        NOm = cb[:, 512:640]
        ones64 = cb[:, 640:704]

        def vcol(l, k):
            o = l * VS + k
            return vecs[:, o:o + 1]
        FIN = DEPTH * VS

        def fence():
            T.fence(lambda e: e.memset(fsc[:, 0:1], 0.0))

        T.add("sp", lambda e: e.dma_start(out=vecs[:], in_=vec_d), writes=["vecs"], dma="vecs")
        T.add("sp", lambda e: e.dma_start(out=id32[:], in_=id_d), writes=["id32"], dma="id32")
        T.add("pool", lambda e: e.dma_start(out=cb[:], in_=cb_d), writes=["cb"], dma="cb")
        T.add("pool", lambda e: e.memset(ones32[:], 1.0 / D), writes=["ones32"])
        T.add("pool", lambda e: e.memset(one1[:], 1.0), writes=["one1"])

        def cast_ffn_weights(l):
            for f in range(FC):
                T.add("pool", lambda e, f=f: e.dma_start(out=wup_s[l][f], in_=wup_d[l][f]),
                      writes=["wup_s%d.%d" % (l, f)], dma="wups")
            for co in range(KC):
                T.add("pool", lambda e, co=co: e.dma_start(out=wdn_s[l][co], in_=wdn_d[l][co]),
                      writes=["wdn_s%d.%d" % (l, co)], dma="wdns")

        ar.reset()
        xin = [ar.f32(D) for _ in range(2)]
        for t in range(NT):
            xb = xin[t % 2]
            T.add("sp", lambda e, t=t, xb=xb: e.dma_start(out=xb, in_=x_d[t * 128:(t + 1) * 128, :]),
                  writes=["xin%d" % (t % 2)], dma="xin%d" % (t % 2))
            for half in range(2):
                pb = ps[(2 * t + half) % 4]
                pbn = "ps%d" % ((2 * t + half) % 4)

                def tr(e, xb=xb, pb=pb, half=half):
                    r = None
                    for q in range(4):
                        c = half * 4 + q
                        r = e.transpose(pb[:, q * 128:(q + 1) * 128], xb[:, c * 128:(c + 1) * 128], id32[:])
                    return r
                T.add("pe", tr, reads=["xin%d" % (t % 2), "id32"], writes=[pbn])
                tgi = t // 4
                wr = ["hsw.%d.%d.%d" % (half * 4 + q, tgi, t % 4) for q in range(4)]
                if half == 0:
                    T.add("dve", lambda e, pb=pb, half=half, t=t: e.tensor_copy(
                        out=hsT[:, half * 4:half * 4 + 4, t * 128:(t + 1) * 128],
                        in_=pb[:].rearrange("p (q n) -> p q n", q=4)), reads=[pbn], writes=wr)
                else:
                    T.add("act", lambda e, pb=pb, half=half, t=t: e.activation(
                        out=hsT[:, half * 4:half * 4 + 4, t * 128:(t + 1) * 128],
                        in_=pb[:].rearrange("p (q n) -> p q n", q=4), func=AF.Copy), reads=[pbn], writes=wr)

        def rstd_for_tg(g, sqb, out_ap, out_res, psn, psn_name):
            for c in range(KC):
                sb_ = sqb[c % 2]
                if c % 3 == 1:
                    T.add("act", lambda e, c=c, sb_=sb_: e.activation(out=sb_, in_=hsT[:, c, g * TG:(g + 1) * TG], func=AF.Square),
                          reads=["hs.%d.%d" % (c, g)], writes=["sq%d" % (c % 2)])
                else:
                    T.add("pool" if c % 3 == 0 else "dve", lambda e, c=c, sb_=sb_: e.tensor_tensor(
                        out=sb_, in0=hsT[:, c, g * TG:(g + 1) * TG], in1=hsT[:, c, g * TG:(g + 1) * TG], op=ALU.mult),
                        reads=["hs.%d.%d" % (c, g)], writes=["sq%d" % (c % 2)])
                T.add("pe", lambda e, c=c, sb_=sb_: e.matmul(psn[:], lhsT=NOm, rhs=sb_, start=(c == 0), stop=(c == KC - 1)),
                      reads=["sq%d" % (c % 2), "cb"] + ([psn_name] if c else []), writes=[psn_name])
            T.add("act", lambda e: e.activation(out=out_ap, in_=psn[:], func=AF.Ln, bias=EPS, scale=-1.0 / D), reads=[psn_name], writes=[out_res])
            T.add("act", lambda e: e.activation(out=out_ap, in_=out_ap, func=AF.Exp, scale=-0.5), reads=[out_res], writes=[out_res])

        def make_xn(g, gcol0, l, rstd_ap, rstd_res, xn, xp="xnT", npool=0):
            for c in range(KC):
                T.add("pool" if c >= KC - npool else "dve", lambda e, c=c: e.scalar_tensor_tensor(
                    out=xn[:, c, :], in0=hsT[:, c, g * TG:(g + 1) * TG], scalar=vcol(l, gcol0 + c), in1=rstd_ap,
                    op0=ALU.mult, op1=ALU.mult), reads=["hs.%d.%d" % (c, g), rstd_res, "vecs"], writes=["%s.%d" % (xp, c)])
        XR = ["xnT.%d" % c for c in range(KC)]

        def attention_layer(l):
            fox = (l % 2 == 0)
            j_ = l // 2
            fence()
            ar.reset()
            rstd_bc = ar.f32(S)
            xnTb = [ar.bf16(KC * TG).rearrange("p (c n) -> p c n", c=KC) for _ in range(2)]
            xnT = xnTb[0]
            mark = ar.off
            sqb = [ar.bf16(TG) for _ in range(2)]
            psS = [ps[0], ps[1], ps[4], ps[5]]
            psSn = ["psb0", "psb1", "psb4", "psb5"]
            psO2, psD, psQ, psK, psV, psN = [ps[2], ps[3]], ps[6], ps[4], ps[5], ps[6], ps[7]

            for g in range(NTG):
                rstd_for_tg(g, sqb, rstd_bc[:, g * TG:(g + 1) * TG], "rstd.%d" % g, psN, "psN")

            CAUG = ["caug_s.%d.%d" % (p_, g_) for p_ in range(3) for g_ in range(NTG)]
            if fox:
                wf = ar.bf16(KC * 96).rearrange("p (c n) -> p c n", c=KC)
                ft = [ar.f32(TG) for _ in range(4)]
                fb = [ar.bf16(TG) for _ in range(3)]
                onesr = ar.f32(TG)
                T.add("pool", lambda e: e.dma_start(out=wf, in_=wf_d[j_].rearrange("p (c n) -> p c n", c=KC)), writes=["wf"], dma="wf")
                T.add("pool", lambda e: e.memset(onesr[:, :], 1.0), writes=["onesr"])

                def fphase(g):
                    make_xn(g, 0, l, rstd_bc[:, g * TG:(g + 1) * TG], "rstd.%d" % g, xnT)

                    def fmm(e):
                        r = None
                        for c in range(KC):
                            r = e.matmul(psN[0:96, :], lhsT=wf[:, c, :], rhs=xnT[:, c, :], start=(c == 0), stop=(c == KC - 1))
                        return r
                    T.add("pe", fmm, reads=["wf"] + XR, writes=["psN"])
                    cprev = ft[2 + (g + 1) % 2]
                    ccur = ft[2 + g % 2]
                    cn = "fc%d" % (g % 2)
                    cpn = "fc%d" % ((g + 1) % 2)
                    T.add("act", lambda e: e.activation(out=ft[0][0:80, :], in_=psN[0:80, :], func=AF.Identity, bias=vcol(l, 192)[0:80, :]),
                          reads=["psN", "vecs"], writes=["ft0"])
                    T.add("act", lambda e: e.activation(out=ft[0][0:80, :], in_=ft[0][0:80, :], func=AF.Exp, scale=-1.0), reads=["ft0"], writes=["ft0"])
                    T.add("act", lambda e: e.activation(out=ft[1][0:80, :], in_=ft[0][0:80, :], func=AF.Ln, bias=1.0), reads=["ft0"], writes=["ft1"])
                    if g == 0:
                        T.add("dve", lambda e: e.tensor_tensor_scan(out=ccur[0:80, :], data0=onesr[0:80, :], data1=ft[1][0:80, :],
                              initial=0.0, op0=ALU.mult, op1=ALU.subtract), reads=["onesr", "ft1"], writes=[cn])
                    else:
                        T.add("dve", lambda e: e.tensor_tensor_scan(out=ccur[0:80, :], data0=onesr[0:80, :], data1=ft[1][0:80, :],
                              initial=cprev[0:80, TG - 1:TG], op0=ALU.mult, op1=ALU.subtract),
                              reads=["onesr", "ft1", cpn], writes=[cn])
                    T.add("dve", lambda e: e.tensor_copy(out=fb[0][0:80, :], in_=ccur[0:80, :]), reads=[cn], writes=["fb0"])
                    T.add("dve", lambda e: e.tensor_tensor(out=ft[0][0:80, :], in0=ccur[0:80, :], in1=fb[0][0:80, :], op=ALU.subtract),
                          reads=[cn, "fb0"], writes=["ft0"])
                    T.add("dve", lambda e: e.tensor_copy(out=fb[1][0:80, :], in_=ft[0][0:80, :]), reads=["ft0"], writes=["fb1"])
                    T.add("dve", lambda e: e.tensor_tensor(out=ft[1][0:80, :], in0=ft[0][0:80, :], in1=fb[1][0:80, :], op=ALU.subtract),
                          reads=["ft0", "fb1"], writes=["ft1"])
                    T.add("dve", lambda e: e.tensor_copy(out=fb[2][0:80, :], in_=ft[1][0:80, :]), reads=["ft1"], writes=["fb2"])
                    for part in range(3):
                        T.add("sp", lambda e, part=part: e.dma_start(out=caug_s[part, :, g * TG:(g + 1) * TG],
                              in_=fb[part][32 * part:32 * part + 16, :]), reads=["fb%d" % part], writes=["caug_s.%d.%d" % (part, g)], dma="caug_w")
                for g in range(NTG):
                    fphase(g)
            fence()
            ar.off = mark
            wh = [ar.bf16(KC * 192).rearrange("p (c n) -> p c n", c=KC) for _ in range(2)]
            Qp = ar.bf16(S)
            Kp = ar.bf16(S)
            VW = 128 if fox else DH + 2
            Vt = ar.bf16(NT * VW).rearrange("p (t d) -> p t d", t=NT)
            OTs = [ar.bf16(TG) for _ in range(1)]
            if fox:
                PT = [ar.bf16(TG) for _ in range(4)]
                rden = ar.f32(TG)
                rbc = ar.f32(TG)
                T.add("pool", lambda e: e.memset(Vt[:, :, DH:VW], 0.0), writes=["V.ones"])
                T.add("pool", lambda e: e.memset(Vt[:, :, DH:DH + 1], 1.0), writes=["V.ones"])
            else:
                Eb = [ar.f32(TG) for _ in range(1)]
                SPb = [ar.bf16(TG) for _ in range(3)]
                aTb = [ar.bf16(TG) for _ in range(3)]
                Rbb = [ar.bf16(TG) for _ in range(4)]
                R32 = ar.f32(TG)
            if fox:
                T.add("pool", lambda e: e.memset(Qp[64:128, :], 0.0), writes=["Qp.aug"])
                T.add("pool", lambda e: e.memset(Kp[64:128, :], 0.0), writes=["Kp.aug"])
                T.add("pool", lambda e: e.memset(Qp[64:70, :], -1.0), writes=["Qp.aug"])
                T.add("pool", lambda e: e.memset(Kp[64:70, :], 1.0), writes=["Kp.aug"])
            else:
                T.add("pool", lambda e: e.memset(Qp[64:128, :], 0.0), writes=["Qp.aug"])
                T.add("pool", lambda e: e.memset(Kp[64:128, :], 0.0), writes=["Kp.aug"])
            KR = 128

            pending = []

            def flush():
                for fn_ in pending:
                    fn_()
                del pending[:]

            def project(h, g, whb, whn, skip_make=False):
                xnT = xnTb[g % 2]
                xp = "xnT" if g % 2 == 0 else "xnU"
                XR = ["%s.%d" % (xp, c) for c in range(KC)]
                if not skip_make:
                    make_xn(g, 0, l, rstd_bc[:, g * TG:(g + 1) * TG], "rstd.%d" % g, xnT, xp, npool=0)

                def qmm(e):
                    r = None
                    for c in range(KC):
                        r = e.matmul(psQ[0:64, :], lhsT=whb[:, c, 0:64], rhs=xnT[:, c, :], start=(c == 0), stop=(c == KC - 1))
                    return r

                def kmm(e):
                    r = None
                    for c in range(KC):
                        r = e.matmul(psK[0:64, :], lhsT=whb[:, c, 64:128], rhs=xnT[:, c, :], start=(c == 0), stop=(c == KC - 1))
                    return r

                def vmm(e):
                    r = None
                    for tt in range(4):
                        for c in range(KC):
                            r = e.matmul(psV[:, tt * 64:(tt + 1) * 64], lhsT=xnT[:, c, tt * 128:(tt + 1) * 128], rhs=whb[:, c, 128:192],
                                         start=(c == 0), stop=(c == KC - 1))
                    return r
                T.add("pe", qmm, reads=XR + [whn], writes=["psb4"])
                T.add("pe", kmm, reads=XR + [whn], writes=["psb5"])
                T.add("pe", vmm, reads=XR + [whn], writes=["psb6"])
                T.add("act", lambda e: e.activation(out=Qp[0:64, g * TG:(g + 1) * TG], in_=psQ[0:64, :], func=AF.Copy, scale=0.125),
                      reads=["psb4"], writes=["Qp.%d" % g])
                T.add("act", lambda e: e.activation(out=Kp[0:64, g * TG:(g + 1) * TG], in_=psK[0:64, :], func=AF.Copy), reads=["psb5"], writes=["Kp.%d" % g])
                T.add("act", lambda e: e.activation(out=Vt[:, g * 4:(g + 1) * 4, 0:DH], in_=psV[:, 0:256].rearrange("p (t d) -> p t d", t=4), func=AF.Copy),
                      reads=["psb6"], writes=["V.%d" % g])

            def attend_head(h, hoist):
                seq = []
                for g in range(NTG):
                    if fox:
                        blocks = [(4 * g + m, m) for m in range(4)] + [(j, None) for j in range(4 * g - 1, -1, -1)]
                    else:
                        blocks = [(4 * g + m, m) for m in range(3, -1, -1)] + [(j, None) for j in range(4 * g - 1, -1, -1)]
                    for i, (j, m) in enumerate(blocks):
                        seq.append((g, i, len(blocks), j, m))
                N = len(seq)
                mk = maskF if fox else maskS
                ob = OTs[0]
                obn = "OTs0"

                def score_op(n):
                    g, i, nb, j, m = seq[n]
                    q0 = g * TG
                    pS = psS[n % 4]
                    c0 = 0 if m is None else m * 128

                    def f(e):
                        kk = Kp[0:KR, j * 128:(j + 1) * 128]
                        if m is None:
                            return e.matmul(pS[:, :], lhsT=kk, rhs=Qp[0:KR, q0:q0 + TG], start=True, stop=True)
                        e.matmul(pS[:, c0:c0 + 128], lhsT=identb, rhs=mk, start=True, stop=False)
                        r = e.matmul(pS[:, c0:c0 + 128], lhsT=kk, rhs=Qp[0:KR, q0 + c0:q0 + c0 + 128], start=False, stop=True)
                        if c0 + 128 < TG:
                            r = e.matmul(pS[:, c0 + 128:TG], lhsT=kk, rhs=Qp[0:KR, q0 + c0 + 128:q0 + TG], start=False, stop=True)
                        return r
                    T.add("pe", f, reads=["Qp.%d" % g, "Qp.aug", "Kp.%d" % (j // 4), "Kp.aug", "cb"], writes=[psSn[n % 4]])

                if fox:
                    def s1(n):
                        score_op(n)

                    def s2(n):
                        g, i, nb, j, m = seq[n]
                        psO = psO2[g % 2]
                        pOn = "psO%d" % (g % 2)
                        pS = psS[n % 4]
                        c0 = 0 if m is None else m * 128
                        pt = PT[n % 4]
                        ptn = "PT%d" % (n % 4)
                        if i == 3:
                            flush()
                        T.add("act", lambda e: e.activation(out=pt[:, c0:TG], in_=pS[:, c0:TG], func=AF.Exp),
                              reads=[psSn[n % 4]], writes=[ptn])
                        T.add("pe", lambda e: e.matmul(psO[:, c0:TG], lhsT=Vt[:, j, :], rhs=pt[:, c0:TG], start=(i == 0), stop=(i == nb - 1)),
                              reads=[ptn, "V.%d" % (j // 4), "V.ones"] + ([pOn] if i else []), writes=[pOn])
                        if i == nb - 1:
                            T.add("dve", lambda e: e.reciprocal(out=rden[64:65, :], in_=psO[64:65, :]), reads=[pOn], writes=["rden"])

                            def part_b():
                                T.add("pe", lambda e: e.matmul(psD[0:64, :], lhsT=one1[64:65, 0:64], rhs=rden[64:65, :], start=True, stop=True),
                                      reads=["rden", "one1"], writes=["psb6"])
                                T.add("dve", lambda e: e.tensor_copy(out=rbc[0:64, :], in_=psD[0:64, :]), reads=["psb6"], writes=["rbc"])
                                T.add("dve", lambda e: e.tensor_tensor(out=ob[0:64, :], in0=psO[0:64, :], in1=rbc[0:64, :], op=ALU.mult),
                                      reads=[pOn, "rbc"], writes=[obn])
                                T.add("sp", lambda e: e.dma_start(out=ot_s[h * 64:(h + 1) * 64, g * TG:(g + 1) * TG], in_=ob[0:64, :]),
                                      reads=[obn], writes=["ot_s.%d.%d" % (h, g)], dma="otw")
                            pending.append(part_b)

                    def s3(n):
                        pass
                else:
                    def s1(n):
                        g, i, nb, j, m = seq[n]
                        c0 = 0 if m is None else m * 128
                        if i == 0:
                            T.add("pool", lambda e: e.memset(R32[:, :], 0.0), writes=["R32"])
                        score_op(n)
                        pS = psS[n % 4]
                        Ei = Eb[0]
                        SPi = SPb[n % 3]
                        T.add("act", lambda e: e.activation(out=Ei[:, c0:TG], in_=pS[:, c0:TG], func=AF.Exp), reads=[psSn[n % 4]], writes=["E0"])
                        T.add("act", lambda e: e.activation(out=SPi[:, c0:TG], in_=Ei[:, c0:TG], func=AF.Ln, bias=1.0), reads=["E0"], writes=["SP%d" % (n % 3)])
                        if i + 1 < nb:
                            g2, i2, nb2, j2, m2 = seq[n + 1]
                            c02 = 0 if m2 is None else m2 * 128
                            Rn = Rbb[(n + 1) % 4]
                            T.add("dve", lambda e: e.tensor_tensor(out=R32[:, c0:TG], in0=R32[:, c0:TG], in1=SPi[:, c0:TG], op=ALU.add),
                                  reads=["R32", "SP%d" % (n % 3)], writes=["R32"])
                            T.add("dve", lambda e: e.tensor_copy(out=Rn[:, c02:TG], in_=R32[:, c02:TG]), reads=["R32"], writes=["Rb%d" % ((n + 1) % 4)])

                    def s2(n):
                        g, i, nb, j, m = seq[n]
                        c0 = 0 if m is None else m * 128
                        pS = psS[n % 4]
                        SPi = SPb[n % 3]
                        ai = aTb[n % 3]
                        Rb = Rbb[n % 4]
                        c1 = c0 + 128 if m is not None else c0

                        def st2(e):
                            r = e.matmul(pS[:, c0:TG], lhsT=NTm, rhs=SPi[:, c0:TG], start=False, stop=(c1 >= TG))
                            if c1 < TG:
                                r = e.matmul(pS[:, c1:TG], lhsT=NOm, rhs=Rb[:, c1:TG], start=False, stop=True)
                            return r
                        T.add("pe", st2, reads=["SP%d" % (n % 3), "cb", psSn[n % 4]] + (["Rb%d" % (n % 4)] if c1 < TG else []), writes=[psSn[n % 4]])
                        T.add("act", lambda e: e.activation(out=ai[:, c0:TG], in_=pS[:, c0:TG], func=AF.Exp), reads=[psSn[n % 4]], writes=["aT%d" % (n % 3)])

                    def s3(n):
                        g, i, nb, j, m = seq[n]
                        psO = psO2[g % 2]
                        pOn = "psO%d" % (g % 2)
                        c0 = 0 if m is None else m * 128
                        ai = aTb[n % 3]
                        last = (i == nb - 1)
                        T.add("pe", lambda e: e.matmul(psO[0:64, c0:TG], lhsT=Vt[:, j, 0:DH], rhs=ai[:, c0:TG], start=(i == 0), stop=last),
                              reads=["aT%d" % (n % 3), "V.%d" % (j // 4)] + ([pOn] if i else []), writes=[pOn])
                        if last:
                            T.add("dve", lambda e: e.tensor_copy(out=ob[0:64, :], in_=psO[0:64, :]), reads=[pOn], writes=[obn])
                            T.add("sp", lambda e: e.dma_start(out=ot_s[h * 64:(h + 1) * 64, g * TG:(g + 1) * TG], in_=ob[0:64, :]),
                                  reads=[obn], writes=["ot_s.%d.%d" % (h, g)], dma="otw")

                s1(0)
                s1(1)
                for n in range(N):
                    if seq[n][0] == NTG - 1 and seq[n][1] == 0 and hoist is not None:
                        hoist()
                    if n + 2 < N:
                        s1(n + 2)
                    s2(n)
                    if n >= 1:
                        s3(n - 1)
                s3(N - 1)

            for h in range(H):
                whb = wh[h % 2]
                whn = "wh%d" % (h % 2)
                T.add("pool", lambda e, h=h, whb=whb: e.dma_start(out=whb, in_=wqkv_d[l][h].rearrange("p (c n) -> p c n", c=KC)),
                      writes=[whn], dma=whn)
                if h == 1:
                    cast_ffn_weights(l)
                if fox:
                    T.add("sp", lambda e, h=h: e.dma_start(out=Qp[64:67, :], in_=caug_s[:, h, :]), reads=CAUG, writes=["Qp.aug"], dma="aug")
                    T.add("sp", lambda e, h=h: e.dma_start(out=Kp[67:70, :], in_=caug_s[:, h, :]), reads=CAUG, writes=["Kp.aug"], dma="aug")
                for g in range(NTG):
                    project(h, g, whb, whn, skip_make=(g == 0 and h > 0))
                hoist = None
                if h + 1 < H:
                    hoist = lambda: make_xn(0, 0, l, rstd_bc[:, 0:TG], "rstd.0", xnTb[0], "xnT", npool=0)
                attend_head(h, hoist)
            flush()

            fence()
            ar.reset()
            wo = ar.bf16(8 * D).rearrange("p (c n) -> p c n", c=8)
            OTin = [ar.bf16(8 * TG).rearrange("p (c n) -> p c n", c=8) for _ in range(2)]
            T.add("pool", lambda e: e.dma_start(out=wo, in_=wo_d[l].rearrange("p (c n) -> p c n", c=8)), writes=["wo"], dma="wo")

            def wo_tg(g):
                ob = OTin[g % 2]
                obn = "OTin%d" % (g % 2)
                T.add("sp", lambda e: e.dma_start(out=ob, in_=ot_s[:, g * TG:(g + 1) * TG].rearrange("(c p) n -> p c n", p=128)),
                      reads=["ot_s.%d.%d" % (h_, g) for h_ in range(H)], writes=[obn], dma=obn)
                for co in range(KC):
                    py = ps[co % 2]
                    pyn = "ps%d" % (co % 2)

                    def omm(e, co=co, py=py):
                        r = None
                        for c in range(8):
                            r = e.matmul(py[:, :], lhsT=wo[:, c, co * 128:(co + 1) * 128], rhs=ob[:, c, :], start=(c == 0), stop=(c == 7))
                        return r
                    T.add("pe", omm, reads=["wo", obn], writes=[pyn])
                    T.add("dve", lambda e, co=co, py=py: e.tensor_tensor(out=hsT[:, co, g * TG:(g + 1) * TG], in0=py[:, :],
                          in1=hsT[:, co, g * TG:(g + 1) * TG], op=ALU.add), reads=[pyn, "hs.%d.%d" % (co, g)], writes=["hs.%d.%d" % (co, g)])
            for g in range(NTG):
                wo_tg(g)

        def ffn_layer(l):
            fence()
            ar.reset()
            xnT = ar.bf16(KC * TG).rearrange("p (c n) -> p c n", c=KC)
            actT = ar.bf16(FC * TG).rearrange("p (c n) -> p c n", c=FC)
            wu = [ar.bf16(KC * 256).rearrange("p (c n) -> p c n", c=KC) for _ in range(2)]
            wd = [ar.bf16(FC * 128).rearrange("p (c n) -> p c n", c=FC) for _ in range(2)]
            hext = [[ar.f32(516) for _ in range(2)] for _ in range(2)]
            tcv = [[ar.f32(TG) for _ in range(2)] for _ in range(3)]
            sqb = [ar.bf16(TG) for _ in range(2)]
            rstd = ar.f32(TG)
            halo = ar.f32(2 * FC * 2).rearrange("p (c n) -> p c n", n=2)
            psU = [ps[0], ps[2]]
            psG = [ps[1], ps[3]]
            psY = [ps[4], ps[5]]
            psN = ps[7]
            T.add("pool", lambda e: e.memset(halo[:, :, :], 0.0), writes=["halo"])

            def prep(g):
                rstd_for_tg(g, sqb, rstd[:, :], "rstdf", psN, "psN")
                make_xn(g, 8, l, rstd[:, :], "rstdf", xnT)

            def gate(f):
                par = f % 3
                T.add("act", lambda e: e.activation(out=tcv[par][1], in_=tcv[par][1], func=AF.Silu), reads=["tcv%d1" % par], writes=["tcv%d1" % par])
                T.add("pool", lambda e: e.tensor_tensor(out=actT[:, f, :], in0=tcv[par][1], in1=tcv[par][0], op=ALU.mult),
                      reads=["tcv%d1" % par, "tcv%d0" % par], writes=["actT.%d" % f])

            def up(g):
                for f in range(FC):
                    par = f % 2
                    wub = wu[f % 2]
                    wun = "wu%d" % (f % 2)
                    T.add("sp", lambda e, f=f, wub=wub: e.dma_start(out=wub, in_=wup_s[l][f].rearrange("p (c n) -> p c n", c=KC)),
                          reads=["wup_s%d.%d" % (l, f)], writes=[wun], dma=wun)
                    for ug in range(2):
                        pp = (psU if ug == 0 else psG)[par]
                        ppn = "psUG%d%d" % (ug, par)

                        def upmm(e, ug=ug, pp=pp, wub=wub):
                            r = None
                            for c in range(KC):
                                r = e.matmul(pp[:, :], lhsT=wub[:, c, ug * 128:(ug + 1) * 128], rhs=xnT[:, c, :], start=(c == 0), stop=(c == KC - 1))
                            return r
                        T.add("pe", upmm, reads=XR + [wun], writes=[ppn])
                        hx = hext[par][ug]
                        hxn = "hext%d%d" % (par, ug)
                        ch = ug * FC + f
                        tc_ = tcv[f % 3][ug]
                        tcn = "tcv%d%d" % (f % 3, ug)
                        wc = 16 + ch * 3
                        T.add("pool", lambda e, hx=hx, ch=ch: e.tensor_copy(out=hx[:, 0:2], in_=halo[:, ch, :]), reads=["halo.%d" % ch, "halo"], writes=[hxn + "h"])
                        T.add("act", lambda e, hx=hx, pp=pp: e.activation(out=hx[:, 2:514], in_=pp[:, :], func=AF.Copy), reads=[ppn], writes=[hxn])
                        T.add("act", lambda e, tc_=tc_, pp=pp, wc=wc, ch=ch: e.activation(out=tc_, in_=pp[:, :], func=AF.Identity,
                              scale=vcol(l, wc + 2), bias=vcol(l, 148 + ch)), reads=[ppn, "vecs"], writes=[tcn])
                        T.add("pool", lambda e, hx=hx, ch=ch: e.tensor_copy(out=halo[:, ch, :], in_=hx[:, 512:514]), reads=[hxn], writes=["halo.%d" % ch])
                        T.add("dve", lambda e, hx=hx, tc_=tc_, wc=wc: e.scalar_tensor_tensor(out=tc_, in0=hx[:, 1:513], scalar=vcol(l, wc + 1), in1=tc_,
                              op0=ALU.mult, op1=ALU.add), reads=[hxn, hxn + "h", tcn, "vecs"], writes=[tcn])
                        T.add("dve", lambda e, hx=hx, tc_=tc_, wc=wc: e.scalar_tensor_tensor(out=tc_, in0=hx[:, 0:512], scalar=vcol(l, wc), in1=tc_,
                              op0=ALU.mult, op1=ALU.add), reads=[hxn, hxn + "h", tcn, "vecs"], writes=[tcn])
                    if f >= 1:
                        gate(f - 1)
                gate(FC - 1)

            def down(g, nxt=None):
                ar_ = ["actT.%d" % f for f in range(FC)]
                for co in range(KC):
                    if nxt is not None and 1 <= co <= 4:
                        nxt[0](2 * (co - 1))
                        nxt[0](2 * (co - 1) + 1)
                    if nxt is not None and co == 5:
                        nxt[1]()
                    wdb = wd[co % 2]
                    wdn = "wd%d" % (co % 2)
                    T.add("sp", lambda e, co=co, wdb=wdb: e.dma_start(out=wdb, in_=wdn_s[l][co].rearrange("p (c n) -> p c n", c=FC)),
                          reads=["wdn_s%d.%d" % (l, co)], writes=[wdn], dma=wdn)
                    py = psY[co % 2]
                    pyn = "psY%d" % (co % 2)

                    def dmm(e, wdb=wdb, py=py):
                        r = None
                        for f in range(FC):
                            r = e.matmul(py[:, :], lhsT=wdb[:, f, :], rhs=actT[:, f, :], start=(f == 0), stop=(f == FC - 1))
                        return r
                    T.add("pe", dmm, reads=ar_ + [wdn], writes=[pyn])
                    T.add("dve", lambda e, co=co, py=py: e.tensor_tensor(out=hsT[:, co, g * TG:(g + 1) * TG], in0=py[:, :],
                          in1=hsT[:, co, g * TG:(g + 1) * TG], op=ALU.add), reads=[pyn, "hs.%d.%d" % (co, g)], writes=["hs.%d.%d" % (co, g)])
            def prep_pieces(g):
                def piece(c):
                    sb_ = sqb[c % 2]
                    T.add("pool", lambda e: e.tensor_tensor(out=sb_, in0=hsT[:, c, g * TG:(g + 1) * TG], in1=hsT[:, c, g * TG:(g + 1) * TG], op=ALU.mult),
                          reads=["hs.%d.%d" % (c, g)], writes=["sq%d" % (c % 2)])
                    T.add("pe", lambda e: e.matmul(psN[:], lhsT=NOm, rhs=sb_, start=(c == 0), stop=(c == KC - 1)),
                          reads=["sq%d" % (c % 2), "cb"] + (["psN"] if c else []), writes=["psN"])

                def finish():
                    T.add("act", lambda e: e.activation(out=rstd[:, :], in_=psN[:], func=AF.Ln, bias=EPS, scale=-1.0 / D), reads=["psN"], writes=["rstdf"])
                    T.add("act", lambda e: e.activation(out=rstd[:, :], in_=rstd[:, :], func=AF.Exp, scale=-0.5), reads=["rstdf"], writes=["rstdf"])
                    make_xn(g, 8, l, rstd[:, :], "rstdf", xnT)
                return piece, finish

            prep(0)
            for g in range(NTG):
                up(g)
                down(g, prep_pieces(g + 1) if g + 1 < NTG else None)

        def final_phase(normed):
            fence()
            ar.reset()
            yT = ar.f32(KC * TG).rearrange("p (c n) -> p c n", c=KC)
            ot = [ar.f32(D) for _ in range(2)]
            sqb = [ar.bf16(TG) for _ in range(2)]
            rstd = ar.f32(TG)
            psN = ps[7]
            outs = []

            def fin_tg(g):
                if normed:
                    rstd_for_tg(g, sqb, rstd[:, :], "rstdf", psN, "psN")
                    for c in range(KC):
                        T.add("dve", lambda e, c=c: e.scalar_tensor_tensor(out=yT[:, c, :], in0=hsT[:, c, g * TG:(g + 1) * TG], scalar=vecs[:, FIN + c:FIN + c + 1],
                              in1=rstd[:, :], op0=ALU.mult, op1=ALU.mult), reads=["hs.%d.%d" % (c, g), "rstdf", "vecs"], writes=["yT.%d" % c])
                else:
                    for c in range(KC):
                        T.add("dve", lambda e, c=c: e.tensor_copy(out=yT[:, c, :], in_=hsT[:, c, g * TG:(g + 1) * TG]),
                              reads=["hs.%d.%d" % (c, g)], writes=["yT.%d" % c])
                for tt in range(4):
                    t = g * 4 + tt
                    ob = ot[t % 2]
                    obn = "ot%d" % (t % 2)
                    for half in range(2):
                        pb = ps[(2 * t + half) % 4]
                        pbn = "psf%d" % ((2 * t + half) % 4)

                        def tr(e, pb=pb, half=half, tt=tt):
                            r = None
                            for q in range(4):
                                c = half * 4 + q
                                r = e.transpose(pb[:, q * 128:(q + 1) * 128], yT[:, c, tt * 128:(tt + 1) * 128], id32[:])
                            return r
                        T.add("pe", tr, reads=["yT.%d" % (half * 4 + q) for q in range(4)] + ["id32"], writes=[pbn])
                        if half == 0:
                            T.add("dve", lambda e, pb=pb, ob=ob: e.tensor_copy(out=ob[:, 0:512], in_=pb[:, :]), reads=[pbn], writes=[obn + "a"])
                        else:
                            T.add("act", lambda e, pb=pb, ob=ob: e.activation(out=ob[:, 512:1024], in_=pb[:, :], func=AF.Copy), reads=[pbn], writes=[obn + "b"])
                    T.add("sp", lambda e, ob=ob, t=t: e.dma_start(out=out_d[t * 128:(t + 1) * 128, :], in_=ob), reads=[obn + "a", obn + "b"],
                          writes=["out.%d" % t], dma="out%d" % (t % 2))
                    outs.append("out.%d" % t)
            for g in range(NTG):
                fin_tg(g)
            T.add("sp", lambda e: e.wait_ge(sems["sp"], 0), reads=outs, writes=[])

        sems = {e: st.enter_context(nc.semaphore("s_" + e)) for e in COMPUTE + ("sp",)}
        for (l, what) in layers:
            if what == "attn":
                attention_layer(l)
            else:
                ffn_layer(l)
        final_phase(final)

        dsems = {k: st.enter_context(nc.semaphore("d_" + k)) for k in T.dma_keys()}
        block = st.enter_context(nc.Block())
        T.emit(nc, block, sems, dsems)
    return nc


_CONST_CACHE = {}


def _consts():
    if "cb" not in _CONST_CACHE:
        p = np.arange(128)[:, None]
        q = np.arange(128)[None, :]
        cbm = np.zeros((128, 768), np.float32)
        cbm[:, 0:128] = np.eye(128, dtype=np.float32)
        cbm[:, 128:256] = np.where(p > q, NEG, 0.0)
        cbm[:, 256:384] = np.where(p >= q, NEG, 0.0)
        cbm[:, 384:512] = np.where(p >= q, -1.0, 0.0)
        cbm[:, 512:640] = -1.0
        cbm[:, 640:704] = 1.0
        _CONST_CACHE["cb"] = cbm
        _CONST_CACHE["ident"] = np.eye(128, dtype=np.float32)
    return _CONST_CACHE["cb"], _CONST_CACHE["ident"]


def layout_inputs(attn_norm, ffn_norm, final_norm, fox_w_qkvf, fox_b_f, fox_w_o,
                  sb_w_qkv, sb_w_o, ffn_w_up, ffn_w_conv, ffn_b_conv, ffn_w_down):
    f32 = np.float32
    m = {}
    cbm, ident = _consts()
    m["cb"] = cbm
    m["ident"] = ident
    NV = DEPTH * VS + 8
    vec = np.zeros((128, NV), f32)
    for l in range(DEPTH):
        o = l * VS
        vec[:, o:o + 8] = np.asarray(attn_norm[l], f32).reshape(8, 128).T
        vec[:, o + 8:o + 16] = np.asarray(ffn_norm[l], f32).reshape(8, 128).T
        wc = np.asarray(ffn_w_conv[l], f32).reshape(3, 44, 128).transpose(2, 1, 0).reshape(128, 132)
        vec[:, o + 16:o + 148] = wc
        vec[:, o + 148:o + 192] = np.asarray(ffn_b_conv[l], f32).reshape(44, 128).T
        if l % 2 == 0:
            bf = np.asarray(fox_b_f[l // 2], f32)
            for r0 in (0, 32, 64):
                vec[r0:r0 + 16, o + 192] = bf
    vec[:, DEPTH * VS:DEPTH * VS + 8] = np.asarray(final_norm, f32).reshape(8, 128).T
    m["vecs"] = vec
    for l in range(DEPTH):
        j = l // 2
        if l % 2 == 0:
            w = np.asarray(fox_w_qkvf[j], f32)
            wo = np.asarray(fox_w_o[j], f32)
            wfm = np.zeros((128, KC, 96), f32)
            wfr = w[:, 3 * D:3 * D + H].reshape(KC, 128, H).transpose(1, 0, 2)
            for r0 in (0, 32, 64):
                wfm[:, :, r0:r0 + 16] = wfr
            m["wf%d" % j] = wfm.reshape(128, KC * 96)
        else:
            w = np.asarray(sb_w_qkv[j], f32)
            wo = np.asarray(sb_w_o[j], f32)
        parts = [w[:, i * D:(i + 1) * D].reshape(KC, 128, H, DH).transpose(2, 1, 0, 3) for i in range(3)]
        m["wqkv%d" % l] = np.ascontiguousarray(np.concatenate(parts, axis=3)).reshape(H, 128, KC * 192)
        m["wo%d" % l] = np.ascontiguousarray(wo.reshape(8, 128, D).transpose(1, 0, 2)).reshape(128, 8 * D)
        wu = np.asarray(ffn_w_up[l], f32)
        pu = wu[:, 0:FF].reshape(KC, 128, FC, 128).transpose(2, 1, 0, 3)
        pg = wu[:, FF:2 * FF].reshape(KC, 128, FC, 128).transpose(2, 1, 0, 3)
        m["wup%d" % l] = np.ascontiguousarray(np.concatenate([pu, pg], axis=3)).reshape(FC, 128, KC * 256)
        wdn = np.asarray(ffn_w_down[l], f32)
        m["wdn%d" % l] = np.ascontiguousarray(wdn.reshape(FC, 128, KC, 128).transpose(2, 1, 0, 3)).reshape(KC, 128, FC * 128)
    return m


ALL_LAYERS = [(l, w) for l in range(DEPTH) for w in ("attn", "ffn")]


def kernel(x, attn_norm, ffn_norm, final_norm, fox_w_qkvf, fox_b_f, fox_w_o,
           sb_w_qkv, sb_w_o, ffn_w_up, ffn_w_conv, ffn_b_conv, ffn_w_down):
    x = np.asarray(x, np.float32)
    B, S, _ = x.shape
    shared = layout_inputs(attn_norm, ffn_norm, final_norm, fox_w_qkvf, fox_b_f, fox_w_o,
                           sb_w_qkv, sb_w_o, ffn_w_up, ffn_w_conv, ffn_b_conv, ffn_w_down)
    nc = build_program(S, ALL_LAYERS, True)
    in_maps = []
    for b in range(B):
        mm = dict(shared)
        mm["x"] = np.ascontiguousarray(x[b])
        in_maps.append(mm)
    res = run_bass_kernel_spmd(nc, in_maps, core_ids=list(range(B)))
    return np.stack([np.asarray(r["out"], np.float32) for r in res.results], axis=0)
```

```python
import contextlib
import numpy as np
import concourse.bass as bass
import concourse.mybir as mybir
from concourse.bass_utils import run_bass_kernel_spmd

F32 = mybir.dt.float32
BF16 = mybir.dt.bfloat16
AF = mybir.ActivationFunctionType
ALU = mybir.AluOpType

D = 1024
KC = 8
TG = 512
H = 16
DH = 64
FF = 2816
FC = 22
DEPTH = 4
EPS = 1e-6
NEG = -30000.0
VS = 193

COMPUTE = ("pe", "act", "dve", "pool")


class _Op:
    __slots__ = ("eng", "fn", "deps", "dma_key", "dma_val", "signal", "count", "idx", "dma_waits")


class Tracker:
    def __init__(self):
        self.ops = []
        self.last_w = {}
        self.readers = {}
        self.dma_cum = {}
        self.fence_idx = None
        self.last_eng = {}
        self.last_dma = {}

    def fence(self, fn):
        deps = set(self.last_eng.values()) | set(self.last_dma.values())
        idx = self.add("pool", fn, extra_deps=deps)
        self.fence_idx = idx
        return idx

    def add(self, eng, fn, reads=(), writes=(), dma=None, extra_deps=()):
        op = _Op()
        op.eng = eng
        op.fn = fn
        op.dma_key = dma
        op.signal = False
        op.count = 0
        op.idx = len(self.ops)
        deps = set()
        for r in reads:
            w = self.last_w.get(r)
            if w is not None:
                deps.add(w)
        for r in writes:
            w = self.last_w.get(r)
            if w is not None:
                deps.add(w)
            for rd in self.readers.get(r, ()):
                deps.add(rd)
        deps |= set(extra_deps)
        if self.fence_idx is not None:
            deps.add(self.fence_idx)
        deps.discard(op.idx)
        op.deps = deps
        op.dma_waits = {}
        for d in deps:
            k = self.ops[d].dma_key
            if k is not None:
                op.dma_waits[k] = self.dma_cum[k]
        if dma is not None:
            self.last_dma[dma] = op.idx
        else:
            self.last_eng[eng] = op.idx
        if dma is not None:
            self.dma_cum[dma] = self.dma_cum.get(dma, 0) + 16
            op.dma_val = self.dma_cum[dma]
        else:
            op.dma_val = 0
        self.ops.append(op)
        for r in writes:
            self.last_w[r] = op.idx
            self.readers[r] = []
        for r in reads:
            if r not in writes:
                self.readers.setdefault(r, []).append(op.idx)
        return op.idx

    def dma_keys(self):
        return list(self.dma_cum.keys())

    def emit(self, nc, block, sems, dma_sems):
        ops = self.ops
        for op in ops:
            for d in op.deps:
                dop = ops[d]
                if dop.dma_key is not None:
                    continue
                if dop.eng == op.eng and op.eng in ("pe", "sp"):
                    continue
                dop.signal = True
        cnt = {e: 0 for e in COMPUTE + ("sp",)}
        for op in ops:
            if op.dma_key is None and op.signal:
                cnt[op.eng] += 1
                op.count = cnt[op.eng]
        per_eng = {e: [] for e in COMPUTE + ("sp",)}
        for op in ops:
            per_eng[op.eng].append(op)

        def run(eng_name, eng):
            waited = {}
            for op in per_eng[eng_name]:
                wl = {}
                for d in op.deps:
                    dop = ops[d]
                    if dop.dma_key is not None:
                        k = ("dma", dop.dma_key)
                        v = op.dma_waits[dop.dma_key]
                    else:
                        if dop.eng == eng_name and eng_name in ("pe", "sp"):
                            continue
                        k = ("eng", dop.eng)
                        v = dop.count
                    if v > wl.get(k, 0):
                        wl[k] = v
                for k, v in wl.items():
                    if waited.get(k, 0) >= v:
                        continue
                    waited[k] = v
                    s = dma_sems[k[1]] if k[0] == "dma" else sems[k[1]]
                    eng.wait_ge(s, v)
                ins = op.fn(eng)
                if op.dma_key is not None:
                    ins.then_inc(dma_sems[op.dma_key], 16)
                elif op.signal:
                    ins.then_inc(sems[op.eng], 1)

        @block.sync
        def _(e):
            run("sp", e)

        @block.scalar
        def _(e):
            run("act", e)

        @block.vector
        def _(e):
            run("dve", e)

        @block.gpsimd
        def _(e):
            run("pool", e)

        @block.tensor
        def _(e):
            run("pe", e)


class Arena:
    def __init__(self, t, nbytes):
        self.t = t
        self.nbytes = nbytes
        self.off = 0

    def reset(self):
        self.off = 0

    def f32(self, n):
        assert self.off % 4 == 0
        a = self.off // 4
        self.off += 4 * n
        assert self.off <= self.nbytes, ("arena overflow", self.off, self.nbytes)
        return self.t[:, a:a + n]

    def bf16(self, n):
        n2 = (n + 1) // 2
        v = self.f32(n2)
        return v.bitcast(BF16)[:, 0:n]


def build_program(S, layers, final=True):
    NTG = S // TG
    NT = S // 128
    NV = DEPTH * VS + 8
    nc = bass.Bass("TRN2", target_bir_lowering=False)

    def din(name, shape, dt=F32):
        return nc.dram_tensor(name, list(shape), dt, kind="ExternalInput").ap()

    x_d = din("x", [S, D])
    vec_d = din("vecs", [128, NV])
    cb_d = din("cb", [128, 6 * 128])
    id_d = din("ident", [128, 128])
    wqkv_d = [din("wqkv%d" % l, [H, 128, KC * 192]) for l in range(DEPTH)]
    wf_d = [din("wf%d" % j, [128, KC * 96]) for j in range(2)]
    wo_d = [din("wo%d" % l, [128, 8 * D]) for l in range(DEPTH)]
    wup_d = [din("wup%d" % l, [FC, 128, KC * 256]) for l in range(DEPTH)]
    wdn_d = [din("wdn%d" % l, [KC, 128, FC * 128]) for l in range(DEPTH)]
    out_d = nc.dram_tensor("out", [S, D], F32, kind="ExternalOutput").ap()
    wup_s = [nc.dram_tensor("wup_s%d" % l, [FC, 128, KC * 256], BF16, kind="Internal").ap() for l in range(DEPTH)]
    wdn_s = [nc.dram_tensor("wdn_s%d" % l, [KC, 128, FC * 128], BF16, kind="Internal").ap() for l in range(DEPTH)]
    ot_s = nc.dram_tensor("ot_s", [D, S], BF16, kind="Internal").ap()
    caug_s = nc.dram_tensor("caug_s", [3, H, S], BF16, kind="Internal").ap()

    T = Tracker()

    with contextlib.ExitStack() as st:
        hsT = st.enter_context(nc.sbuf_tensor("hsT", [128, KC, S], F32))
        vecs = st.enter_context(nc.sbuf_tensor("vecs_sb", [128, NV], F32))
        cb = st.enter_context(nc.sbuf_tensor("cb_sb", [128, 6 * 128], BF16))
        id32 = st.enter_context(nc.sbuf_tensor("id32", [128, 128], F32))
        ones32 = st.enter_context(nc.sbuf_tensor("ones32", [128, 128], F32))
        fsc = st.enter_context(nc.sbuf_tensor("fsc", [128, 8], F32))
        one1 = st.enter_context(nc.sbuf_tensor("one1", [128, 64], F32))
        AR_BYTES = 75600
        art = st.enter_context(nc.sbuf_tensor("arena", [128, AR_BYTES // 4], F32))
        ar = Arena(art, AR_BYTES)
        psA = st.enter_context(nc.psum_tensor("psA", [128, 1024], F32))
        psB = st.enter_context(nc.psum_tensor("psB", [128, 1024], F32))
        psx = {i: st.enter_context(nc.psum_tensor("ps%d" % i, [128, 512], F32)) for i in (2, 3, 6, 7)}
        ps = [psA[:, 0:512], psA[:, 512:1024], psx[2][:, :], psx[3][:, :], psB[:, 0:512], psB[:, 512:1024], psx[6][:, :], psx[7][:, :]]

        identb = cb[:, 0:128]
        maskF = cb[:, 128:256]
        maskS = cb[:, 256:384]
        NTm = cb[:, 384:512]
        NOm = cb[:, 512:640]
        ones64 = cb[:, 640:704]

        def vcol(l, k):
            o = l * VS + k
            return vecs[:, o:o + 1]
        FIN = DEPTH * VS

        def fence():
            T.fence(lambda e: e.memset(fsc[:, 0:1], 0.0))

        T.add("sp", lambda e: e.dma_start(out=vecs[:], in_=vec_d), writes=["vecs"], dma="vecs")
        T.add("sp", lambda e: e.dma_start(out=id32[:], in_=id_d), writes=["id32"], dma="id32")
        T.add("pool", lambda e: e.dma_start(out=cb[:], in_=cb_d), writes=["cb"], dma="cb")
        T.add("pool", lambda e: e.memset(ones32[:], 1.0 / D), writes=["ones32"])
        T.add("pool", lambda e: e.memset(one1[:], 1.0), writes=["one1"])

        def cast_ffn_weights(l):
            for f in range(FC):
                T.add("pool", lambda e, f=f: e.dma_start(out=wup_s[l][f], in_=wup_d[l][f]),
                      writes=["wup_s%d.%d" % (l, f)], dma="wups")
            for co in range(KC):
                T.add("pool", lambda e, co=co: e.dma_start(out=wdn_s[l][co], in_=wdn_d[l][co]),
                      writes=["wdn_s%d.%d" % (l, co)], dma="wdns")

        ar.reset()
        xin = [ar.f32(D) for _ in range(2)]
        for t in range(NT):
            xb = xin[t % 2]
            T.add("sp", lambda e, t=t, xb=xb: e.dma_start(out=xb, in_=x_d[t * 128:(t + 1) * 128, :]),
                  writes=["xin%d" % (t % 2)], dma="xin%d" % (t % 2))
            for half in range(2):
                pb = ps[(2 * t + half) % 4]
                pbn = "ps%d" % ((2 * t + half) % 4)

                def tr(e, xb=xb, pb=pb, half=half):
                    r = None
                    for q in range(4):
                        c = half * 4 + q
                        r = e.transpose(pb[:, q * 128:(q + 1) * 128], xb[:, c * 128:(c + 1) * 128], id32[:])
                    return r
                T.add("pe", tr, reads=["xin%d" % (t % 2), "id32"], writes=[pbn])
                tgi = t // 4
                wr = ["hsw.%d.%d.%d" % (half * 4 + q, tgi, t % 4) for q in range(4)]
                if half == 0:
                    T.add("dve", lambda e, pb=pb, half=half, t=t: e.tensor_copy(
                        out=hsT[:, half * 4:half * 4 + 4, t * 128:(t + 1) * 128],
                        in_=pb[:].rearrange("p (q n) -> p q n", q=4)), reads=[pbn], writes=wr)
                else:
                    T.add("act", lambda e, pb=pb, half=half, t=t: e.activation(
                        out=hsT[:, half * 4:half * 4 + 4, t * 128:(t + 1) * 128],
                        in_=pb[:].rearrange("p (q n) -> p q n", q=4), func=AF.Copy), reads=[pbn], writes=wr)

        def rstd_for_tg(g, sqb, out_ap, out_res, psn, psn_name):
            for c in range(KC):
                sb_ = sqb[c % 2]
                if c % 3 == 1:
                    T.add("act", lambda e, c=c, sb_=sb_: e.activation(out=sb_, in_=hsT[:, c, g * TG:(g + 1) * TG], func=AF.Square),
                          reads=["hs.%d.%d" % (c, g)], writes=["sq%d" % (c % 2)])
                else:
                    T.add("pool" if c % 3 == 0 else "dve", lambda e, c=c, sb_=sb_: e.tensor_tensor(
                        out=sb_, in0=hsT[:, c, g * TG:(g + 1) * TG], in1=hsT[:, c, g * TG:(g + 1) * TG], op=ALU.mult),
                        reads=["hs.%d.%d" % (c, g)], writes=["sq%d" % (c % 2)])
                T.add("pe", lambda e, c=c, sb_=sb_: e.matmul(psn[:], lhsT=NOm, rhs=sb_, start=(c == 0), stop=(c == KC - 1)),
                      reads=["sq%d" % (c % 2), "cb"] + ([psn_name] if c else []), writes=[psn_name])
            T.add("act", lambda e: e.activation(out=out_ap, in_=psn[:], func=AF.Ln, bias=EPS, scale=-1.0 / D), reads=[psn_name], writes=[out_res])
            T.add("act", lambda e: e.activation(out=out_ap, in_=out_ap, func=AF.Exp, scale=-0.5), reads=[out_res], writes=[out_res])

        def make_xn(g, gcol0, l, rstd_ap, rstd_res, xn, xp="xnT", npool=0):
            for c in range(KC):
                T.add("pool" if c >= KC - npool else "dve", lambda e, c=c: e.scalar_tensor_tensor(
                    out=xn[:, c, :], in0=hsT[:, c, g * TG:(g + 1) * TG], scalar=vcol(l, gcol0 + c), in1=rstd_ap,
                    op0=ALU.mult, op1=ALU.mult), reads=["hs.%d.%d" % (c, g), rstd_res, "vecs"], writes=["%s.%d" % (xp, c)])
        XR = ["xnT.%d" % c for c in range(KC)]

        def attention_layer(l):
            fox = (l % 2 == 0)
            j_ = l // 2
            fence()
            ar.reset()
            rstd_bc = ar.f32(S)
            xnTb = [ar.bf16(KC * TG).rearrange("p (c n) -> p c n", c=KC) for _ in range(2)]
            xnT = xnTb[0]
            mark = ar.off
            sqb = [ar.bf16(TG) for _ in range(2)]
            psS = [ps[0], ps[1], ps[4], ps[5]]
            psSn = ["psb0", "psb1", "psb4", "psb5"]
            psO2, psD, psQ, psK, psV, psN = [ps[2], ps[3]], ps[6], ps[4], ps[5], ps[6], ps[7]

            for g in range(NTG):
                rstd_for_tg(g, sqb, rstd_bc[:, g * TG:(g + 1) * TG], "rstd.%d" % g, psN, "psN")

            CAUG = ["caug_s.%d.%d" % (p_, g_) for p_ in range(3) for g_ in range(NTG)]
            if fox:
                wf = ar.bf16(KC * 96).rearrange("p (c n) -> p c n", c=KC)
                ft = [ar.f32(TG) for _ in range(4)]
                fb = [ar.bf16(TG) for _ in range(3)]
                onesr = ar.f32(TG)
                T.add("pool", lambda e: e.dma_start(out=wf, in_=wf_d[j_].rearrange("p (c n) -> p c n", c=KC)), writes=["wf"], dma="wf")
                T.add("pool", lambda e: e.memset(onesr[:, :], 1.0), writes=["onesr"])

                def fphase(g):
                    make_xn(g, 0, l, rstd_bc[:, g * TG:(g + 1) * TG], "rstd.%d" % g, xnT)

                    def fmm(e):
                        r = None
                        for c in range(KC):
                            r = e.matmul(psN[0:96, :], lhsT=wf[:, c, :], rhs=xnT[:, c, :], start=(c == 0), stop=(c == KC - 1))
                        return r
                    T.add("pe", fmm, reads=["wf"] + XR, writes=["psN"])
                    cprev = ft[2 + (g + 1) % 2]
                    ccur = ft[2 + g % 2]
                    cn = "fc%d" % (g % 2)
                    cpn = "fc%d" % ((g + 1) % 2)
                    T.add("act", lambda e: e.activation(out=ft[0][0:80, :], in_=psN[0:80, :], func=AF.Identity, bias=vcol(l, 192)[0:80, :]),
                          reads=["psN", "vecs"], writes=["ft0"])
                    T.add("act", lambda e: e.activation(out=ft[0][0:80, :], in_=ft[0][0:80, :], func=AF.Exp, scale=-1.0), reads=["ft0"], writes=["ft0"])
                    T.add("act", lambda e: e.activation(out=ft[1][0:80, :], in_=ft[0][0:80, :], func=AF.Ln, bias=1.0), reads=["ft0"], writes=["ft1"])
                    if g == 0:
                        T.add("dve", lambda e: e.tensor_tensor_scan(out=ccur[0:80, :], data0=onesr[0:80, :], data1=ft[1][0:80, :],
                              initial=0.0, op0=ALU.mult, op1=ALU.subtract), reads=["onesr", "ft1"], writes=[cn])
                    else:
                        T.add("dve", lambda e: e.tensor_tensor_scan(out=ccur[0:80, :], data0=onesr[0:80, :], data1=ft[1][0:80, :],
                              initial=cprev[0:80, TG - 1:TG], op0=ALU.mult, op1=ALU.subtract),
                              reads=["onesr", "ft1", cpn], writes=[cn])
                    T.add("dve", lambda e: e.tensor_copy(out=fb[0][0:80, :], in_=ccur[0:80, :]), reads=[cn], writes=["fb0"])
                    T.add("dve", lambda e: e.tensor_tensor(out=ft[0][0:80, :], in0=ccur[0:80, :], in1=fb[0][0:80, :], op=ALU.subtract),
                          reads=[cn, "fb0"], writes=["ft0"])
                    T.add("dve", lambda e: e.tensor_copy(out=fb[1][0:80, :], in_=ft[0][0:80, :]), reads=["ft0"], writes=["fb1"])
                    T.add("dve", lambda e: e.tensor_tensor(out=ft[1][0:80, :], in0=ft[0][0:80, :], in1=fb[1][0:80, :], op=ALU.subtract),
                          reads=["ft0", "fb1"], writes=["ft1"])
                    T.add("dve", lambda e: e.tensor_copy(out=fb[2][0:80, :], in_=ft[1][0:80, :]), reads=["ft1"], writes=["fb2"])
                    for part in range(3):
                        T.add("sp", lambda e, part=part: e.dma_start(out=caug_s[part, :, g * TG:(g + 1) * TG],
                              in_=fb[part][32 * part:32 * part + 16, :]), reads=["fb%d" % part], writes=["caug_s.%d.%d" % (part, g)], dma="caug_w")
                for g in range(NTG):
                    fphase(g)
            fence()
            ar.off = mark
            wh = [ar.bf16(KC * 192).rearrange("p (c n) -> p c n", c=KC) for _ in range(2)]
            Qp = ar.bf16(S)
            Kp = ar.bf16(S)
            VW = 128 if fox else DH + 2
            Vt = ar.bf16(NT * VW).rearrange("p (t d) -> p t d", t=NT)
            OTs = [ar.bf16(TG) for _ in range(1)]
            if fox:
                PT = [ar.bf16(2 * TG) for _ in range(2)]
                rden = ar.f32(TG)
                rbc = ar.f32(TG)
                T.add("pool", lambda e: e.memset(Vt[:, :, DH:VW], 0.0), writes=["V.ones"])
                T.add("pool", lambda e: e.memset(Vt[:, :, DH:DH + 1], 1.0), writes=["V.ones"])
            else:
                Eb = [ar.f32(TG) for _ in range(1)]
                SPb = [ar.bf16(TG) for _ in range(3)]
                aTb = [ar.bf16(TG) for _ in range(3)]
                Rbb = [ar.bf16(TG) for _ in range(4)]
                R32 = ar.f32(TG)
            if fox:
                T.add("pool", lambda e: e.memset(Qp[64:128, :], 0.0), writes=["Qp.aug"])
                T.add("pool", lambda e: e.memset(Kp[64:128, :], 0.0), writes=["Kp.aug"])
                T.add("pool", lambda e: e.memset(Qp[64:70, :], -1.0), writes=["Qp.aug"])
                T.add("pool", lambda e: e.memset(Kp[64:70, :], 1.0), writes=["Kp.aug"])
            else:
                T.add("pool", lambda e: e.memset(Qp[64:128, :], 0.0), writes=["Qp.aug"])
                T.add("pool", lambda e: e.memset(Kp[64:128, :], 0.0), writes=["Kp.aug"])
            KR = 128

            pending = []

            def flush():
                for fn_ in pending:
                    fn_()
                del pending[:]

            def project(h, g, whb, whn, skip_make=False):
                xnT = xnTb[g % 2]
                xp = "xnT" if g % 2 == 0 else "xnU"
                XR = ["%s.%d" % (xp, c) for c in range(KC)]
                if not skip_make:
                    make_xn(g, 0, l, rstd_bc[:, g * TG:(g + 1) * TG], "rstd.%d" % g, xnT, xp, npool=0)

                def qmm(e):
                    r = None
                    for c in range(KC):
                        r = e.matmul(psQ[0:64, :], lhsT=whb[:, c, 0:64], rhs=xnT[:, c, :], start=(c == 0), stop=(c == KC - 1))
                    return r

                def kmm(e):
                    r = None
                    for c in range(KC):
                        r = e.matmul(psK[0:64, :], lhsT=whb[:, c, 64:128], rhs=xnT[:, c, :], start=(c == 0), stop=(c == KC - 1))
                    return r

                def vmm(e):
                    r = None
                    for tt in range(4):
                        for c in range(KC):
                            r = e.matmul(psV[:, tt * 64:(tt + 1) * 64], lhsT=xnT[:, c, tt * 128:(tt + 1) * 128], rhs=whb[:, c, 128:192],
                                         start=(c == 0), stop=(c == KC - 1))
                    return r
                T.add("pe", qmm, reads=XR + [whn], writes=["psb4"])
                T.add("pe", kmm, reads=XR + [whn], writes=["psb5"])
                T.add("pe", vmm, reads=XR + [whn], writes=["psb6"])
                T.add("act", lambda e: e.activation(out=Qp[0:64, g * TG:(g + 1) * TG], in_=psQ[0:64, :], func=AF.Copy, scale=0.125),
                      reads=["psb4"], writes=["Qp.%d" % g])
                T.add("act", lambda e: e.activation(out=Kp[0:64, g * TG:(g + 1) * TG], in_=psK[0:64, :], func=AF.Copy), reads=["psb5"], writes=["Kp.%d" % g])
                T.add("act", lambda e: e.activation(out=Vt[:, g * 4:(g + 1) * 4, 0:DH], in_=psV[:, 0:256].rearrange("p (t d) -> p t d", t=4), func=AF.Copy),
                      reads=["psb6"], writes=["V.%d" % g])

            def attend_head(h, hoist):
                seq = []
                for g in range(NTG):
                    if fox:
                        blocks = [(4 * g + m, m) for m in range(4)] + [(j, None) for j in range(4 * g - 1, -1, -1)]
                    else:
                        blocks = [(4 * g + m, m) for m in range(3, -1, -1)] + [(j, None) for j in range(4 * g - 1, -1, -1)]
                    for i, (j, m) in enumerate(blocks):
                        seq.append((g, i, len(blocks), j, m))
                N = len(seq)
                mk = maskF if fox else maskS
                ob = OTs[0]
                obn = "OTs0"

                def score_op(n):
                    g, i, nb, j, m = seq[n]
                    q0 = g * TG
                    pS = psS[n % 4]
                    c0 = 0 if m is None else m * 128

                    def f(e):
                        kk = Kp[0:KR, j * 128:(j + 1) * 128]
                        if m is None:
                            return e.matmul(pS[:, :], lhsT=kk, rhs=Qp[0:KR, q0:q0 + TG], start=True, stop=True)
                        e.matmul(pS[:, c0:c0 + 128], lhsT=identb, rhs=mk, start=True, stop=False)
                        r = e.matmul(pS[:, c0:c0 + 128], lhsT=kk, rhs=Qp[0:KR, q0 + c0:q0 + c0 + 128], start=False, stop=True)
                        if c0 + 128 < TG:
                            r = e.matmul(pS[:, c0 + 128:TG], lhsT=kk, rhs=Qp[0:KR, q0 + c0 + 128:q0 + TG], start=False, stop=True)
                        return r
                    T.add("pe", f, reads=["Qp.%d" % g, "Qp.aug", "Kp.%d" % (j // 4), "Kp.aug", "cb"], writes=[psSn[n % 4]])

                if fox:
                    units = []
                    n_ = 0
                    while n_ < N:
                        g, i, nb, j, m = seq[n_]
                        if m is None and n_ + 1 < N and seq[n_ + 1][0] == g and seq[n_ + 1][4] is None:
                            units.append([seq[n_], seq[n_ + 1]])
                            n_ += 2
                        else:
                            units.append([seq[n_]])
                            n_ += 1
                    U_ = len(units)
                    psP = [psA, psB]
                    psPn = [["psb0", "psb1"], ["psb4", "psb5"]]
                    PT2 = [PT[0], PT[1]]

                    def S1(u):
                        pP = psP[u % 2]
                        for k, (g, i, nb, j, m) in enumerate(units[u]):
                            q0 = g * TG
                            c0 = 0 if m is None else m * 128
                            o = k * TG

                            def f(e, j=j, m=m, q0=q0, c0=c0, o=o):
                                kk = Kp[0:KR, j * 128:(j + 1) * 128]
                                if m is None:
                                    return e.matmul(pP[:, o:o + TG], lhsT=kk, rhs=Qp[0:KR, q0:q0 + TG], start=True, stop=True)
                                e.matmul(pP[:, c0:c0 + 128], lhsT=identb, rhs=mk, start=True, stop=False)
                                r = e.matmul(pP[:, c0:c0 + 128], lhsT=kk, rhs=Qp[0:KR, q0 + c0:q0 + c0 + 128], start=False, stop=True)
                                if c0 + 128 < TG:
                                    r = e.matmul(pP[:, c0 + 128:TG], lhsT=kk, rhs=Qp[0:KR, q0 + c0 + 128:q0 + TG], start=False, stop=True)
                                return r
                            T.add("pe", f, reads=["Qp.%d" % g, "Qp.aug", "Kp.%d" % (j // 4), "Kp.aug", "cb"], writes=[psPn[u % 2][k]])

                    def S2(u):
                        pP = psP[u % 2]
                        pt = PT2[u % 2]
                        ptn = "PTw%d" % (u % 2)
                        blk = units[u]
                        g = blk[0][0]
                        psO = psO2[g % 2]
                        pOn = "psO%d" % (g % 2)
                        if blk[0][1] == 3:
                            flush()
                        if len(blk) == 2:
                            T.add("act", lambda e: e.activation(out=pt[:, 0:2 * TG], in_=pP[:, 0:2 * TG], func=AF.Exp),
                                  reads=psPn[u % 2], writes=[ptn])
                        else:
                            m = blk[0][4]
                            c0 = 0 if m is None else m * 128
                            T.add("act", lambda e: e.activation(out=pt[:, c0:TG], in_=pP[:, c0:TG], func=AF.Exp),
                                  reads=[psPn[u % 2][0]], writes=[ptn])
                        for k, (g_, i, nb, j, m) in enumerate(blk):
                            c0 = 0 if m is None else m * 128
                            o = k * TG
                            T.add("pe", lambda e, j=j, c0=c0, o=o, i=i, nb=nb: e.matmul(psO[:, c0:TG], lhsT=Vt[:, j, :], rhs=pt[:, o + c0:o + TG],
                                  start=(i == 0), stop=(i == nb - 1)),
                                  reads=[ptn, "V.%d" % (j // 4), "V.ones"] + ([pOn] if i else []), writes=[pOn])
                        g_, i, nb, j, m = blk[-1]
                        if i == nb - 1:
                            T.add("dve", lambda e: e.reciprocal(out=rden[64:65, :], in_=psO[64:65, :]), reads=[pOn], writes=["rden"])

                            def part_b():
                                T.add("pe", lambda e: e.matmul(psD[0:64, :], lhsT=one1[64:65, 0:64], rhs=rden[64:65, :], start=True, stop=True),
                                      reads=["rden", "one1"], writes=["psb6"])
                                T.add("dve", lambda e: e.tensor_copy(out=rbc[0:64, :], in_=psD[0:64, :]), reads=["psb6"], writes=["rbc"])
                                T.add("dve", lambda e: e.tensor_tensor(out=ob[0:64, :], in0=psO[0:64, :], in1=rbc[0:64, :], op=ALU.mult),
                                      reads=[pOn, "rbc"], writes=[obn])
                                T.add("sp", lambda e: e.dma_start(out=ot_s[h * 64:(h + 1) * 64, g * TG:(g + 1) * TG], in_=ob[0:64, :]),
                                      reads=[obn], writes=["ot_s.%d.%d" % (h, g)], dma="otw")
                            pending.append(part_b)

                    S1(0)
                    for u in range(U_):
                        if units[u][0][0] == NTG - 1 and units[u][0][1] == 0 and hoist is not None:
                            hoist()
                        if u + 1 < U_:
                            S1(u + 1)
                        S2(u)
                    return
                else:
                    def s1(n):
                        g, i, nb, j, m = seq[n]
                        c0 = 0 if m is None else m * 128
                        if i == 0:
                            T.add("pool", lambda e: e.memset(R32[:, :], 0.0), writes=["R32"])
                        score_op(n)
                        pS = psS[n % 4]
                        Ei = Eb[0]
                        SPi = SPb[n % 3]
                        T.add("act", lambda e: e.activation(out=Ei[:, c0:TG], in_=pS[:, c0:TG], func=AF.Exp), reads=[psSn[n % 4]], writes=["E0"])
                        T.add("act", lambda e: e.activation(out=SPi[:, c0:TG], in_=Ei[:, c0:TG], func=AF.Ln, bias=1.0), reads=["E0"], writes=["SP%d" % (n % 3)])
                        if i + 1 < nb:
                            g2, i2, nb2, j2, m2 = seq[n + 1]
                            c02 = 0 if m2 is None else m2 * 128
                            Rn = Rbb[(n + 1) % 4]
                            T.add("dve", lambda e: e.tensor_tensor(out=R32[:, c0:TG], in0=R32[:, c0:TG], in1=SPi[:, c0:TG], op=ALU.add),
                                  reads=["R32", "SP%d" % (n % 3)], writes=["R32"])
                            T.add("dve", lambda e: e.tensor_copy(out=Rn[:, c02:TG], in_=R32[:, c02:TG]), reads=["R32"], writes=["Rb%d" % ((n + 1) % 4)])

                    def s2(n):
                        g, i, nb, j, m = seq[n]
                        c0 = 0 if m is None else m * 128
                        pS = psS[n % 4]
                        SPi = SPb[n % 3]
                        ai = aTb[n % 3]
                        Rb = Rbb[n % 4]
                        c1 = c0 + 128 if m is not None else c0

                        def st2(e):
                            r = e.matmul(pS[:, c0:TG], lhsT=NTm, rhs=SPi[:, c0:TG], start=False, stop=(c1 >= TG))
                            if c1 < TG:
                                r = e.matmul(pS[:, c1:TG], lhsT=NOm, rhs=Rb[:, c1:TG], start=False, stop=True)
                            return r
                        T.add("pe", st2, reads=["SP%d" % (n % 3), "cb", psSn[n % 4]] + (["Rb%d" % (n % 4)] if c1 < TG else []), writes=[psSn[n % 4]])
                        T.add("act", lambda e: e.activation(out=ai[:, c0:TG], in_=pS[:, c0:TG], func=AF.Exp), reads=[psSn[n % 4]], writes=["aT%d" % (n % 3)])

                    def s3(n):
                        g, i, nb, j, m = seq[n]
                        psO = psO2[g % 2]
                        pOn = "psO%d" % (g % 2)
                        c0 = 0 if m is None else m * 128
                        ai = aTb[n % 3]
                        last = (i == nb - 1)
                        T.add("pe", lambda e: e.matmul(psO[0:64, c0:TG], lhsT=Vt[:, j, 0:DH], rhs=ai[:, c0:TG], start=(i == 0), stop=last),
                              reads=["aT%d" % (n % 3), "V.%d" % (j // 4)] + ([pOn] if i else []), writes=[pOn])
                        if last:
                            T.add("dve", lambda e: e.tensor_copy(out=ob[0:64, :], in_=psO[0:64, :]), reads=[pOn], writes=[obn])
                            T.add("sp", lambda e: e.dma_start(out=ot_s[h * 64:(h + 1) * 64, g * TG:(g + 1) * TG], in_=ob[0:64, :]),
                                  reads=[obn], writes=["ot_s.%d.%d" % (h, g)], dma="otw")

                s1(0)
                s1(1)
                for n in range(N):
                    if seq[n][0] == NTG - 1 and seq[n][1] == 0 and hoist is not None:
                        hoist()
                    if n + 2 < N:
                        s1(n + 2)
                    s2(n)
                    if n >= 1:
                        s3(n - 1)
                s3(N - 1)

            for h in range(H):
                whb = wh[h % 2]
                whn = "wh%d" % (h % 2)
                T.add("pool", lambda e, h=h, whb=whb: e.dma_start(out=whb, in_=wqkv_d[l][h].rearrange("p (c n) -> p c n", c=KC)),
                      writes=[whn], dma=whn)
                if h == 1:
                    cast_ffn_weights(l)
                if fox:
                    T.add("sp", lambda e, h=h: e.dma_start(out=Qp[64:67, :], in_=caug_s[:, h, :]), reads=CAUG, writes=["Qp.aug"], dma="aug")
                    T.add("sp", lambda e, h=h: e.dma_start(out=Kp[67:70, :], in_=caug_s[:, h, :]), reads=CAUG, writes=["Kp.aug"], dma="aug")
                for g in range(NTG):
                    project(h, g, whb, whn, skip_make=(g == 0 and h > 0))
                hoist = None
                if h + 1 < H:
                    hoist = lambda: make_xn(0, 0, l, rstd_bc[:, 0:TG], "rstd.0", xnTb[0], "xnT", npool=0)
                attend_head(h, hoist)
            flush()

            fence()
            ar.reset()
            wo = ar.bf16(8 * D).rearrange("p (c n) -> p c n", c=8)
            OTin = [ar.bf16(8 * TG).rearrange("p (c n) -> p c n", c=8) for _ in range(2)]
            T.add("pool", lambda e: e.dma_start(out=wo, in_=wo_d[l].rearrange("p (c n) -> p c n", c=8)), writes=["wo"], dma="wo")

            def wo_tg(g):
                ob = OTin[g % 2]
                obn = "OTin%d" % (g % 2)
                T.add("sp", lambda e: e.dma_start(out=ob, in_=ot_s[:, g * TG:(g + 1) * TG].rearrange("(c p) n -> p c n", p=128)),
                      reads=["ot_s.%d.%d" % (h_, g) for h_ in range(H)], writes=[obn], dma=obn)
                for co in range(KC):
                    py = ps[co % 2]
                    pyn = "ps%d" % (co % 2)

                    def omm(e, co=co, py=py):
                        r = None
                        for c in range(8):
                            r = e.matmul(py[:, :], lhsT=wo[:, c, co * 128:(co + 1) * 128], rhs=ob[:, c, :], start=(c == 0), stop=(c == 7))
                        return r
                    T.add("pe", omm, reads=["wo", obn], writes=[pyn])
                    T.add("dve", lambda e, co=co, py=py: e.tensor_tensor(out=hsT[:, co, g * TG:(g + 1) * TG], in0=py[:, :],
                          in1=hsT[:, co, g * TG:(g + 1) * TG], op=ALU.add), reads=[pyn, "hs.%d.%d" % (co, g)], writes=["hs.%d.%d" % (co, g)])
            for g in range(NTG):
                wo_tg(g)

        def ffn_layer(l):
            fence()
            ar.reset()
            xnT = ar.bf16(KC * TG).rearrange("p (c n) -> p c n", c=KC)
            actT = ar.bf16(FC * TG).rearrange("p (c n) -> p c n", c=FC)
            wu = [ar.bf16(KC * 256).rearrange("p (c n) -> p c n", c=KC) for _ in range(2)]
            wd = [ar.bf16(FC * 128).rearrange("p (c n) -> p c n", c=FC) for _ in range(2)]
            hext = [[ar.f32(516) for _ in range(2)] for _ in range(2)]
            tcv = [[ar.f32(TG) for _ in range(2)] for _ in range(3)]
            sqb = [ar.bf16(TG) for _ in range(2)]
            rstd = ar.f32(TG)
            halo = ar.f32(2 * FC * 2).rearrange("p (c n) -> p c n", n=2)
            psU = [ps[0], ps[2]]
            psG = [ps[1], ps[3]]
            psY = [ps[4], ps[5]]
            psN = ps[7]
            T.add("pool", lambda e: e.memset(halo[:, :, :], 0.0), writes=["halo"])

            def prep(g):
                rstd_for_tg(g, sqb, rstd[:, :], "rstdf", psN, "psN")
                make_xn(g, 8, l, rstd[:, :], "rstdf", xnT)

            def gate(f):
                par = f % 3
                T.add("act", lambda e: e.activation(out=tcv[par][1], in_=tcv[par][1], func=AF.Silu), reads=["tcv%d1" % par], writes=["tcv%d1" % par])
                T.add("pool", lambda e: e.tensor_tensor(out=actT[:, f, :], in0=tcv[par][1], in1=tcv[par][0], op=ALU.mult),
                      reads=["tcv%d1" % par, "tcv%d0" % par], writes=["actT.%d" % f])

            def up(g):
                for f in range(FC):
                    par = f % 2
                    wub = wu[f % 2]
                    wun = "wu%d" % (f % 2)
                    T.add("sp", lambda e, f=f, wub=wub: e.dma_start(out=wub, in_=wup_s[l][f].rearrange("p (c n) -> p c n", c=KC)),
                          reads=["wup_s%d.%d" % (l, f)], writes=[wun], dma=wun)
                    for ug in range(2):
                        pp = (psU if ug == 0 else psG)[par]
                        ppn = "psUG%d%d" % (ug, par)

                        def upmm(e, ug=ug, pp=pp, wub=wub):
                            r = None
                            for c in range(KC):
                                r = e.matmul(pp[:, :], lhsT=wub[:, c, ug * 128:(ug + 1) * 128], rhs=xnT[:, c, :], start=(c == 0), stop=(c == KC - 1))
                            return r
                        T.add("pe", upmm, reads=XR + [wun], writes=[ppn])
                        hx = hext[par][ug]
                        hxn = "hext%d%d" % (par, ug)
                        ch = ug * FC + f
                        tc_ = tcv[f % 3][ug]
                        tcn = "tcv%d%d" % (f % 3, ug)
                        wc = 16 + ch * 3
                        T.add("pool", lambda e, hx=hx, ch=ch: e.tensor_copy(out=hx[:, 0:2], in_=halo[:, ch, :]), reads=["halo.%d" % ch, "halo"], writes=[hxn + "h"])
                        T.add("act", lambda e, hx=hx, pp=pp: e.activation(out=hx[:, 2:514], in_=pp[:, :], func=AF.Copy), reads=[ppn], writes=[hxn])
                        T.add("act", lambda e, tc_=tc_, pp=pp, wc=wc, ch=ch: e.activation(out=tc_, in_=pp[:, :], func=AF.Identity,
                              scale=vcol(l, wc + 2), bias=vcol(l, 148 + ch)), reads=[ppn, "vecs"], writes=[tcn])
                        T.add("pool", lambda e, hx=hx, ch=ch: e.tensor_copy(out=halo[:, ch, :], in_=hx[:, 512:514]), reads=[hxn], writes=["halo.%d" % ch])
                        T.add("dve", lambda e, hx=hx, tc_=tc_, wc=wc: e.scalar_tensor_tensor(out=tc_, in0=hx[:, 1:513], scalar=vcol(l, wc + 1), in1=tc_,
                              op0=ALU.mult, op1=ALU.add), reads=[hxn, hxn + "h", tcn, "vecs"], writes=[tcn])
                        T.add("dve", lambda e, hx=hx, tc_=tc_, wc=wc: e.scalar_tensor_tensor(out=tc_, in0=hx[:, 0:512], scalar=vcol(l, wc), in1=tc_,
                              op0=ALU.mult, op1=ALU.add), reads=[hxn, hxn + "h", tcn, "vecs"], writes=[tcn])
                    if f >= 1:
                        gate(f - 1)
                gate(FC - 1)

            def down(g, nxt=None):
                ar_ = ["actT.%d" % f for f in range(FC)]
                for co in range(KC):
                    if nxt is not None and 1 <= co <= 4:
                        nxt[0](2 * (co - 1))
                        nxt[0](2 * (co - 1) + 1)
                    if nxt is not None and co == 5:
                        nxt[1]()
                    wdb = wd[co % 2]
                    wdn = "wd%d" % (co % 2)
                    T.add("sp", lambda e, co=co, wdb=wdb: e.dma_start(out=wdb, in_=wdn_s[l][co].rearrange("p (c n) -> p c n", c=FC)),
                          reads=["wdn_s%d.%d" % (l, co)], writes=[wdn], dma=wdn)
                    py = psY[co % 2]
                    pyn = "psY%d" % (co % 2)

                    def dmm(e, wdb=wdb, py=py):
                        r = None
                        for f in range(FC):
                            r = e.matmul(py[:, :], lhsT=wdb[:, f, :], rhs=actT[:, f, :], start=(f == 0), stop=(f == FC - 1))
                        return r
                    T.add("pe", dmm, reads=ar_ + [wdn], writes=[pyn])
                    T.add("dve", lambda e, co=co, py=py: e.tensor_tensor(out=hsT[:, co, g * TG:(g + 1) * TG], in0=py[:, :],
                          in1=hsT[:, co, g * TG:(g + 1) * TG], op=ALU.add), reads=[pyn, "hs.%d.%d" % (co, g)], writes=["hs.%d.%d" % (co, g)])
            def prep_pieces(g):
                def piece(c):
                    sb_ = sqb[c % 2]
                    T.add("pool", lambda e: e.tensor_tensor(out=sb_, in0=hsT[:, c, g * TG:(g + 1) * TG], in1=hsT[:, c, g * TG:(g + 1) * TG], op=ALU.mult),
                          reads=["hs.%d.%d" % (c, g)], writes=["sq%d" % (c % 2)])
                    T.add("pe", lambda e: e.matmul(psN[:], lhsT=NOm, rhs=sb_, start=(c == 0), stop=(c == KC - 1)),
                          reads=["sq%d" % (c % 2), "cb"] + (["psN"] if c else []), writes=["psN"])

                def finish():
                    T.add("act", lambda e: e.activation(out=rstd[:, :], in_=psN[:], func=AF.Ln, bias=EPS, scale=-1.0 / D), reads=["psN"], writes=["rstdf"])
                    T.add("act", lambda e: e.activation(out=rstd[:, :], in_=rstd[:, :], func=AF.Exp, scale=-0.5), reads=["rstdf"], writes=["rstdf"])
                    make_xn(g, 8, l, rstd[:, :], "rstdf", xnT)
                return piece, finish

            prep(0)
            for g in range(NTG):
                up(g)
                down(g, prep_pieces(g + 1) if g + 1 < NTG else None)

        def final_phase(normed):
            fence()
            ar.reset()
            yT = ar.f32(KC * TG).rearrange("p (c n) -> p c n", c=KC)
            ot = [ar.f32(D) for _ in range(2)]
            sqb = [ar.bf16(TG) for _ in range(2)]
            rstd = ar.f32(TG)
            psN = ps[7]
            outs = []

            def fin_tg(g):
                if normed:
                    rstd_for_tg(g, sqb, rstd[:, :], "rstdf", psN, "psN")
                    for c in range(KC):
                        T.add("dve", lambda e, c=c: e.scalar_tensor_tensor(out=yT[:, c, :], in0=hsT[:, c, g * TG:(g + 1) * TG], scalar=vecs[:, FIN + c:FIN + c + 1],
                              in1=rstd[:, :], op0=ALU.mult, op1=ALU.mult), reads=["hs.%d.%d" % (c, g), "rstdf", "vecs"], writes=["yT.%d" % c])
                else:
                    for c in range(KC):
                        T.add("dve", lambda e, c=c: e.tensor_copy(out=yT[:, c, :], in_=hsT[:, c, g * TG:(g + 1) * TG]),
                              reads=["hs.%d.%d" % (c, g)], writes=["yT.%d" % c])
                for tt in range(4):
                    t = g * 4 + tt
                    ob = ot[t % 2]
                    obn = "ot%d" % (t % 2)
                    for half in range(2):
                        pb = ps[(2 * t + half) % 4]
                        pbn = "psf%d" % ((2 * t + half) % 4)

                        def tr(e, pb=pb, half=half, tt=tt):
                            r = None
                            for q in range(4):
                                c = half * 4 + q
                                r = e.transpose(pb[:, q * 128:(q + 1) * 128], yT[:, c, tt * 128:(tt + 1) * 128], id32[:])
                            return r
                        T.add("pe", tr, reads=["yT.%d" % (half * 4 + q) for q in range(4)] + ["id32"], writes=[pbn])
                        if half == 0:
                            T.add("dve", lambda e, pb=pb, ob=ob: e.tensor_copy(out=ob[:, 0:512], in_=pb[:, :]), reads=[pbn], writes=[obn + "a"])
                        else:
                            T.add("act", lambda e, pb=pb, ob=ob: e.activation(out=ob[:, 512:1024], in_=pb[:, :], func=AF.Copy), reads=[pbn], writes=[obn + "b"])
                    T.add("sp", lambda e, ob=ob, t=t: e.dma_start(out=out_d[t * 128:(t + 1) * 128, :], in_=ob), reads=[obn + "a", obn + "b"],
                          writes=["out.%d" % t], dma="out%d" % (t % 2))
                    outs.append("out.%d" % t)
            for g in range(NTG):
                fin_tg(g)
            T.add("sp", lambda e: e.wait_ge(sems["sp"], 0), reads=outs, writes=[])

        sems = {e: st.enter_context(nc.semaphore("s_" + e)) for e in COMPUTE + ("sp",)}
        for (l, what) in layers:
            if what == "attn":
                attention_layer(l)
            else:
                ffn_layer(l)
        final_phase(final)

        dsems = {k: st.enter_context(nc.semaphore("d_" + k)) for k in T.dma_keys()}
        block = st.enter_context(nc.Block())
        T.emit(nc, block, sems, dsems)
    return nc


_CONST_CACHE = {}


def _consts():
    if "cb" not in _CONST_CACHE:
        p = np.arange(128)[:, None]
        q = np.arange(128)[None, :]
        cbm = np.zeros((128, 768), np.float32)
        cbm[:, 0:128] = np.eye(128, dtype=np.float32)
        cbm[:, 128:256] = np.where(p > q, NEG, 0.0)
        cbm[:, 256:384] = np.where(p >= q, NEG, 0.0)
        cbm[:, 384:512] = np.where(p >= q, -1.0, 0.0)
        cbm[:, 512:640] = -1.0
        cbm[:, 640:704] = 1.0
        _CONST_CACHE["cb"] = cbm
        _CONST_CACHE["ident"] = np.eye(128, dtype=np.float32)
    return _CONST_CACHE["cb"], _CONST_CACHE["ident"]


def layout_inputs(attn_norm, ffn_norm, final_norm, fox_w_qkvf, fox_b_f, fox_w_o,
                  sb_w_qkv, sb_w_o, ffn_w_up, ffn_w_conv, ffn_b_conv, ffn_w_down):
    f32 = np.float32
    m = {}
    cbm, ident = _consts()
    m["cb"] = cbm
    m["ident"] = ident
    NV = DEPTH * VS + 8
    vec = np.zeros((128, NV), f32)
    for l in range(DEPTH):
        o = l * VS
        vec[:, o:o + 8] = np.asarray(attn_norm[l], f32).reshape(8, 128).T
        vec[:, o + 8:o + 16] = np.asarray(ffn_norm[l], f32).reshape(8, 128).T
        wc = np.asarray(ffn_w_conv[l], f32).reshape(3, 44, 128).transpose(2, 1, 0).reshape(128, 132)
        vec[:, o + 16:o + 148] = wc
        vec[:, o + 148:o + 192] = np.asarray(ffn_b_conv[l], f32).reshape(44, 128).T
        if l % 2 == 0:
            bf = np.asarray(fox_b_f[l // 2], f32)
            for r0 in (0, 32, 64):
                vec[r0:r0 + 16, o + 192] = bf
    vec[:, DEPTH * VS:DEPTH * VS + 8] = np.asarray(final_norm, f32).reshape(8, 128).T
    m["vecs"] = vec
    for l in range(DEPTH):
        j = l // 2
        if l % 2 == 0:
            w = np.asarray(fox_w_qkvf[j], f32)
            wo = np.asarray(fox_w_o[j], f32)
            wfm = np.zeros((128, KC, 96), f32)
            wfr = w[:, 3 * D:3 * D + H].reshape(KC, 128, H).transpose(1, 0, 2)
            for r0 in (0, 32, 64):
                wfm[:, :, r0:r0 + 16] = wfr
            m["wf%d" % j] = wfm.reshape(128, KC * 96)
        else:
            w = np.asarray(sb_w_qkv[j], f32)
            wo = np.asarray(sb_w_o[j], f32)
        parts = [w[:, i * D:(i + 1) * D].reshape(KC, 128, H, DH).transpose(2, 1, 0, 3) for i in range(3)]
        m["wqkv%d" % l] = np.ascontiguousarray(np.concatenate(parts, axis=3)).reshape(H, 128, KC * 192)
        m["wo%d" % l] = np.ascontiguousarray(wo.reshape(8, 128, D).transpose(1, 0, 2)).reshape(128, 8 * D)
        wu = np.asarray(ffn_w_up[l], f32)
        pu = wu[:, 0:FF].reshape(KC, 128, FC, 128).transpose(2, 1, 0, 3)
        pg = wu[:, FF:2 * FF].reshape(KC, 128, FC, 128).transpose(2, 1, 0, 3)
        m["wup%d" % l] = np.ascontiguousarray(np.concatenate([pu, pg], axis=3)).reshape(FC, 128, KC * 256)
        wdn = np.asarray(ffn_w_down[l], f32)
        m["wdn%d" % l] = np.ascontiguousarray(wdn.reshape(FC, 128, KC, 128).transpose(2, 1, 0, 3)).reshape(KC, 128, FC * 128)
    return m


ALL_LAYERS = [(l, w) for l in range(DEPTH) for w in ("attn", "ffn")]


def kernel(x, attn_norm, ffn_norm, final_norm, fox_w_qkvf, fox_b_f, fox_w_o,
           sb_w_qkv, sb_w_o, ffn_w_up, ffn_w_conv, ffn_b_conv, ffn_w_down):
    x = np.asarray(x, np.float32)
    B, S, _ = x.shape
    shared = layout_inputs(attn_norm, ffn_norm, final_norm, fox_w_qkvf, fox_b_f, fox_w_o,
                           sb_w_qkv, sb_w_o, ffn_w_up, ffn_w_conv, ffn_b_conv, ffn_w_down)
    nc = build_program(S, ALL_LAYERS, True)
    in_maps = []
    for b in range(B):
        mm = dict(shared)
        mm["x"] = np.ascontiguousarray(x[b])
        in_maps.append(mm)
    res = run_bass_kernel_spmd(nc, in_maps, core_ids=list(range(B)))
    return np.stack([np.asarray(r["out"], np.float32) for r in res.results], axis=0)
```

```python
import contextlib
import numpy as np
import concourse.bass as bass
import concourse.mybir as mybir
from concourse.bass_utils import run_bass_kernel_spmd

F32 = mybir.dt.float32
BF16 = mybir.dt.bfloat16
AF = mybir.ActivationFunctionType
ALU = mybir.AluOpType

D = 1024
KC = 8
TG = 512
H = 16
DH = 64
FF = 2816
FC = 22
DEPTH = 4
EPS = 1e-6
NEG = -30000.0
VS = 193

COMPUTE = ("pe", "act", "dve", "pool")


class _Op:
    __slots__ = ("eng", "fn", "deps", "dma_key", "dma_val", "signal", "count", "idx", "dma_waits")


class Tracker:
    def __init__(self):
        self.ops = []
        self.last_w = {}
        self.readers = {}
        self.dma_cum = {}
        self.fence_idx = None
        self.last_eng = {}
        self.last_dma = {}

    def fence(self, fn):
        deps = set(self.last_eng.values()) | set(self.last_dma.values())
        idx = self.add("pool", fn, extra_deps=deps)
        self.fence_idx = idx
        return idx

    def add(self, eng, fn, reads=(), writes=(), dma=None, extra_deps=()):
        op = _Op()
        op.eng = eng
        op.fn = fn
        op.dma_key = dma
        op.signal = False
        op.count = 0
        op.idx = len(self.ops)
        deps = set()
        for r in reads:
            w = self.last_w.get(r)
            if w is not None:
                deps.add(w)
        for r in writes:
            w = self.last_w.get(r)
            if w is not None:
                deps.add(w)
            for rd in self.readers.get(r, ()):
                deps.add(rd)
        deps |= set(extra_deps)
        if self.fence_idx is not None:
            deps.add(self.fence_idx)
        deps.discard(op.idx)
        op.deps = deps
        op.dma_waits = {}
        for d in deps:
            k = self.ops[d].dma_key
            if k is not None:
                op.dma_waits[k] = self.dma_cum[k]
        if dma is not None:
            self.last_dma[dma] = op.idx
        else:
            self.last_eng[eng] = op.idx
        if dma is not None:
            self.dma_cum[dma] = self.dma_cum.get(dma, 0) + 16
            op.dma_val = self.dma_cum[dma]
        else:
            op.dma_val = 0
        self.ops.append(op)
        for r in writes:
            self.last_w[r] = op.idx
            self.readers[r] = []
        for r in reads:
            if r not in writes:
                self.readers.setdefault(r, []).append(op.idx)
        return op.idx

    def dma_keys(self):
        return list(self.dma_cum.keys())

    def emit(self, nc, block, sems, dma_sems):
        ops = self.ops
        for op in ops:
            for d in op.deps:
                dop = ops[d]
                if dop.dma_key is not None:
                    continue
                if dop.eng == op.eng and op.eng in ("pe", "sp"):
                    continue
                dop.signal = True
        cnt = {e: 0 for e in COMPUTE + ("sp",)}
        for op in ops:
            if op.dma_key is None and op.signal:
                cnt[op.eng] += 1
                op.count = cnt[op.eng]
        per_eng = {e: [] for e in COMPUTE + ("sp",)}
        for op in ops:
            per_eng[op.eng].append(op)

        def run(eng_name, eng):
            waited = {}
            for op in per_eng[eng_name]:
                wl = {}
                for d in op.deps:
                    dop = ops[d]
                    if dop.dma_key is not None:
                        k = ("dma", dop.dma_key)
                        v = op.dma_waits[dop.dma_key]
                    else:
                        if dop.eng == eng_name and eng_name in ("pe", "sp"):
                            continue
                        k = ("eng", dop.eng)
                        v = dop.count
                    if v > wl.get(k, 0):
                        wl[k] = v
                for k, v in wl.items():
                    if waited.get(k, 0) >= v:
                        continue
                    waited[k] = v
                    s = dma_sems[k[1]] if k[0] == "dma" else sems[k[1]]
                    eng.wait_ge(s, v)
                ins = op.fn(eng)
                if op.dma_key is not None:
                    ins.then_inc(dma_sems[op.dma_key], 16)
                elif op.signal:
                    ins.then_inc(sems[op.eng], 1)

        @block.sync
        def _(e):
            run("sp", e)

        @block.scalar
        def _(e):
            run("act", e)

        @block.vector
        def _(e):
            run("dve", e)

        @block.gpsimd
        def _(e):
            run("pool", e)

        @block.tensor
        def _(e):
            run("pe", e)


class Arena:
    def __init__(self, t, nbytes):
        self.t = t
        self.nbytes = nbytes
        self.off = 0

    def reset(self):
        self.off = 0

    def f32(self, n):
        assert self.off % 4 == 0
        a = self.off // 4
        self.off += 4 * n
        assert self.off <= self.nbytes, ("arena overflow", self.off, self.nbytes)
        return self.t[:, a:a + n]

    def bf16(self, n):
        n2 = (n + 1) // 2
        v = self.f32(n2)
        return v.bitcast(BF16)[:, 0:n]


def build_program(S, layers, final=True):
    NTG = S // TG
    NT = S // 128
    NV = DEPTH * VS + 8
    nc = bass.Bass("TRN2", target_bir_lowering=False)

    def din(name, shape, dt=F32):
        return nc.dram_tensor(name, list(shape), dt, kind="ExternalInput").ap()

    x_d = din("x", [S, D])
    vec_d = din("vecs", [128, NV])
    cb_d = din("cb", [128, 6 * 128])
    id_d = din("ident", [128, 128])
    wqkv_d = [din("wqkv%d" % l, [H, 128, KC * 192]) for l in range(DEPTH)]
    wf_d = [din("wf%d" % j, [128, KC * 96]) for j in range(2)]
    wo_d = [din("wo%d" % l, [128, 8 * D]) for l in range(DEPTH)]
    wup_d = [din("wup%d" % l, [FC, 128, KC * 256]) for l in range(DEPTH)]
    wdn_d = [din("wdn%d" % l, [KC, 128, FC * 128]) for l in range(DEPTH)]
    out_d = nc.dram_tensor("out", [S, D], F32, kind="ExternalOutput").ap()
    wup_s = [nc.dram_tensor("wup_s%d" % l, [FC, 128, KC * 256], BF16, kind="Internal").ap() for l in range(DEPTH)]
    wdn_s = [nc.dram_tensor("wdn_s%d" % l, [KC, 128, FC * 128], BF16, kind="Internal").ap() for l in range(DEPTH)]
    ot_s = nc.dram_tensor("ot_s", [D, S], BF16, kind="Internal").ap()
    caug_s = nc.dram_tensor("caug_s", [3, H, S], BF16, kind="Internal").ap()

    T = Tracker()

    with contextlib.ExitStack() as st:
        hsT = st.enter_context(nc.sbuf_tensor("hsT", [128, KC, S], F32))
        vecs = st.enter_context(nc.sbuf_tensor("vecs_sb", [128, NV], F32))
        cb = st.enter_context(nc.sbuf_tensor("cb_sb", [128, 6 * 128], BF16))
        id32 = st.enter_context(nc.sbuf_tensor("id32", [128, 128], F32))
        ones32 = st.enter_context(nc.sbuf_tensor("ones32", [128, 128], F32))
        fsc = st.enter_context(nc.sbuf_tensor("fsc", [128, 8], F32))
        one1 = st.enter_context(nc.sbuf_tensor("one1", [128, 64], F32))
        AR_BYTES = 75600
        art = st.enter_context(nc.sbuf_tensor("arena", [128, AR_BYTES // 4], F32))
        ar = Arena(art, AR_BYTES)
        psA = st.enter_context(nc.psum_tensor("psA", [128, 1024], F32))
        psB = st.enter_context(nc.psum_tensor("psB", [128, 1024], F32))
        psx = {i: st.enter_context(nc.psum_tensor("ps%d" % i, [128, 512], F32)) for i in (2, 3, 6, 7)}
        ps = [psA[:, 0:512], psA[:, 512:1024], psx[2][:, :], psx[3][:, :], psB[:, 0:512], psB[:, 512:1024], psx[6][:, :], psx[7][:, :]]

        identb = cb[:, 0:128]
        maskF = cb[:, 128:256]
        maskS = cb[:, 256:384]
        NTm = cb[:, 384:512]
        NOm = cb[:, 512:640]
        ones64 = cb[:, 640:704]

        def vcol(l, k):
            o = l * VS + k
            return vecs[:, o:o + 1]
        FIN = DEPTH * VS

        def fence():
            T.fence(lambda e: e.memset(fsc[:, 0:1], 0.0))

        T.add("sp", lambda e: e.dma_start(out=vecs[:], in_=vec_d), writes=["vecs"], dma="vecs")
        T.add("sp", lambda e: e.dma_start(out=id32[:], in_=id_d), writes=["id32"], dma="id32")
        T.add("pool", lambda e: e.dma_start(out=cb[:], in_=cb_d), writes=["cb"], dma="cb")
        T.add("pool", lambda e: e.memset(ones32[:], 1.0 / D), writes=["ones32"])
        T.add("pool", lambda e: e.memset(one1[:], 1.0), writes=["one1"])

        def cast_ffn_weights(l):
            for f in range(FC):
                T.add("pool", lambda e, f=f: e.dma_start(out=wup_s[l][f], in_=wup_d[l][f]),
                      writes=["wup_s%d.%d" % (l, f)], dma="wups")
            for co in range(KC):
                T.add("pool", lambda e, co=co: e.dma_start(out=wdn_s[l][co], in_=wdn_d[l][co]),
                      writes=["wdn_s%d.%d" % (l, co)], dma="wdns")

        ar.reset()
        xin = [ar.f32(D) for _ in range(2)]
        for t in range(NT):
            xb = xin[t % 2]
            T.add("sp", lambda e, t=t, xb=xb: e.dma_start(out=xb, in_=x_d[t * 128:(t + 1) * 128, :]),
                  writes=["xin%d" % (t % 2)], dma="xin%d" % (t % 2))
            for half in range(2):
                pb = ps[(2 * t + half) % 4]
                pbn = "ps%d" % ((2 * t + half) % 4)

                def tr(e, xb=xb, pb=pb, half=half):
                    r = None
                    for q in range(4):
                        c = half * 4 + q
                        r = e.transpose(pb[:, q * 128:(q + 1) * 128], xb[:, c * 128:(c + 1) * 128], id32[:])
                    return r
                T.add("pe", tr, reads=["xin%d" % (t % 2), "id32"], writes=[pbn])
                tgi = t // 4
                wr = ["hsw.%d.%d.%d" % (half * 4 + q, tgi, t % 4) for q in range(4)]
                if half == 0:
                    T.add("dve", lambda e, pb=pb, half=half, t=t: e.tensor_copy(
                        out=hsT[:, half * 4:half * 4 + 4, t * 128:(t + 1) * 128],
                        in_=pb[:].rearrange("p (q n) -> p q n", q=4)), reads=[pbn], writes=wr)
                else:
                    T.add("act", lambda e, pb=pb, half=half, t=t: e.activation(
                        out=hsT[:, half * 4:half * 4 + 4, t * 128:(t + 1) * 128],
                        in_=pb[:].rearrange("p (q n) -> p q n", q=4), func=AF.Copy), reads=[pbn], writes=wr)

        def rstd_for_tg(g, sqb, out_ap, out_res, psn, psn_name):
            for c in range(KC):
                sb_ = sqb[c % 2]
                if c % 3 == 1:
                    T.add("act", lambda e, c=c, sb_=sb_: e.activation(out=sb_, in_=hsT[:, c, g * TG:(g + 1) * TG], func=AF.Square),
                          reads=["hs.%d.%d" % (c, g)], writes=["sq%d" % (c % 2)])
                else:
                    T.add("pool" if c % 3 == 0 else "dve", lambda e, c=c, sb_=sb_: e.tensor_tensor(
                        out=sb_, in0=hsT[:, c, g * TG:(g + 1) * TG], in1=hsT[:, c, g * TG:(g + 1) * TG], op=ALU.mult),
                        reads=["hs.%d.%d" % (c, g)], writes=["sq%d" % (c % 2)])
                T.add("pe", lambda e, c=c, sb_=sb_: e.matmul(psn[:], lhsT=NOm, rhs=sb_, start=(c == 0), stop=(c == KC - 1)),
                      reads=["sq%d" % (c % 2), "cb"] + ([psn_name] if c else []), writes=[psn_name])
            T.add("act", lambda e: e.activation(out=out_ap, in_=psn[:], func=AF.Ln, bias=EPS, scale=-1.0 / D), reads=[psn_name], writes=[out_res])
            T.add("act", lambda e: e.activation(out=out_ap, in_=out_ap, func=AF.Exp, scale=-0.5), reads=[out_res], writes=[out_res])

        def make_xn(g, gcol0, l, rstd_ap, rstd_res, xn, xp="xnT", npool=0):
            for c in range(KC):
                T.add("pool" if c >= KC - npool else "dve", lambda e, c=c: e.scalar_tensor_tensor(
                    out=xn[:, c, :], in0=hsT[:, c, g * TG:(g + 1) * TG], scalar=vcol(l, gcol0 + c), in1=rstd_ap,
                    op0=ALU.mult, op1=ALU.mult), reads=["hs.%d.%d" % (c, g), rstd_res, "vecs"], writes=["%s.%d" % (xp, c)])
        XR = ["xnT.%d" % c for c in range(KC)]

        def attention_layer(l):
            fox = (l % 2 == 0)
            j_ = l // 2
            fence()
            ar.reset()
            rstd_bc = ar.f32(S)
            xnTb = [ar.bf16(KC * TG).rearrange("p (c n) -> p c n", c=KC) for _ in range(2)]
            xnT = xnTb[0]
            mark = ar.off
            sqb = [ar.bf16(TG) for _ in range(2)]
            psS = [ps[0], ps[1], ps[4], ps[5]]
            psSn = ["psb0", "psb1", "psb4", "psb5"]
            psO2, psD, psQ, psK, psV, psN = [ps[2], ps[3]], ps[6], ps[4], ps[5], ps[6], ps[7]

            for g in range(NTG):
                rstd_for_tg(g, sqb, rstd_bc[:, g * TG:(g + 1) * TG], "rstd.%d" % g, psN, "psN")

            CAUG = ["caug_s.%d.%d" % (p_, g_) for p_ in range(3) for g_ in range(NTG)]
            if fox:
                wf = ar.bf16(KC * 96).rearrange("p (c n) -> p c n", c=KC)
                ft = [ar.f32(TG) for _ in range(4)]
                fb = [ar.bf16(TG) for _ in range(3)]
                onesr = ar.f32(TG)
                T.add("pool", lambda e: e.dma_start(out=wf, in_=wf_d[j_].rearrange("p (c n) -> p c n", c=KC)), writes=["wf"], dma="wf")
                T.add("pool", lambda e: e.memset(onesr[:, :], 1.0), writes=["onesr"])

                def fphase(g):
                    make_xn(g, 0, l, rstd_bc[:, g * TG:(g + 1) * TG], "rstd.%d" % g, xnT)

                    def fmm(e):
                        r = None
                        for c in range(KC):
                            r = e.matmul(psN[0:96, :], lhsT=wf[:, c, :], rhs=xnT[:, c, :], start=(c == 0), stop=(c == KC - 1))
                        return r
                    T.add("pe", fmm, reads=["wf"] + XR, writes=["psN"])
                    cprev = ft[2 + (g + 1) % 2]
                    ccur = ft[2 + g % 2]
                    cn = "fc%d" % (g % 2)
                    cpn = "fc%d" % ((g + 1) % 2)
                    T.add("act", lambda e: e.activation(out=ft[0][0:80, :], in_=psN[0:80, :], func=AF.Identity, bias=vcol(l, 192)[0:80, :]),
                          reads=["psN", "vecs"], writes=["ft0"])
                    T.add("act", lambda e: e.activation(out=ft[0][0:80, :], in_=ft[0][0:80, :], func=AF.Exp, scale=-1.0), reads=["ft0"], writes=["ft0"])
                    T.add("act", lambda e: e.activation(out=ft[1][0:80, :], in_=ft[0][0:80, :], func=AF.Ln, bias=1.0), reads=["ft0"], writes=["ft1"])
                    if g == 0:
                        T.add("dve", lambda e: e.tensor_tensor_scan(out=ccur[0:80, :], data0=onesr[0:80, :], data1=ft[1][0:80, :],
                              initial=0.0, op0=ALU.mult, op1=ALU.subtract), reads=["onesr", "ft1"], writes=[cn])
                    else:
                        T.add("dve", lambda e: e.tensor_tensor_scan(out=ccur[0:80, :], data0=onesr[0:80, :], data1=ft[1][0:80, :],
                              initial=cprev[0:80, TG - 1:TG], op0=ALU.mult, op1=ALU.subtract),
                              reads=["onesr", "ft1", cpn], writes=[cn])
                    T.add("dve", lambda e: e.tensor_copy(out=fb[0][0:80, :], in_=ccur[0:80, :]), reads=[cn], writes=["fb0"])
                    T.add("dve", lambda e: e.tensor_tensor(out=ft[0][0:80, :], in0=ccur[0:80, :], in1=fb[0][0:80, :], op=ALU.subtract),
                          reads=[cn, "fb0"], writes=["ft0"])
                    T.add("dve", lambda e: e.tensor_copy(out=fb[1][0:80, :], in_=ft[0][0:80, :]), reads=["ft0"], writes=["fb1"])
                    T.add("dve", lambda e: e.tensor_tensor(out=ft[1][0:80, :], in0=ft[0][0:80, :], in1=fb[1][0:80, :], op=ALU.subtract),
                          reads=["ft0", "fb1"], writes=["ft1"])
                    T.add("dve", lambda e: e.tensor_copy(out=fb[2][0:80, :], in_=ft[1][0:80, :]), reads=["ft1"], writes=["fb2"])
                    for part in range(3):
                        T.add("sp", lambda e, part=part: e.dma_start(out=caug_s[part, :, g * TG:(g + 1) * TG],
                              in_=fb[part][32 * part:32 * part + 16, :]), reads=["fb%d" % part], writes=["caug_s.%d.%d" % (part, g)], dma="caug_w")
                for g in range(NTG):
                    fphase(g)
            fence()
            ar.off = mark
            wh = [ar.bf16(KC * 192).rearrange("p (c n) -> p c n", c=KC) for _ in range(2)]
            Qp = ar.bf16(S)
            Kp = ar.bf16(S)
            VW = 128 if fox else DH + 2
            Vt = ar.bf16(NT * VW).rearrange("p (t d) -> p t d", t=NT)
            OTs = [ar.bf16(TG) for _ in range(1)]
            if fox:
                PT = [ar.bf16(2 * TG) for _ in range(2)]
                rden = ar.f32(TG)
                rbc = ar.f32(TG)
                T.add("pool", lambda e: e.memset(Vt[:, :, DH:VW], 0.0), writes=["V.ones"])
                T.add("pool", lambda e: e.memset(Vt[:, :, DH:DH + 1], 1.0), writes=["V.ones"])
            else:
                Eb = [ar.f32(TG) for _ in range(1)]
                SPb = [ar.bf16(TG) for _ in range(3)]
                aTb = [ar.bf16(TG) for _ in range(3)]
                Rbb = [ar.bf16(TG) for _ in range(4)]
                R32 = ar.f32(TG)
            if fox:
                T.add("pool", lambda e: e.memset(Qp[64:128, :], 0.0), writes=["Qp.aug"])
                T.add("pool", lambda e: e.memset(Kp[64:128, :], 0.0), writes=["Kp.aug"])
                T.add("pool", lambda e: e.memset(Qp[64:70, :], -1.0), writes=["Qp.aug"])
                T.add("pool", lambda e: e.memset(Kp[64:70, :], 1.0), writes=["Kp.aug"])
            else:
                T.add("pool", lambda e: e.memset(Qp[64:128, :], 0.0), writes=["Qp.aug"])
                T.add("pool", lambda e: e.memset(Kp[64:128, :], 0.0), writes=["Kp.aug"])
            KR = 128

            pending = []

            def flush():
                for fn_ in pending:
                    fn_()
                del pending[:]

            def project(h, g, whb, whn, skip_make=False):
                xnT = xnTb[g % 2]
                xp = "xnT" if g % 2 == 0 else "xnU"
                XR = ["%s.%d" % (xp, c) for c in range(KC)]
                if not skip_make:
                    make_xn(g, 0, l, rstd_bc[:, g * TG:(g + 1) * TG], "rstd.%d" % g, xnT, xp, npool=0)

                def qmm(e):
                    r = None
                    for c in range(KC):
                        r = e.matmul(psQ[0:64, :], lhsT=whb[:, c, 0:64], rhs=xnT[:, c, :], start=(c == 0), stop=(c == KC - 1))
                    return r

                def kmm(e):
                    r = None
                    for c in range(KC):
                        r = e.matmul(psK[0:64, :], lhsT=whb[:, c, 64:128], rhs=xnT[:, c, :], start=(c == 0), stop=(c == KC - 1))
                    return r

                def vmm(e):
                    r = None
                    for tt in range(4):
                        for c in range(KC):
                            r = e.matmul(psV[:, tt * 64:(tt + 1) * 64], lhsT=xnT[:, c, tt * 128:(tt + 1) * 128], rhs=whb[:, c, 128:192],
                                         start=(c == 0), stop=(c == KC - 1))
                    return r
                T.add("pe", qmm, reads=XR + [whn], writes=["psb4"])
                T.add("pe", kmm, reads=XR + [whn], writes=["psb5"])
                T.add("pe", vmm, reads=XR + [whn], writes=["psb6"])
                T.add("act", lambda e: e.activation(out=Qp[0:64, g * TG:(g + 1) * TG], in_=psQ[0:64, :], func=AF.Copy, scale=0.125),
                      reads=["psb4"], writes=["Qp.%d" % g])
                T.add("act", lambda e: e.activation(out=Kp[0:64, g * TG:(g + 1) * TG], in_=psK[0:64, :], func=AF.Copy), reads=["psb5"], writes=["Kp.%d" % g])
                T.add("act", lambda e: e.activation(out=Vt[:, g * 4:(g + 1) * 4, 0:DH], in_=psV[:, 0:256].rearrange("p (t d) -> p t d", t=4), func=AF.Copy),
                      reads=["psb6"], writes=["V.%d" % g])

            def attend_head(h, hoist):
                seq = []
                for g in range(NTG):
                    if fox:
                        blocks = [(4 * g + m, m) for m in range(4)] + [(j, None) for j in range(4 * g - 1, -1, -1)]
                    else:
                        blocks = [(4 * g + m, m) for m in range(3, -1, -1)] + [(j, None) for j in range(4 * g - 1, -1, -1)]
                    for i, (j, m) in enumerate(blocks):
                        seq.append((g, i, len(blocks), j, m))
                N = len(seq)
                mk = maskF if fox else maskS
                ob = OTs[0]
                obn = "OTs0"

                def score_op(n):
                    g, i, nb, j, m = seq[n]
                    q0 = g * TG
                    pS = psS[n % 4]
                    c0 = 0 if m is None else m * 128

                    def f(e):
                        kk = Kp[0:KR, j * 128:(j + 1) * 128]
                        if m is None:
                            return e.matmul(pS[:, :], lhsT=kk, rhs=Qp[0:KR, q0:q0 + TG], start=True, stop=False)
                        e.matmul(pS[:, c0:c0 + 128], lhsT=identb, rhs=mk, start=True, stop=False)
                        r = e.matmul(pS[:, c0:c0 + 128], lhsT=kk, rhs=Qp[0:KR, q0 + c0:q0 + c0 + 128], start=False, stop=False)
                        if c0 + 128 < TG:
                            r = e.matmul(pS[:, c0 + 128:TG], lhsT=kk, rhs=Qp[0:KR, q0 + c0 + 128:q0 + TG], start=False, stop=False)
                        return r
                    T.add("pe", f, reads=["Qp.%d" % g, "Qp.aug", "Kp.%d" % (j // 4), "Kp.aug", "cb"], writes=[psSn[n % 4]])

                if fox:
                    units = []
                    n_ = 0
                    while n_ < N:
                        g, i, nb, j, m = seq[n_]
                        if m is None and n_ + 1 < N and seq[n_ + 1][0] == g and seq[n_ + 1][4] is None:
                            units.append([seq[n_], seq[n_ + 1]])
                            n_ += 2
                        else:
                            units.append([seq[n_]])
                            n_ += 1
                    U_ = len(units)
                    psP = [psA, psB]
                    psPn = [["psb0", "psb1"], ["psb4", "psb5"]]
                    PT2 = [PT[0], PT[1]]

                    def S1(u):
                        pP = psP[u % 2]
                        for k, (g, i, nb, j, m) in enumerate(units[u]):
                            q0 = g * TG
                            c0 = 0 if m is None else m * 128
                            o = k * TG

                            def f(e, j=j, m=m, q0=q0, c0=c0, o=o):
                                kk = Kp[0:KR, j * 128:(j + 1) * 128]
                                if m is None:
                                    return e.matmul(pP[:, o:o + TG], lhsT=kk, rhs=Qp[0:KR, q0:q0 + TG], start=True, stop=True)
                                e.matmul(pP[:, c0:c0 + 128], lhsT=identb, rhs=mk, start=True, stop=False)
                                r = e.matmul(pP[:, c0:c0 + 128], lhsT=kk, rhs=Qp[0:KR, q0 + c0:q0 + c0 + 128], start=False, stop=(c0 + 128 >= TG))
                                if c0 + 128 < TG:
                                    r = e.matmul(pP[:, c0 + 128:TG], lhsT=kk, rhs=Qp[0:KR, q0 + c0 + 128:q0 + TG], start=False, stop=True)
                                return r
                            T.add("pe", f, reads=["Qp.%d" % g, "Qp.aug", "Kp.%d" % (j // 4), "Kp.aug", "cb"], writes=[psPn[u % 2][k]])

                    def S2(u):
                        pP = psP[u % 2]
                        pt = PT2[u % 2]
                        ptn = "PTw%d" % (u % 2)
                        blk = units[u]
                        g = blk[0][0]
                        psO = psO2[g % 2]
                        pOn = "psO%d" % (g % 2)
                        if blk[0][1] == 3:
                            flush()
                        if len(blk) == 2:
                            T.add("act", lambda e: e.activation(out=pt[:, 0:2 * TG], in_=pP[:, 0:2 * TG], func=AF.Exp),
                                  reads=psPn[u % 2], writes=[ptn])
                        else:
                            m = blk[0][4]
                            c0 = 0 if m is None else m * 128
                            T.add("act", lambda e: e.activation(out=pt[:, c0:TG], in_=pP[:, c0:TG], func=AF.Exp),
                                  reads=[psPn[u % 2][0]], writes=[ptn])
                        for k, (g_, i, nb, j, m) in enumerate(blk):
                            c0 = 0 if m is None else m * 128
                            o = k * TG
                            T.add("pe", lambda e, j=j, c0=c0, o=o, i=i, nb=nb: e.matmul(psO[:, c0:TG], lhsT=Vt[:, j, :], rhs=pt[:, o + c0:o + TG],
                                  start=(i == 0), stop=(i == nb - 1)),
                                  reads=[ptn, "V.%d" % (j // 4), "V.ones"] + ([pOn] if i else []), writes=[pOn])
                        g_, i, nb, j, m = blk[-1]
                        if i == nb - 1:
                            T.add("dve", lambda e: e.reciprocal(out=rden[64:65, :], in_=psO[64:65, :]), reads=[pOn], writes=["rden"])

                            def part_b():
                                T.add("pe", lambda e: e.matmul(psD[0:64, :], lhsT=one1[64:65, 0:64], rhs=rden[64:65, :], start=True, stop=True),
                                      reads=["rden", "one1"], writes=["psb6"])
                                T.add("dve", lambda e: e.tensor_copy(out=rbc[0:64, :], in_=psD[0:64, :]), reads=["psb6"], writes=["rbc"])
                                T.add("dve", lambda e: e.tensor_tensor(out=ob[0:64, :], in0=psO[0:64, :], in1=rbc[0:64, :], op=ALU.mult),
                                      reads=[pOn, "rbc"], writes=[obn])
                                T.add("sp", lambda e: e.dma_start(out=ot_s[h * 64:(h + 1) * 64, g * TG:(g + 1) * TG], in_=ob[0:64, :]),
                                      reads=[obn], writes=["ot_s.%d.%d" % (h, g)], dma="otw")
                            pending.append(part_b)

                    S1(0)
                    for u in range(U_):
                        if units[u][0][0] == NTG - 1 and units[u][0][1] == 0 and hoist is not None:
                            hoist()
                        if u + 1 < U_:
                            S1(u + 1)
                        S2(u)
                    return
                else:
                    def s1(n):
                        g, i, nb, j, m = seq[n]
                        c0 = 0 if m is None else m * 128
                        if i == 0:
                            T.add("pool", lambda e: e.memset(R32[:, :], 0.0), writes=["R32"])
                        score_op(n)
                        pS = psS[n % 4]
                        Ei = Eb[0]
                        SPi = SPb[n % 3]
                        T.add("act", lambda e: e.activation(out=Ei[:, c0:TG], in_=pS[:, c0:TG], func=AF.Exp), reads=[psSn[n % 4]], writes=["E0"])
                        T.add("act", lambda e: e.activation(out=SPi[:, c0:TG], in_=Ei[:, c0:TG], func=AF.Ln, bias=1.0), reads=["E0"], writes=["SP%d" % (n % 3)])
                        if i + 1 < nb:
                            g2, i2, nb2, j2, m2 = seq[n + 1]
                            c02 = 0 if m2 is None else m2 * 128
                            Rn = Rbb[(n + 1) % 4]
                            T.add("dve", lambda e: e.tensor_tensor(out=R32[:, c0:TG], in0=R32[:, c0:TG], in1=SPi[:, c0:TG], op=ALU.add),
                                  reads=["R32", "SP%d" % (n % 3)], writes=["R32"])
                            T.add("dve", lambda e: e.tensor_copy(out=Rn[:, c02:TG], in_=R32[:, c02:TG]), reads=["R32"], writes=["Rb%d" % ((n + 1) % 4)])

                    def s2(n):
                        g, i, nb, j, m = seq[n]
                        c0 = 0 if m is None else m * 128
                        pS = psS[n % 4]
                        SPi = SPb[n % 3]
                        ai = aTb[n % 3]
                        Rb = Rbb[n % 4]
                        c1 = c0 + 128 if m is not None else c0

                        def st2(e):
                            r = e.matmul(pS[:, c0:TG], lhsT=NTm, rhs=SPi[:, c0:TG], start=False, stop=(c1 >= TG))
                            if c1 < TG:
                                r = e.matmul(pS[:, c1:TG], lhsT=NOm, rhs=Rb[:, c1:TG], start=False, stop=True)
                            return r
                        T.add("pe", st2, reads=["SP%d" % (n % 3), "cb", psSn[n % 4]] + (["Rb%d" % (n % 4)] if c1 < TG else []), writes=[psSn[n % 4]])
                        T.add("act", lambda e: e.activation(out=ai[:, c0:TG], in_=pS[:, c0:TG], func=AF.Exp), reads=[psSn[n % 4]], writes=["aT%d" % (n % 3)])

                    def s3(n):
                        g, i, nb, j, m = seq[n]
                        psO = psO2[g % 2]
                        pOn = "psO%d" % (g % 2)
                        c0 = 0 if m is None else m * 128
                        ai = aTb[n % 3]
                        last = (i == nb - 1)
                        T.add("pe", lambda e: e.matmul(psO[0:64, c0:TG], lhsT=Vt[:, j, 0:DH], rhs=ai[:, c0:TG], start=(i == 0), stop=last),
                              reads=["aT%d" % (n % 3), "V.%d" % (j // 4)] + ([pOn] if i else []), writes=[pOn])
                        if last:
                            T.add("dve", lambda e: e.tensor_copy(out=ob[0:64, :], in_=psO[0:64, :]), reads=[pOn], writes=[obn])
                            T.add("sp", lambda e: e.dma_start(out=ot_s[h * 64:(h + 1) * 64, g * TG:(g + 1) * TG], in_=ob[0:64, :]),
                                  reads=[obn], writes=["ot_s.%d.%d" % (h, g)], dma="otw")

                s1(0)
                s1(1)
                for n in range(N):
                    if seq[n][0] == NTG - 1 and seq[n][1] == 0 and hoist is not None:
                        hoist()
                    if n + 2 < N:
                        s1(n + 2)
                    s2(n)
                    if n >= 1:
                        s3(n - 1)
                s3(N - 1)

            for h in range(H):
                whb = wh[h % 2]
                whn = "wh%d" % (h % 2)
                T.add("pool", lambda e, h=h, whb=whb: e.dma_start(out=whb, in_=wqkv_d[l][h].rearrange("p (c n) -> p c n", c=KC)),
                      writes=[whn], dma=whn)
                if h == 1:
                    cast_ffn_weights(l)
                if fox:
                    T.add("sp", lambda e, h=h: e.dma_start(out=Qp[64:67, :], in_=caug_s[:, h, :]), reads=CAUG, writes=["Qp.aug"], dma="aug")
                    T.add("sp", lambda e, h=h: e.dma_start(out=Kp[67:70, :], in_=caug_s[:, h, :]), reads=CAUG, writes=["Kp.aug"], dma="aug")
                for g in range(NTG):
                    project(h, g, whb, whn, skip_make=(g == 0 and h > 0))
                hoist = None
                if h + 1 < H:
                    hoist = lambda: make_xn(0, 0, l, rstd_bc[:, 0:TG], "rstd.0", xnTb[0], "xnT", npool=0)
                attend_head(h, hoist)
            flush()

            fence()
            ar.reset()
            wo = ar.bf16(8 * D).rearrange("p (c n) -> p c n", c=8)
            OTin = [ar.bf16(8 * TG).rearrange("p (c n) -> p c n", c=8) for _ in range(2)]
            T.add("pool", lambda e: e.dma_start(out=wo, in_=wo_d[l].rearrange("p (c n) -> p c n", c=8)), writes=["wo"], dma="wo")

            def wo_tg(g):
                ob = OTin[g % 2]
                obn = "OTin%d" % (g % 2)
                T.add("sp", lambda e: e.dma_start(out=ob, in_=ot_s[:, g * TG:(g + 1) * TG].rearrange("(c p) n -> p c n", p=128)),
                      reads=["ot_s.%d.%d" % (h_, g) for h_ in range(H)], writes=[obn], dma=obn)
                for co in range(KC):
                    py = ps[co % 2]
                    pyn = "ps%d" % (co % 2)

                    def omm(e, co=co, py=py):
                        r = None
                        for c in range(8):
                            r = e.matmul(py[:, :], lhsT=wo[:, c, co * 128:(co + 1) * 128], rhs=ob[:, c, :], start=(c == 0), stop=(c == 7))
                        return r
                    T.add("pe", omm, reads=["wo", obn], writes=[pyn])
                    T.add("dve", lambda e, co=co, py=py: e.tensor_tensor(out=hsT[:, co, g * TG:(g + 1) * TG], in0=py[:, :],
                          in1=hsT[:, co, g * TG:(g + 1) * TG], op=ALU.add), reads=[pyn, "hs.%d.%d" % (co, g)], writes=["hs.%d.%d" % (co, g)])
            for g in range(NTG):
                wo_tg(g)

        def ffn_layer(l):
            fence()
            ar.reset()
            xnT = ar.bf16(KC * TG).rearrange("p (c n) -> p c n", c=KC)
            actT = ar.bf16(FC * TG).rearrange("p (c n) -> p c n", c=FC)
            wu = [ar.bf16(KC * 256).rearrange("p (c n) -> p c n", c=KC) for _ in range(2)]
            wd = [ar.bf16(FC * 128).rearrange("p (c n) -> p c n", c=FC) for _ in range(2)]
            hext = [[ar.f32(516) for _ in range(2)] for _ in range(2)]
            tcv = [[ar.f32(TG) for _ in range(2)] for _ in range(3)]
            sqb = [ar.bf16(TG) for _ in range(2)]
            rstd = ar.f32(TG)
            halo = ar.f32(2 * FC * 2).rearrange("p (c n) -> p c n", n=2)
            psU = [ps[0], ps[2]]
            psG = [ps[1], ps[3]]
            psY = [ps[4], ps[5]]
            psN = ps[7]
            T.add("pool", lambda e: e.memset(halo[:, :, :], 0.0), writes=["halo"])

            def prep(g):
                rstd_for_tg(g, sqb, rstd[:, :], "rstdf", psN, "psN")
                make_xn(g, 8, l, rstd[:, :], "rstdf", xnT)

            def gate(f):
                par = f % 3
                T.add("act", lambda e: e.activation(out=tcv[par][1], in_=tcv[par][1], func=AF.Silu), reads=["tcv%d1" % par], writes=["tcv%d1" % par])
                T.add("pool", lambda e: e.tensor_tensor(out=actT[:, f, :], in0=tcv[par][1], in1=tcv[par][0], op=ALU.mult),
                      reads=["tcv%d1" % par, "tcv%d0" % par], writes=["actT.%d" % f])

            def up(g):
                for f in range(FC):
                    par = f % 2
                    wub = wu[f % 2]
                    wun = "wu%d" % (f % 2)
                    T.add("sp", lambda e, f=f, wub=wub: e.dma_start(out=wub, in_=wup_s[l][f].rearrange("p (c n) -> p c n", c=KC)),
                          reads=["wup_s%d.%d" % (l, f)], writes=[wun], dma=wun)
                    for ug in range(2):
                        pp = (psU if ug == 0 else psG)[par]
                        ppn = "psUG%d%d" % (ug, par)

                        def upmm(e, ug=ug, pp=pp, wub=wub):
                            r = None
                            for c in range(KC):
                                r = e.matmul(pp[:, :], lhsT=wub[:, c, ug * 128:(ug + 1) * 128], rhs=xnT[:, c, :], start=(c == 0), stop=(c == KC - 1))
                            return r
                        T.add("pe", upmm, reads=XR + [wun], writes=[ppn])
                        hx = hext[par][ug]
                        hxn = "hext%d%d" % (par, ug)
                        ch = ug * FC + f
                        tc_ = tcv[f % 3][ug]
                        tcn = "tcv%d%d" % (f % 3, ug)
                        wc = 16 + ch * 3
                        T.add("pool", lambda e, hx=hx, ch=ch: e.tensor_copy(out=hx[:, 0:2], in_=halo[:, ch, :]), reads=["halo.%d" % ch, "halo"], writes=[hxn + "h"])
                        T.add("act", lambda e, hx=hx, pp=pp: e.activation(out=hx[:, 2:514], in_=pp[:, :], func=AF.Copy), reads=[ppn], writes=[hxn])
                        T.add("act", lambda e, tc_=tc_, pp=pp, wc=wc, ch=ch: e.activation(out=tc_, in_=pp[:, :], func=AF.Identity,
                              scale=vcol(l, wc + 2), bias=vcol(l, 148 + ch)), reads=[ppn, "vecs"], writes=[tcn])
                        T.add("pool", lambda e, hx=hx, ch=ch: e.tensor_copy(out=halo[:, ch, :], in_=hx[:, 512:514]), reads=[hxn], writes=["halo.%d" % ch])
                        T.add("dve", lambda e, hx=hx, tc_=tc_, wc=wc: e.scalar_tensor_tensor(out=tc_, in0=hx[:, 1:513], scalar=vcol(l, wc + 1), in1=tc_,
                              op0=ALU.mult, op1=ALU.add), reads=[hxn, hxn + "h", tcn, "vecs"], writes=[tcn])
                        T.add("dve", lambda e, hx=hx, tc_=tc_, wc=wc: e.scalar_tensor_tensor(out=tc_, in0=hx[:, 0:512], scalar=vcol(l, wc), in1=tc_,
                              op0=ALU.mult, op1=ALU.add), reads=[hxn, hxn + "h", tcn, "vecs"], writes=[tcn])
                    if f >= 1:
                        gate(f - 1)
                gate(FC - 1)

            def down(g, nxt=None):
                ar_ = ["actT.%d" % f for f in range(FC)]
                for co in range(KC):
                    if nxt is not None and 1 <= co <= 4:
                        nxt[0](2 * (co - 1))
                        nxt[0](2 * (co - 1) + 1)
                    if nxt is not None and co == 5:
                        nxt[1]()
                    wdb = wd[co % 2]
                    wdn = "wd%d" % (co % 2)
                    T.add("sp", lambda e, co=co, wdb=wdb: e.dma_start(out=wdb, in_=wdn_s[l][co].rearrange("p (c n) -> p c n", c=FC)),
                          reads=["wdn_s%d.%d" % (l, co)], writes=[wdn], dma=wdn)
                    py = psY[co % 2]
                    pyn = "psY%d" % (co % 2)

                    def dmm(e, wdb=wdb, py=py):
                        r = None
                        for f in range(FC):
                            r = e.matmul(py[:, :], lhsT=wdb[:, f, :], rhs=actT[:, f, :], start=(f == 0), stop=(f == FC - 1))
                        return r
                    T.add("pe", dmm, reads=ar_ + [wdn], writes=[pyn])
                    T.add("dve", lambda e, co=co, py=py: e.tensor_tensor(out=hsT[:, co, g * TG:(g + 1) * TG], in0=py[:, :],
                          in1=hsT[:, co, g * TG:(g + 1) * TG], op=ALU.add), reads=[pyn, "hs.%d.%d" % (co, g)], writes=["hs.%d.%d" % (co, g)])
            def prep_pieces(g):
                def piece(c):
                    sb_ = sqb[c % 2]
                    T.add("pool", lambda e: e.tensor_tensor(out=sb_, in0=hsT[:, c, g * TG:(g + 1) * TG], in1=hsT[:, c, g * TG:(g + 1) * TG], op=ALU.mult),
                          reads=["hs.%d.%d" % (c, g)], writes=["sq%d" % (c % 2)])
                    T.add("pe", lambda e: e.matmul(psN[:], lhsT=NOm, rhs=sb_, start=(c == 0), stop=(c == KC - 1)),
                          reads=["sq%d" % (c % 2), "cb"] + (["psN"] if c else []), writes=["psN"])

                def finish():
                    T.add("act", lambda e: e.activation(out=rstd[:, :], in_=psN[:], func=AF.Ln, bias=EPS, scale=-1.0 / D), reads=["psN"], writes=["rstdf"])
                    T.add("act", lambda e: e.activation(out=rstd[:, :], in_=rstd[:, :], func=AF.Exp, scale=-0.5), reads=["rstdf"], writes=["rstdf"])
                    make_xn(g, 8, l, rstd[:, :], "rstdf", xnT)
                return piece, finish

            prep(0)
            for g in range(NTG):
                up(g)
                down(g, prep_pieces(g + 1) if g + 1 < NTG else None)

        def final_phase(normed):
            fence()
            ar.reset()
            yT = ar.f32(KC * TG).rearrange("p (c n) -> p c n", c=KC)
            ot = [ar.f32(D) for _ in range(2)]
            sqb = [ar.bf16(TG) for _ in range(2)]
            rstd = ar.f32(TG)
            psN = ps[7]
            outs = []

            def fin_tg(g):
                if normed:
                    rstd_for_tg(g, sqb, rstd[:, :], "rstdf", psN, "psN")
                    for c in range(KC):
                        T.add("dve", lambda e, c=c: e.scalar_tensor_tensor(out=yT[:, c, :], in0=hsT[:, c, g * TG:(g + 1) * TG], scalar=vecs[:, FIN + c:FIN + c + 1],
                              in1=rstd[:, :], op0=ALU.mult, op1=ALU.mult), reads=["hs.%d.%d" % (c, g), "rstdf", "vecs"], writes=["yT.%d" % c])
                else:
                    for c in range(KC):
                        T.add("dve", lambda e, c=c: e.tensor_copy(out=yT[:, c, :], in_=hsT[:, c, g * TG:(g + 1) * TG]),
                              reads=["hs.%d.%d" % (c, g)], writes=["yT.%d" % c])
                for tt in range(4):
                    t = g * 4 + tt
                    ob = ot[t % 2]
                    obn = "ot%d" % (t % 2)
                    for half in range(2):
                        pb = ps[(2 * t + half) % 4]
                        pbn = "psf%d" % ((2 * t + half) % 4)

                        def tr(e, pb=pb, half=half, tt=tt):
                            r = None
                            for q in range(4):
                                c = half * 4 + q
                                r = e.transpose(pb[:, q * 128:(q + 1) * 128], yT[:, c, tt * 128:(tt + 1) * 128], id32[:])
                            return r
                        T.add("pe", tr, reads=["yT.%d" % (half * 4 + q) for q in range(4)] + ["id32"], writes=[pbn])
                        if half == 0:
                            T.add("dve", lambda e, pb=pb, ob=ob: e.tensor_copy(out=ob[:, 0:512], in_=pb[:, :]), reads=[pbn], writes=[obn + "a"])
                        else:
                            T.add("act", lambda e, pb=pb, ob=ob: e.activation(out=ob[:, 512:1024], in_=pb[:, :], func=AF.Copy), reads=[pbn], writes=[obn + "b"])
                    T.add("sp", lambda e, ob=ob, t=t: e.dma_start(out=out_d[t * 128:(t + 1) * 128, :], in_=ob), reads=[obn + "a", obn + "b"],
                          writes=["out.%d" % t], dma="out%d" % (t % 2))
                    outs.append("out.%d" % t)
            for g in range(NTG):
                fin_tg(g)
            T.add("sp", lambda e: e.wait_ge(sems["sp"], 0), reads=outs, writes=[])

        sems = {e: st.enter_context(nc.semaphore("s_" + e)) for e in COMPUTE + ("sp",)}
        for (l, what) in layers:
            if what == "attn":
                attention_layer(l)
            else:
                ffn_layer(l)
        final_phase(final)

        dsems = {k: st.enter_context(nc.semaphore("d_" + k)) for k in T.dma_keys()}
        block = st.enter_context(nc.Block())
        T.emit(nc, block, sems, dsems)
    return nc


_CONST_CACHE = {}


def _consts():
    if "cb" not in _CONST_CACHE:
        p = np.arange(128)[:, None]
        q = np.arange(128)[None, :]
        cbm = np.zeros((128, 768), np.float32)
        cbm[:, 0:128] = np.eye(128, dtype=np.float32)
        cbm[:, 128:256] = np.where(p > q, NEG, 0.0)
        cbm[:, 256:384] = np.where(p >= q, NEG, 0.0)
        cbm[:, 384:512] = np.where(p >= q, -1.0, 0.0)
        cbm[:, 512:640] = -1.0
        cbm[:, 640:704] = 1.0
        _CONST_CACHE["cb"] = cbm
        _CONST_CACHE["ident"] = np.eye(128, dtype=np.float32)
    return _CONST_CACHE["cb"], _CONST_CACHE["ident"]


def layout_inputs(attn_norm, ffn_norm, final_norm, fox_w_qkvf, fox_b_f, fox_w_o,
                  sb_w_qkv, sb_w_o, ffn_w_up, ffn_w_conv, ffn_b_conv, ffn_w_down):
    f32 = np.float32
    m = {}
    cbm, ident = _consts()
    m["cb"] = cbm
    m["ident"] = ident
    NV = DEPTH * VS + 8
    vec = np.zeros((128, NV), f32)
    for l in range(DEPTH):
        o = l * VS
        vec[:, o:o + 8] = np.asarray(attn_norm[l], f32).reshape(8, 128).T
        vec[:, o + 8:o + 16] = np.asarray(ffn_norm[l], f32).reshape(8, 128).T
        wc = np.asarray(ffn_w_conv[l], f32).reshape(3, 44, 128).transpose(2, 1, 0).reshape(128, 132)
        vec[:, o + 16:o + 148] = wc
        vec[:, o + 148:o + 192] = np.asarray(ffn_b_conv[l], f32).reshape(44, 128).T
        if l % 2 == 0:
            bf = np.asarray(fox_b_f[l // 2], f32)
            for r0 in (0, 32, 64):
                vec[r0:r0 + 16, o + 192] = bf
    vec[:, DEPTH * VS:DEPTH * VS + 8] = np.asarray(final_norm, f32).reshape(8, 128).T
    m["vecs"] = vec
    for l in range(DEPTH):
        j = l // 2
        if l % 2 == 0:
            w = np.asarray(fox_w_qkvf[j], f32)
            wo = np.asarray(fox_w_o[j], f32)
            wfm = np.zeros((128, KC, 96), f32)
            wfr = w[:, 3 * D:3 * D + H].reshape(KC, 128, H).transpose(1, 0, 2)
            for r0 in (0, 32, 64):
                wfm[:, :, r0:r0 + 16] = wfr
            m["wf%d" % j] = wfm.reshape(128, KC * 96)
        else:
            w = np.asarray(sb_w_qkv[j], f32)
            wo = np.asarray(sb_w_o[j], f32)
        parts = [w[:, i * D:(i + 1) * D].reshape(KC, 128, H, DH).transpose(2, 1, 0, 3) for i in range(3)]
        m["wqkv%d" % l] = np.ascontiguousarray(np.concatenate(parts, axis=3)).reshape(H, 128, KC * 192)
        m["wo%d" % l] = np.ascontiguousarray(wo.reshape(8, 128, D).transpose(1, 0, 2)).reshape(128, 8 * D)
        wu = np.asarray(ffn_w_up[l], f32)
        pu = wu[:, 0:FF].reshape(KC, 128, FC, 128).transpose(2, 1, 0, 3)
        pg = wu[:, FF:2 * FF].reshape(KC, 128, FC, 128).transpose(2, 1, 0, 3)
        m["wup%d" % l] = np.ascontiguousarray(np.concatenate([pu, pg], axis=3)).reshape(FC, 128, KC * 256)
        wdn = np.asarray(ffn_w_down[l], f32)
        m["wdn%d" % l] = np.ascontiguousarray(wdn.reshape(FC, 128, KC, 128).transpose(2, 1, 0, 3)).reshape(KC, 128, FC * 128)
    return m


ALL_LAYERS = [(l, w) for l in range(DEPTH) for w in ("attn", "ffn")]


def kernel(x, attn_norm, ffn_norm, final_norm, fox_w_qkvf, fox_b_f, fox_w_o,
           sb_w_qkv, sb_w_o, ffn_w_up, ffn_w_conv, ffn_b_conv, ffn_w_down):
    x = np.asarray(x, np.float32)
    B, S, _ = x.shape
    shared = layout_inputs(attn_norm, ffn_norm, final_norm, fox_w_qkvf, fox_b_f, fox_w_o,
                           sb_w_qkv, sb_w_o, ffn_w_up, ffn_w_conv, ffn_b_conv, ffn_w_down)
    nc = build_program(S, ALL_LAYERS, True)
    in_maps = []
    for b in range(B):
        mm = dict(shared)
        mm["x"] = np.ascontiguousarray(x[b])
        in_maps.append(mm)
    res = run_bass_kernel_spmd(nc, in_maps, core_ids=list(range(B)))
    return np.stack([np.asarray(r["out"], np.float32) for r in res.results], axis=0)
```

```python
import contextlib
import numpy as np
import concourse.bass as bass
import concourse.mybir as mybir
from concourse.bass_utils import run_bass_kernel_spmd

F32 = mybir.dt.float32
BF16 = mybir.dt.bfloat16
AF = mybir.ActivationFunctionType
ALU = mybir.AluOpType

D = 1024
KC = 8
TG = 512
H = 16
DH = 64
FF = 2816
FC = 22
DEPTH = 4
EPS = 1e-6
NEG = -30000.0
VS = 193

COMPUTE = ("pe", "act", "dve", "pool")


class _Op:
    __slots__ = ("eng", "fn", "deps", "dma_key", "dma_val", "signal", "count", "idx", "dma_waits")


class Tracker:
    def __init__(self):
        self.ops = []
        self.last_w = {}
        self.readers = {}
        self.dma_cum = {}
        self.fence_idx = None
        self.last_eng = {}
        self.last_dma = {}

    def fence(self, fn):
        deps = set(self.last_eng.values()) | set(self.last_dma.values())
        idx = self.add("pool", fn, extra_deps=deps)
        self.fence_idx = idx
        return idx

    def add(self, eng, fn, reads=(), writes=(), dma=None, extra_deps=()):
        op = _Op()
        op.eng = eng
        op.fn = fn
        op.dma_key = dma
        op.signal = False
        op.count = 0
        op.idx = len(self.ops)
        deps = set()
        for r in reads:
            w = self.last_w.get(r)
            if w is not None:
                deps.add(w)
        for r in writes:
            w = self.last_w.get(r)
            if w is not None:
                deps.add(w)
            for rd in self.readers.get(r, ()):
                deps.add(rd)
        deps |= set(extra_deps)
        if self.fence_idx is not None:
            deps.add(self.fence_idx)
        deps.discard(op.idx)
        op.deps = deps
        op.dma_waits = {}
        for d in deps:
            k = self.ops[d].dma_key
            if k is not None:
                op.dma_waits[k] = self.dma_cum[k]
        if dma is not None:
            self.last_dma[dma] = op.idx
        else:
            self.last_eng[eng] = op.idx
        if dma is not None:
            self.dma_cum[dma] = self.dma_cum.get(dma, 0) + 16
            op.dma_val = self.dma_cum[dma]
        else:
            op.dma_val = 0
        self.ops.append(op)
        for r in writes:
            self.last_w[r] = op.idx
            self.readers[r] = []
        for r in reads:
            if r not in writes:
                self.readers.setdefault(r, []).append(op.idx)
        return op.idx

    def dma_keys(self):
        return list(self.dma_cum.keys())

    def emit(self, nc, block, sems, dma_sems):
        ops = self.ops
        for op in ops:
            for d in op.deps:
                dop = ops[d]
                if dop.dma_key is not None:
                    continue
                if dop.eng == op.eng and op.eng in ("pe", "sp"):
                    continue
                dop.signal = True
        cnt = {e: 0 for e in COMPUTE + ("sp",)}
        for op in ops:
            if op.dma_key is None and op.signal:
                cnt[op.eng] += 1
                op.count = cnt[op.eng]
        per_eng = {e: [] for e in COMPUTE + ("sp",)}
        for op in ops:
            per_eng[op.eng].append(op)

        def run(eng_name, eng):
            waited = {}
            for op in per_eng[eng_name]:
                wl = {}
                for d in op.deps:
                    dop = ops[d]
                    if dop.dma_key is not None:
                        k = ("dma", dop.dma_key)
                        v = op.dma_waits[dop.dma_key]
                    else:
                        if dop.eng == eng_name and eng_name in ("pe", "sp"):
                            continue
                        k = ("eng", dop.eng)
                        v = dop.count
                    if v > wl.get(k, 0):
                        wl[k] = v
                for k, v in wl.items():
                    if waited.get(k, 0) >= v:
                        continue
                    waited[k] = v
                    s = dma_sems[k[1]] if k[0] == "dma" else sems[k[1]]
                    eng.wait_ge(s, v)
                ins = op.fn(eng)
                if op.dma_key is not None:
                    ins.then_inc(dma_sems[op.dma_key], 16)
                elif op.signal:
                    ins.then_inc(sems[op.eng], 1)

        @block.sync
        def _(e):
            run("sp", e)

        @block.scalar
        def _(e):
            run("act", e)

        @block.vector
        def _(e):
            run("dve", e)

        @block.gpsimd
        def _(e):
            run("pool", e)

        @block.tensor
        def _(e):
            run("pe", e)


class Arena:
    def __init__(self, t, nbytes):
        self.t = t
        self.nbytes = nbytes
        self.off = 0

    def reset(self):
        self.off = 0

    def f32(self, n):
        assert self.off % 4 == 0
        a = self.off // 4
        self.off += 4 * n
        assert self.off <= self.nbytes, ("arena overflow", self.off, self.nbytes)
        return self.t[:, a:a + n]

    def bf16(self, n):
        n2 = (n + 1) // 2
        v = self.f32(n2)
        return v.bitcast(BF16)[:, 0:n]


def build_program(S, layers, final=True):
    NTG = S // TG
    NT = S // 128
    NV = DEPTH * VS + 8
    nc = bass.Bass("TRN2", target_bir_lowering=False)

    def din(name, shape, dt=F32):
        return nc.dram_tensor(name, list(shape), dt, kind="ExternalInput").ap()

    x_d = din("x", [S, D])
    vec_d = din("vecs", [128, NV])
    cb_d = din("cb", [128, 6 * 128])
    id_d = din("ident", [128, 128])
    wqkv_d = [din("wqkv%d" % l, [H, 128, KC * 192]) for l in range(DEPTH)]
    wf_d = [din("wf%d" % j, [128, KC * 96]) for j in range(2)]
    wo_d = [din("wo%d" % l, [128, 8 * D]) for l in range(DEPTH)]
    wup_d = [din("wup%d" % l, [FC, 128, KC * 256]) for l in range(DEPTH)]
    wdn_d = [din("wdn%d" % l, [KC, 128, FC * 128]) for l in range(DEPTH)]
    out_d = nc.dram_tensor("out", [S, D], F32, kind="ExternalOutput").ap()
    wup_s = [nc.dram_tensor("wup_s%d" % l, [FC, 128, KC * 256], BF16, kind="Internal").ap() for l in range(DEPTH)]
    wdn_s = [nc.dram_tensor("wdn_s%d" % l, [KC, 128, FC * 128], BF16, kind="Internal").ap() for l in range(DEPTH)]
    ot_s = nc.dram_tensor("ot_s", [D, S], BF16, kind="Internal").ap()
    caug_s = nc.dram_tensor("caug_s", [3, H, S], BF16, kind="Internal").ap()

    T = Tracker()

    with contextlib.ExitStack() as st:
        hsT = st.enter_context(nc.sbuf_tensor("hsT", [128, KC, S], F32))
        vecs = st.enter_context(nc.sbuf_tensor("vecs_sb", [128, NV], F32))
        cb = st.enter_context(nc.sbuf_tensor("cb_sb", [128, 6 * 128], BF16))
        id32 = st.enter_context(nc.sbuf_tensor("id32", [128, 128], F32))
        ones32 = st.enter_context(nc.sbuf_tensor("ones32", [128, 128], F32))
        fsc = st.enter_context(nc.sbuf_tensor("fsc", [128, 8], F32))
        one1 = st.enter_context(nc.sbuf_tensor("one1", [128, 64], F32))
        AR_BYTES = 75600
        art = st.enter_context(nc.sbuf_tensor("arena", [128, AR_BYTES // 4], F32))
        ar = Arena(art, AR_BYTES)
        psA = st.enter_context(nc.psum_tensor("psA", [128, 1024], F32))
        psB = st.enter_context(nc.psum_tensor("psB", [128, 1024], F32))
        psx = {i: st.enter_context(nc.psum_tensor("ps%d" % i, [128, 512], F32)) for i in (2, 3, 6, 7)}
        ps = [psA[:, 0:512], psA[:, 512:1024], psx[2][:, :], psx[3][:, :], psB[:, 0:512], psB[:, 512:1024], psx[6][:, :], psx[7][:, :]]

        identb = cb[:, 0:128]
        maskF = cb[:, 128:256]
        maskS = cb[:, 256:384]
        NTm = cb[:, 384:512]
        NOm = cb[:, 512:640]
        ones64 = cb[:, 640:704]

        def vcol(l, k):
            o = l * VS + k
            return vecs[:, o:o + 1]
        FIN = DEPTH * VS

        def fence():
            T.fence(lambda e: e.memset(fsc[:, 0:1], 0.0))

        T.add("sp", lambda e: e.dma_start(out=vecs[:], in_=vec_d), writes=["vecs"], dma="vecs")
        T.add("sp", lambda e: e.dma_start(out=id32[:], in_=id_d), writes=["id32"], dma="id32")
        T.add("pool", lambda e: e.dma_start(out=cb[:], in_=cb_d), writes=["cb"], dma="cb")
        T.add("pool", lambda e: e.memset(ones32[:], 1.0 / D), writes=["ones32"])
        T.add("pool", lambda e: e.memset(one1[:], 1.0), writes=["one1"])

        def cast_ffn_weights(l):
            for f in range(FC):
                T.add("pool", lambda e, f=f: e.dma_start(out=wup_s[l][f], in_=wup_d[l][f]),
                      writes=["wup_s%d.%d" % (l, f)], dma="wups")
            for co in range(KC):
                T.add("pool", lambda e, co=co: e.dma_start(out=wdn_s[l][co], in_=wdn_d[l][co]),
                      writes=["wdn_s%d.%d" % (l, co)], dma="wdns")

        ar.reset()
        xin = [ar.f32(D) for _ in range(2)]
        for t in range(NT):
            xb = xin[t % 2]
            T.add("sp", lambda e, t=t, xb=xb: e.dma_start(out=xb, in_=x_d[t * 128:(t + 1) * 128, :]),
                  writes=["xin%d" % (t % 2)], dma="xin%d" % (t % 2))
            for half in range(2):
                pb = ps[(2 * t + half) % 4]
                pbn = "ps%d" % ((2 * t + half) % 4)

                def tr(e, xb=xb, pb=pb, half=half):
                    r = None
                    for q in range(4):
                        c = half * 4 + q
                        r = e.transpose(pb[:, q * 128:(q + 1) * 128], xb[:, c * 128:(c + 1) * 128], id32[:])
                    return r
                T.add("pe", tr, reads=["xin%d" % (t % 2), "id32"], writes=[pbn])
                tgi = t // 4
                wr = ["hsw.%d.%d.%d" % (half * 4 + q, tgi, t % 4) for q in range(4)]
                if half == 0:
                    T.add("dve", lambda e, pb=pb, half=half, t=t: e.tensor_copy(
                        out=hsT[:, half * 4:half * 4 + 4, t * 128:(t + 1) * 128],
                        in_=pb[:].rearrange("p (q n) -> p q n", q=4)), reads=[pbn], writes=wr)
                else:
                    T.add("act", lambda e, pb=pb, half=half, t=t: e.activation(
                        out=hsT[:, half * 4:half * 4 + 4, t * 128:(t + 1) * 128],
                        in_=pb[:].rearrange("p (q n) -> p q n", q=4), func=AF.Copy), reads=[pbn], writes=wr)

        def rstd_for_tg(g, sqb, out_ap, out_res, psn, psn_name):
            for c in range(KC):
                sb_ = sqb[c % 2]
                if c % 3 == 1:
                    T.add("act", lambda e, c=c, sb_=sb_: e.activation(out=sb_, in_=hsT[:, c, g * TG:(g + 1) * TG], func=AF.Square),
                          reads=["hs.%d.%d" % (c, g)], writes=["sq%d" % (c % 2)])
                else:
                    T.add("pool" if c % 3 == 0 else "dve", lambda e, c=c, sb_=sb_: e.tensor_tensor(
                        out=sb_, in0=hsT[:, c, g * TG:(g + 1) * TG], in1=hsT[:, c, g * TG:(g + 1) * TG], op=ALU.mult),
                        reads=["hs.%d.%d" % (c, g)], writes=["sq%d" % (c % 2)])
                T.add("pe", lambda e, c=c, sb_=sb_: e.matmul(psn[:], lhsT=NOm, rhs=sb_, start=(c == 0), stop=(c == KC - 1)),
                      reads=["sq%d" % (c % 2), "cb"] + ([psn_name] if c else []), writes=[psn_name])
            T.add("act", lambda e: e.activation(out=out_ap, in_=psn[:], func=AF.Ln, bias=EPS, scale=-1.0 / D), reads=[psn_name], writes=[out_res])
            T.add("act", lambda e: e.activation(out=out_ap, in_=out_ap, func=AF.Exp, scale=-0.5), reads=[out_res], writes=[out_res])

        def make_xn(g, gcol0, l, rstd_ap, rstd_res, xn, xp="xnT", npool=0):
            for c in range(KC):
                T.add("pool" if c >= KC - npool else "dve", lambda e, c=c: e.scalar_tensor_tensor(
                    out=xn[:, c, :], in0=hsT[:, c, g * TG:(g + 1) * TG], scalar=vcol(l, gcol0 + c), in1=rstd_ap,
                    op0=ALU.mult, op1=ALU.mult), reads=["hs.%d.%d" % (c, g), rstd_res, "vecs"], writes=["%s.%d" % (xp, c)])
        XR = ["xnT.%d" % c for c in range(KC)]

        def make_xn_chunk(g, c, l, rstd_ap, rstd_res, xn, xp):
            T.add("dve", lambda e: e.scalar_tensor_tensor(
                out=xn[:, c, :], in0=hsT[:, c, g * TG:(g + 1) * TG], scalar=vcol(l, c), in1=rstd_ap,
                op0=ALU.mult, op1=ALU.mult), reads=["hs.%d.%d" % (c, g), rstd_res, "vecs"], writes=["%s.%d" % (xp, c)])

        def attention_layer(l):
            fox = (l % 2 == 0)
            j_ = l // 2
            fence()
            ar.reset()
            rstd_bc = ar.f32(S)
            xnTb = [ar.bf16(KC * TG).rearrange("p (c n) -> p c n", c=KC) for _ in range(2)]
            xnT = xnTb[0]
            mark = ar.off
            sqb = [ar.bf16(TG) for _ in range(2)]
            psS = [ps[0], ps[1], ps[4], ps[5]]
            psSn = ["psb0", "psb1", "psb4", "psb5"]
            psO2, psD, psQ, psK, psV, psN = [ps[2], ps[3]], ps[6], ps[4], ps[5], ps[6], ps[7]

            for g in range(NTG):
                rstd_for_tg(g, sqb, rstd_bc[:, g * TG:(g + 1) * TG], "rstd.%d" % g, psN, "psN")

            CAUG = ["caug_s.%d.%d" % (p_, g_) for p_ in range(3) for g_ in range(NTG)]
            if fox:
                wf = ar.bf16(KC * 96).rearrange("p (c n) -> p c n", c=KC)
                ft = [ar.f32(TG) for _ in range(4)]
                fb = [ar.bf16(TG) for _ in range(3)]
                onesr = ar.f32(TG)
                T.add("pool", lambda e: e.dma_start(out=wf, in_=wf_d[j_].rearrange("p (c n) -> p c n", c=KC)), writes=["wf"], dma="wf")
                T.add("pool", lambda e: e.memset(onesr[:, :], 1.0), writes=["onesr"])

                def fphase(g):
                    make_xn(g, 0, l, rstd_bc[:, g * TG:(g + 1) * TG], "rstd.%d" % g, xnT)

                    def fmm(e):
                        r = None
                        for c in range(KC):
                            r = e.matmul(psN[0:96, :], lhsT=wf[:, c, :], rhs=xnT[:, c, :], start=(c == 0), stop=(c == KC - 1))
                        return r
                    T.add("pe", fmm, reads=["wf"] + XR, writes=["psN"])
                    cprev = ft[2 + (g + 1) % 2]
                    ccur = ft[2 + g % 2]
                    cn = "fc%d" % (g % 2)
                    cpn = "fc%d" % ((g + 1) % 2)
                    T.add("act", lambda e: e.activation(out=ft[0][0:80, :], in_=psN[0:80, :], func=AF.Identity, bias=vcol(l, 192)[0:80, :]),
                          reads=["psN", "vecs"], writes=["ft0"])
                    T.add("act", lambda e: e.activation(out=ft[0][0:80, :], in_=ft[0][0:80, :], func=AF.Exp, scale=-1.0), reads=["ft0"], writes=["ft0"])
                    T.add("act", lambda e: e.activation(out=ft[1][0:80, :], in_=ft[0][0:80, :], func=AF.Ln, bias=1.0), reads=["ft0"], writes=["ft1"])
                    if g == 0:
                        T.add("dve", lambda e: e.tensor_tensor_scan(out=ccur[0:80, :], data0=onesr[0:80, :], data1=ft[1][0:80, :],
                              initial=0.0, op0=ALU.mult, op1=ALU.subtract), reads=["onesr", "ft1"], writes=[cn])
                    else:
                        T.add("dve", lambda e: e.tensor_tensor_scan(out=ccur[0:80, :], data0=onesr[0:80, :], data1=ft[1][0:80, :],
                              initial=cprev[0:80, TG - 1:TG], op0=ALU.mult, op1=ALU.subtract),
                              reads=["onesr", "ft1", cpn], writes=[cn])
                    T.add("dve", lambda e: e.tensor_copy(out=fb[0][0:80, :], in_=ccur[0:80, :]), reads=[cn], writes=["fb0"])
                    T.add("dve", lambda e: e.tensor_tensor(out=ft[0][0:80, :], in0=ccur[0:80, :], in1=fb[0][0:80, :], op=ALU.subtract),
                          reads=[cn, "fb0"], writes=["ft0"])
                    T.add("dve", lambda e: e.tensor_copy(out=fb[1][0:80, :], in_=ft[0][0:80, :]), reads=["ft0"], writes=["fb1"])
                    T.add("dve", lambda e: e.tensor_tensor(out=ft[1][0:80, :], in0=ft[0][0:80, :], in1=fb[1][0:80, :], op=ALU.subtract),
                          reads=["ft0", "fb1"], writes=["ft1"])
                    T.add("dve", lambda e: e.tensor_copy(out=fb[2][0:80, :], in_=ft[1][0:80, :]), reads=["ft1"], writes=["fb2"])
                    for part in range(3):
                        T.add("sp", lambda e, part=part: e.dma_start(out=caug_s[part, :, g * TG:(g + 1) * TG],
                              in_=fb[part][32 * part:32 * part + 16, :]), reads=["fb%d" % part], writes=["caug_s.%d.%d" % (part, g)], dma="caug_w")
                for g in range(NTG):
                    fphase(g)
            fence()
            ar.off = mark
            wh = [ar.bf16(KC * 192).rearrange("p (c n) -> p c n", c=KC) for _ in range(2)]
            Qp = ar.bf16(S)
            Kp = ar.bf16(S)
            VW = 128 if fox else DH + 2
            Vt = ar.bf16(NT * VW).rearrange("p (t d) -> p t d", t=NT)
            OTs = [ar.bf16(TG) for _ in range(1)]
            if fox:
                PT = [ar.bf16(2 * TG) for _ in range(2)]
                rden = ar.f32(TG)
                rbc = ar.f32(TG)
                T.add("pool", lambda e: e.memset(Vt[:, :, DH:VW], 0.0), writes=["V.ones"])
                T.add("pool", lambda e: e.memset(Vt[:, :, DH:DH + 1], 1.0), writes=["V.ones"])
            else:
                Eb = [ar.f32(TG) for _ in range(1)]
                SPb = [ar.bf16(TG) for _ in range(3)]
                aTb = [ar.bf16(TG) for _ in range(3)]
                Rbb = [ar.bf16(TG) for _ in range(4)]
                R32 = ar.f32(TG)
            if fox:
                T.add("pool", lambda e: e.memset(Qp[64:128, :], 0.0), writes=["Qp.aug"])
                T.add("pool", lambda e: e.memset(Kp[64:128, :], 0.0), writes=["Kp.aug"])
                T.add("pool", lambda e: e.memset(Qp[64:70, :], -1.0), writes=["Qp.aug"])
                T.add("pool", lambda e: e.memset(Kp[64:70, :], 1.0), writes=["Kp.aug"])
            else:
                T.add("pool", lambda e: e.memset(Qp[64:128, :], 0.0), writes=["Qp.aug"])
                T.add("pool", lambda e: e.memset(Kp[64:128, :], 0.0), writes=["Kp.aug"])
            KR = 128

            pending = []

            def flush():
                for fn_ in pending:
                    fn_()
                del pending[:]

            def project(h, g, whb, whn, skip_make=False):
                xnT = xnTb[g % 2]
                xp = "xnT" if g % 2 == 0 else "xnU"
                XR = ["%s.%d" % (xp, c) for c in range(KC)]
                if not skip_make:
                    make_xn(g, 0, l, rstd_bc[:, g * TG:(g + 1) * TG], "rstd.%d" % g, xnT, xp, npool=0)

                def qmm(e):
                    r = None
                    for c in range(KC):
                        r = e.matmul(psQ[0:64, :], lhsT=whb[:, c, 0:64], rhs=xnT[:, c, :], start=(c == 0), stop=(c == KC - 1))
                    return r

                def kmm(e):
                    r = None
                    for c in range(KC):
                        r = e.matmul(psK[0:64, :], lhsT=whb[:, c, 64:128], rhs=xnT[:, c, :], start=(c == 0), stop=(c == KC - 1))
                    return r

                def vmm(e):
                    r = None
                    for tt in range(4):
                        for c in range(KC):
                            r = e.matmul(psV[:, tt * 64:(tt + 1) * 64], lhsT=xnT[:, c, tt * 128:(tt + 1) * 128], rhs=whb[:, c, 128:192],
                                         start=(c == 0), stop=(c == KC - 1))
                    return r
                T.add("pe", qmm, reads=XR + [whn], writes=["psb4"])
                T.add("pe", kmm, reads=XR + [whn], writes=["psb5"])
                T.add("pe", vmm, reads=XR + [whn], writes=["psb6"])
                T.add("act", lambda e: e.activation(out=Qp[0:64, g * TG:(g + 1) * TG], in_=psQ[0:64, :], func=AF.Copy, scale=0.125),
                      reads=["psb4"], writes=["Qp.%d" % g])
                T.add("act", lambda e: e.activation(out=Kp[0:64, g * TG:(g + 1) * TG], in_=psK[0:64, :], func=AF.Copy), reads=["psb5"], writes=["Kp.%d" % g])
                T.add("act", lambda e: e.activation(out=Vt[:, g * 4:(g + 1) * 4, 0:DH], in_=psV[:, 0:256].rearrange("p (t d) -> p t d", t=4), func=AF.Copy),
                      reads=["psb6"], writes=["V.%d" % g])

            def attend_head(h, hoist):
                seq = []
                for g in range(NTG):
                    if fox:
                        blocks = [(4 * g + m, m) for m in range(4)] + [(j, None) for j in range(4 * g - 1, -1, -1)]
                    else:
                        blocks = [(4 * g + m, m) for m in range(3, -1, -1)] + [(j, None) for j in range(4 * g - 1, -1, -1)]
                    for i, (j, m) in enumerate(blocks):
                        seq.append((g, i, len(blocks), j, m))
                N = len(seq)
                mk = maskF if fox else maskS
                ob = OTs[0]
                obn = "OTs0"

                def score_op(n):
                    g, i, nb, j, m = seq[n]
                    q0 = g * TG
                    pS = psS[n % 4]
                    c0 = 0 if m is None else m * 128

                    def f(e):
                        kk = Kp[0:KR, j * 128:(j + 1) * 128]
                        if m is None:
                            return e.matmul(pS[:, :], lhsT=kk, rhs=Qp[0:KR, q0:q0 + TG], start=True, stop=False)
                        e.matmul(pS[:, c0:c0 + 128], lhsT=identb, rhs=mk, start=True, stop=False)
                        r = e.matmul(pS[:, c0:c0 + 128], lhsT=kk, rhs=Qp[0:KR, q0 + c0:q0 + c0 + 128], start=False, stop=False)
                        if c0 + 128 < TG:
                            r = e.matmul(pS[:, c0 + 128:TG], lhsT=kk, rhs=Qp[0:KR, q0 + c0 + 128:q0 + TG], start=False, stop=False)
                        return r
                    T.add("pe", f, reads=["Qp.%d" % g, "Qp.aug", "Kp.%d" % (j // 4), "Kp.aug", "cb"], writes=[psSn[n % 4]])

                if fox:
                    units = []
                    n_ = 0
                    while n_ < N:
                        g, i, nb, j, m = seq[n_]
                        if m is None and n_ + 1 < N and seq[n_ + 1][0] == g and seq[n_ + 1][4] is None:
                            units.append([seq[n_], seq[n_ + 1]])
                            n_ += 2
                        else:
                            units.append([seq[n_]])
                            n_ += 1
                    U_ = len(units)
                    psP = [psA, psB]
                    psPn = [["psb0", "psb1"], ["psb4", "psb5"]]
                    PT2 = [PT[0], PT[1]]

                    def S1(u):
                        pP = psP[u % 2]
                        for k, (g, i, nb, j, m) in enumerate(units[u]):
                            q0 = g * TG
                            c0 = 0 if m is None else m * 128
                            o = k * TG

                            def f(e, j=j, m=m, q0=q0, c0=c0, o=o):
                                kk = Kp[0:KR, j * 128:(j + 1) * 128]
                                if m is None:
                                    return e.matmul(pP[:, o:o + TG], lhsT=kk, rhs=Qp[0:KR, q0:q0 + TG], start=True, stop=True)
                                e.matmul(pP[:, c0:c0 + 128], lhsT=identb, rhs=mk, start=True, stop=False)
                                r = e.matmul(pP[:, c0:c0 + 128], lhsT=kk, rhs=Qp[0:KR, q0 + c0:q0 + c0 + 128], start=False, stop=(c0 + 128 >= TG))
                                if c0 + 128 < TG:
                                    r = e.matmul(pP[:, c0 + 128:TG], lhsT=kk, rhs=Qp[0:KR, q0 + c0 + 128:q0 + TG], start=False, stop=True)
                                return r
                            T.add("pe", f, reads=["Qp.%d" % g, "Qp.aug", "Kp.%d" % (j // 4), "Kp.aug", "cb"], writes=[psPn[u % 2][k]])

                    def S2(u):
                        pP = psP[u % 2]
                        pt = PT2[u % 2]
                        ptn = "PTw%d" % (u % 2)
                        blk = units[u]
                        g = blk[0][0]
                        psO = psO2[g % 2]
                        pOn = "psO%d" % (g % 2)
                        if blk[0][1] == 3:
                            flush()
                        if len(blk) == 2:
                            T.add("act", lambda e: e.activation(out=pt[:, 0:2 * TG], in_=pP[:, 0:2 * TG], func=AF.Exp),
                                  reads=psPn[u % 2], writes=[ptn])
                        else:
                            m = blk[0][4]
                            c0 = 0 if m is None else m * 128
                            T.add("act", lambda e: e.activation(out=pt[:, c0:TG], in_=pP[:, c0:TG], func=AF.Exp),
                                  reads=[psPn[u % 2][0]], writes=[ptn])
                        for k, (g_, i, nb, j, m) in enumerate(blk):
                            c0 = 0 if m is None else m * 128
                            o = k * TG
                            T.add("pe", lambda e, j=j, c0=c0, o=o, i=i, nb=nb: e.matmul(psO[:, c0:TG], lhsT=Vt[:, j, :], rhs=pt[:, o + c0:o + TG],
                                  start=(i == 0), stop=(i == nb - 1)),
                                  reads=[ptn, "V.%d" % (j // 4), "V.ones"] + ([pOn] if i else []), writes=[pOn])
                        g_, i, nb, j, m = blk[-1]
                        if i == nb - 1:
                            T.add("dve", lambda e: e.reciprocal(out=rden[64:65, :], in_=psO[64:65, :]), reads=[pOn], writes=["rden"])

                            def part_b():
                                T.add("pe", lambda e: e.matmul(psD[0:64, :], lhsT=one1[64:65, 0:64], rhs=rden[64:65, :], start=True, stop=True),
                                      reads=["rden", "one1"], writes=["psb6"])
                                T.add("dve", lambda e: e.tensor_copy(out=rbc[0:64, :], in_=psD[0:64, :]), reads=["psb6"], writes=["rbc"])
                                T.add("dve", lambda e: e.tensor_tensor(out=ob[0:64, :], in0=psO[0:64, :], in1=rbc[0:64, :], op=ALU.mult),
                                      reads=[pOn, "rbc"], writes=[obn])
                                T.add("sp", lambda e: e.dma_start(out=ot_s[h * 64:(h + 1) * 64, g * TG:(g + 1) * TG], in_=ob[0:64, :]),
                                      reads=[obn], writes=["ot_s.%d.%d" % (h, g)], dma="otw")
                            pending.append(part_b)

                    S1(0)
                    for u in range(U_):
                        if units[u][0][0] == NTG - 1 and hoist:
                            hoist.pop(0)()
                        if u + 1 < U_:
                            S1(u + 1)
                        S2(u)
                    while hoist:
                        hoist.pop(0)()
                    return
                else:
                    def s1(n):
                        g, i, nb, j, m = seq[n]
                        c0 = 0 if m is None else m * 128
                        if i == 0:
                            T.add("pool", lambda e: e.memset(R32[:, :], 0.0), writes=["R32"])
                        score_op(n)
                        pS = psS[n % 4]
                        Ei = Eb[0]
                        SPi = SPb[n % 3]
                        T.add("act", lambda e: e.activation(out=Ei[:, c0:TG], in_=pS[:, c0:TG], func=AF.Exp), reads=[psSn[n % 4]], writes=["E0"])
                        T.add("act", lambda e: e.activation(out=SPi[:, c0:TG], in_=Ei[:, c0:TG], func=AF.Ln, bias=1.0), reads=["E0"], writes=["SP%d" % (n % 3)])
                        if i + 1 < nb:
                            g2, i2, nb2, j2, m2 = seq[n + 1]
                            c02 = 0 if m2 is None else m2 * 128
                            Rn = Rbb[(n + 1) % 4]
                            T.add("dve", lambda e: e.tensor_tensor(out=R32[:, c0:TG], in0=R32[:, c0:TG], in1=SPi[:, c0:TG], op=ALU.add),
                                  reads=["R32", "SP%d" % (n % 3)], writes=["R32"])
                            T.add("dve", lambda e: e.tensor_copy(out=Rn[:, c02:TG], in_=R32[:, c02:TG]), reads=["R32"], writes=["Rb%d" % ((n + 1) % 4)])

                    def s2(n):
                        g, i, nb, j, m = seq[n]
                        c0 = 0 if m is None else m * 128
                        pS = psS[n % 4]
                        SPi = SPb[n % 3]
                        ai = aTb[n % 3]
                        Rb = Rbb[n % 4]
                        c1 = c0 + 128 if m is not None else c0

                        def st2(e):
                            r = e.matmul(pS[:, c0:TG], lhsT=NTm, rhs=SPi[:, c0:TG], start=False, stop=(c1 >= TG))
                            if c1 < TG:
                                r = e.matmul(pS[:, c1:TG], lhsT=NOm, rhs=Rb[:, c1:TG], start=False, stop=True)
                            return r
                        T.add("pe", st2, reads=["SP%d" % (n % 3), "cb", psSn[n % 4]] + (["Rb%d" % (n % 4)] if c1 < TG else []), writes=[psSn[n % 4]])
                        T.add("act", lambda e: e.activation(out=ai[:, c0:TG], in_=pS[:, c0:TG], func=AF.Exp), reads=[psSn[n % 4]], writes=["aT%d" % (n % 3)])

                    def s3(n):
                        g, i, nb, j, m = seq[n]
                        psO = psO2[g % 2]
                        pOn = "psO%d" % (g % 2)
                        c0 = 0 if m is None else m * 128
                        ai = aTb[n % 3]
                        last = (i == nb - 1)
                        T.add("pe", lambda e: e.matmul(psO[0:64, c0:TG], lhsT=Vt[:, j, 0:DH], rhs=ai[:, c0:TG], start=(i == 0), stop=last),
                              reads=["aT%d" % (n % 3), "V.%d" % (j // 4)] + ([pOn] if i else []), writes=[pOn])
                        if last:
                            T.add("dve", lambda e: e.tensor_copy(out=ob[0:64, :], in_=psO[0:64, :]), reads=[pOn], writes=[obn])
                            T.add("sp", lambda e: e.dma_start(out=ot_s[h * 64:(h + 1) * 64, g * TG:(g + 1) * TG], in_=ob[0:64, :]),
                                  reads=[obn], writes=["ot_s.%d.%d" % (h, g)], dma="otw")

                s1(0)
                s1(1)
                for n in range(N):
                    if seq[n][0] == NTG - 1 and hoist:
                        hoist.pop(0)()
                    if n + 2 < N:
                        s1(n + 2)
                    s2(n)
                    if n >= 1:
                        s3(n - 1)
                s3(N - 1)
                while hoist:
                    hoist.pop(0)()

            for h in range(H):
                whb = wh[h % 2]
                whn = "wh%d" % (h % 2)
                T.add("pool", lambda e, h=h, whb=whb: e.dma_start(out=whb, in_=wqkv_d[l][h].rearrange("p (c n) -> p c n", c=KC)),
                      writes=[whn], dma=whn)
                if h == 1:
                    cast_ffn_weights(l)
                if fox:
                    T.add("sp", lambda e, h=h: e.dma_start(out=Qp[64:67, :], in_=caug_s[:, h, :]), reads=CAUG, writes=["Qp.aug"], dma="aug")
                    T.add("sp", lambda e, h=h: e.dma_start(out=Kp[67:70, :], in_=caug_s[:, h, :]), reads=CAUG, writes=["Kp.aug"], dma="aug")
                for g in range(NTG):
                    project(h, g, whb, whn, skip_make=(g <= 1 and h > 0))
                hoist = None
                if h + 1 < H:
                    hoist = []
                    for g_ in (0, 1):
                        for c_ in range(KC):
                            hoist.append(lambda g_=g_, c_=c_: make_xn_chunk(g_, c_, l, rstd_bc[:, g_ * TG:(g_ + 1) * TG], "rstd.%d" % g_,
                                                                             xnTb[g_], "xnT" if g_ == 0 else "xnU"))
                attend_head(h, hoist)
            flush()

            fence()
            ar.reset()
            wo = ar.bf16(8 * D).rearrange("p (c n) -> p c n", c=8)
            OTin = [ar.bf16(8 * TG).rearrange("p (c n) -> p c n", c=8) for _ in range(2)]
            T.add("pool", lambda e: e.dma_start(out=wo, in_=wo_d[l].rearrange("p (c n) -> p c n", c=8)), writes=["wo"], dma="wo")

            def wo_tg(g):
                ob = OTin[g % 2]
                obn = "OTin%d" % (g % 2)
                T.add("sp", lambda e: e.dma_start(out=ob, in_=ot_s[:, g * TG:(g + 1) * TG].rearrange("(c p) n -> p c n", p=128)),
                      reads=["ot_s.%d.%d" % (h_, g) for h_ in range(H)], writes=[obn], dma=obn)
                for co in range(KC):
                    py = ps[co % 2]
                    pyn = "ps%d" % (co % 2)

                    def omm(e, co=co, py=py):
                        r = None
                        for c in range(8):
                            r = e.matmul(py[:, :], lhsT=wo[:, c, co * 128:(co + 1) * 128], rhs=ob[:, c, :], start=(c == 0), stop=(c == 7))
                        return r
                    T.add("pe", omm, reads=["wo", obn], writes=[pyn])
                    T.add("dve", lambda e, co=co, py=py: e.tensor_tensor(out=hsT[:, co, g * TG:(g + 1) * TG], in0=py[:, :],
                          in1=hsT[:, co, g * TG:(g + 1) * TG], op=ALU.add), reads=[pyn, "hs.%d.%d" % (co, g)], writes=["hs.%d.%d" % (co, g)])
            for g in range(NTG):
                wo_tg(g)

        def ffn_layer(l):
            fence()
            ar.reset()
            xnT = ar.bf16(KC * TG).rearrange("p (c n) -> p c n", c=KC)
            actT = ar.bf16(FC * TG).rearrange("p (c n) -> p c n", c=FC)
            wu = [ar.bf16(KC * 256).rearrange("p (c n) -> p c n", c=KC) for _ in range(2)]
            wd = [ar.bf16(FC * 128).rearrange("p (c n) -> p c n", c=FC) for _ in range(2)]
            hext = [[ar.f32(516) for _ in range(2)] for _ in range(2)]
            tcv = [[ar.f32(TG) for _ in range(2)] for _ in range(3)]
            sqb = [ar.bf16(TG) for _ in range(2)]
            rstd = ar.f32(TG)
            halo = ar.f32(2 * FC * 2).rearrange("p (c n) -> p c n", n=2)
            psU = [ps[0], ps[2]]
            psG = [ps[1], ps[3]]
            psY = [ps[4], ps[5]]
            psN = ps[7]
            T.add("pool", lambda e: e.memset(halo[:, :, :], 0.0), writes=["halo"])

            def prep(g):
                rstd_for_tg(g, sqb, rstd[:, :], "rstdf", psN, "psN")
                make_xn(g, 8, l, rstd[:, :], "rstdf", xnT)

            def gate(f):
                par = f % 3
                T.add("act", lambda e: e.activation(out=tcv[par][1], in_=tcv[par][1], func=AF.Silu), reads=["tcv%d1" % par], writes=["tcv%d1" % par])
                T.add("pool", lambda e: e.tensor_tensor(out=actT[:, f, :], in0=tcv[par][1], in1=tcv[par][0], op=ALU.mult),
                      reads=["tcv%d1" % par, "tcv%d0" % par], writes=["actT.%d" % f])

            def up(g):
                for f in range(FC):
                    par = f % 2
                    wub = wu[f % 2]
                    wun = "wu%d" % (f % 2)
                    T.add("sp", lambda e, f=f, wub=wub: e.dma_start(out=wub, in_=wup_s[l][f].rearrange("p (c n) -> p c n", c=KC)),
                          reads=["wup_s%d.%d" % (l, f)], writes=[wun], dma=wun)
                    for ug in range(2):
                        pp = (psU if ug == 0 else psG)[par]
                        ppn = "psUG%d%d" % (ug, par)

                        def upmm(e, ug=ug, pp=pp, wub=wub):
                            r = None
                            for c in range(KC):
                                r = e.matmul(pp[:, :], lhsT=wub[:, c, ug * 128:(ug + 1) * 128], rhs=xnT[:, c, :], start=(c == 0), stop=(c == KC - 1))
                            return r
                        T.add("pe", upmm, reads=XR + [wun], writes=[ppn])
                        hx = hext[par][ug]
                        hxn = "hext%d%d" % (par, ug)
                        ch = ug * FC + f
                        tc_ = tcv[f % 3][ug]
                        tcn = "tcv%d%d" % (f % 3, ug)
                        wc = 16 + ch * 3
                        T.add("pool", lambda e, hx=hx, ch=ch: e.tensor_copy(out=hx[:, 0:2], in_=halo[:, ch, :]), reads=["halo.%d" % ch, "halo"], writes=[hxn + "h"])
                        T.add("act", lambda e, hx=hx, pp=pp: e.activation(out=hx[:, 2:514], in_=pp[:, :], func=AF.Copy), reads=[ppn], writes=[hxn])
                        T.add("act", lambda e, tc_=tc_, pp=pp, wc=wc, ch=ch: e.activation(out=tc_, in_=pp[:, :], func=AF.Identity,
                              scale=vcol(l, wc + 2), bias=vcol(l, 148 + ch)), reads=[ppn, "vecs"], writes=[tcn])
                        T.add("pool", lambda e, hx=hx, ch=ch: e.tensor_copy(out=halo[:, ch, :], in_=hx[:, 512:514]), reads=[hxn], writes=["halo.%d" % ch])
                        T.add("dve", lambda e, hx=hx, tc_=tc_, wc=wc: e.scalar_tensor_tensor(out=tc_, in0=hx[:, 1:513], scalar=vcol(l, wc + 1), in1=tc_,
                              op0=ALU.mult, op1=ALU.add), reads=[hxn, hxn + "h", tcn, "vecs"], writes=[tcn])
                        T.add("dve", lambda e, hx=hx, tc_=tc_, wc=wc: e.scalar_tensor_tensor(out=tc_, in0=hx[:, 0:512], scalar=vcol(l, wc), in1=tc_,
                              op0=ALU.mult, op1=ALU.add), reads=[hxn, hxn + "h", tcn, "vecs"], writes=[tcn])
                    if f >= 1:
                        gate(f - 1)
                gate(FC - 1)

            def down(g, nxt=None):
                ar_ = ["actT.%d" % f for f in range(FC)]
                for co in range(KC):
                    if nxt is not None and 1 <= co <= 4:
                        nxt[0](2 * (co - 1))
                        nxt[0](2 * (co - 1) + 1)
                    if nxt is not None and co == 5:
                        nxt[1]()
                    wdb = wd[co % 2]
                    wdn = "wd%d" % (co % 2)
                    T.add("sp", lambda e, co=co, wdb=wdb: e.dma_start(out=wdb, in_=wdn_s[l][co].rearrange("p (c n) -> p c n", c=FC)),
                          reads=["wdn_s%d.%d" % (l, co)], writes=[wdn], dma=wdn)
                    py = psY[co % 2]
                    pyn = "psY%d" % (co % 2)

                    def dmm(e, wdb=wdb, py=py):
                        r = None
                        for f in range(FC):
                            r = e.matmul(py[:, :], lhsT=wdb[:, f, :], rhs=actT[:, f, :], start=(f == 0), stop=(f == FC - 1))
                        return r
                    T.add("pe", dmm, reads=ar_ + [wdn], writes=[pyn])
                    T.add("dve", lambda e, co=co, py=py: e.tensor_tensor(out=hsT[:, co, g * TG:(g + 1) * TG], in0=py[:, :],
                          in1=hsT[:, co, g * TG:(g + 1) * TG], op=ALU.add), reads=[pyn, "hs.%d.%d" % (co, g)], writes=["hs.%d.%d" % (co, g)])
            def prep_pieces(g):
                def piece(c):
                    sb_ = sqb[c % 2]
                    T.add("pool", lambda e: e.tensor_tensor(out=sb_, in0=hsT[:, c, g * TG:(g + 1) * TG], in1=hsT[:, c, g * TG:(g + 1) * TG], op=ALU.mult),
                          reads=["hs.%d.%d" % (c, g)], writes=["sq%d" % (c % 2)])
                    T.add("pe", lambda e: e.matmul(psN[:], lhsT=NOm, rhs=sb_, start=(c == 0), stop=(c == KC - 1)),
                          reads=["sq%d" % (c % 2), "cb"] + (["psN"] if c else []), writes=["psN"])

                def finish():
                    T.add("act", lambda e: e.activation(out=rstd[:, :], in_=psN[:], func=AF.Ln, bias=EPS, scale=-1.0 / D), reads=["psN"], writes=["rstdf"])
                    T.add("act", lambda e: e.activation(out=rstd[:, :], in_=rstd[:, :], func=AF.Exp, scale=-0.5), reads=["rstdf"], writes=["rstdf"])
                    make_xn(g, 8, l, rstd[:, :], "rstdf", xnT)
                return piece, finish

            prep(0)
            for g in range(NTG):
                up(g)
                down(g, prep_pieces(g + 1) if g + 1 < NTG else None)

        def final_phase(normed):
            fence()
            ar.reset()
            yT = ar.f32(KC * TG).rearrange("p (c n) -> p c n", c=KC)
            ot = [ar.f32(D) for _ in range(2)]
            sqb = [ar.bf16(TG) for _ in range(2)]
            rstd = ar.f32(TG)
            psN = ps[7]
            outs = []

            def fin_tg(g):
                if normed:
                    rstd_for_tg(g, sqb, rstd[:, :], "rstdf", psN, "psN")
                    for c in range(KC):
                        T.add("dve", lambda e, c=c: e.scalar_tensor_tensor(out=yT[:, c, :], in0=hsT[:, c, g * TG:(g + 1) * TG], scalar=vecs[:, FIN + c:FIN + c + 1],
                              in1=rstd[:, :], op0=ALU.mult, op1=ALU.mult), reads=["hs.%d.%d" % (c, g), "rstdf", "vecs"], writes=["yT.%d" % c])
                else:
                    for c in range(KC):
                        T.add("dve", lambda e, c=c: e.tensor_copy(out=yT[:, c, :], in_=hsT[:, c, g * TG:(g + 1) * TG]),
                              reads=["hs.%d.%d" % (c, g)], writes=["yT.%d" % c])
                for tt in range(4):
                    t = g * 4 + tt
                    ob = ot[t % 2]
                    obn = "ot%d" % (t % 2)
                    for half in range(2):
                        pb = ps[(2 * t + half) % 4]
                        pbn = "psf%d" % ((2 * t + half) % 4)

                        def tr(e, pb=pb, half=half, tt=tt):
                            r = None
                            for q in range(4):
                                c = half * 4 + q
                                r = e.transpose(pb[:, q * 128:(q + 1) * 128], yT[:, c, tt * 128:(tt + 1) * 128], id32[:])
                            return r
                        T.add("pe", tr, reads=["yT.%d" % (half * 4 + q) for q in range(4)] + ["id32"], writes=[pbn])
                        if half == 0:
                            T.add("dve", lambda e, pb=pb, ob=ob: e.tensor_copy(out=ob[:, 0:512], in_=pb[:, :]), reads=[pbn], writes=[obn + "a"])
                        else:
                            T.add("act", lambda e, pb=pb, ob=ob: e.activation(out=ob[:, 512:1024], in_=pb[:, :], func=AF.Copy), reads=[pbn], writes=[obn + "b"])
                    T.add("sp", lambda e, ob=ob, t=t: e.dma_start(out=out_d[t * 128:(t + 1) * 128, :], in_=ob), reads=[obn + "a", obn + "b"],
                          writes=["out.%d" % t], dma="out%d" % (t % 2))
                    outs.append("out.%d" % t)
            for g in range(NTG):
                fin_tg(g)
            T.add("sp", lambda e: e.wait_ge(sems["sp"], 0), reads=outs, writes=[])

        sems = {e: st.enter_context(nc.semaphore("s_" + e)) for e in COMPUTE + ("sp",)}
        for (l, what) in layers:
            if what == "attn":
                attention_layer(l)
            else:
                ffn_layer(l)
        final_phase(final)

        dsems = {k: st.enter_context(nc.semaphore("d_" + k)) for k in T.dma_keys()}
        block = st.enter_context(nc.Block())
        T.emit(nc, block, sems, dsems)
    return nc


_CONST_CACHE = {}


def _consts():
    if "cb" not in _CONST_CACHE:
        p = np.arange(128)[:, None]
        q = np.arange(128)[None, :]
        cbm = np.zeros((128, 768), np.float32)
        cbm[:, 0:128] = np.eye(128, dtype=np.float32)
        cbm[:, 128:256] = np.where(p > q, NEG, 0.0)
        cbm[:, 256:384] = np.where(p >= q, NEG, 0.0)
        cbm[:, 384:512] = np.where(p >= q, -1.0, 0.0)
        cbm[:, 512:640] = -1.0
        cbm[:, 640:704] = 1.0
        _CONST_CACHE["cb"] = cbm
        _CONST_CACHE["ident"] = np.eye(128, dtype=np.float32)
    return _CONST_CACHE["cb"], _CONST_CACHE["ident"]


def layout_inputs(attn_norm, ffn_norm, final_norm, fox_w_qkvf, fox_b_f, fox_w_o,
                  sb_w_qkv, sb_w_o, ffn_w_up, ffn_w_conv, ffn_b_conv, ffn_w_down):
    f32 = np.float32
    m = {}
    cbm, ident = _consts()
    m["cb"] = cbm
    m["ident"] = ident
    NV = DEPTH * VS + 8
    vec = np.zeros((128, NV), f32)
    for l in range(DEPTH):
        o = l * VS
        vec[:, o:o + 8] = np.asarray(attn_norm[l], f32).reshape(8, 128).T
        vec[:, o + 8:o + 16] = np.asarray(ffn_norm[l], f32).reshape(8, 128).T
        wc = np.asarray(ffn_w_conv[l], f32).reshape(3, 44, 128).transpose(2, 1, 0).reshape(128, 132)
        vec[:, o + 16:o + 148] = wc
        vec[:, o + 148:o + 192] = np.asarray(ffn_b_conv[l], f32).reshape(44, 128).T
        if l % 2 == 0:
            bf = np.asarray(fox_b_f[l // 2], f32)
            for r0 in (0, 32, 64):
                vec[r0:r0 + 16, o + 192] = bf
    vec[:, DEPTH * VS:DEPTH * VS + 8] = np.asarray(final_norm, f32).reshape(8, 128).T
    m["vecs"] = vec
    for l in range(DEPTH):
        j = l // 2
        if l % 2 == 0:
            w = np.asarray(fox_w_qkvf[j], f32)
            wo = np.asarray(fox_w_o[j], f32)
            wfm = np.zeros((128, KC, 96), f32)
            wfr = w[:, 3 * D:3 * D + H].reshape(KC, 128, H).transpose(1, 0, 2)
            for r0 in (0, 32, 64):
                wfm[:, :, r0:r0 + 16] = wfr
            m["wf%d" % j] = wfm.reshape(128, KC * 96)
        else:
            w = np.asarray(sb_w_qkv[j], f32)
            wo = np.asarray(sb_w_o[j], f32)
        parts = [w[:, i * D:(i + 1) * D].reshape(KC, 128, H, DH).transpose(2, 1, 0, 3) for i in range(3)]
        m["wqkv%d" % l] = np.ascontiguousarray(np.concatenate(parts, axis=3)).reshape(H, 128, KC * 192)
        m["wo%d" % l] = np.ascontiguousarray(wo.reshape(8, 128, D).transpose(1, 0, 2)).reshape(128, 8 * D)
        wu = np.asarray(ffn_w_up[l], f32)
        pu = wu[:, 0:FF].reshape(KC, 128, FC, 128).transpose(2, 1, 0, 3)
        pg = wu[:, FF:2 * FF].reshape(KC, 128, FC, 128).transpose(2, 1, 0, 3)
        m["wup%d" % l] = np.ascontiguousarray(np.concatenate([pu, pg], axis=3)).reshape(FC, 128, KC * 256)
        wdn = np.asarray(ffn_w_down[l], f32)
        m["wdn%d" % l] = np.ascontiguousarray(wdn.reshape(FC, 128, KC, 128).transpose(2, 1, 0, 3)).reshape(KC, 128, FC * 128)
    return m


ALL_LAYERS = [(l, w) for l in range(DEPTH) for w in ("attn", "ffn")]


def kernel(x, attn_norm, ffn_norm, final_norm, fox_w_qkvf, fox_b_f, fox_w_o,
           sb_w_qkv, sb_w_o, ffn_w_up, ffn_w_conv, ffn_b_conv, ffn_w_down):
    x = np.asarray(x, np.float32)
    B, S, _ = x.shape
    shared = layout_inputs(attn_norm, ffn_norm, final_norm, fox_w_qkvf, fox_b_f, fox_w_o,
                           sb_w_qkv, sb_w_o, ffn_w_up, ffn_w_conv, ffn_b_conv, ffn_w_down)
    nc = build_program(S, ALL_LAYERS, True)
    in_maps = []
    for b in range(B):
        mm = dict(shared)
        mm["x"] = np.ascontiguousarray(x[b])
        in_maps.append(mm)
    res = run_bass_kernel_spmd(nc, in_maps, core_ids=list(range(B)))
    return np.stack([np.asarray(r["out"], np.float32) for r in res.results], axis=0)
```
